# Optimizing a Trainium2 kernel written in Bass

```python
import math
import jax, jax.numpy as jnp
from jax import lax
import numpy as np

D_MODEL = 1024
BATCH = 2
SEQ = 8192
DEPTH = 2

N_MEM = 256
EPS = 1e-6
MOBA_HEADS = 8
MOBA_HEAD_DIM = 64
MOBA_BLOCK = 256
MOBA_TOPK = 3
MOBA_Q_CHUNK = 64
A_WIDTH = MOBA_HEADS * MOBA_HEAD_DIM
POOL_WINDOWS = (2, 4, 8, 16)
POOL_GROUP = 128
POOL_WIDTH = POOL_GROUP * len(POOL_WINDOWS)
EV_IN = 3 * A_WIDTH + POOL_WIDTH
EV_MIX = A_WIDTH + POOL_WIDTH
SGU_GROUPS = 4
SGU_GROUP = 128
SGU_CHUNK = 128
SGU_WIDTH = SGU_GROUPS * SGU_GROUP
DN_HEADS = 4
DN_HEAD_DIM = 128
DN_WIDTH = DN_HEADS * DN_HEAD_DIM
DN_CONV = 4
DN_CHUNK = 64
OD_SPLITS = [2 * SGU_WIDTH, 2 * SGU_WIDTH + 3 * DN_WIDTH,
             2 * SGU_WIDTH + 4 * DN_WIDTH, 2 * SGU_WIDTH + 4 * DN_WIDTH + DN_HEADS]
OD_IN = 2 * SGU_WIDTH + 4 * DN_WIDTH + 2 * DN_HEADS
OD_MIX = SGU_WIDTH + DN_WIDTH
XATTN_HEADS = 4
XATTN_HEAD_DIM = D_MODEL // XATTN_HEADS
D_FF = 256 * ((8 * D_MODEL // 3 + 255) // 256)
FFN_CONV = 3
N_EVEN = (DEPTH + 1) // 2
N_ODD = DEPTH // 2

kernel_name = 'hybrid_moba_pool_sgu_gdeltanet'


def rmsnorm(x, g):
    xf = x.astype(jnp.float32)
    y = xf * lax.rsqrt(jnp.mean(xf * xf, axis=-1, keepdims=True) + EPS)
    return (y * g.astype(jnp.float32)).astype(x.dtype)


def causal_dwconv(x, w):
    K = w.shape[0]
    S = x.shape[1]
    xp = jnp.pad(x, ((0, 0), (K - 1, 0), (0, 0)))
    y = xp[:, 0:S] * w[0]
    for j in range(1, K):
        y = y + xp[:, j:j + S] * w[j]
    return y


def split_heads(t, n_heads, head_dim):
    B, S, _ = t.shape
    return t.reshape(B, S, n_heads, head_dim).transpose(0, 2, 1, 3)


def moba_attention(q, k, v):
    B, H, S, Dh = q.shape
    BS, QC = MOBA_BLOCK, MOBA_Q_CHUNK
    nb = -(-S // BS)
    pad = nb * BS - S
    k_blk = jnp.pad(k, ((0, 0), (0, 0), (0, pad), (0, 0))).reshape(B, H, nb, BS, Dh)
    v_blk = jnp.pad(v, ((0, 0), (0, 0), (0, pad), (0, 0))).reshape(B, H, nb, BS, Dh)
    k_mean = jnp.mean(k_blk.astype(jnp.float32), axis=3)
    q_blk_id = jnp.arange(S) // BS
    gate = jnp.einsum('bhsd,bhnd->bhsn', q.astype(jnp.float32), k_mean)
    fully_past = jnp.arange(nb)[None, :] < q_blk_id[:, None]
    gate = jnp.where(fully_past, gate, -jnp.inf)
    n_sel = min(MOBA_TOPK, nb)
    _, sel = lax.top_k(gate, n_sel)
    sel_valid = sel < q_blk_id[:, None]
    scale = Dh ** -0.5
    gather = jax.vmap(jax.vmap(lambda blk, ix: blk[ix]))

    def chunk(c):
        t0 = c * QC
        qc = lax.dynamic_slice_in_dim(q, t0, QC, axis=2)
        sc = lax.dynamic_slice_in_dim(sel, t0, QC, axis=2)
        vc = lax.dynamic_slice_in_dim(sel_valid, t0, QC, axis=2)
        own = t0 // BS
        k_own = lax.dynamic_index_in_dim(k_blk, own, axis=2, keepdims=False)
        v_own = lax.dynamic_index_in_dim(v_blk, own, axis=2, keepdims=False)
        k_sel = gather(k_blk, sc)
        v_sel = gather(v_blk, sc)
        s_sel = jnp.einsum('bhqd,bhqnkd->bhqnk', qc, k_sel).astype(jnp.float32) * scale
        s_sel = jnp.where(vc[..., None], s_sel, -jnp.inf).reshape(B, H, QC, n_sel * BS)
        tq = t0 + jnp.arange(QC)
        tk = own * BS + jnp.arange(BS)
        s_own = jnp.einsum('bhqd,bhkd->bhqk', qc, k_own).astype(jnp.float32) * scale
        s_own = jnp.where(tk[None, :] <= tq[:, None], s_own, -jnp.inf)
        p = jax.nn.softmax(jnp.concatenate([s_sel, s_own], axis=-1), axis=-1).astype(v.dtype)
        p_sel = p[..., :n_sel * BS].reshape(B, H, QC, n_sel, BS)
        p_own = p[..., n_sel * BS:]
        return (jnp.einsum('bhqnk,bhqnkd->bhqd', p_sel, v_sel)
                + jnp.einsum('bhqk,bhkd->bhqd', p_own, v_own))

    out = lax.map(chunk, jnp.arange(S // QC))
    return out.transpose(1, 2, 0, 3, 4).reshape(B, H, S, Dh)


def multiscale_pool(p, pool_w, pool_scale):
    B, S, _ = p.shape
    G = len(POOL_WINDOWS)
    pf = p.astype(jnp.float32).reshape(B, S, G, POOL_GROUP)
    cs = jnp.cumsum(pf, axis=1)
    t1 = jnp.arange(1, S + 1, dtype=jnp.float32)
    outs = []
    for g, w in enumerate(POOL_WINDOWS):
        c = cs[:, :, g]
        c_prev = jnp.pad(c, ((0, 0), (w, 0), (0, 0)))[:, :S]
        cnt = jnp.minimum(t1, float(w))[None, :, None]
        outs.append((c - c_prev) / cnt - pf[:, :, g])
    pooled = jnp.stack(outs, axis=2).astype(p.dtype)
    y = jnp.einsum('bsgc,gcd->bsgd', pooled, pool_w).reshape(B, S, POOL_WIDTH)
    return y * pool_scale


def spatial_gating(z, ln_g, ln_b, w_s, b_s):
    B, S, _ = z.shape
    u, v = jnp.split(z, 2, axis=-1)
    vf = v.astype(jnp.float32)
    mu = jnp.mean(vf, axis=-1, keepdims=True)
    var = jnp.mean(jnp.square(vf - mu), axis=-1, keepdims=True)
    vn = ((vf - mu) * lax.rsqrt(var + EPS) * ln_g + ln_b).astype(z.dtype)
    vn = vn.reshape(B, S // SGU_CHUNK, SGU_CHUNK, SGU_GROUPS, SGU_GROUP)
    causal = jnp.tril(jnp.ones((SGU_CHUNK, SGU_CHUNK), dtype=bool))
    w = jnp.where(causal[None], w_s, 0)
    s = jnp.einsum('gts,bnsgc->bntgc', w, vn) + b_s.T[None, None, :, :, None]
    return u * s.reshape(B, S, SGU_WIDTH)


def l2norm(t):
    return t * lax.rsqrt(jnp.sum(t * t, axis=-1, keepdims=True) + EPS)


def chunked_gated_delta_rule(q, k, v, g, beta):
    B, H, S, DK = q.shape
    DV = v.shape[-1]
    C = DN_CHUNK
    N = S // C
    q = q.reshape(B, H, N, C, DK)
    k = k.reshape(B, H, N, C, DK)
    v = v.reshape(B, H, N, C, DV)
    beta = beta.reshape(B, H, N, C)
    gc = jnp.cumsum(g.reshape(B, H, N, C), axis=-1)
    idx = jnp.arange(C)
    causal = idx[:, None] >= idx[None, :]
    strict = idx[:, None] > idx[None, :]
    decay = jnp.exp(jnp.where(causal, gc[..., :, None] - gc[..., None, :], -jnp.inf))
    kb = k * beta[..., None]
    a = jnp.where(strict, jnp.einsum('bhnid,bhnjd->bhnij', kb, k) * decay, 0.0)
    m = a + jnp.eye(C, dtype=a.dtype)
    rhs = jnp.concatenate([v * beta[..., None], kb * jnp.exp(gc)[..., None]], axis=-1)
    sol = lax.linalg.triangular_solve(m, rhs, left_side=True, lower=True, unit_diagonal=True)
    u, w = sol[..., :DV], sol[..., DV:]
    qk = jnp.einsum('bhnid,bhnjd->bhnij', q, k) * decay
    q_dec = q * jnp.exp(gc)[..., None]
    k_dec = k * jnp.exp(gc[..., -1:] - gc)[..., None]
    g_last = jnp.exp(gc[..., -1])

    def step(state, inp):
        qk_n, qd_n, kd_n, u_n, w_n, gl_n = inp
        v_new = u_n - jnp.einsum('bhck,bhkv->bhcv', w_n, state)
        o = (jnp.einsum('bhck,bhkv->bhcv', qd_n, state)
             + jnp.einsum('bhij,bhjv->bhiv', qk_n, v_new))
        state = state * gl_n[..., None, None] + jnp.einsum('bhck,bhcv->bhkv', kd_n, v_new)
        return state, o

    xs = (jnp.moveaxis(qk, 2, 0), jnp.moveaxis(q_dec, 2, 0), jnp.moveaxis(k_dec, 2, 0),
          jnp.moveaxis(u, 2, 0), jnp.moveaxis(w, 2, 0), jnp.moveaxis(g_last, 2, 0))
    state0 = jnp.zeros((B, H, DK, DV), q.dtype)
    _, o = lax.scan(step, state0, xs)
    return jnp.moveaxis(o, 0, 2).reshape(B, H, S, DV)


def gated_deltanet(qkv, gate, b_raw, a_raw, conv_w, a_log, dt_bias, norm_g):
    B, S, _ = qkv.shape
    dt = qkv.dtype
    qkv = jax.nn.silu(causal_dwconv(qkv, conv_w))
    q, k, v = jnp.split(qkv, 3, axis=-1)
    q = l2norm(split_heads(q, DN_HEADS, DN_HEAD_DIM).astype(jnp.float32)) * DN_HEAD_DIM ** -0.5
    k = l2norm(split_heads(k, DN_HEADS, DN_HEAD_DIM).astype(jnp.float32))
    v = split_heads(v, DN_HEADS, DN_HEAD_DIM).astype(jnp.float32)
    beta = jax.nn.sigmoid(b_raw.astype(jnp.float32)).transpose(0, 2, 1)
    g = (-jnp.exp(a_log.astype(jnp.float32))
         * jax.nn.softplus(a_raw.astype(jnp.float32) + dt_bias.astype(jnp.float32))).transpose(0, 2, 1)
    o = chunked_gated_delta_rule(q, k, v, g, beta)
    o = o * lax.rsqrt(jnp.mean(o * o, axis=-1, keepdims=True) + EPS) * norm_g.astype(jnp.float32)
    o = o.transpose(0, 2, 1, 3) * jax.nn.silu(gate.astype(jnp.float32).reshape(B, S, DN_HEADS, DN_HEAD_DIM))
    return o.reshape(B, S, DN_WIDTH).astype(dt)


def memory_cross_attention(xn, mem_n, wq, wkv, wo):
    B, S, D = xn.shape
    M = mem_n.shape[1]
    q = (xn @ wq).reshape(B, S, XATTN_HEADS, XATTN_HEAD_DIM)
    k, v = jnp.split(mem_n @ wkv, 2, axis=-1)
    k = k.reshape(B, M, XATTN_HEADS, XATTN_HEAD_DIM)
    v = v.reshape(B, M, XATTN_HEADS, XATTN_HEAD_DIM)
    s = jnp.einsum('bshd,bmhd->bhsm', q, k).astype(jnp.float32) * XATTN_HEAD_DIM ** -0.5
    p = jax.nn.softmax(s, axis=-1).astype(v.dtype)
    o = jnp.einsum('bhsm,bmhd->bshd', p, v).reshape(B, S, D)
    return o @ wo


def conv_ffn(xn, w_up, conv_w, w_down):
    h = causal_dwconv(xn @ w_up, conv_w)
    g, u = jnp.split(h, 2, axis=-1)
    return (jax.nn.silu(g) * u) @ w_down


def setup_inputs(seed: int = 0) -> dict:
    key = jax.random.key(seed)
    ks = iter(jax.random.split(key, 32))
    D = D_MODEL

    def nrm(shape, scale):
        return jax.random.normal(next(ks), shape, jnp.float32) * scale

    def gain(shape):
        return 1.0 + nrm(shape, 0.02)

    x = nrm((BATCH, SEQ, D), 1.0)
    mem = nrm((BATCH, N_MEM, D), 1.0)
    mem_norm = gain((D,))
    norm_mix = gain((DEPTH, D))
    norm_xattn = gain((DEPTH, D))
    norm_ffn = gain((DEPTH, D))
    ev_w_in = nrm((N_EVEN, D, EV_IN), D ** -0.5)
    pool_w = nrm((N_EVEN, len(POOL_WINDOWS), POOL_GROUP, POOL_GROUP), POOL_GROUP ** -0.5)
    pool_scale = 1.0 + nrm((N_EVEN, POOL_WIDTH), 0.1)
    ev_w_out = nrm((N_EVEN, EV_MIX, D), EV_MIX ** -0.5)
    od_w_in = nrm((N_ODD, D, OD_IN), D ** -0.5)
    sgu_ln_g = gain((N_ODD, SGU_WIDTH))
    sgu_ln_b = nrm((N_ODD, SGU_WIDTH), 0.02)
    sgu_w = nrm((N_ODD, SGU_GROUPS, SGU_CHUNK, SGU_CHUNK), SGU_CHUNK ** -0.5)
    sgu_b = 1.0 + nrm((N_ODD, SGU_GROUPS, SGU_CHUNK), 0.02)
    dn_conv = nrm((N_ODD, DN_CONV, 3 * DN_WIDTH), DN_CONV ** -0.5)
    dn_a_log = jnp.log(jax.random.uniform(next(ks), (N_ODD, DN_HEADS), jnp.float32, 1.0, 16.0))
    dt = jnp.exp(jax.random.uniform(next(ks), (N_ODD, DN_HEADS), jnp.float32,
                                    math.log(1e-3), math.log(1e-1)))
    dn_dt_bias = dt + jnp.log(-jnp.expm1(-dt))
    dn_norm_g = gain((N_ODD, DN_HEAD_DIM))
    od_w_out = nrm((N_ODD, OD_MIX, D), OD_MIX ** -0.5)
    xattn_wq = nrm((DEPTH, D, D), D ** -0.5)
    xattn_wkv = nrm((DEPTH, D, 2 * D), D ** -0.5)
    xattn_wo = nrm((DEPTH, D, D), D ** -0.5)
    ffn_w_up = nrm((DEPTH, D, 2 * D_FF), D ** -0.5)
    ffn_conv = nrm((DEPTH, FFN_CONV, 2 * D_FF), FFN_CONV ** -0.5)
    ffn_w_down = nrm((DEPTH, D_FF, D), D_FF ** -0.5)
    final_norm = gain((D,))
    return {'x': x, 'mem': mem, 'mem_norm': mem_norm, 'norm_mix': norm_mix,
            'norm_xattn': norm_xattn, 'norm_ffn': norm_ffn,
            'ev_w_in': ev_w_in, 'pool_w': pool_w, 'pool_scale': pool_scale, 'ev_w_out': ev_w_out,
            'od_w_in': od_w_in, 'sgu_ln_g': sgu_ln_g, 'sgu_ln_b': sgu_ln_b, 'sgu_w': sgu_w,
            'sgu_b': sgu_b, 'dn_conv': dn_conv, 'dn_a_log': dn_a_log, 'dn_dt_bias': dn_dt_bias,
            'dn_norm_g': dn_norm_g, 'od_w_out': od_w_out,
            'xattn_wq': xattn_wq, 'xattn_wkv': xattn_wkv, 'xattn_wo': xattn_wo,
            'ffn_w_up': ffn_w_up, 'ffn_conv': ffn_conv, 'ffn_w_down': ffn_w_down,
            'final_norm': final_norm}


def reference(x, mem, mem_norm, norm_mix, norm_xattn, norm_ffn,
              ev_w_in, pool_w, pool_scale, ev_w_out,
              od_w_in, sgu_ln_g, sgu_ln_b, sgu_w, sgu_b,
              dn_conv, dn_a_log, dn_dt_bias, dn_norm_g, od_w_out,
              xattn_wq, xattn_wkv, xattn_wo,
              ffn_w_up, ffn_conv, ffn_w_down, final_norm):
    B, S, _ = x.shape
    mem_n = rmsnorm(mem, mem_norm)
    h = x
    for layer in range(DEPTH):
        xn = rmsnorm(h, norm_mix[layer])
        i = layer // 2
        if layer % 2 == 0:
            proj = xn @ ev_w_in[i]
            q, k, v, p = jnp.split(proj, [A_WIDTH, 2 * A_WIDTH, 3 * A_WIDTH], axis=-1)
            a_out = moba_attention(split_heads(q, MOBA_HEADS, MOBA_HEAD_DIM),
                                   split_heads(k, MOBA_HEADS, MOBA_HEAD_DIM),
                                   split_heads(v, MOBA_HEADS, MOBA_HEAD_DIM))
            a_out = a_out.transpose(0, 2, 1, 3).reshape(B, S, A_WIDTH)
            b_out = multiscale_pool(p, pool_w[i], pool_scale[i])
            mix = jnp.concatenate([a_out, b_out], axis=-1) @ ev_w_out[i]
        else:
            proj = xn @ od_w_in[i]
            z, qkv, gate, b_raw, a_raw = jnp.split(proj, OD_SPLITS, axis=-1)
            c_out = spatial_gating(jax.nn.gelu(z), sgu_ln_g[i], sgu_ln_b[i], sgu_w[i], sgu_b[i])
            d_out = gated_deltanet(qkv, gate, b_raw, a_raw, dn_conv[i], dn_a_log[i],
                                   dn_dt_bias[i], dn_norm_g[i])
            mix = jnp.concatenate([c_out, d_out], axis=-1) @ od_w_out[i]
        h = h + mix
        h = h + memory_cross_attention(rmsnorm(h, norm_xattn[layer]), mem_n,
                                       xattn_wq[layer], xattn_wkv[layer], xattn_wo[layer])
        h = h + conv_ffn(rmsnorm(h, norm_ffn[layer]), ffn_w_up[layer], ffn_conv[layer], ffn_w_down[layer])
    return rmsnorm(h, final_norm)
```

```python
import numpy as np
import ml_dtypes
from contextlib import ExitStack
import concourse.bass as bass
import concourse.mybir as mybir
from concourse.bass_utils import run_bass_kernel_spmd

F32 = mybir.dt.float32
BF16 = mybir.dt.bfloat16
AF = mybir.ActivationFunctionType
ALU = mybir.AluOpType
AX = mybir.AxisListType

D = 1024
NMEM = 256
EPS = 1e-6
DFF = 2816
NEG = -30000.0


class Res:
    __slots__ = ("name", "writers", "readers", "dsem")

    def __init__(self, name):
        self.name = name
        self.writers = []
        self.readers = []
        self.dsem = None


class Op:
    __slots__ = ("eng", "fn", "deps", "signal", "count", "dtok", "idx", "after", "dur", "gidx",
                 "nun", "rdy", "fin", "users")

    def __init__(self, eng, fn):
        self.eng = eng
        self.fn = fn
        self.deps = []
        self.signal = False
        self.count = 0
        self.dtok = None
        self.idx = 0
        self.after = []
        self.dur = 300.0


ENGS = ("pe", "act", "dve", "pool", "sp")


class Sched:
    def __init__(self, nc, n_dma_sems=40):
        self.nc = nc
        self.ops = {e: [] for e in ENGS}
        self.n_dma_sems = n_dma_sems
        self.dma_counts = [0] * n_dma_sems
        self.n_sp_sems = n_dma_sems - 8
        self.free_dsems = list(range(self.n_sp_sems))
        self.free_dsems_pool = list(range(self.n_sp_sems, n_dma_sems))
        self.phase_res = []
        self.all_res = []
        self.reorder_on = True
        self.seg_flags = []

    def res(self, name):
        r = Res(name)
        self.all_res.append(r)
        return r

    def _add(self, eng, fn, reads, writes, acc):
        o = Op(eng, fn)
        o.idx = len(self.ops[eng])
        deps = []
        for r in reads:
            deps.extend(r.writers)
        for w in writes:
            deps.extend(w.readers)
            if not acc:
                deps.extend(w.writers)
            elif acc == 'dma':
                deps.extend(t for t in w.writers if t[0] != 'D')
            else:
                for t in w.writers:
                    if t[0] == 'E' and t[1].eng == eng:
                        o.after.append(t[1])
                    else:
                        deps.append(t)
        o.deps = deps
        self.ops[eng].append(o)
        return o

    def _commit(self, tok, reads, writes):
        for r in reads:
            r.readers.append(tok)
        for w in writes:
            if w.readers:
                w.writers = [tok]
                w.readers = []
            else:
                w.writers.append(tok)
                if len(w.writers) > 64:
                    w.writers = w.writers[-64:]

    def op(self, eng, fn, reads=(), writes=(), acc=False):
        o = self._add(eng, fn, reads, writes, acc)
        self._commit(('E', o), reads, writes)
        return o

    def dma(self, eng, out, in_, reads=(), writes=(), sem_res=None):
        sr = sem_res if sem_res is not None else writes[0]
        if sr.dsem is None:
            sr.dsem = (self.free_dsems_pool if eng == "pool" else self.free_dsems).pop(0)
        o = self._add(eng, lambda e, out=out, in_=in_: e.dma_start(out=out, in_=in_), reads, writes, 'dma')
        self.dma_counts[sr.dsem] += 16
        o.dtok = (sr.dsem, self.dma_counts[sr.dsem])
        self._commit(('D', sr.dsem, o.dtok[1]), reads, writes)
        return o

    def end_phase(self):
        toks = []
        for e in ENGS:
            if self.ops[e]:
                last = self.ops[e][-1]
                if last.dtok is None:
                    toks.append(('E', last))
        for i in range(self.n_dma_sems):
            if self.dma_counts[i] > 0:
                toks.append(('D', i, self.dma_counts[i]))
        self.seg_flags.append(self.reorder_on)
        for e in ENGS:
            o = Op(e, None)
            o.idx = len(self.ops[e])
            o.deps = list(toks)
            self.ops[e].append(o)
        for r in self.all_res:
            r.dsem = None
            r.writers = []
            r.readers = []
        self.free_dsems = list(range(self.n_sp_sems))
        self.free_dsems_pool = list(range(self.n_sp_sems, self.n_dma_sems))

    def reorder(self, window=24):
        import heapq
        dma_of = {}
        for e in ENGS:
            for o in self.ops[e]:
                if o.dtok is not None:
                    dma_of[o.dtok] = o
        pos = {e: 0 for e in ENGS}
        new_ops = {e: [] for e in ENGS}
        segi = -1
        while any(pos[e] < len(self.ops[e]) for e in ENGS):
            segi += 1
            seg = {}
            for e in ENGS:
                lst = self.ops[e]
                i = pos[e]
                j = i
                while j < len(lst) and lst[j].fn is not None:
                    j += 1
                seg[e] = lst[i:j]
                pos[e] = j + 1 if j < len(lst) else j
                bar = lst[j] if j < len(lst) else None
                seg[e + "_bar"] = bar
            if segi < len(self.seg_flags) and not self.seg_flags[segi]:
                self._fix_barrier(seg, {e: seg[e] for e in ENGS})
                for e in ENGS:
                    new_ops[e].extend(seg[e])
                    if seg[e + "_bar"] is not None:
                        new_ops[e].append(seg[e + "_bar"])
                continue
            allops = [o for e in ENGS for o in seg[e]]
            inseg = set(id(o) for o in allops)
            for o in allops:
                o.users = []
                o.fin = None
            for o in allops:
                n = 0
                dl = []
                for t in o.deps:
                    d = t[1] if t[0] == 'E' else dma_of.get((t[1], t[2]))
                    if d is not None and id(d) in inseg and d.fn is not None:
                        dl.append(d)
                for d in o.after:
                    if id(d) in inseg:
                        dl.append(d)
                o.nun = len(dl)
                o.rdy = 0.0
                for d in dl:
                    d.users.append(o)
            free = {e: 0.0 for e in ENGS}
            pend = {e: list(seg[e]) for e in ENGS}
            head = {e: 0 for e in ENGS}
            issued = {e: [] for e in ENGS}
            remaining = len(allops)
            while remaining:
                best = None
                for e in ENGS:
                    lst = pend[e]
                    h = head[e]
                    while h < len(lst) and lst[h] is None:
                        h += 1
                    head[e] = h
                    if h >= len(lst):
                        continue
                    w = 1 if e == "sp" else window
                    cnt = 0
                    i = h
                    while i < len(lst) and cnt < w:
                        o = lst[i]
                        if o is not None:
                            cnt += 1
                            if o.nun == 0:
                                stt = o.rdy if o.rdy > free[e] else free[e]
                                if best is None or stt < best[0]:
                                    best = (stt, e, i)
                                if stt <= free[e]:
                                    break
                        i += 1
                if best is None:
                    raise RuntimeError("reorder: no schedulable op (cyclic deps?)")
                stt, e, i = best
                o = pend[e][i]
                pend[e][i] = None
                issued[e].append(o)
                remaining -= 1
                if o.dtok is not None:
                    free[e] = stt + 60.0
                    o.fin = stt + 2500.0 + o.dur
                else:
                    free[e] = stt + o.dur
                    o.fin = stt + o.dur + 150.0
                for u in o.users:
                    u.nun -= 1
                    if o in u.after and o not in [ (t[1] if t[0] == 'E' else None) for t in u.deps]:
                        r = stt
                    else:
                        r = o.fin
                    if r > u.rdy:
                        u.rdy = r
            self._fix_barrier(seg, issued)
            for e in ENGS:
                new_ops[e].extend(issued[e])
                if seg[e + "_bar"] is not None:
                    new_ops[e].append(seg[e + "_bar"])
        self.ops = new_ops

    @staticmethod
    def _fix_barrier(seg, order):
        etoks = []
        for e in ENGS:
            for o in reversed(order[e]):
                if o.fn is not None and o.dtok is None:
                    etoks.append(('E', o))
                    break
        for e in ENGS:
            bar = seg[e + "_bar"]
            if bar is not None:
                bar.deps = [t for t in bar.deps if t[0] == 'D'] + etoks

    def emit(self, esems, dsems):
        nc = self.nc
        if getattr(self, "do_reorder", True):
            self.reorder()
        for e in ENGS:
            for o in self.ops[e]:
                for t in o.deps:
                    if t[0] == 'E':
                        t[1].signal = True
        for e in ENGS:
            c = 0
            for o in self.ops[e]:
                if o.signal and o.fn is not None and o.dtok is None:
                    c += 1
                    o.count = c
                elif o.signal:
                    o.count = c
        sched = self

        def run(e_name, eng):
            seen = {}
            for o in sched.ops[e_name]:
                need = {}
                for t in o.deps:
                    if t[0] == 'E':
                        d = t[1]
                        if d.count == 0:
                            continue
                        key = ('E', d.eng)
                        val = d.count
                    else:
                        key = ('D', t[1])
                        val = t[2]
                    if need.get(key, 0) < val:
                        need[key] = val
                for key, val in need.items():
                    if seen.get(key, 0) >= val:
                        continue
                    seen[key] = val
                    sem = esems[key[1]] if key[0] == 'E' else dsems[key[1]]
                    eng.wait_ge(sem, val)
                if o.fn is None:
                    continue
                ins = o.fn(eng)
                if o.dtok is not None:
                    ins.then_inc(dsems[o.dtok[0]], 16)
                elif o.signal:
                    ins.then_inc(esems[e_name], 1)

        with nc.Block() as block:
            @block.tensor
            def _(eng):
                run("pe", eng)

            @block.scalar
            def _(eng):
                run("act", eng)

            @block.vector
            def _(eng):
                run("dve", eng)

            @block.gpsimd
            def _(eng):
                run("pool", eng)

            @block.sync
            def _(eng):
                run("sp", eng)


class Buf:
    def __init__(self, S, t, name):
        self.t = t
        self.r = S.res(name)

    def __getitem__(self, k):
        return self.t[k]


class K:
    def __init__(self, nc):
        self.nc = nc
        self.S = Sched(nc)
        self.stack = ExitStack()
        self.n = 0

    def sb(self, shape, dt, name=None, ctx=None):
        self.n += 1
        name = f"{name or 'sb'}_{self.n}"
        t = (ctx or self.stack).enter_context(self.nc.sbuf_tensor(name, list(shape), dt))
        return Buf(self.S, t, name)

    def psum(self, shape, dt=F32, name=None, ctx=None):
        self.n += 1
        name = f"{name or 'ps'}_{self.n}"
        t = (ctx or self.stack).enter_context(self.nc.psum_tensor(name, list(shape), dt))
        return Buf(self.S, t, name)

    def dram(self, shape, dt, name):
        t = self.nc.dram_tensor(name, list(shape), dt)
        b = Buf(self.S, t.ap(), name)
        return b

    def _rw(self, reads, writes):
        return [b.r for b in reads], [b.r for b in writes]

    def op(self, eng, fn, reads, writes, acc=False, dur=None):
        r, w = self._rw(reads, writes)
        o = self.S.op(eng, fn, r, w, acc)
        if dur is not None:
            o.dur = dur
        return o

    @staticmethod
    def _fsz(ap):
        n = 1
        for d in list(ap.shape)[1:]:
            n *= int(d)
        return n

    def dma(self, out, in_, reads, writes, eng="sp", sem=None):
        r, w = self._rw(reads, writes)
        return self.S.dma(eng, out, in_, r, w, sem.r if sem is not None else None)

    def mm(self, out, lhsT, rhs, start, stop, reads, writes):
        return self.op("pe", lambda e: e.matmul(out, lhsT, rhs, start=start, stop=stop), reads, writes, acc=True,
                       dur=70.0 + 0.45 * self._fsz(rhs) * (4 if rhs.dtype == F32 else 1))

    def tr(self, out, in_, ident, reads, writes):
        return self.op("pe", lambda e: e.transpose(out, in_, ident), reads, writes, acc=True, dur=130.0)

    def act(self, out, in_, func, reads, writes, bias=None, scale=None, accum_out=None, eng="act"):
        kw = {}
        if bias is not None:
            kw["bias"] = bias
        if scale is not None:
            kw["scale"] = scale
        if accum_out is not None:
            kw["accum_out"] = accum_out
        return self.op("act", lambda e: e.activation(out, in_, func, **kw), reads, writes,
                       dur=220.0 + 0.95 * self._fsz(out))

    def tt(self, eng, out, in0, in1, op, reads, writes):
        return self.op(eng, lambda e: e.tensor_tensor(out, in0, in1, op), reads, writes,
                       dur=(120.0 + 1.0 * self._fsz(out)) * (2.0 if eng == "pool" else 1.0))

    def ts(self, eng, out, in0, s1, s2, op0, op1, reads, writes):
        du = 120.0 + 0.9 * self._fsz(out)
        if op1 is None:
            return self.op(eng, lambda e: e.tensor_scalar(out, in0, s1, None, op0), reads, writes, dur=du)
        return self.op(eng, lambda e: e.tensor_scalar(out, in0, s1, s2, op0, op1), reads, writes, dur=du)

    def stt(self, eng, out, in0, scalar, in1, op0, op1, reads, writes):
        return self.op(eng, lambda e: e.scalar_tensor_tensor(out, in0, scalar, in1, op0, op1), reads, writes,
                       dur=120.0 + 1.4 * self._fsz(out))

    def copy(self, eng, out, in_, reads, writes):
        if eng == "act":
            return self.op("act", lambda e: e.copy(out, in_), reads, writes, dur=220.0 + 0.8 * self._fsz(out))
        return self.op(eng, lambda e: e.tensor_copy(out, in_), reads, writes,
                       dur=(120.0 + 0.7 * self._fsz(out)) * (2.0 if eng == "pool" else 1.0))

    def memset(self, eng, ap, val, writes):
        return self.op(eng, lambda e: e.memset(ap, val), [], writes)


TT = 512


class Ctx:
    pass


def load_w(k, ctx, src, kin, n, name, gain=None, stg=None, col0=0, w=None):
    kc_n = kin // 128
    if w is None:
        w = k.sb([128, kc_n, n], BF16, name, ctx)
    i = 0
    for kc in range(kc_n):
        for n0 in range(0, n, 2048):
            wd = min(2048, n - n0)
            s = stg[i % len(stg)]
            k.dma(s[:, 0:wd], src[kc * 128:(kc + 1) * 128, col0 + n0:col0 + n0 + wd], [], [s],
                  eng="sp")
            if gain is None:
                k.copy(("dve", "pool")[i % 2], w[:, kc, n0:n0 + wd], s[:, 0:wd], [s], [w])
            elif i % 2 == 0:
                k.ts("dve", w[:, kc, n0:n0 + wd], s[:, 0:wd], gain[:, kc:kc + 1], None, ALU.mult, None,
                     [s, gain], [w])
            else:
                k.act(w[:, kc, n0:n0 + wd], s[:, 0:wd], AF.Copy, [s, gain], [w], scale=gain[:, kc:kc + 1])
            i += 1
    return w


def rmsnorm(k, C, x, xn, T, kc_n=8, dim=D, nb=None, xoff=0):
    sq, r, ps = nb if nb is not None else (C.sq, C.rr, C.ps_norm)
    k.act(sq[:, 0:kc_n, 0:T], x[:, 0:kc_n, 0:T], AF.Square, [x], [sq])
    for kc in range(kc_n):
        k.mm(ps[:, 0:T], C.ones_bf[:, :], sq[:, kc, 0:T], kc == 0, kc == kc_n - 1, [sq, C.ones_bf], [ps])
    k.act(r[:, 0:T], ps[:, 0:T], AF.Ln, [ps, C.eps_col], [r], bias=C.eps_col[:, 0:1], scale=1.0 / dim)
    k.act(r[:, 0:T], r[:, 0:T], AF.Exp, [r], [r], scale=-0.5)
    k.tt("dve", xn[:, 0:kc_n, xoff:xoff + T], x[:, 0:kc_n, 0:T],
         r[:, 0:T].rearrange("p (o t) -> p o t", o=1).to_broadcast([128, kc_n, T]), ALU.mult, [x, r], [xn])


def run_pipelined(make_gen, n, lag, maxlive=2):
    gens = [make_gen(t) for t in range(n)]
    prog = [0] * n
    done = [False] * n
    first = 0
    while first < n:
        for t in range(first, n):
            if t > first and not (done[t - 1] or prog[t - 1] >= lag):
                break
            if t - maxlive >= 0 and not done[t - maxlive]:
                break
            if done[t]:
                continue
            try:
                next(gens[t])
                prog[t] += 1
            except StopIteration:
                done[t] = True
        while first < n and done[first]:
            first += 1


def proj_fm(k, ps, w, xn, m0, T, kc_n, reads, col=None):
    for kc in range(kc_n):
        k.mm(ps[:, 0:T], w[:, kc, m0:m0 + 128], xn[:, kc, 0:T], kc == 0, kc == kc_n - 1, reads, [ps])


def phase_l0_inproj(k, C, S):
    nc = k.nc
    NT = S // TT
    with ExitStack() as ctx:
        C.sq = k.sb([128, 8, TT], BF16, "sq", ctx)
        C.rr = k.sb([128, TT], F32, "rr", ctx)
        stg = [k.sb([128, 2048], F32, "stg", ctx) for _ in range(2)]
        gain = k.sb([128, 8], F32, "gain", ctx)
        k.dma(gain[:, :], C.norm_mix[0], [], [gain])
        w_in = load_w(k, ctx, C.ev_w_in, D, 2048, "w_in", gain=gain, stg=stg)
        pw_f = k.sb([128, 4, 128], F32, "pw_f", ctx)
        pw = k.sb([128, 4, 128], BF16, "pw", ctx)
        k.dma(pw_f[:, :, :], C.pool_w.rearrange("g c d -> c g d"), [], [pw_f])
        k.copy("dve", pw[:, :, :], pw_f[:, :, :], [pw_f], [pw])
        pscale = k.sb([128, 4], F32, "pscale", ctx)
        k.dma(pscale[:, :], C.pool_scale, [], [pscale])
        corr = k.sb([128, 4, 16], F32, "corr", ctx)
        k.dma(corr[:, :, :], C.pool_corr, [], [corr])

        xt = [k.sb([128, 8, TT], F32, "xt", ctx) for _ in range(2)]
        xn = k.sb([128, 8, TT], BF16, "xn", ctx)
        qk = [k.sb([128, 8, TT], BF16, "qk", ctx) for _ in range(2)]
        vt = [k.sb([128, 4, 8, 65], BF16, "vt", ctx) for _ in range(2)]
        for b in vt:
            k.memset("pool", b[:, :, :, :], 1.0, [b])
        pb = k.sb([128, 4, 16 + TT], F32, "pb", ctx)
        k.memset("pool", pb[:, :, :], 0.0, [pb])
        wa = k.sb([128, 16 + TT], F32, "wa", ctx)
        wb = k.sb([128, 16 + TT], F32, "wb", ctx)
        k.memset("pool", wa[:, :], 0.0, [wa])
        k.memset("pool", wb[:, :], 0.0, [wb])
        pooled = k.sb([128, 4, TT], BF16, "pooled", ctx)
        bout = [k.sb([128, 4, TT], BF16, "bout", ctx) for _ in range(2)]
        ksum = k.sb([128, 4, 2], F32, "ksum", ctx)
        ps = C.ps
        pi = 0

        def load_x(t):
            b = xt[t % 2]
            k.dma(b[:, :, :], C.xT[:, t * TT:(t + 1) * TT].rearrange("(c p) t -> p c t", p=128), [], [b])

        load_x(0)
        for t in range(NT):
            if t + 1 < NT:
                load_x(t + 1)
            x = xt[t % 2]
            t0 = t * TT
            rmsnorm(k, C, x, xn, TT)
            qkb = qk[t % 2]
            for m in range(8):
                p = ps[pi % 6]; pi += 1
                proj_fm(k, p, w_in, xn, m * 128, TT, 8, [w_in, xn])
                k.copy("act", qkb[:, m, :], p[:, 0:TT], [p], [qkb])
            k.op("dve", lambda e, qkb=qkb: e.tensor_reduce(
                ksum[:, :, :], qkb[:, 4:8, :].rearrange("p c (b t) -> p c b t", b=2), AX.X, ALU.add),
                [qkb], [ksum])
            k.ts("dve", C.kmean[:, :, 2 * t:2 * t + 2], ksum[:, :, :], 1.0 / 256, None, ALU.mult, None,
                 [ksum], [C.kmean])
            for c in range(4):
                for hp in range(2):
                    k.dma(C.qaugT[2 * c + hp, 0:64, t0:t0 + TT], qkb[hp * 64:(hp + 1) * 64, c, :],
                          [qkb], [C.qaugT_r[t]], sem=qkb)
            k.dma(C.kT[:, t0:t0 + TT].rearrange("(c p) t -> p c t", p=128), qkb[:, 4:8, :],
                  [qkb], [C.kT_r[t]], sem=qkb)
            vb = vt[t % 2]
            for sub in range(4):
                p = ps[pi % 6]; pi += 1
                for kc in range(8):
                    k.mm(p[:, 0:512], xn[:, kc, sub * 128:(sub + 1) * 128], w_in[:, kc, 1024:1536],
                         kc == 0, kc == 7, [w_in, xn], [p])
                k.copy("dve", vb[:, sub, :, 1:65], p[:, 0:512].rearrange("p (h d) -> p h d", h=8), [p], [vb])
            k.dma(C.vaug[t0:t0 + TT, :].rearrange("(s p) c -> p s c", p=128),
                  vb[:, :, :, :].rearrange("p s h d -> p s (h d)"), [vb], [C.vaug_r[t]], sem=vb)
            for g in range(4):
                p = ps[pi % 6]; pi += 1
                proj_fm(k, p, w_in, xn, 1536 + g * 128, TT, 8, [w_in, xn])
                k.copy("act", pb[:, g, 16:16 + TT], p[:, 0:TT], [p], [pb])
            W = 16 + TT
            for g in range(4):
                eng = ("dve", "pool")[g % 2]
                src = pb
                cur = pb[:, g, :]
                bufs = [wa, wb]
                for lvl in range(g + 1):
                    sh = 1 << lvl
                    dst = bufs[lvl % 2]
                    k.tt(eng, dst[:, sh:W], cur[:, sh:W], cur[:, 0:W - sh], ALU.add, [src], [dst])
                    src = dst
                    cur = dst[:, :]
                wnd = 2 << g
                if t == 0:
                    k.tt(eng, cur[:, 16:32], cur[:, 16:32], corr[:, g, :], ALU.mult, [src, corr], [src])
                k.stt("dve", pooled[:, g, :], cur[:, 16:W], 1.0 / wnd, pb[:, g, 16:W], ALU.mult, ALU.subtract,
                      [src, pb], [pooled])
            k.copy("pool", pb[:, :, 0:16], pb[:, :, TT:TT + 16], [pb], [pb])
            bo = bout[t % 2]
            for g in range(4):
                p = ps[pi % 6]; pi += 1
                k.mm(p[:, 0:TT], pw[:, g, :], pooled[:, g, :], True, True, [pw, pooled], [p])
                k.ts("dve", bo[:, g, :], p[:, 0:TT], pscale[:, g:g + 1], None, ALU.mult, None, [p, pscale], [bo])
            k.dma(C.mixT[512:1024, t0:t0 + TT].rearrange("(g p) t -> p g t", p=128), bo[:, :, :],
                  [bo], [C.mixT_r[t]], sem=bo)
    k.S.end_phase()


class RB:
    def __init__(self, S, name):
        self.r = S.res(name)


def host_consts(S):
    c = {}
    c["ident_bf"] = np.eye(128, dtype=np.float32).astype(ml_dtypes.bfloat16)
    c["ident_f"] = np.eye(128, dtype=np.float32)
    corr = np.ones((4, 16), np.float32)
    for g, w in enumerate((2, 4, 8, 16)):
        for t in range(16):
            corr[g, t] = w / min(t + 1, w)
    c["pool_corr"] = np.ascontiguousarray(np.broadcast_to(corr[None], (128, 4, 16)))
    vb = np.concatenate([np.zeros(32, np.float32), np.full(32, -1e30, np.float32)])
    c["validbias"] = np.ascontiguousarray(np.broadcast_to(vb[None], (128, 64)))
    khot = np.zeros((32, S), np.float32)
    for j in range(S // 256):
        khot[j, j * 256:(j + 1) * 256] = 30000.0
    c["khot"] = khot.astype(ml_dtypes.bfloat16)
    kk = np.arange(128)[:, None, None]
    jj = np.arange(2)[None, :, None]
    qq = np.arange(256)[None, None, :]
    c["causal01"] = ((jj * 128 + kk) <= qq).astype(np.float32).astype(ml_dtypes.bfloat16)
    a = np.arange(128)
    c["tril01"] = (a[None, :] <= a[:, None]).astype(np.float32)
    c["tri_le"] = (a[:, None] <= a[None, :]).astype(np.float32)
    c["causT_neg"] = np.where(a[None, :] >= a[:, None], 0.0, -1e4).astype(np.float32)
    c["strictT01"] = (a[None, :] > a[:, None]).astype(np.float32)
    lv = np.zeros((128, 8, 128), np.float32)
    ii, jj2 = a[:, None], a[None, :]
    for l in range(7):
        b = 1 << l
        m = ((ii // (2 * b)) == (jj2 // (2 * b))) & ((ii % (2 * b)) >= b) & ((jj2 % (2 * b)) < b)
        lv[:, l, :] = m.T
        if l == 0:
            lv[:, 7, :] = m
    c["lvlmask"] = lv.astype(ml_dtypes.bfloat16)
    return c


def build(S, phases=None, debug_out=()):
    nc = bass.Bass("TRN2", target_bir_lowering=False)
    k = K(nc)
    C = Ctx()
    NT = S // TT

    def ext(name, shape, dt=F32):
        return Buf(k.S, nc.dram_tensor(name, list(shape), dt, kind="ExternalInput").ap(), name)

    def scratch(name, shape, dt):
        kind = "ExternalOutput" if name in debug_out else "Internal"
        return nc.dram_tensor(name, list(shape), dt, kind=kind).ap()

    C.xT = ext("xT", [D, S]).t
    C.norm_mix = [ext(f"norm_mix{l}", [128, 8]).t for l in range(2)]
    C.ev_w_in = ext("ev_w_in", [D, 2048]).t
    C.pool_w = ext("pool_w", [4, 128, 128]).t
    C.pool_scale = ext("pool_scale", [128, 4]).t
    C.pool_corr = ext("pool_corr", [128, 4, 16]).t
    C.validbias = ext("validbias", [128, 64]).t
    C.khot = ext("khot", [32, S], BF16).t
    C.causal01 = ext("causal01", [128, 2, 256], BF16).t
    C.memT = ext("memT", [D, NMEM]).t
    C.mem_norm = ext("mem_norm", [128, 8]).t
    C.norm_xattn = [ext(f"norm_xattn{l}", [128, 8]).t for l in range(2)]
    C.norm_ffn = [ext(f"norm_ffn{l}", [128, 8]).t for l in range(2)]
    C.final_norm = ext("final_norm", [128, 8]).t
    C.ev_w_out = ext("ev_w_out", [D, D]).t
    C.od_w_out = ext("od_w_out", [D, D]).t
    C.xattn_wq = [ext(f"xattn_wq{l}", [D, D]).t for l in range(2)]
    C.xattn_wkv = [ext(f"xattn_wkv{l}", [D, 2 * D]).t for l in range(2)]
    C.xattn_wo = [ext(f"xattn_wo{l}", [D, D]).t for l in range(2)]
    C.ffn_w_up = [ext(f"ffn_w_up{l}", [D, 2 * DFF]).t for l in range(2)]
    C.ffn_conv = [ext(f"ffn_conv{l}", [128, 2 * DFF // 128, 3]).t for l in range(2)]
    C.ffn_w_down = [ext(f"ffn_w_down{l}", [DFF, D]).t for l in range(2)]
    C.od_w_in = ext("od_w_in", [D, 3080]).t
    C.sgu_w = ext("sgu_w", [4, 128, 128]).t
    C.sgu_ln_g = ext("sgu_ln_g", [128, 4]).t
    C.sgu_ln_b = ext("sgu_ln_b", [128, 4]).t
    C.sgu_b = ext("sgu_b", [128, 4, 128]).t
    C.dn_conv = ext("dn_conv", [128, 12, 4]).t
    C.dn_a_log = ext("dn_a_log", [128, 4]).t
    C.dn_dt_bias = ext("dn_dt_bias", [128, 4]).t
    C.dn_norm_g = ext("dn_norm_g", [128, 128]).t
    C.tril01 = ext("tril01", [128, 128]).t
    C.tri_le = ext("tri_le", [128, 128]).t
    C.causT_neg = ext("causT_neg", [128, 128]).t
    C.strictT01 = ext("strictT01", [128, 128]).t
    C.lvlmask = ext("lvlmask", [128, 8, 128], BF16).t
    ident_bf_d = ext("ident_bf", [128, 128], BF16).t
    ident_f_d = ext("ident_f", [128, 128]).t
    C.qaugT = scratch("qaugT", [8, 96, S], BF16)
    C.kT = scratch("kT", [512, S], BF16)
    C.vaug = scratch("vaug", [S, 520], BF16)
    C.mixT = scratch("mixT", [D, S], BF16)
    C.dnT = scratch("dnT", [1536, S], BF16)
    C.gtok = scratch("gtok", [S, 512], BF16)
    C.batok = scratch("batok", [S, 8], F32)
    C.hA = scratch("hA", [D, S], F32)
    C.hB = scratch("hB", [D, S], F32)
    C.outT = nc.dram_tensor("outT", [D, S], F32, kind="ExternalOutput").ap()
    for nm in ("qaugT", "kT", "vaug", "mixT", "hA", "hB", "xT", "outT", "dnT", "gtok", "batok"):
        setattr(C, nm + "_r", [RB(k.S, f"{nm}{t}") for t in range(NT)])

    with k.stack:
        C.ps = [k.psum([128, 512], F32, "ps") for _ in range(6)]
        C.ps_norm = k.psum([128, 512], F32, "psn")
        C.ps_bf = k.psum([128, 1024], BF16, "psbf")
        C.ones_bf = k.sb([128, 128], BF16, "ones")
        k.memset("pool", C.ones_bf[:, :], 1.0, [C.ones_bf])
        C.ident_bf = k.sb([128, 128], BF16, "identb")
        k.dma(C.ident_bf[:, :], ident_bf_d, [], [C.ident_bf])
        C.ident_f = k.sb([128, 128], F32, "identf")
        k.dma(C.ident_f[:, :], ident_f_d, [], [C.ident_f])
        C.eps_col = k.sb([128, 1], F32, "epsc")
        k.memset("pool", C.eps_col[:, :], EPS, [C.eps_col])
        C.kmean = k.sb([128, 4, 32], F32, "kmean")
        k.memset("pool", C.kmean[:, :, :], 0.0, [C.kmean])
        k.S.end_phase()

        if phases is None or 1 in phases:
            phase_l0_inproj(k, C, S)
        if phases is None or 2 in phases:
            phase_moba_gate(k, C, S)
        if phases is None or 22 in phases:
            phase_moba_attn(k, C, S)
        if phases is None or 3 in phases:
            phase_outproj_xattn(k, C, S, 0, C.ev_w_out, C.xT, C.xT_r, C.hA, C.hA_r)
        if phases is None or 4 in phases:
            phase_ffn(k, C, S, 0, C.hA, C.hA_r, C.hB, C.hB_r, False)
        if phases is None or 5 in phases:
            phase_l1_inproj(k, C, S)
        if phases is None or 6 in phases:
            phase_deltanet(k, C, S)
        if phases is None or 7 in phases:
            phase_outproj_xattn(k, C, S, 1, C.od_w_out, C.hB, C.hB_r, C.hA, C.hA_r)
        if phases is None or 8 in phases:
            phase_ffn(k, C, S, 1, C.hA, C.hA_r, C.outT, C.outT_r, True)

        esems = {}
        for e in ENGS:
            esems[e] = k.stack.enter_context(nc.semaphore(f"es_{e}"))
        dsems = [k.stack.enter_context(nc.semaphore(f"ds_{i}")) for i in range(k.S.n_dma_sems)]
        k.S.emit(esems, dsems)
    return nc


def phase_moba_gate(k, C, S):
    GS = 9
    NT = S // TT
    with ExitStack() as ctx:
        kmb = k.sb([128, 4, 64], BF16, "kmb", ctx)
        k.memset("pool", kmb[:, :, :], 0.0, [kmb])
        k.copy("dve", kmb[0:64, :, 0:32], C.kmean[0:64, :, :], [C.kmean], [kmb])
        k.copy("dve", kmb[64:128, :, 32:64], C.kmean[64:128, :, :], [C.kmean], [kmb])
        vbias = k.sb([128, 64], F32, "vbias", ctx)
        k.dma(vbias[:, :], C.validbias, [], [vbias])
        qc = [k.sb([128, 4, TT], BF16, "qc", ctx) for _ in range(2)]
        gsb = k.sb([128, 8, 32], F32, "gsb", ctx)
        top8 = k.sb([128, 8, 8], F32, "top8", ctx)
        sel = k.sb([128, 8, 32], F32, "sel", ctx)
        mb = k.sb([128, 8, 96], BF16, "mb", ctx)
        k.memset("pool", mb[:, :, :], 0.0, [mb])
        mrow = [k.sb([128, 8, TT], BF16, "mrow", ctx) for _ in range(2)]
        ps = C.ps

        def load_q(t):
            b = qc[t % 2]
            for h in range(8):
                k.dma(b[(h % 2) * 64:(h % 2) * 64 + 64, h // 2, :], C.qaugT[h, 0:64, t * TT:(t + 1) * TT],
                      [C.qaugT_r[t]], [b])

        load_q(0)
        for t in range(NT):
            if t + 1 < NT:
                load_q(t + 1)
            q = qc[t % 2]
            mr = mrow[t % 2]
            for sub in range(4):
                qb = (t * TT + sub * 128) // 256
                gp = ps[sub % 2]
                for c in range(4):
                    k.mm(gp[:, c * 64:(c + 1) * 64], q[:, c, sub * 128:(sub + 1) * 128],
                         kmb[:, c, :], True, True, [q, kmb], [gp])
                k.tt("dve", gsb[:, :, :], gp[:, 0:256].rearrange("p (h n) -> p h n", h=8),
                     vbias[:, 32 - qb:64 - qb].rearrange("p (o n) -> p o n", o=1).to_broadcast([128, 8, 32]),
                     ALU.add, [gp, vbias], [gsb])
                if GS <= 1:
                    continue
                for h in range(8):
                    k.op("dve", lambda e, h=h: e.max(top8[:, h, :], gsb[:, h, :]), [gsb], [top8], acc=True)
                k.tt("dve", sel[:, :, :], gsb[:, :, :], top8[:, :, 2:3].to_broadcast([128, 8, 32]), ALU.is_ge,
                     [gsb, top8], [sel])
                k.ts("dve", mb[:, :, 64:96], sel[:, :, :], -1.0, None, ALU.add, None, [sel], [mb])
                k.memset("dve", mb[:, :, 64 + qb:65 + qb], 0.0, [mb])
                if GS <= 2:
                    continue
                for half in range(2):
                    mp = ps[2 + (sub * 2 + half) % 4]
                    for hh in range(4):
                        h = half * 4 + hh
                        k.mm(mp[0:96, hh * 128:(hh + 1) * 128], mb[:, h, :], C.ident_bf[:, :], True, True,
                             [mb, C.ident_bf], [mp])
                    k.copy("act", mr[64:96, half * 4:half * 4 + 4, sub * 128:(sub + 1) * 128],
                           mp[64:96, 0:512].rearrange("p (h q) -> p h q", h=4), [mp], [mr])
            if GS <= 3:
                continue
            k.dma(C.qaugT[:, 64:96, t * TT:(t + 1) * TT].rearrange("h r t -> r h t"), mr[64:96, :, :],
                  [mr], [C.qaugT_r[t]], sem=mr)
    k.S.end_phase()


def phase_moba_attn(k, C, S):
    NB = S // 256
    NKT = S // 128
    with ExitStack() as ctx:
        vsb = k.sb([128, NKT, 520], BF16, "vsb", ctx)
        for i in range(0, NKT, 8):
            n = min(8, NKT - i)
            k.dma(vsb[:, i:i + n, :], C.vaug[i * 128:(i + n) * 128, :].rearrange("(t p) c -> p t c", p=128),
                  [C.vaug_r[(i * 128) // TT], C.vaug_r[min(S // TT - 1, ((i + n) * 128 - 1) // TT)]], [vsb])
        kaug = [k.sb([96, S], BF16, "kaug", ctx) for _ in range(2)]
        for b in kaug:
            k.dma(b[64:96, :], C.khot, [], [b])
        caus = k.sb([128, 2, 256], BF16, "caus", ctx)
        k.dma(caus[:, :, :], C.causal01, [], [caus])
        onesf = k.sb([128, 65], F32, "onesf", ctx)
        k.memset("pool", onesf[:, :], 1.0, [onesf])
        pt = [k.sb([128, 2, 256], BF16, "pt", ctx) for _ in range(4)]
        rden = [k.sb([1, 256], F32, "rden", ctx) for _ in range(2)]
        bcs = k.sb([65, 256], F32, "bcs", ctx)
        ao = [k.sb([65, 512], BF16, "ao", ctx) for _ in range(2)]
        sps = C.ps[0:3]
        ops = [C.ps[3], C.ps[4], C.ps_norm]
        bps = C.ps[5]
        all_k = [C.kT_r[t] for t in range(S // TT)]
        NQ = 4
        qa = [k.sb([96, 256], BF16, "qa", ctx) for _ in range(NQ)]
        groups = [(h, qb) for h in range(8) for qb in range(NB)]
        tasks = []
        for gi, (h, qb) in enumerate(groups):
            for pr in range(qb + 1):
                tasks.append((gi, h, qb, pr))

        def load_q(gi):
            h, qb = groups[gi]
            b = qa[gi % NQ]
            k.dma(b[:, :], C.qaugT[h, :, qb * 256:(qb + 1) * 256], [C.qaugT_r[(qb * 256) // TT]], [b])

        def load_k(h):
            kb = kaug[h % 2]
            k.dma(kb[0:64, :], C.kT[h * 64:(h + 1) * 64, :], all_k, [kb])

        def emit_S(i):
            gi, h, qb, pr = tasks[i]
            if pr == 0:
                if qb == 0 and h + 1 < 8:
                    load_k(h + 1)
                if gi + NQ - 1 < len(groups):
                    load_q(gi + NQ - 1)
            kb = kaug[h % 2]
            q = qa[gi % NQ]
            sp = sps[i % 3]
            p = pt[i % 4]
            for j in range(2):
                kt = 2 * pr + j
                k.mm(sp[:, j * 256:(j + 1) * 256], kb[0:96, kt * 128:(kt + 1) * 128], q[0:96, :],
                     True, True, [kb, q], [sp])
            k.act(p[:, :, :], sp[:, 0:512].rearrange("p (j q) -> p j q", j=2), AF.Exp, [sp], [p], scale=0.125)
            if pr == qb:
                k.tt("pool", p[:, :, :], p[:, :, :], caus[:, :, :], ALU.mult, [p, caus], [p])

        deferred = []

        def emit_PV(i):
            gi, h, qb, pr = tasks[i]
            p = pt[i % 4]
            op_ = ops[gi % 3]
            for j in range(2):
                kt = 2 * pr + j
                k.mm(op_[0:65, 0:256], vsb[:, kt, h * 65:(h + 1) * 65], p[:, j, :],
                     pr == 0 and j == 0, pr == qb and j == 1, [vsb, p], [op_])
            if pr == qb:
                rd = rden[gi % 2]
                k.op("dve", lambda e, op_=op_, rd=rd: e.reciprocal(rd[0:1, :], op_[0:1, 0:256]), [op_], [rd])

                def tail(gi=gi, h=h, qb=qb, op_=op_, rd=rd):
                    k.mm(bps[0:65, 0:256], onesf[0:1, 0:65], rd[0:1, :], True, True, [onesf, rd], [bps])
                    k.copy("act", bcs[:, :], bps[0:65, 0:256], [bps], [bcs])
                    a = ao[(qb // 2) % 2]
                    k.tt("dve", a[:, (qb % 2) * 256:(qb % 2 + 1) * 256], op_[0:65, 0:256], bcs[:, :], ALU.mult,
                         [op_, bcs], [a])
                    if qb % 2 == 1:
                        t = qb // 2
                        k.dma(C.mixT[h * 64:(h + 1) * 64, t * TT:(t + 1) * TT], a[1:65, :], [a], [C.mixT_r[t]], sem=a)
                deferred.append((i + 2, tail))

        load_k(0)
        for gi in range(min(NQ - 1, len(groups))):
            load_q(gi)
        n = len(tasks)
        emit_S(0)
        if n > 1:
            emit_S(1)
        for i in range(n):
            while deferred and deferred[0][0] <= i:
                deferred.pop(0)[1]()
            if i + 2 < n:
                emit_S(i + 2)
            emit_PV(i)
        while deferred:
            deferred.pop(0)[1]()
    k.S.end_phase()


def phase_outproj_xattn(k, C, S, l, w_out_d, hin, hin_r, hout, hout_r):
    NT = S // TT
    k.S.reorder_on = False
    with ExitStack() as ctx:
        C.sq = k.sb([128, 8, TT], BF16, "sq", ctx)
        C.rr = k.sb([128, TT], F32, "rr", ctx)
        kmemT = k.sb([128, 8, NMEM], BF16, "kmemT", ctx)
        vmem = k.sb([128, 2, D], BF16, "vmem", ctx)
        with ExitStack() as c2:
            stg = [k.sb([128, 2048], F32, "stg", c2) for _ in range(4)]
            gm = k.sb([128, 8], F32, "gm", c2)
            k.dma(gm[:, :], C.mem_norm, [], [gm])
            wkv = load_w(k, c2, C.xattn_wkv[l], D, 2048, "wkv", gain=gm, stg=stg)
            mt_ = k.sb([128, 8, NMEM], F32, "memt", c2)
            k.dma(mt_[:, :, :], C.memT.rearrange("(c p) m -> p c m", p=128), [], [mt_])
            memn = k.sb([128, 8, NMEM], BF16, "memn", c2)
            rmsnorm(k, C, mt_, memn, NMEM)
            for dc in range(8):
                p = C.ps[dc % 6]
                proj_fm(k, p, wkv, memn, dc * 128, NMEM, 8, [wkv, memn])
                k.copy("act", kmemT[:, dc, :], p[:, 0:NMEM], [p], [kmemT])
            for mt in range(2):
                for half in range(2):
                    p = C.ps[(mt * 2 + half) % 6]
                    for kc in range(8):
                        k.mm(p[:, 0:512], memn[:, kc, mt * 128:(mt + 1) * 128],
                             wkv[:, kc, 1024 + half * 512:1536 + half * 512], kc == 0, kc == 7, [memn, wkv], [p])
                    k.copy("dve", vmem[:, mt, half * 512:(half + 1) * 512], p[:, 0:512], [p], [vmem])
        k.S.end_phase()
        stg = [k.sb([128, 2048], F32, "stg", ctx) for _ in range(2)]
        gq = k.sb([128, 8], F32, "gq", ctx)
        k.dma(gq[:, :], C.norm_xattn[l], [], [gq])
        w_out = load_w(k, ctx, w_out_d, D, D, "w_out", stg=stg)
        wq = load_w(k, ctx, C.xattn_wq[l], D, D, "wq", gain=gq, stg=stg)
        wo = load_w(k, ctx, C.xattn_wo[l], D, D, "wo", stg=stg)
        xt = [k.sb([128, 8, TT], F32, "xt", ctx) for _ in range(2)]
        mx = [k.sb([128, 8, TT], BF16, "mx", ctx) for _ in range(2)]
        xn = [k.sb([128, 8, TT], BF16, "xn", ctx) for _ in range(2)]
        qx = [k.sb([128, 8, TT], BF16, "qx", ctx) for _ in range(2)]
        ox = [k.sb([128, 8, TT], BF16, "ox", ctx) for _ in range(2)]
        pt = [[k.sb([128, 2, TT], BF16, "pt", ctx) for _ in range(2)] for _ in range(2)]
        rec = [[k.sb([128, TT], F32, "rec", ctx) for _ in range(2)] for _ in range(2)]
        rrb = [C.rr, k.sb([128, TT], F32, "rr2", ctx)]
        ps = C.ps
        st = {"pi": 0}

        def nps():
            p = ps[st["pi"] % 6]
            st["pi"] += 1
            return p

        def load(t):
            k.dma(xt[t % 2][:, :, :], hin[:, t * TT:(t + 1) * TT].rearrange("(c p) t -> p c t", p=128),
                  [hin_r[t]], [xt[t % 2]])
            k.dma(mx[t % 2][:, :, :], C.mixT[:, t * TT:(t + 1) * TT].rearrange("(c p) t -> p c t", p=128),
                  [C.mixT_r[t]], [mx[t % 2]])

        def tile(t):
            x, m_, xn_, qx_, ox_ = xt[t % 2], mx[t % 2], xn[t % 2], qx[t % 2], ox[t % 2]
            for m in range(8):
                p = nps()
                proj_fm(k, p, w_out, m_, m * 128, TT, 8, [w_out, m_])
                k.tt("dve", x[:, m, :], x[:, m, :], p[:, 0:TT], ALU.add, [x, p], [x])
            yield
            rmsnorm(k, C, x, xn_, TT, nb=(C.sq, rrb[t % 2], C.ps_norm))
            yield
            for m in range(8):
                p = nps()
                proj_fm(k, p, wq, xn_, m * 128, TT, 8, [wq, xn_])
                k.copy("act", qx_[:, m, :], p[:, 0:TT], [p], [qx_])
                if m == 3:
                    yield
            yield
            for hd in range(4):
                ptb = pt[t % 2][hd % 2]
                rc = rec[t % 2][hd % 2]
                for mt in range(2):
                    p = nps()
                    for dd in range(2):
                        dc = 2 * hd + dd
                        k.mm(p[:, 0:TT], kmemT[:, dc, mt * 128:(mt + 1) * 128], qx_[:, dc, :], dd == 0, dd == 1,
                             [kmemT, qx_], [p])
                    k.act(ptb[:, mt, :], p[:, 0:TT], AF.Exp, [p], [ptb], scale=1.0 / 16)
                p = nps()
                for mt in range(2):
                    k.mm(p[:, 0:TT], C.ones_bf[:, :], ptb[:, mt, :], mt == 0, mt == 1, [C.ones_bf, ptb], [p])
                k.op("dve", lambda e, rc=rc, p=p: e.reciprocal(rc[:, :], p[:, 0:TT]), [p], [rc])
                for dd in range(2):
                    p = nps()
                    c0 = hd * 256 + dd * 128
                    for mt in range(2):
                        k.mm(p[:, 0:TT], vmem[:, mt, c0:c0 + 128], ptb[:, mt, :], mt == 0, mt == 1, [vmem, ptb], [p])
                    k.tt("dve", ox_[:, 2 * hd + dd, :], p[:, 0:TT], rc[:, :], ALU.mult, [p, rc], [ox_])
                yield
            for m in range(8):
                p = nps()
                proj_fm(k, p, wo, ox_, m * 128, TT, 8, [wo, ox_])
                k.tt("dve", x[:, m, :], x[:, m, :], p[:, 0:TT], ALU.add, [x, p], [x])
                if m == 3:
                    yield
            k.dma(hout[:, t * TT:(t + 1) * TT].rearrange("(c p) t -> p c t", p=128), x[:, :, :],
                  [x], [hout_r[t]], sem=x)
            if t + 2 < NT:
                load(t + 2)

        load(0)
        if NT > 1:
            load(1)
        run_pipelined(tile, NT, 6)
    k.S.end_phase()
    k.S.reorder_on = True


TF = 256


def phase_ffn(k, C, S, l, hin, hin_r, hout, hout_r, final):
    k.S.reorder_on = False
    NTF = S // TF
    NJ = DFF // 128
    with ExitStack() as ctx:
        gf = k.sb([128, 8], F32, "gf", ctx)
        k.dma(gf[:, :], C.norm_ffn[l], [], [gf])
        w_up = k.sb([128, 8, 2 * DFF], BF16, "w_up", ctx)
        w_dn = k.sb([128, NJ, D], BF16, "w_dn", ctx)
        with ExitStack() as c2:
            stg = [k.sb([128, 2048], F32, "stg", c2) for _ in range(4)]
            load_w(k, ctx, C.ffn_w_up[l], D, 2 * DFF, "w_up", gain=gf, stg=stg, w=w_up)
            load_w(k, ctx, C.ffn_w_down[l], DFF, D, "w_dn", stg=stg, w=w_dn)
        k.S.end_phase()
        cw = k.sb([128, 2 * NJ, 3], F32, "cw", ctx)
        k.dma(cw[:, :, :], C.ffn_conv[l], [], [cw])
        gfin = k.sb([128, 8], F32, "gfin", ctx)
        k.dma(gfin[:, :], C.final_norm, [], [gfin])
        xt = [k.sb([128, 8, TF], F32, "xt", ctx) for _ in range(2)]
        xn = [k.sb([128, 8, 2 + TF], BF16, "xn", ctx) for _ in range(2)]
        for b in xn:
            k.memset("pool", b[:, :, :], 0.0, [b])
        sqb = [k.sb([128, 8, TF], BF16, "sq", ctx)] * 2
        rrb = [k.sb([128, TF], F32, "rr", ctx) for _ in range(2)]
        actb = [[k.sb([128, TF], BF16, "actb", ctx) for _ in range(NJ)] for _ in range(2)]
        yb = [[k.sb([128, TF], F32, "yb", ctx) for _ in range(4)] for _ in range(2)]
        sg = [[k.sb([128, TF], F32, "sg", ctx) for _ in range(2)] for _ in range(2)]
        ps = C.ps
        st = {"pi": 0}
        W2 = 2 + TF

        def nps():
            p = ps[st["pi"] % 6]
            st["pi"] += 1
            return p

        def load(t):
            k.dma(xt[t % 2][:, :, :], hin[:, t * TF:(t + 1) * TF].rearrange("(c p) t -> p c t", p=128),
                  [hin_r[(t * TF) // TT]], [xt[t % 2]])

        def tile(t):
            x, xn_, ab = xt[t % 2], xn[t % 2], actb[t % 2]
            nb = (sqb[t % 2], rrb[t % 2], C.ps_norm)
            if t > 0:
                k.copy("pool", xn_[:, :, 0:2], xn[(t - 1) % 2][:, :, TF:TF + 2], [xn[(t - 1) % 2]], [xn_])
            rmsnorm(k, C, x, xn_, TF, nb=nb, xoff=2)
            yield
            pend = None
            for j in range(NJ):
                ys = []
                for which in range(2):
                    c = which * NJ + j
                    p = nps()
                    y = yb[t % 2][(2 * j + which) % 4]
                    for kc in range(8):
                        k.mm(p[:, 0:W2], w_up[:, kc, c * 128:(c + 1) * 128], xn_[:, kc, 0:W2], kc == 0, kc == 7,
                             [w_up, xn_], [p])
                    k.act(y[:, :], p[:, 2:W2], AF.Copy, [p, cw], [y], scale=cw[:, c, 2:3])
                    k.stt("dve", y[:, :], p[:, 1:1 + TF], cw[:, c, 1:2], y[:, :], ALU.mult, ALU.add, [p, cw, y], [y])
                    k.stt("dve", y[:, :], p[:, 0:TF], cw[:, c, 0:1], y[:, :], ALU.mult, ALU.add, [p, cw, y], [y])
                    ys.append(y)
                if pend is not None:
                    pend()

                def fin(j=j, ys=ys):
                    s_ = sg[t % 2][j % 2]
                    k.act(s_[:, :], ys[0][:, :], AF.Silu, [ys[0]], [s_])
                    k.tt("pool", ab[j][:, :], s_[:, :], ys[1][:, :], ALU.mult, [s_, ys[1]], [ab[j]])
                pend = fin
                if j % 2 == 1:
                    yield
            pend()
            yield
            for m in range(8):
                p = nps()
                for j in range(NJ):
                    k.mm(p[:, 0:TF], w_dn[:, j, m * 128:(m + 1) * 128], ab[j][:, :], j == 0, j == NJ - 1,
                         [w_dn, ab[j]], [p])
                k.tt("dve", x[:, m, :], x[:, m, :], p[:, 0:TF], ALU.add, [x, p], [x])
                if m == 3:
                    yield
            if final:
                fin_x = k.sb
                rmsnorm(k, C, x, xfin, TF, nb=nb)
                for m in range(8):
                    k.stt("dve", x[:, m, :], x[:, m, :], gfin[:, m:m + 1], nb[1][:, 0:TF], ALU.mult, ALU.mult,
                          [x, gfin, nb[1]], [x])
            k.dma(hout[:, t * TF:(t + 1) * TF].rearrange("(c p) t -> p c t", p=128), x[:, :, :],
                  [x], [hout_r[(t * TF) // TT]], sem=x)
            if t + 2 < NTF:
                load(t + 2)

        xfin = k.sb([128, 8, TF], BF16, "xfin", ctx) if final else None
        load(0)
        load(1)
        run_pipelined(tile, NTF, 6)
    k.S.end_phase()
    k.S.reorder_on = True


def _v8(v):
    return np.ascontiguousarray(np.asarray(v, np.float32).reshape(-1, 128).T)


def core_inputs(inp, b, S, consts):
    f = lambda a: np.ascontiguousarray(np.asarray(a, np.float32))
    d = dict(consts)
    d["xT"] = f(np.asarray(inp["x"][b]).T)
    d["memT"] = f(np.asarray(inp["mem"][b]).T)
    d["mem_norm"] = _v8(inp["mem_norm"])
    d["final_norm"] = _v8(inp["final_norm"])
    for l in range(2):
        d[f"norm_mix{l}"] = _v8(inp["norm_mix"][l])
        d[f"norm_xattn{l}"] = _v8(inp["norm_xattn"][l])
        d[f"norm_ffn{l}"] = _v8(inp["norm_ffn"][l])
        d[f"xattn_wq{l}"] = f(inp["xattn_wq"][l])
        d[f"xattn_wkv{l}"] = f(inp["xattn_wkv"][l])
        d[f"xattn_wo{l}"] = f(inp["xattn_wo"][l])
        d[f"ffn_w_up{l}"] = f(inp["ffn_w_up"][l])
        d[f"ffn_w_down{l}"] = f(inp["ffn_w_down"][l])
        d[f"ffn_conv{l}"] = f(np.asarray(inp["ffn_conv"][l]).reshape(3, -1, 128).transpose(2, 1, 0))
    d["ev_w_in"] = f(inp["ev_w_in"][0])
    d["ev_w_out"] = f(inp["ev_w_out"][0])
    d["od_w_out"] = f(inp["od_w_out"][0])
    d["pool_w"] = f(inp["pool_w"][0])
    d["pool_scale"] = _v8(inp["pool_scale"][0])
    d["od_w_in"] = f(inp["od_w_in"][0])
    d["sgu_w"] = f(inp["sgu_w"][0])
    d["sgu_ln_g"] = _v8(inp["sgu_ln_g"][0])
    d["sgu_ln_b"] = _v8(inp["sgu_ln_b"][0])
    d["sgu_b"] = f(np.broadcast_to(np.asarray(inp["sgu_b"][0])[None], (128, 4, 128)))
    d["dn_conv"] = f(np.asarray(inp["dn_conv"][0]).reshape(4, 12, 128).transpose(2, 1, 0))
    d["dn_a_log"] = f(np.broadcast_to(np.asarray(inp["dn_a_log"][0])[None], (128, 4)))
    d["dn_dt_bias"] = f(np.broadcast_to(np.asarray(inp["dn_dt_bias"][0])[None], (128, 4)))
    d["dn_norm_g"] = f(np.broadcast_to(np.asarray(inp["dn_norm_g"][0])[None], (128, 128)))
    return d


def phase_l1_inproj(k, C, S):
    NT = S // TT
    GC1 = 1.5957691216057308
    with ExitStack() as ctx:
        C.sq = k.sb([128, 8, TT], BF16, "sq", ctx)
        C.rr = k.sb([128, TT], F32, "rr", ctx)
        gain = k.sb([128, 8], F32, "gain", ctx)
        k.dma(gain[:, :], C.norm_mix[1], [], [gain])
        w_in = k.sb([128, 8, 3080], BF16, "w_in1", ctx)
        wsT = k.sb([128, 4, 128], BF16, "wsT", ctx)
        with ExitStack() as c2:
            stg = [k.sb([128, 2048], F32, "stg", c2) for _ in range(4)]
            load_w(k, ctx, C.od_w_in, D, 3080, "w_in1", gain=gain, stg=stg, w=w_in)
            wsf = k.sb([128, 4, 128], F32, "wsf", c2)
            k.dma(wsf[:, :, :], C.sgu_w.rearrange("g t s -> t g s"), [], [wsf])
            tril = k.sb([128, 128], F32, "tril", c2)
            k.dma(tril[:, :], C.tril01, [], [tril])
            wsm = k.sb([128, 4, 128], BF16, "wsm", c2)
            k.tt("dve", wsm[:, :, :], wsf[:, :, :],
                 tril[:, :].rearrange("p (o s) -> p o s", o=1).to_broadcast([128, 4, 128]), ALU.mult,
                 [wsf, tril], [wsm])
            for g in range(4):
                k.tr(C.ps_bf[:, g * 128:(g + 1) * 128], wsm[:, g, :], C.ident_bf[:, :], [wsm, C.ident_bf], [C.ps_bf])
            k.copy("act", wsT[:, :, :], C.ps_bf[:, 0:512].rearrange("p (g t) -> p g t", g=4), [C.ps_bf], [wsT])
        k.S.end_phase()
        lng = k.sb([128, 4], F32, "lng", ctx)
        lnb = k.sb([128, 4], F32, "lnb", ctx)
        k.dma(lng[:, :], C.sgu_ln_g, [], [lng])
        k.dma(lnb[:, :], C.sgu_ln_b, [], [lnb])
        bsb = k.sb([128, 4, 128], F32, "bsb", ctx)
        k.dma(bsb[:, :, :], C.sgu_b, [], [bsb])
        dcw = k.sb([128, 12, 4], F32, "dcw", ctx)
        k.dma(dcw[:, :, :], C.dn_conv, [], [dcw])
        qsc = k.sb([128, 1], F32, "qsc", ctx)
        k.memset("pool", qsc[:, :], float(np.log(128.0 ** -0.5)), [qsc])
        zero_c = k.sb([128, 1], F32, "zero_c", ctx)
        k.memset("pool", zero_c[:, :], 0.0, [zero_c])

        xt = [k.sb([128, 8, TT], F32, "xt", ctx) for _ in range(2)]
        xn = k.sb([128, 8, TT], BF16, "xn", ctx)
        x2 = [k.sb([128, TT], F32, "x2", ctx) for _ in range(2)]
        sgm = [k.sb([128, TT], F32, "sgm", ctx) for _ in range(2)]
        ub_ = k.sb([128, 4, TT], F32, "u", ctx)
        v_ = k.sb([128, 4, TT], F32, "v", ctx)
        vb = k.sb([128, 4, TT], BF16, "vb", ctx)
        vsq = k.sb([128, 4, TT], BF16, "vsq", ctx)
        mean = k.sb([128, TT], F32, "mean", ctx)
        m2 = k.sb([128, TT], F32, "m2", ctx)
        rstd = k.sb([128, TT], F32, "rstd", ctx)
        vn = k.sb([128, 4, TT], BF16, "vn", ctx)
        vtok = k.sb([128, 4, 4, 128], BF16, "vtok", ctx)
        t1 = k.sb([128, 4, 128], F32, "t1", ctx)
        cout = [k.sb([128, 4, TT], BF16, "cout", ctx) for _ in range(2)]
        cu = [k.sb([128, 3 + TT], F32, "cu", ctx) for _ in range(2)]
        cy = [k.sb([128, TT], F32, "cy", ctx) for _ in range(2)]
        chist = k.sb([128, 12, 3], F32, "chist", ctx)
        k.memset("pool", chist[:, :, :], 0.0, [chist])
        dq4 = k.sb([128, 4, TT], F32, "dq4", ctx)
        dsq4 = k.sb([128, 4, TT], BF16, "dsq4", ctx)
        rn4 = k.sb([128, 4, TT], F32, "rn4", ctx)
        dno = [k.sb([128, 12, TT], BF16, "dno", ctx)] * 2
        gto = [k.sb([128, 4, 512], BF16, "gto", ctx)] * 2
        bao = [k.sb([128, 4, 8], F32, "bao", ctx) for _ in range(2)]
        ps = C.ps
        pi = 0

        def load(t):
            k.dma(xt[t % 2][:, :, :], C.hB[:, t * TT:(t + 1) * TT].rearrange("(c p) t -> p c t", p=128),
                  [C.hB_r[t]], [xt[t % 2]])

        load(0)
        for t in range(NT):
            if t + 1 < NT:
                load(t + 1)
            x = xt[t % 2]
            t0 = t * TT
            rmsnorm(k, C, x, xn, TT)
            for m in range(8):
                p = ps[pi % 6]; pi += 1
                a2, sg = x2[m % 2], sgm[m % 2]
                proj_fm(k, p, w_in, xn, m * 128, TT, 8, [w_in, xn])
                k.act(a2[:, :], p[:, 0:TT], AF.Square, [p], [a2])
                k.ts("dve", a2[:, :], a2[:, :], 0.044715, 1.0, ALU.mult, ALU.add, [a2], [a2])
                k.tt("dve", a2[:, :], a2[:, :], p[:, 0:TT], ALU.mult, [a2, p], [a2])
                k.act(sg[:, :], a2[:, :], AF.Sigmoid, [a2], [sg], scale=GC1)
                dst = ub_ if m < 4 else v_
                k.tt("dve", dst[:, m % 4, :], sg[:, :], p[:, 0:TT], ALU.mult, [sg, p], [dst])
            k.copy("pool", vb[:, :, :], v_[:, :, :], [v_], [vb])
            k.act(vsq[:, :, :], v_[:, :, :], AF.Square, [v_], [vsq])
            pm = ps[pi % 6]; pi += 1
            pq = ps[pi % 6]; pi += 1
            for c in range(4):
                k.mm(pm[:, 0:TT], C.ones_bf[:, :], vb[:, c, :], c == 0, c == 3, [C.ones_bf, vb], [pm])
            for c in range(4):
                k.mm(pq[:, 0:TT], C.ones_bf[:, :], vsq[:, c, :], c == 0, c == 3, [C.ones_bf, vsq], [pq])
            k.ts("dve", mean[:, :], pm[:, 0:TT], 1.0 / 512, None, ALU.mult, None, [pm], [mean])
            k.tt("pool", m2[:, :], mean[:, :], mean[:, :], ALU.mult, [mean], [m2])
            k.stt("dve", m2[:, :], pq[:, 0:TT], 1.0 / 512, m2[:, :], ALU.mult, ALU.subtract, [pq, m2], [m2])
            k.act(rstd[:, :], m2[:, :], AF.Ln, [m2, C.eps_col], [rstd], bias=C.eps_col[:, 0:1])
            k.act(rstd[:, :], rstd[:, :], AF.Exp, [rstd], [rstd], scale=-0.5)
            bc = lambda a: a[:, :].rearrange("p (o t) -> p o t", o=1).to_broadcast([128, 4, TT])
            k.tt("dve", v_[:, :, :], v_[:, :, :], bc(mean), ALU.subtract, [v_, mean], [v_])
            k.tt("pool", v_[:, :, :], v_[:, :, :], bc(rstd), ALU.mult, [v_, rstd], [v_])
            for c in range(4):
                k.ts("dve", vn[:, c, :], v_[:, c, :], lng[:, c:c + 1], lnb[:, c:c + 1], ALU.mult, ALU.add,
                     [v_, lng, lnb], [vn])
            for half in range(2):
                for nn in range(2):
                    n = half * 2 + nn
                    for c in range(4):
                        k.tr(C.ps_bf[:, (nn * 4 + c) * 128:(nn * 4 + c + 1) * 128], vn[:, c, n * 128:(n + 1) * 128],
                             C.ident_bf[:, :], [vn, C.ident_bf], [C.ps_bf])
                k.copy("act", vtok[:, half * 2:half * 2 + 2, :, :],
                       C.ps_bf[:, 0:1024].rearrange("p (n c s) -> p n c s", n=2, c=4), [C.ps_bf], [vtok])
            co = cout[t % 2]
            for g in range(4):
                p = ps[pi % 6]; pi += 1
                for n in range(4):
                    k.mm(p[:, n * 128:(n + 1) * 128], vtok[:, n, g, :], wsT[:, g, :], True, True, [vtok, wsT], [p])
                for n in range(4):
                    pass
                k.tt("dve", t1[:, :, :], p[:, 0:512].rearrange("p (n t) -> p n t", n=4),
                     bsb[:, g, :].rearrange("p (o t) -> p o t", o=1).to_broadcast([128, 4, 128]), ALU.add,
                     [p, bsb], [t1])
                k.tt("pool", co[:, g, :], t1[:, :, :].rearrange("p n t -> p (n t)"), ub_[:, g, :], ALU.mult,
                     [t1, ub_], [co])
            k.dma(C.mixT[0:512, t0:t0 + TT].rearrange("(g p) t -> p g t", p=128), co[:, :, :],
                  [co], [C.mixT_r[t]], sem=co)
            do = dno[t % 2]
            for grp in ((0, 1, 2, 3), (4, 5, 6, 7), (8, 9, 10, 11)):
                for c in grp:
                    p = ps[pi % 6]; pi += 1
                    u, y = cu[c % 2], cy[c % 2]
                    proj_fm(k, p, w_in, xn, 1024 + c * 128, TT, 8, [w_in, xn])
                    k.copy("pool", u[:, 0:3], chist[:, c, :], [chist], [u])
                    k.copy("act", u[:, 3:3 + TT], p[:, 0:TT], [p], [u])
                    k.act(y[:, :], p[:, 0:TT], AF.Copy, [p, dcw], [y], scale=dcw[:, c, 3:4])
                    k.copy("pool", chist[:, c, :], u[:, TT:TT + 3], [u], [chist])
                    for kk in range(3):
                        k.stt("dve", y[:, :], u[:, kk:kk + TT], dcw[:, c, kk:kk + 1], y[:, :], ALU.mult, ALU.add,
                              [u, dcw, y], [y])
                    if c >= 8:
                        k.act(do[:, c, :], y[:, :], AF.Silu, [y], [do])
                    else:
                        k.act(dq4[:, c % 4, :], y[:, :], AF.Silu, [y], [dq4])
                if grp[0] >= 8:
                    continue
                k.act(dsq4[:, :, :], dq4[:, :, :], AF.Square, [dq4], [dsq4])
                for c in grp:
                    p2 = ps[pi % 6]; pi += 1
                    k.mm(p2[:, 0:TT], C.ones_bf[:, :], dsq4[:, c % 4, :], True, True, [C.ones_bf, dsq4], [p2])
                    k.act(rn4[:, c % 4, :], p2[:, 0:TT], AF.Ln, [p2, C.eps_col], [rn4], bias=C.eps_col[:, 0:1])
                k.act(rn4[:, :, :], rn4[:, :, :], AF.Exp, [rn4, qsc, zero_c], [rn4], scale=-0.5,
                      bias=(qsc if grp[0] < 4 else zero_c)[:, 0:1])
                k.tt("dve", do[:, grp[0]:grp[0] + 4, :], dq4[:, :, :], rn4[:, :, :], ALU.mult, [dq4, rn4], [do])
            k.dma(C.dnT[:, t0:t0 + TT].rearrange("(c p) t -> p c t", p=128), do[:, :, :], [do], [C.dnT_r[t]], sem=do)
            go, bo = gto[t % 2], bao[t % 2]
            for n in range(4):
                p = ps[pi % 6]; pi += 1
                for kc in range(8):
                    k.mm(p[:, 0:512], xn[:, kc, n * 128:(n + 1) * 128], w_in[:, kc, 2560:3072], kc == 0, kc == 7,
                         [xn, w_in], [p])
                k.act(go[:, n, :], p[:, 0:512], AF.Silu, [p], [go])
                p = ps[pi % 6]; pi += 1
                for kc in range(8):
                    k.mm(p[:, 0:8], xn[:, kc, n * 128:(n + 1) * 128], w_in[:, kc, 3072:3080], kc == 0, kc == 7,
                         [xn, w_in], [p])
                k.copy("dve", bo[:, n, :], p[:, 0:8], [p], [bo])
            k.dma(C.gtok[t0:t0 + TT, :].rearrange("(n p) c -> p n c", p=128), go[:, :, :], [go], [C.gtok_r[t]], sem=go)
            k.dma(C.batok[t0:t0 + TT, :].rearrange("(n p) c -> p n c", p=128), bo[:, :, :], [bo], [C.batok_r[t]], sem=bo)
    k.S.end_phase()


def phase_deltanet(k, C, S):
    NCH = S // 128
    with ExitStack() as ctx:
        tri = k.sb([128, 128], F32, "tri", ctx)
        k.dma(tri[:, :], C.tri_le, [], [tri])
        causT = k.sb([128, 128], F32, "causT", ctx)
        k.dma(causT[:, :], C.causT_neg, [], [causT])
        strT = k.sb([128, 128], F32, "strT", ctx)
        k.dma(strT[:, :], C.strictT01, [], [strT])
        onesf = k.sb([128, 128], F32, "onesf", ctx)
        k.memset("pool", onesf[:, :], 1.0, [onesf])
        one_c = k.sb([128, 1], F32, "one_c", ctx)
        k.memset("pool", one_c[:, :], 1.0, [one_c])
        dtb = k.sb([128, 4], F32, "dtb", ctx)
        k.dma(dtb[:, :], C.dn_dt_bias, [], [dtb])
        nexpA = k.sb([128, 4], F32, "nexpA", ctx)
        k.dma(nexpA[:, :], C.dn_a_log, [], [nexpA])
        k.act(nexpA[:, :], nexpA[:, :], AF.Exp, [nexpA], [nexpA])
        k.ts("dve", nexpA[:, :], nexpA[:, :], -1.0, None, ALU.mult, None, [nexpA], [nexpA])
        ngb = k.sb([128, 128], F32, "ngb", ctx)
        k.dma(ngb[:, :], C.dn_norm_g, [], [ngb])

        St = k.sb([128, 4, 128], F32, "St", ctx)
        Sb = k.sb([128, 4, 128], BF16, "Sb", ctx)
        k.memset("pool", St[:, :, :], 0.0, [St])
        k.memset("pool", Sb[:, :, :], 0.0, [Sb])
        lvl = k.sb([128, 8, 128], BF16, "lvl", ctx)
        k.dma(lvl[:, :, :], C.lvlmask, [], [lvl])
        NB_ = 4
        BETA, G, GC, GLAST, E, F_, GL, NGC, BE, TMP = range(10)

        def mk():
            b = Ctx()
            b.dn = k.sb([128, 12, 128], BF16, "dn", ctx)
            b.gt = k.sb([128, 512], BF16, "gt", ctx)
            b.ba = k.sb([128, 8], F32, "ba", ctx)
            b.sc = k.sb([128, 10, 4], F32, "sc", ctx)
            b.dg = [k.sb([128, 4, 128], F32, "dg", ctx) for _ in range(3)]
            b.kbT = k.sb([128, 4, 128], BF16, "kbT", ctx)
            b.qeT = k.sb([128, 4, 128], BF16, "qeT", ctx)
            b.X = k.sb([128, 4, 128], F32, "X", ctx)
            b.DcT = k.sb([128, 4, 128], F32, "DcT", ctx)
            b.DcsT = k.sb([128, 4, 128], F32, "DcsT", ctx)
            b.tok = k.sb([128, 8, 128], BF16, "tok", ctx)
            b.rhs0 = k.sb([128, 4, 256], BF16, "rhs0", ctx)
            b.y = k.sb([128, 4, 256], BF16, "y", ctx)
            b.kd = k.sb([128, 4, 128], BF16, "kd", ctx)
            b.qkT = k.sb([128, 4, 128], BF16, "qkT", ctx)
            b.A = k.sb([128, 4, 128], BF16, "A", ctx)
            b.AT = k.sb([128, 4, 128], BF16, "AT", ctx)
            b.Tm = [k.sb([128, 4, 128], BF16, "Tm", ctx) for _ in range(2)]
            b.TTm = [k.sb([128, 4, 128], BF16, "TTm", ctx) for _ in range(2)]
            b.Am = k.sb([128, 4, 128], BF16, "Am", ctx)
            b.AmT = [k.sb([128, 4, 128], BF16, "AmT", ctx) for _ in range(2)]
            b.Um = k.sb([128, 4, 128], BF16, "Um", ctx)
            b.wT = k.sb([128, 4, 128], BF16, "wT", ctx)
            b.vnew = k.sb([128, 4, 128], BF16, "vnew", ctx)
            b.osb = k.sb([128, 4, 128], F32, "osb", ctx)
            b.junk = k.sb([128, 128], F32, "junk", ctx)
            b.ss = k.sb([128, 4], F32, "ss", ctx)
            b.dtk = k.sb([128, 4, 128], BF16, "dtk", ctx)
            b.doT = k.sb([128, 4, 128], BF16, "doT", ctx)
            return b

        BUFS = [mk() for _ in range(NB_)]
        ps = C.ps + [C.ps_norm]
        st_ = {"pi": 0}

        def nxt():
            p = ps[st_["pi"] % 7]
            st_["pi"] += 1
            return p

        def load(n):
            b = BUFS[n % NB_]
            tl = (n * 128) // TT
            k.dma(b.dn[:, :, :], C.dnT[:, n * 128:(n + 1) * 128].rearrange("(c p) t -> p c t", p=128),
                  [C.dnT_r[tl]], [b.dn])
            k.dma(b.gt[:, :], C.gtok[n * 128:(n + 1) * 128, :], [C.gtok_r[tl]], [b.gt])
            k.dma(b.ba[:, :], C.batok[n * 128:(n + 1) * 128, :], [C.batok_r[tl]], [b.ba])

        ident4 = C.ident_f[:, :].rearrange("p (o t) -> p o t", o=1).to_broadcast([128, 4, 128])
        m4 = lambda a: a[:, :].rearrange("p (o t) -> p o t", o=1).to_broadcast([128, 4, 128])
        v4 = lambda p_, w=128: p_[:, 0:4 * w].rearrange("p (h t) -> p h t", h=4)
        lm4 = lambda l: lvl[:, l, :].rearrange("p (o t) -> p o t", o=1).to_broadcast([128, 4, 128])
        identb4 = C.ident_bf[:, :].rearrange("p (o t) -> p o t", o=1).to_broadcast([128, 4, 128])

        def chunk(n):
            b = BUFS[n % NB_]
            sc, d, g_, b_ = b.sc, b.dn, b.gt, b.ba

            def bcol(j):
                return sc[:, j, :].rearrange("p (h o) -> p h o", o=1).to_broadcast([128, 4, 128])
            k.act(sc[:, BETA, :], b_[:, 0:4], AF.Exp, [b_], [sc], scale=-1.0)
            k.ts("dve", sc[:, BETA, :], sc[:, BETA, :], 1.0, None, ALU.add, None, [sc], [sc])
            k.op("dve", lambda e: e.reciprocal(sc[:, BETA, :], sc[:, BETA, :]), [sc], [sc])
            k.tt("dve", sc[:, TMP, :], b_[:, 4:8], dtb[:, :], ALU.add, [b_, dtb], [sc])
            k.act(sc[:, TMP, :], sc[:, TMP, :], AF.Exp, [sc], [sc])
            k.act(sc[:, TMP, :], sc[:, TMP, :], AF.Ln, [sc, one_c], [sc], bias=one_c[:, 0:1])
            k.tt("dve", sc[:, G, :], sc[:, TMP, :], nexpA[:, :], ALU.mult, [sc, nexpA], [sc])
            p = nxt()
            k.mm(p[:, 0:4], tri[:, :], sc[:, G, :], True, True, [tri, sc], [p])
            k.mm(p[:, 4:8], onesf[:, :], sc[:, G, :], True, True, [onesf, sc], [p])
            k.copy("dve", sc[:, GC:GLAST + 1, :], p[:, 0:8].rearrange("p (a h) -> p a h", a=2), [p], [sc])
            yield
            k.act(sc[:, E, :], sc[:, GC, :], AF.Exp, [sc], [sc])
            k.tt("dve", sc[:, TMP, :], sc[:, GLAST, :], sc[:, GC, :], ALU.subtract, [sc], [sc])
            k.act(sc[:, F_, :], sc[:, TMP, :], AF.Exp, [sc], [sc])
            k.act(sc[:, GL, :], sc[:, GLAST, :], AF.Exp, [sc], [sc])
            k.ts("dve", sc[:, NGC, :], sc[:, GC, :], -1.0, None, ALU.mult, None, [sc], [sc])
            k.tt("dve", sc[:, BE, :], sc[:, BETA, :], sc[:, E, :], ALU.mult, [sc], [sc])
            for j in range(8):
                k.tr(C.ps_bf[:, j * 128:(j + 1) * 128], d[:, 4 + j, :], C.ident_bf[:, :], [d, C.ident_bf], [C.ps_bf])
            k.copy("act", b.tok[:, :, :], C.ps_bf[:, 0:1024].rearrange("p (j t) -> p j t", j=8), [C.ps_bf], [b.tok])
            yield
            pB, pE, pG = nxt(), nxt(), nxt()
            for dgi, (col, pp) in enumerate(((BETA, pB), (E, pE), (GC, pG))):
                k.tt("pool" if dgi == 1 else "dve", b.dg[dgi][:, :, :], ident4, bcol(col), ALU.mult,
                     [C.ident_f, sc], [b.dg[dgi]])
                k.mm(pp[:, 0:512], onesf[:, :], b.dg[dgi][:, :, :].rearrange("p h t -> p (h t)"), True, True,
                     [onesf, b.dg[dgi]], [pp])
            k.tt("dve", b.kbT[:, :, :], d[:, 4:8, :], v4(pB), ALU.mult, [d, pB], [b.kbT])
            k.tt("dve", b.qeT[:, :, :], d[:, 0:4, :], v4(pE), ALU.mult, [d, pE], [b.qeT])
            k.tt("dve", b.X[:, :, :], v4(pG), m4(causT), ALU.add, [pG, causT], [b.X])
            for h in range(4):
                k.ts("dve", b.rhs0[:, h, 0:128], b.tok[:, 4 + h, :], sc[:, BETA, h:h + 1], None, ALU.mult, None,
                     [b.tok, sc], [b.rhs0])
                k.act(b.rhs0[:, h, 128:256], b.tok[:, h, :], AF.Copy, [b.tok, sc], [b.rhs0], scale=sc[:, BE, h:h + 1])
                k.act(b.kd[:, h, :], b.tok[:, h, :], AF.Copy, [b.tok, sc], [b.kd], scale=sc[:, F_, h:h + 1])
            yield
            for h in range(4):
                k.act(b.DcT[:, h, :], b.X[:, h, :], AF.Exp, [b.X, sc], [b.DcT], bias=sc[:, NGC, h:h + 1])
            k.tt("pool", b.DcsT[:, :, :], b.DcT[:, :, :], m4(strT), ALU.mult, [b.DcT, strT], [b.DcsT])
            pQ, pK = nxt(), nxt()
            for h in range(4):
                k.mm(pQ[:, h * 128:(h + 1) * 128], d[:, 4 + h, :], d[:, h, :], True, True, [d], [pQ])
                k.mm(pK[:, h * 128:(h + 1) * 128], d[:, 4 + h, :], b.kbT[:, h, :], True, True, [d, b.kbT], [pK])
            k.tt("dve", b.qkT[:, :, :], v4(pQ), b.DcT[:, :, :], ALU.mult, [pQ, b.DcT], [b.qkT])
            k.tt("dve", b.AT[:, :, :], v4(pK), b.DcsT[:, :, :], ALU.mult, [pK, b.DcsT], [b.AT])
            yield
            for h in range(4):
                k.tr(C.ps_bf[:, h * 128:(h + 1) * 128], b.AT[:, h, :], C.ident_bf[:, :], [b.AT, C.ident_bf], [C.ps_bf])
            k.copy("act", b.A[:, :, :], C.ps_bf[:, 0:512].rearrange("p (h t) -> p h t", h=4), [C.ps_bf], [b.A])
            Tc, TTc = b.Tm[0], b.TTm[0]
            k.tt("pool", b.AmT[0][:, :, :], b.AT[:, :, :], lm4(0), ALU.mult, [b.AT, lvl], [b.AmT[0]])
            k.tt("dve", TTc[:, :, :], identb4, b.AmT[0][:, :, :], ALU.subtract, [C.ident_bf, b.AmT[0]], [TTc])
            k.tt("pool", b.Am[:, :, :], b.A[:, :, :], lm4(7), ALU.mult, [b.A, lvl], [b.Am])
            k.tt("dve", Tc[:, :, :], identb4, b.Am[:, :, :], ALU.subtract, [C.ident_bf, b.Am], [Tc])
            yield
            for l in range(1, 7):
                Tn, TTn = b.Tm[l % 2], b.TTm[l % 2]
                amt = b.AmT[l % 2]
                k.tt("pool", amt[:, :, :], b.AT[:, :, :], lm4(l), ALU.mult, [b.AT, lvl], [amt])
                pU = nxt()
                for h in range(4):
                    k.mm(pU[:, h * 128:(h + 1) * 128], amt[:, h, :], Tc[:, h, :], True, True, [amt, Tc], [pU])
                k.copy("act", b.Um[:, :, :], v4(pU), [pU], [b.Um])
                pVT = nxt()
                for h in range(4):
                    k.mm(pVT[:, h * 128:(h + 1) * 128], b.Um[:, h, :], TTc[:, h, :], True, True, [b.Um, TTc], [pVT])
                if l < 6:
                    pV = nxt()
                    for h in range(4):
                        k.mm(pV[:, h * 128:(h + 1) * 128], TTc[:, h, :], b.Um[:, h, :], True, True, [TTc, b.Um], [pV])
                k.tt("dve", TTn[:, :, :], TTc[:, :, :], v4(pVT), ALU.subtract, [TTc, pVT], [TTn])
                if l < 6:
                    k.tt("dve", Tn[:, :, :], Tc[:, :, :], v4(pV), ALU.subtract, [Tc, pV], [Tn])
                Tc, TTc = Tn, TTn
                yield
            pY0, pY1 = nxt(), nxt()
            for h in range(4):
                py = (pY0, pY1)[h // 2]
                k.mm(py[:, (h % 2) * 256:(h % 2 + 1) * 256], TTc[:, h, :], b.rhs0[:, h, :], True, True,
                     [TTc, b.rhs0], [py])
            ycur = b.y
            k.copy("act", ycur[:, 0:2, :], pY0[:, 0:512].rearrange("p (h t) -> p h t", h=2), [pY0], [ycur])
            k.copy("dve", ycur[:, 2:4, :], pY1[:, 0:512].rearrange("p (h t) -> p h t", h=2), [pY1], [ycur])
            for h in range(4):
                k.tr(C.ps_bf[:, h * 128:(h + 1) * 128], ycur[:, h, 128:256], C.ident_bf[:, :], [ycur, C.ident_bf],
                     [C.ps_bf])
            k.copy("act", b.wT[:, :, :], C.ps_bf[:, 0:512].rearrange("p (h t) -> p h t", h=4), [C.ps_bf], [b.wT])
            yield
            p1 = nxt()
            for h in range(4):
                k.mm(p1[:, h * 128:(h + 1) * 128], b.wT[:, h, :], Sb[:, h, :], True, True, [b.wT, Sb], [p1])
            k.tt("dve", b.vnew[:, :, :], ycur[:, :, 0:128], v4(p1), ALU.subtract, [ycur, p1], [b.vnew])
            p2, p3 = nxt(), nxt()
            for h in range(4):
                k.mm(p3[:, h * 128:(h + 1) * 128], b.kd[:, h, :], b.vnew[:, h, :], True, True, [b.kd, b.vnew], [p3])
            for h in range(4):
                k.mm(p2[:, h * 128:(h + 1) * 128], b.qeT[:, h, :], Sb[:, h, :], True, False, [b.qeT, Sb], [p2])
                k.mm(p2[:, h * 128:(h + 1) * 128], b.qkT[:, h, :], b.vnew[:, h, :], False, True, [b.qkT, b.vnew], [p2])
            for h in range(4):
                k.stt("dve", St[:, h, :], St[:, h, :], sc[:, GL, h:h + 1], p3[:, h * 128:(h + 1) * 128],
                      ALU.mult, ALU.add, [St, sc, p3], [St])
            k.copy("pool", Sb[:, :, :], St[:, :, :], [St], [Sb])
            k.copy("act", b.osb[:, :, :], v4(p2), [p2], [b.osb])
            yield
            k.memset("pool", b.ss[:, :], 0.0, [b.ss])
            for h in range(4):
                k.act(b.junk[:, :], b.osb[:, h, :], AF.Square, [b.osb], [b.junk, b.ss], accum_out=b.ss[:, h:h + 1])
            k.act(b.ss[:, :], b.ss[:, :], AF.Ln, [b.ss, C.eps_col], [b.ss], bias=C.eps_col[:, 0:1], scale=1.0 / 128)
            k.act(b.ss[:, :], b.ss[:, :], AF.Exp, [b.ss], [b.ss], scale=-0.5)
            for h in range(4):
                k.stt("dve", b.osb[:, h, :], b.osb[:, h, :], b.ss[:, h:h + 1], ngb[:, :], ALU.mult, ALU.mult,
                      [b.osb, b.ss, ngb], [b.osb])
            k.tt("pool", b.dtk[:, :, :], b.osb[:, :, :], g_[:, :].rearrange("p (h t) -> p h t", h=4), ALU.mult,
                 [b.osb, g_], [b.dtk])
            yield
            for h in range(4):
                k.tr(C.ps_bf[:, h * 128:(h + 1) * 128], b.dtk[:, h, :], C.ident_bf[:, :], [b.dtk, C.ident_bf], [C.ps_bf])
            k.copy("act", b.doT[:, :, :], C.ps_bf[:, 0:512].rearrange("p (h t) -> p h t", h=4), [C.ps_bf], [b.doT])
            k.dma(C.mixT[512:1024, n * 128:(n + 1) * 128].rearrange("(h p) t -> p h t", p=128), b.doT[:, :, :],
                  [b.doT], [C.mixT_r[(n * 128) // TT]], sem=b.doT)
            if n + NB_ < NCH:
                load(n + NB_)

        for n in range(min(NB_, NCH)):
            load(n)
        run_pipelined(chunk, NCH, 5, maxlive=NB_)
    k.S.end_phase()


SEQ = 8192
N_ACTIVE = 2


def kernel(**inputs):
    S = SEQ
    nc = build(S)
    consts = host_consts(S)
    in_maps = [core_inputs(inputs, b, S, consts) for b in range(N_ACTIVE)]
    res = run_bass_kernel_spmd(nc, in_maps, core_ids=list(range(N_ACTIVE)))
    out = np.stack([np.ascontiguousarray(res.results[b]["outT"].T) for b in range(N_ACTIVE)], axis=0)
    return out.astype(np.float32)
```

```python
import numpy as np
import ml_dtypes
from contextlib import ExitStack
import concourse.bass as bass
import concourse.mybir as mybir
from concourse.bass_utils import run_bass_kernel_spmd

F32 = mybir.dt.float32
BF16 = mybir.dt.bfloat16
AF = mybir.ActivationFunctionType
ALU = mybir.AluOpType
AX = mybir.AxisListType

D = 1024
NMEM = 256
EPS = 1e-6
DFF = 2816
NEG = -30000.0


class Res:
    __slots__ = ("name", "writers", "readers", "dsem")

    def __init__(self, name):
        self.name = name
        self.writers = []
        self.readers = []
        self.dsem = None


class Op:
    __slots__ = ("eng", "fn", "deps", "signal", "count", "dtok", "idx", "after", "dur", "gidx",
                 "nun", "rdy", "fin", "users")

    def __init__(self, eng, fn):
        self.eng = eng
        self.fn = fn
        self.deps = []
        self.signal = False
        self.count = 0
        self.dtok = None
        self.idx = 0
        self.after = []
        self.dur = 300.0


ENGS = ("pe", "act", "dve", "pool", "sp")


class Sched:
    def __init__(self, nc, n_dma_sems=40):
        self.nc = nc
        self.ops = {e: [] for e in ENGS}
        self.n_dma_sems = n_dma_sems
        self.dma_counts = [0] * n_dma_sems
        self.n_sp_sems = n_dma_sems - 8
        self.free_dsems = list(range(self.n_sp_sems))
        self.free_dsems_pool = list(range(self.n_sp_sems, n_dma_sems))
        self.phase_res = []
        self.all_res = []
        self.reorder_on = True
        self.seg_flags = []

    def res(self, name):
        r = Res(name)
        self.all_res.append(r)
        return r

    def _add(self, eng, fn, reads, writes, acc):
        o = Op(eng, fn)
        o.idx = len(self.ops[eng])
        deps = []
        for r in reads:
            deps.extend(r.writers)
        for w in writes:
            deps.extend(w.readers)
            if not acc:
                deps.extend(w.writers)
            elif acc == 'dma':
                deps.extend(t for t in w.writers if t[0] != 'D')
            else:
                for t in w.writers:
                    if t[0] == 'E' and t[1].eng == eng:
                        o.after.append(t[1])
                    else:
                        deps.append(t)
        o.deps = deps
        self.ops[eng].append(o)
        return o

    def _commit(self, tok, reads, writes):
        for r in reads:
            r.readers.append(tok)
        for w in writes:
            if w.readers:
                w.writers = [tok]
                w.readers = []
            else:
                w.writers.append(tok)
                if len(w.writers) > 64:
                    w.writers = w.writers[-64:]

    def op(self, eng, fn, reads=(), writes=(), acc=False):
        o = self._add(eng, fn, reads, writes, acc)
        self._commit(('E', o), reads, writes)
        return o

    def dma(self, eng, out, in_, reads=(), writes=(), sem_res=None):
        sr = sem_res if sem_res is not None else writes[0]
        if sr.dsem is None:
            sr.dsem = (self.free_dsems_pool if eng == "pool" else self.free_dsems).pop(0)
        o = self._add(eng, lambda e, out=out, in_=in_: e.dma_start(out=out, in_=in_), reads, writes, 'dma')
        self.dma_counts[sr.dsem] += 16
        o.dtok = (sr.dsem, self.dma_counts[sr.dsem])
        self._commit(('D', sr.dsem, o.dtok[1]), reads, writes)
        return o

    def end_phase(self):
        toks = []
        for e in ENGS:
            if self.ops[e]:
                last = self.ops[e][-1]
                if last.dtok is None:
                    toks.append(('E', last))
        for i in range(self.n_dma_sems):
            if self.dma_counts[i] > 0:
                toks.append(('D', i, self.dma_counts[i]))
        self.seg_flags.append(self.reorder_on)
        for e in ENGS:
            o = Op(e, None)
            o.idx = len(self.ops[e])
            o.deps = list(toks)
            self.ops[e].append(o)
        for r in self.all_res:
            r.dsem = None
            r.writers = []
            r.readers = []
        self.free_dsems = list(range(self.n_sp_sems))
        self.free_dsems_pool = list(range(self.n_sp_sems, self.n_dma_sems))

    def reorder(self, window=24):
        import heapq
        dma_of = {}
        for e in ENGS:
            for o in self.ops[e]:
                if o.dtok is not None:
                    dma_of[o.dtok] = o
        pos = {e: 0 for e in ENGS}
        new_ops = {e: [] for e in ENGS}
        segi = -1
        while any(pos[e] < len(self.ops[e]) for e in ENGS):
            segi += 1
            seg = {}
            for e in ENGS:
                lst = self.ops[e]
                i = pos[e]
                j = i
                while j < len(lst) and lst[j].fn is not None:
                    j += 1
                seg[e] = lst[i:j]
                pos[e] = j + 1 if j < len(lst) else j
                bar = lst[j] if j < len(lst) else None
                seg[e + "_bar"] = bar
            if segi < len(self.seg_flags) and not self.seg_flags[segi]:
                self._fix_barrier(seg, {e: seg[e] for e in ENGS})
                for e in ENGS:
                    new_ops[e].extend(seg[e])
                    if seg[e + "_bar"] is not None:
                        new_ops[e].append(seg[e + "_bar"])
                continue
            allops = [o for e in ENGS for o in seg[e]]
            inseg = set(id(o) for o in allops)
            for o in allops:
                o.users = []
                o.fin = None
            for o in allops:
                n = 0
                dl = []
                for t in o.deps:
                    d = t[1] if t[0] == 'E' else dma_of.get((t[1], t[2]))
                    if d is not None and id(d) in inseg and d.fn is not None:
                        dl.append(d)
                for d in o.after:
                    if id(d) in inseg:
                        dl.append(d)
                o.nun = len(dl)
                o.rdy = 0.0
                for d in dl:
                    d.users.append(o)
            free = {e: 0.0 for e in ENGS}
            pend = {e: list(seg[e]) for e in ENGS}
            head = {e: 0 for e in ENGS}
            issued = {e: [] for e in ENGS}
            remaining = len(allops)
            while remaining:
                best = None
                for e in ENGS:
                    lst = pend[e]
                    h = head[e]
                    while h < len(lst) and lst[h] is None:
                        h += 1
                    head[e] = h
                    if h >= len(lst):
                        continue
                    w = 1 if e == "sp" else window
                    cnt = 0
                    i = h
                    while i < len(lst) and cnt < w:
                        o = lst[i]
                        if o is not None:
                            cnt += 1
                            if o.nun == 0:
                                stt = o.rdy if o.rdy > free[e] else free[e]
                                if best is None or stt < best[0]:
                                    best = (stt, e, i)
                                if stt <= free[e]:
                                    break
                        i += 1
                if best is None:
                    raise RuntimeError("reorder: no schedulable op (cyclic deps?)")
                stt, e, i = best
                o = pend[e][i]
                pend[e][i] = None
                issued[e].append(o)
                remaining -= 1
                if o.dtok is not None:
                    free[e] = stt + 60.0
                    o.fin = stt + 2500.0 + o.dur
                else:
                    free[e] = stt + o.dur
                    o.fin = stt + o.dur + 150.0
                for u in o.users:
                    u.nun -= 1
                    if o in u.after and o not in [ (t[1] if t[0] == 'E' else None) for t in u.deps]:
                        r = stt
                    else:
                        r = o.fin
                    if r > u.rdy:
                        u.rdy = r
            self._fix_barrier(seg, issued)
            for e in ENGS:
                new_ops[e].extend(issued[e])
                if seg[e + "_bar"] is not None:
                    new_ops[e].append(seg[e + "_bar"])
        self.ops = new_ops

    @staticmethod
    def _fix_barrier(seg, order):
        etoks = []
        for e in ENGS:
            for o in reversed(order[e]):
                if o.fn is not None and o.dtok is None:
                    etoks.append(('E', o))
                    break
        for e in ENGS:
            bar = seg[e + "_bar"]
            if bar is not None:
                bar.deps = [t for t in bar.deps if t[0] == 'D'] + etoks

    def emit(self, esems, dsems):
        nc = self.nc
        if getattr(self, "do_reorder", True):
            self.reorder()
        for e in ENGS:
            for o in self.ops[e]:
                for t in o.deps:
                    if t[0] == 'E':
                        t[1].signal = True
        for e in ENGS:
            c = 0
            for o in self.ops[e]:
                if o.signal and o.fn is not None and o.dtok is None:
                    c += 1
                    o.count = c
                elif o.signal:
                    o.count = c
        sched = self

        def run(e_name, eng):
            seen = {}
            for o in sched.ops[e_name]:
                need = {}
                for t in o.deps:
                    if t[0] == 'E':
                        d = t[1]
                        if d.count == 0:
                            continue
                        key = ('E', d.eng)
                        val = d.count
                    else:
                        key = ('D', t[1])
                        val = t[2]
                    if need.get(key, 0) < val:
                        need[key] = val
                for key, val in need.items():
                    if seen.get(key, 0) >= val:
                        continue
                    seen[key] = val
                    sem = esems[key[1]] if key[0] == 'E' else dsems[key[1]]
                    eng.wait_ge(sem, val)
                if o.fn is None:
                    continue
                ins = o.fn(eng)
                if o.dtok is not None:
                    ins.then_inc(dsems[o.dtok[0]], 16)
                elif o.signal:
                    ins.then_inc(esems[e_name], 1)

        with nc.Block() as block:
            @block.tensor
            def _(eng):
                run("pe", eng)

            @block.scalar
            def _(eng):
                run("act", eng)

            @block.vector
            def _(eng):
                run("dve", eng)

            @block.gpsimd
            def _(eng):
                run("pool", eng)

            @block.sync
            def _(eng):
                run("sp", eng)


class Buf:
    def __init__(self, S, t, name):
        self.t = t
        self.r = S.res(name)

    def __getitem__(self, k):
        return self.t[k]


class K:
    def __init__(self, nc):
        self.nc = nc
        self.S = Sched(nc)
        self.stack = ExitStack()
        self.n = 0

    def sb(self, shape, dt, name=None, ctx=None):
        self.n += 1
        name = f"{name or 'sb'}_{self.n}"
        t = (ctx or self.stack).enter_context(self.nc.sbuf_tensor(name, list(shape), dt))
        return Buf(self.S, t, name)

    def psum(self, shape, dt=F32, name=None, ctx=None):
        self.n += 1
        name = f"{name or 'ps'}_{self.n}"
        t = (ctx or self.stack).enter_context(self.nc.psum_tensor(name, list(shape), dt))
        return Buf(self.S, t, name)

    def dram(self, shape, dt, name):
        t = self.nc.dram_tensor(name, list(shape), dt)
        b = Buf(self.S, t.ap(), name)
        return b

    def _rw(self, reads, writes):
        return [b.r for b in reads], [b.r for b in writes]

    def op(self, eng, fn, reads, writes, acc=False, dur=None):
        r, w = self._rw(reads, writes)
        o = self.S.op(eng, fn, r, w, acc)
        if dur is not None:
            o.dur = dur
        return o

    @staticmethod
    def _fsz(ap):
        n = 1
        for d in list(ap.shape)[1:]:
            n *= int(d)
        return n

    def dma(self, out, in_, reads, writes, eng="sp", sem=None):
        r, w = self._rw(reads, writes)
        return self.S.dma(eng, out, in_, r, w, sem.r if sem is not None else None)

    def mm(self, out, lhsT, rhs, start, stop, reads, writes):
        return self.op("pe", lambda e: e.matmul(out, lhsT, rhs, start=start, stop=stop), reads, writes, acc=True,
                       dur=70.0 + 0.45 * self._fsz(rhs) * (4 if rhs.dtype == F32 else 1))

    def tr(self, out, in_, ident, reads, writes):
        return self.op("pe", lambda e: e.transpose(out, in_, ident), reads, writes, acc=True, dur=130.0)

    def act(self, out, in_, func, reads, writes, bias=None, scale=None, accum_out=None, eng="act"):
        kw = {}
        if bias is not None:
            kw["bias"] = bias
        if scale is not None:
            kw["scale"] = scale
        if accum_out is not None:
            kw["accum_out"] = accum_out
        return self.op("act", lambda e: e.activation(out, in_, func, **kw), reads, writes,
                       dur=220.0 + 0.95 * self._fsz(out))

    def tt(self, eng, out, in0, in1, op, reads, writes):
        return self.op(eng, lambda e: e.tensor_tensor(out, in0, in1, op), reads, writes,
                       dur=(120.0 + 1.0 * self._fsz(out)) * (2.0 if eng == "pool" else 1.0))

    def ts(self, eng, out, in0, s1, s2, op0, op1, reads, writes):
        du = 120.0 + 0.9 * self._fsz(out)
        if op1 is None:
            return self.op(eng, lambda e: e.tensor_scalar(out, in0, s1, None, op0), reads, writes, dur=du)
        return self.op(eng, lambda e: e.tensor_scalar(out, in0, s1, s2, op0, op1), reads, writes, dur=du)

    def stt(self, eng, out, in0, scalar, in1, op0, op1, reads, writes):
        return self.op(eng, lambda e: e.scalar_tensor_tensor(out, in0, scalar, in1, op0, op1), reads, writes,
                       dur=120.0 + 1.4 * self._fsz(out))

    def copy(self, eng, out, in_, reads, writes):
        if eng == "act":
            return self.op("act", lambda e: e.copy(out, in_), reads, writes, dur=220.0 + 0.8 * self._fsz(out))
        return self.op(eng, lambda e: e.tensor_copy(out, in_), reads, writes,
                       dur=(120.0 + 0.7 * self._fsz(out)) * (2.0 if eng == "pool" else 1.0))

    def memset(self, eng, ap, val, writes):
        return self.op(eng, lambda e: e.memset(ap, val), [], writes)


TT = 512


class Ctx:
    pass


def load_w(k, ctx, src, kin, n, name, gain=None, stg=None, col0=0, w=None):
    kc_n = kin // 128
    if w is None:
        w = k.sb([128, kc_n, n], BF16, name, ctx)
    i = 0
    for kc in range(kc_n):
        for n0 in range(0, n, 2048):
            wd = min(2048, n - n0)
            s = stg[i % len(stg)]
            k.dma(s[:, 0:wd], src[kc * 128:(kc + 1) * 128, col0 + n0:col0 + n0 + wd], [], [s],
                  eng="sp")
            if gain is None:
                k.copy(("dve", "pool")[i % 2], w[:, kc, n0:n0 + wd], s[:, 0:wd], [s], [w])
            elif i % 2 == 0:
                k.ts("dve", w[:, kc, n0:n0 + wd], s[:, 0:wd], gain[:, kc:kc + 1], None, ALU.mult, None,
                     [s, gain], [w])
            else:
                k.act(w[:, kc, n0:n0 + wd], s[:, 0:wd], AF.Copy, [s, gain], [w], scale=gain[:, kc:kc + 1])
            i += 1
    return w


def rmsnorm(k, C, x, xn, T, kc_n=8, dim=D, nb=None, xoff=0):
    sq, r, ps = nb if nb is not None else (C.sq, C.rr, C.ps_norm)
    k.act(sq[:, 0:kc_n, 0:T], x[:, 0:kc_n, 0:T], AF.Square, [x], [sq])
    for kc in range(kc_n):
        k.mm(ps[:, 0:T], C.ones_bf[:, :], sq[:, kc, 0:T], kc == 0, kc == kc_n - 1, [sq, C.ones_bf], [ps])
    k.act(r[:, 0:T], ps[:, 0:T], AF.Ln, [ps, C.eps_col], [r], bias=C.eps_col[:, 0:1], scale=1.0 / dim)
    k.act(r[:, 0:T], r[:, 0:T], AF.Exp, [r], [r], scale=-0.5)
    k.tt("dve", xn[:, 0:kc_n, xoff:xoff + T], x[:, 0:kc_n, 0:T],
         r[:, 0:T].rearrange("p (o t) -> p o t", o=1).to_broadcast([128, kc_n, T]), ALU.mult, [x, r], [xn])


def run_pipelined(make_gen, n, lag, maxlive=2):
    gens = [make_gen(t) for t in range(n)]
    prog = [0] * n
    done = [False] * n
    first = 0
    while first < n:
        for t in range(first, n):
            if t > first and not (done[t - 1] or prog[t - 1] >= lag):
                break
            if t - maxlive >= 0 and not done[t - maxlive]:
                break
            if done[t]:
                continue
            try:
                next(gens[t])
                prog[t] += 1
            except StopIteration:
                done[t] = True
        while first < n and done[first]:
            first += 1


def proj_fm(k, ps, w, xn, m0, T, kc_n, reads, col=None):
    for kc in range(kc_n):
        k.mm(ps[:, 0:T], w[:, kc, m0:m0 + 128], xn[:, kc, 0:T], kc == 0, kc == kc_n - 1, reads, [ps])


def phase_l0_inproj(k, C, S):
    nc = k.nc
    NT = S // TT
    with ExitStack() as ctx:
        C.sq = k.sb([128, 8, TT], BF16, "sq", ctx)
        C.rr = k.sb([128, TT], F32, "rr", ctx)
        stg = [k.sb([128, 2048], F32, "stg", ctx) for _ in range(2)]
        gain = k.sb([128, 8], F32, "gain", ctx)
        k.dma(gain[:, :], C.norm_mix[0], [], [gain])
        w_in = load_w(k, ctx, C.ev_w_in, D, 2048, "w_in", gain=gain, stg=stg)
        pw_f = k.sb([128, 4, 128], F32, "pw_f", ctx)
        pw = k.sb([128, 4, 128], BF16, "pw", ctx)
        k.dma(pw_f[:, :, :], C.pool_w.rearrange("g c d -> c g d"), [], [pw_f])
        k.copy("dve", pw[:, :, :], pw_f[:, :, :], [pw_f], [pw])
        pscale = k.sb([128, 4], F32, "pscale", ctx)
        k.dma(pscale[:, :], C.pool_scale, [], [pscale])
        corr = k.sb([128, 4, 16], F32, "corr", ctx)
        k.dma(corr[:, :, :], C.pool_corr, [], [corr])

        xt = [k.sb([128, 8, TT], F32, "xt", ctx) for _ in range(2)]
        xn = k.sb([128, 8, TT], BF16, "xn", ctx)
        qk = [k.sb([128, 8, TT], BF16, "qk", ctx) for _ in range(2)]
        vt = [k.sb([128, 4, 8, 65], BF16, "vt", ctx) for _ in range(2)]
        for b in vt:
            k.memset("pool", b[:, :, :, :], 1.0, [b])
        pb = k.sb([128, 4, 16 + TT], F32, "pb", ctx)
        k.memset("pool", pb[:, :, :], 0.0, [pb])
        wa = k.sb([128, 16 + TT], F32, "wa", ctx)
        wb = k.sb([128, 16 + TT], F32, "wb", ctx)
        k.memset("pool", wa[:, :], 0.0, [wa])
        k.memset("pool", wb[:, :], 0.0, [wb])
        pooled = k.sb([128, 4, TT], BF16, "pooled", ctx)
        bout = [k.sb([128, 4, TT], BF16, "bout", ctx) for _ in range(2)]
        ksum = k.sb([128, 4, 2], F32, "ksum", ctx)
        ps = C.ps
        pi = 0

        def load_x(t):
            b = xt[t % 2]
            k.dma(b[:, :, :], C.xT[:, t * TT:(t + 1) * TT].rearrange("(c p) t -> p c t", p=128), [], [b])

        load_x(0)
        for t in range(NT):
            if t + 1 < NT:
                load_x(t + 1)
            x = xt[t % 2]
            t0 = t * TT
            rmsnorm(k, C, x, xn, TT)
            qkb = qk[t % 2]
            for m in range(8):
                p = ps[pi % 6]; pi += 1
                proj_fm(k, p, w_in, xn, m * 128, TT, 8, [w_in, xn])
                k.copy("act", qkb[:, m, :], p[:, 0:TT], [p], [qkb])
            k.op("dve", lambda e, qkb=qkb: e.tensor_reduce(
                ksum[:, :, :], qkb[:, 4:8, :].rearrange("p c (b t) -> p c b t", b=2), AX.X, ALU.add),
                [qkb], [ksum])
            k.ts("dve", C.kmean[:, :, 2 * t:2 * t + 2], ksum[:, :, :], 1.0 / 256, None, ALU.mult, None,
                 [ksum], [C.kmean])
            for c in range(4):
                for hp in range(2):
                    k.dma(C.qaugT[2 * c + hp, 0:64, t0:t0 + TT], qkb[hp * 64:(hp + 1) * 64, c, :],
                          [qkb], [C.qaugT_r[t]], sem=qkb)
            k.dma(C.kT[:, t0:t0 + TT].rearrange("(c p) t -> p c t", p=128), qkb[:, 4:8, :],
                  [qkb], [C.kT_r[t]], sem=qkb)
            vb = vt[t % 2]
            for sub in range(4):
                p = ps[pi % 6]; pi += 1
                for kc in range(8):
                    k.mm(p[:, 0:512], xn[:, kc, sub * 128:(sub + 1) * 128], w_in[:, kc, 1024:1536],
                         kc == 0, kc == 7, [w_in, xn], [p])
                k.copy("dve", vb[:, sub, :, 1:65], p[:, 0:512].rearrange("p (h d) -> p h d", h=8), [p], [vb])
            k.dma(C.vaug[t0:t0 + TT, :].rearrange("(s p) c -> p s c", p=128),
                  vb[:, :, :, :].rearrange("p s h d -> p s (h d)"), [vb], [C.vaug_r[t]], sem=vb)
            for g in range(4):
                p = ps[pi % 6]; pi += 1
                proj_fm(k, p, w_in, xn, 1536 + g * 128, TT, 8, [w_in, xn])
                k.copy("act", pb[:, g, 16:16 + TT], p[:, 0:TT], [p], [pb])
            W = 16 + TT
            for g in range(4):
                eng = ("dve", "pool")[g % 2]
                src = pb
                cur = pb[:, g, :]
                bufs = [wa, wb]
                for lvl in range(g + 1):
                    sh = 1 << lvl
                    dst = bufs[lvl % 2]
                    k.tt(eng, dst[:, sh:W], cur[:, sh:W], cur[:, 0:W - sh], ALU.add, [src], [dst])
                    src = dst
                    cur = dst[:, :]
                wnd = 2 << g
                if t == 0:
                    k.tt(eng, cur[:, 16:32], cur[:, 16:32], corr[:, g, :], ALU.mult, [src, corr], [src])
                k.stt("dve", pooled[:, g, :], cur[:, 16:W], 1.0 / wnd, pb[:, g, 16:W], ALU.mult, ALU.subtract,
                      [src, pb], [pooled])
            k.copy("pool", pb[:, :, 0:16], pb[:, :, TT:TT + 16], [pb], [pb])
            bo = bout[t % 2]
            for g in range(4):
                p = ps[pi % 6]; pi += 1
                k.mm(p[:, 0:TT], pw[:, g, :], pooled[:, g, :], True, True, [pw, pooled], [p])
                k.ts("dve", bo[:, g, :], p[:, 0:TT], pscale[:, g:g + 1], None, ALU.mult, None, [p, pscale], [bo])
            k.dma(C.mixT[512:1024, t0:t0 + TT].rearrange("(g p) t -> p g t", p=128), bo[:, :, :],
                  [bo], [C.mixT_r[t]], sem=bo)
    k.S.end_phase()


class RB:
    def __init__(self, S, name):
        self.r = S.res(name)


def host_consts(S):
    c = {}
    c["ident_bf"] = np.eye(128, dtype=np.float32).astype(ml_dtypes.bfloat16)
    c["ident_f"] = np.eye(128, dtype=np.float32)
    corr = np.ones((4, 16), np.float32)
    for g, w in enumerate((2, 4, 8, 16)):
        for t in range(16):
            corr[g, t] = w / min(t + 1, w)
    c["pool_corr"] = np.ascontiguousarray(np.broadcast_to(corr[None], (128, 4, 16)))
    vb = np.concatenate([np.zeros(32, np.float32), np.full(32, -1e30, np.float32)])
    c["validbias"] = np.ascontiguousarray(np.broadcast_to(vb[None], (128, 64)))
    khot = np.zeros((32, S), np.float32)
    for j in range(S // 256):
        khot[j, j * 256:(j + 1) * 256] = 30000.0
    c["khot"] = khot.astype(ml_dtypes.bfloat16)
    kk = np.arange(128)[:, None, None]
    jj = np.arange(2)[None, :, None]
    qq = np.arange(256)[None, None, :]
    c["causal01"] = ((jj * 128 + kk) <= qq).astype(np.float32).astype(ml_dtypes.bfloat16)
    a = np.arange(128)
    c["tril01"] = (a[None, :] <= a[:, None]).astype(np.float32)
    c["tri_le"] = (a[:, None] <= a[None, :]).astype(np.float32)
    c["causT_neg"] = np.where(a[None, :] >= a[:, None], 0.0, -1e4).astype(np.float32)
    c["strictT01"] = (a[None, :] > a[:, None]).astype(np.float32)
    lv = np.zeros((128, 8, 128), np.float32)
    ii, jj2 = a[:, None], a[None, :]
    for l in range(7):
        b = 1 << l
        m = ((ii // (2 * b)) == (jj2 // (2 * b))) & ((ii % (2 * b)) >= b) & ((jj2 % (2 * b)) < b)
        lv[:, l, :] = m.T
        if l == 0:
            lv[:, 7, :] = m
    c["lvlmask"] = lv.astype(ml_dtypes.bfloat16)
    return c


def build(S, phases=None, debug_out=()):
    nc = bass.Bass("TRN2", target_bir_lowering=False)
    k = K(nc)
    C = Ctx()
    NT = S // TT

    def ext(name, shape, dt=F32):
        return Buf(k.S, nc.dram_tensor(name, list(shape), dt, kind="ExternalInput").ap(), name)

    def scratch(name, shape, dt):
        kind = "ExternalOutput" if name in debug_out else "Internal"
        return nc.dram_tensor(name, list(shape), dt, kind=kind).ap()

    C.xT = ext("xT", [D, S]).t
    C.norm_mix = [ext(f"norm_mix{l}", [128, 8]).t for l in range(2)]
    C.ev_w_in = ext("ev_w_in", [D, 2048]).t
    C.pool_w = ext("pool_w", [4, 128, 128]).t
    C.pool_scale = ext("pool_scale", [128, 4]).t
    C.pool_corr = ext("pool_corr", [128, 4, 16]).t
    C.validbias = ext("validbias", [128, 64]).t
    C.khot = ext("khot", [32, S], BF16).t
    C.causal01 = ext("causal01", [128, 2, 256], BF16).t
    C.memT = ext("memT", [D, NMEM]).t
    C.mem_norm = ext("mem_norm", [128, 8]).t
    C.norm_xattn = [ext(f"norm_xattn{l}", [128, 8]).t for l in range(2)]
    C.norm_ffn = [ext(f"norm_ffn{l}", [128, 8]).t for l in range(2)]
    C.final_norm = ext("final_norm", [128, 8]).t
    C.ev_w_out = ext("ev_w_out", [D, D]).t
    C.od_w_out = ext("od_w_out", [D, D]).t
    C.xattn_wq = [ext(f"xattn_wq{l}", [D, D]).t for l in range(2)]
    C.xattn_wkv = [ext(f"xattn_wkv{l}", [D, 2 * D]).t for l in range(2)]
    C.xattn_wo = [ext(f"xattn_wo{l}", [D, D]).t for l in range(2)]
    C.ffn_w_up = [ext(f"ffn_w_up{l}", [D, 2 * DFF]).t for l in range(2)]
    C.ffn_conv = [ext(f"ffn_conv{l}", [128, 2 * DFF // 128, 3]).t for l in range(2)]
    C.ffn_w_down = [ext(f"ffn_w_down{l}", [DFF, D]).t for l in range(2)]
    C.od_w_in = ext("od_w_in", [D, 3080]).t
    C.sgu_w = ext("sgu_w", [4, 128, 128]).t
    C.sgu_ln_g = ext("sgu_ln_g", [128, 4]).t
    C.sgu_ln_b = ext("sgu_ln_b", [128, 4]).t
    C.sgu_b = ext("sgu_b", [128, 4, 128]).t
    C.dn_conv = ext("dn_conv", [128, 12, 4]).t
    C.dn_a_log = ext("dn_a_log", [128, 4]).t
    C.dn_dt_bias = ext("dn_dt_bias", [128, 4]).t
    C.dn_norm_g = ext("dn_norm_g", [128, 128]).t
    C.tril01 = ext("tril01", [128, 128]).t
    C.tri_le = ext("tri_le", [128, 128]).t
    C.causT_neg = ext("causT_neg", [128, 128]).t
    C.strictT01 = ext("strictT01", [128, 128]).t
    C.lvlmask = ext("lvlmask", [128, 8, 128], BF16).t
    ident_bf_d = ext("ident_bf", [128, 128], BF16).t
    ident_f_d = ext("ident_f", [128, 128]).t
    C.qaugT = scratch("qaugT", [8, 96, S], BF16)
    C.kT = scratch("kT", [512, S], BF16)
    C.vaug = scratch("vaug", [S, 520], BF16)
    C.mixT = scratch("mixT", [D, S], BF16)
    C.dnT = scratch("dnT", [1536, S], BF16)
    C.gtok = scratch("gtok", [S, 512], BF16)
    C.batok = scratch("batok", [S, 8], F32)
    C.hA = scratch("hA", [D, S], F32)
    C.hB = scratch("hB", [D, S], F32)
    C.outT = nc.dram_tensor("outT", [D, S], F32, kind="ExternalOutput").ap()
    for nm in ("qaugT", "kT", "vaug", "mixT", "hA", "hB", "xT", "outT", "dnT", "gtok", "batok"):
        setattr(C, nm + "_r", [RB(k.S, f"{nm}{t}") for t in range(NT)])

    with k.stack:
        C.ps = [k.psum([128, 512], F32, "ps") for _ in range(6)]
        C.ps_norm = k.psum([128, 512], F32, "psn")
        C.ps_bf = k.psum([128, 1024], BF16, "psbf")
        C.ones_bf = k.sb([128, 128], BF16, "ones")
        k.memset("pool", C.ones_bf[:, :], 1.0, [C.ones_bf])
        C.ident_bf = k.sb([128, 128], BF16, "identb")
        k.dma(C.ident_bf[:, :], ident_bf_d, [], [C.ident_bf])
        C.ident_f = k.sb([128, 128], F32, "identf")
        k.dma(C.ident_f[:, :], ident_f_d, [], [C.ident_f])
        C.eps_col = k.sb([128, 1], F32, "epsc")
        k.memset("pool", C.eps_col[:, :], EPS, [C.eps_col])
        C.kmean = k.sb([128, 4, 32], F32, "kmean")
        k.memset("pool", C.kmean[:, :, :], 0.0, [C.kmean])
        k.S.end_phase()

        if phases is None or 1 in phases:
            phase_l0_inproj(k, C, S)
        if phases is None or 2 in phases:
            phase_moba_gate(k, C, S)
        if phases is None or 22 in phases:
            phase_moba_attn(k, C, S)
        if phases is None or 3 in phases:
            phase_outproj_xattn(k, C, S, 0, C.ev_w_out, C.xT, C.xT_r, C.hA, C.hA_r)
        if phases is None or 4 in phases:
            phase_ffn(k, C, S, 0, C.hA, C.hA_r, C.hB, C.hB_r, False)
        if phases is None or 5 in phases:
            phase_l1_inproj(k, C, S)
        if phases is None or 6 in phases:
            phase_deltanet(k, C, S)
        if phases is None or 7 in phases:
            phase_outproj_xattn(k, C, S, 1, C.od_w_out, C.hB, C.hB_r, C.hA, C.hA_r)
        if phases is None or 8 in phases:
            phase_ffn(k, C, S, 1, C.hA, C.hA_r, C.outT, C.outT_r, True)

        esems = {}
        for e in ENGS:
            esems[e] = k.stack.enter_context(nc.semaphore(f"es_{e}"))
        dsems = [k.stack.enter_context(nc.semaphore(f"ds_{i}")) for i in range(k.S.n_dma_sems)]
        k.S.emit(esems, dsems)
    return nc


def phase_moba_gate(k, C, S):
    GS = 9
    NT = S // TT
    with ExitStack() as ctx:
        kmb = k.sb([128, 4, 64], BF16, "kmb", ctx)
        k.memset("pool", kmb[:, :, :], 0.0, [kmb])
        k.copy("dve", kmb[0:64, :, 0:32], C.kmean[0:64, :, :], [C.kmean], [kmb])
        k.copy("dve", kmb[64:128, :, 32:64], C.kmean[64:128, :, :], [C.kmean], [kmb])
        vbias = k.sb([128, 64], F32, "vbias", ctx)
        k.dma(vbias[:, :], C.validbias, [], [vbias])
        qc = [k.sb([128, 4, TT], BF16, "qc", ctx) for _ in range(2)]
        gsb = k.sb([128, 8, 32], F32, "gsb", ctx)
        top8 = k.sb([128, 8, 8], F32, "top8", ctx)
        sel = k.sb([128, 8, 32], F32, "sel", ctx)
        mb = k.sb([128, 8, 96], BF16, "mb", ctx)
        k.memset("pool", mb[:, :, :], 0.0, [mb])
        mrow = [k.sb([128, 8, TT], BF16, "mrow", ctx) for _ in range(2)]
        ps = C.ps

        def load_q(t):
            b = qc[t % 2]
            for h in range(8):
                k.dma(b[(h % 2) * 64:(h % 2) * 64 + 64, h // 2, :], C.qaugT[h, 0:64, t * TT:(t + 1) * TT],
                      [C.qaugT_r[t]], [b])

        load_q(0)
        for t in range(NT):
            if t + 1 < NT:
                load_q(t + 1)
            q = qc[t % 2]
            mr = mrow[t % 2]
            for sub in range(4):
                qb = (t * TT + sub * 128) // 256
                gp = ps[sub % 2]
                for c in range(4):
                    k.mm(gp[:, c * 64:(c + 1) * 64], q[:, c, sub * 128:(sub + 1) * 128],
                         kmb[:, c, :], True, True, [q, kmb], [gp])
                k.tt("dve", gsb[:, :, :], gp[:, 0:256].rearrange("p (h n) -> p h n", h=8),
                     vbias[:, 32 - qb:64 - qb].rearrange("p (o n) -> p o n", o=1).to_broadcast([128, 8, 32]),
                     ALU.add, [gp, vbias], [gsb])
                if GS <= 1:
                    continue
                for h in range(8):
                    k.op("dve", lambda e, h=h: e.max(top8[:, h, :], gsb[:, h, :]), [gsb], [top8], acc=True)
                k.tt("dve", sel[:, :, :], gsb[:, :, :], top8[:, :, 2:3].to_broadcast([128, 8, 32]), ALU.is_ge,
                     [gsb, top8], [sel])
                k.ts("dve", mb[:, :, 64:96], sel[:, :, :], -1.0, None, ALU.add, None, [sel], [mb])
                k.memset("dve", mb[:, :, 64 + qb:65 + qb], 0.0, [mb])
                if GS <= 2:
                    continue
                for half in range(2):
                    mp = ps[2 + (sub * 2 + half) % 4]
                    for hh in range(4):
                        h = half * 4 + hh
                        k.mm(mp[0:96, hh * 128:(hh + 1) * 128], mb[:, h, :], C.ident_bf[:, :], True, True,
                             [mb, C.ident_bf], [mp])
                    k.copy("act", mr[64:96, half * 4:half * 4 + 4, sub * 128:(sub + 1) * 128],
                           mp[64:96, 0:512].rearrange("p (h q) -> p h q", h=4), [mp], [mr])
            if GS <= 3:
                continue
            k.dma(C.qaugT[:, 64:96, t * TT:(t + 1) * TT].rearrange("h r t -> r h t"), mr[64:96, :, :],
                  [mr], [C.qaugT_r[t]], sem=mr)
    k.S.end_phase()


def phase_moba_attn(k, C, S):
    NB = S // 256
    NKT = S // 128
    with ExitStack() as ctx:
        vsb = k.sb([128, NKT, 520], BF16, "vsb", ctx)
        for i in range(0, NKT, 8):
            n = min(8, NKT - i)
            k.dma(vsb[:, i:i + n, :], C.vaug[i * 128:(i + n) * 128, :].rearrange("(t p) c -> p t c", p=128),
                  [C.vaug_r[(i * 128) // TT], C.vaug_r[min(S // TT - 1, ((i + n) * 128 - 1) // TT)]], [vsb])
        kaug = [k.sb([96, S], BF16, "kaug", ctx) for _ in range(2)]
        for b in kaug:
            k.dma(b[64:96, :], C.khot, [], [b])
        caus = k.sb([128, 2, 256], BF16, "caus", ctx)
        k.dma(caus[:, :, :], C.causal01, [], [caus])
        onesf = k.sb([128, 65], F32, "onesf", ctx)
        k.memset("pool", onesf[:, :], 1.0, [onesf])
        pt = [k.sb([128, 2, 256], BF16, "pt", ctx) for _ in range(4)]
        rden = [k.sb([1, 256], F32, "rden", ctx) for _ in range(2)]
        bcs = k.sb([65, 256], F32, "bcs", ctx)
        ao = [k.sb([65, 512], BF16, "ao", ctx) for _ in range(2)]
        sps = C.ps[0:3]
        ops = [C.ps[3], C.ps[4], C.ps_norm]
        bps = C.ps[5]
        all_k = [C.kT_r[t] for t in range(S // TT)]
        NQ = 4
        qa = [k.sb([96, 256], BF16, "qa", ctx) for _ in range(NQ)]
        groups = [(h, qb) for h in range(8) for qb in range(NB)]
        tasks = []
        for gi, (h, qb) in enumerate(groups):
            for pr in range(qb + 1):
                tasks.append((gi, h, qb, pr))

        def load_q(gi):
            h, qb = groups[gi]
            b = qa[gi % NQ]
            k.dma(b[:, :], C.qaugT[h, :, qb * 256:(qb + 1) * 256], [C.qaugT_r[(qb * 256) // TT]], [b])

        def load_k(h):
            kb = kaug[h % 2]
            k.dma(kb[0:64, :], C.kT[h * 64:(h + 1) * 64, :], all_k, [kb])

        def emit_S(i):
            gi, h, qb, pr = tasks[i]
            if pr == 0:
                if qb == 0 and h + 1 < 8:
                    load_k(h + 1)
                if gi + NQ - 1 < len(groups):
                    load_q(gi + NQ - 1)
            kb = kaug[h % 2]
            q = qa[gi % NQ]
            sp = sps[i % 3]
            p = pt[i % 4]
            for j in range(2):
                kt = 2 * pr + j
                k.mm(sp[:, j * 256:(j + 1) * 256], kb[0:96, kt * 128:(kt + 1) * 128], q[0:96, :],
                     True, True, [kb, q], [sp])
            k.act(p[:, :, :], sp[:, 0:512].rearrange("p (j q) -> p j q", j=2), AF.Exp, [sp], [p], scale=0.125)
            if pr == qb:
                k.tt("pool", p[:, :, :], p[:, :, :], caus[:, :, :], ALU.mult, [p, caus], [p])

        deferred = []

        def emit_PV(i):
            gi, h, qb, pr = tasks[i]
            p = pt[i % 4]
            op_ = ops[gi % 3]
            for j in range(2):
                kt = 2 * pr + j
                k.mm(op_[0:65, 0:256], vsb[:, kt, h * 65:(h + 1) * 65], p[:, j, :],
                     pr == 0 and j == 0, pr == qb and j == 1, [vsb, p], [op_])
            if pr == qb:
                rd = rden[gi % 2]
                k.op("dve", lambda e, op_=op_, rd=rd: e.reciprocal(rd[0:1, :], op_[0:1, 0:256]), [op_], [rd])

                def tail(gi=gi, h=h, qb=qb, op_=op_, rd=rd):
                    k.mm(bps[0:65, 0:256], onesf[0:1, 0:65], rd[0:1, :], True, True, [onesf, rd], [bps])
                    k.copy("act", bcs[:, :], bps[0:65, 0:256], [bps], [bcs])
                    a = ao[(qb // 2) % 2]
                    k.tt("dve", a[:, (qb % 2) * 256:(qb % 2 + 1) * 256], op_[0:65, 0:256], bcs[:, :], ALU.mult,
                         [op_, bcs], [a])
                    if qb % 2 == 1:
                        t = qb // 2
                        k.dma(C.mixT[h * 64:(h + 1) * 64, t * TT:(t + 1) * TT], a[1:65, :], [a], [C.mixT_r[t]], sem=a)
                deferred.append((i + 2, tail))

        load_k(0)
        for gi in range(min(NQ - 1, len(groups))):
            load_q(gi)
        n = len(tasks)
        emit_S(0)
        if n > 1:
            emit_S(1)
        for i in range(n):
            while deferred and deferred[0][0] <= i:
                deferred.pop(0)[1]()
            if i + 2 < n:
                emit_S(i + 2)
            emit_PV(i)
        while deferred:
            deferred.pop(0)[1]()
    k.S.end_phase()


def phase_outproj_xattn(k, C, S, l, w_out_d, hin, hin_r, hout, hout_r):
    NT = S // TT
    k.S.reorder_on = False
    with ExitStack() as ctx:
        C.sq = k.sb([128, 8, TT], BF16, "sq", ctx)
        C.rr = k.sb([128, TT], F32, "rr", ctx)
        kmemT = k.sb([128, 8, NMEM], BF16, "kmemT", ctx)
        vmem = k.sb([128, 2, D], BF16, "vmem", ctx)
        with ExitStack() as c2:
            stg = [k.sb([128, 2048], F32, "stg", c2) for _ in range(4)]
            gm = k.sb([128, 8], F32, "gm", c2)
            k.dma(gm[:, :], C.mem_norm, [], [gm])
            wkv = load_w(k, c2, C.xattn_wkv[l], D, 2048, "wkv", gain=gm, stg=stg)
            mt_ = k.sb([128, 8, NMEM], F32, "memt", c2)
            k.dma(mt_[:, :, :], C.memT.rearrange("(c p) m -> p c m", p=128), [], [mt_])
            memn = k.sb([128, 8, NMEM], BF16, "memn", c2)
            rmsnorm(k, C, mt_, memn, NMEM)
            for dc in range(8):
                p = C.ps[dc % 6]
                proj_fm(k, p, wkv, memn, dc * 128, NMEM, 8, [wkv, memn])
                k.copy("act", kmemT[:, dc, :], p[:, 0:NMEM], [p], [kmemT])
            for mt in range(2):
                for half in range(2):
                    p = C.ps[(mt * 2 + half) % 6]
                    for kc in range(8):
                        k.mm(p[:, 0:512], memn[:, kc, mt * 128:(mt + 1) * 128],
                             wkv[:, kc, 1024 + half * 512:1536 + half * 512], kc == 0, kc == 7, [memn, wkv], [p])
                    k.copy("dve", vmem[:, mt, half * 512:(half + 1) * 512], p[:, 0:512], [p], [vmem])
        k.S.end_phase()
        stg = [k.sb([128, 2048], F32, "stg", ctx) for _ in range(2)]
        gq = k.sb([128, 8], F32, "gq", ctx)
        k.dma(gq[:, :], C.norm_xattn[l], [], [gq])
        w_out = load_w(k, ctx, w_out_d, D, D, "w_out", stg=stg)
        wq = load_w(k, ctx, C.xattn_wq[l], D, D, "wq", gain=gq, stg=stg)
        wo = load_w(k, ctx, C.xattn_wo[l], D, D, "wo", stg=stg)
        xt = [k.sb([128, 8, TT], F32, "xt", ctx) for _ in range(2)]
        mx = [k.sb([128, 8, TT], BF16, "mx", ctx) for _ in range(2)]
        xn = [k.sb([128, 8, TT], BF16, "xn", ctx) for _ in range(2)]
        qx = [k.sb([128, 8, TT], BF16, "qx", ctx) for _ in range(2)]
        ox = [k.sb([128, 8, TT], BF16, "ox", ctx) for _ in range(2)]
        pt = [[k.sb([128, 2, TT], BF16, "pt", ctx) for _ in range(2)] for _ in range(2)]
        rec = [[k.sb([128, TT], F32, "rec", ctx) for _ in range(2)] for _ in range(2)]
        rrb = [C.rr, k.sb([128, TT], F32, "rr2", ctx)]
        ps = C.ps
        st = {"pi": 0}

        def nps():
            p = ps[st["pi"] % 6]
            st["pi"] += 1
            return p

        def load(t):
            k.dma(xt[t % 2][:, :, :], hin[:, t * TT:(t + 1) * TT].rearrange("(c p) t -> p c t", p=128),
                  [hin_r[t]], [xt[t % 2]])
            k.dma(mx[t % 2][:, :, :], C.mixT[:, t * TT:(t + 1) * TT].rearrange("(c p) t -> p c t", p=128),
                  [C.mixT_r[t]], [mx[t % 2]])

        def tile(t):
            x, m_, xn_, qx_, ox_ = xt[t % 2], mx[t % 2], xn[t % 2], qx[t % 2], ox[t % 2]
            for m in range(8):
                p = nps()
                proj_fm(k, p, w_out, m_, m * 128, TT, 8, [w_out, m_])
                k.tt("dve", x[:, m, :], x[:, m, :], p[:, 0:TT], ALU.add, [x, p], [x])
            yield
            rmsnorm(k, C, x, xn_, TT, nb=(C.sq, rrb[t % 2], C.ps_norm))
            yield
            for m in range(8):
                p = nps()
                proj_fm(k, p, wq, xn_, m * 128, TT, 8, [wq, xn_])
                k.copy("act", qx_[:, m, :], p[:, 0:TT], [p], [qx_])
                if m == 3:
                    yield
            yield
            for hd in range(4):
                ptb = pt[t % 2][hd % 2]
                rc = rec[t % 2][hd % 2]
                for mt in range(2):
                    p = nps()
                    for dd in range(2):
                        dc = 2 * hd + dd
                        k.mm(p[:, 0:TT], kmemT[:, dc, mt * 128:(mt + 1) * 128], qx_[:, dc, :], dd == 0, dd == 1,
                             [kmemT, qx_], [p])
                    k.act(ptb[:, mt, :], p[:, 0:TT], AF.Exp, [p], [ptb], scale=1.0 / 16)
                p = nps()
                for mt in range(2):
                    k.mm(p[:, 0:TT], C.ones_bf[:, :], ptb[:, mt, :], mt == 0, mt == 1, [C.ones_bf, ptb], [p])
                k.op("dve", lambda e, rc=rc, p=p: e.reciprocal(rc[:, :], p[:, 0:TT]), [p], [rc])
                for dd in range(2):
                    p = nps()
                    c0 = hd * 256 + dd * 128
                    for mt in range(2):
                        k.mm(p[:, 0:TT], vmem[:, mt, c0:c0 + 128], ptb[:, mt, :], mt == 0, mt == 1, [vmem, ptb], [p])
                    k.tt("dve", ox_[:, 2 * hd + dd, :], p[:, 0:TT], rc[:, :], ALU.mult, [p, rc], [ox_])
                yield
            for m in range(8):
                p = nps()
                proj_fm(k, p, wo, ox_, m * 128, TT, 8, [wo, ox_])
                k.tt("dve", x[:, m, :], x[:, m, :], p[:, 0:TT], ALU.add, [x, p], [x])
                if m == 3:
                    yield
            k.dma(hout[:, t * TT:(t + 1) * TT].rearrange("(c p) t -> p c t", p=128), x[:, :, :],
                  [x], [hout_r[t]], sem=x)
            if t + 2 < NT:
                load(t + 2)

        load(0)
        if NT > 1:
            load(1)
        run_pipelined(tile, NT, 6)
    k.S.end_phase()
    k.S.reorder_on = True


TF = 256


def phase_ffn(k, C, S, l, hin, hin_r, hout, hout_r, final):
    k.S.reorder_on = False
    NTF = S // TF
    NJ = DFF // 128
    with ExitStack() as ctx:
        gf = k.sb([128, 8], F32, "gf", ctx)
        k.dma(gf[:, :], C.norm_ffn[l], [], [gf])
        w_up = k.sb([128, 8, 2 * DFF], BF16, "w_up", ctx)
        w_dn = k.sb([128, NJ, D], BF16, "w_dn", ctx)
        with ExitStack() as c2:
            stg = [k.sb([128, 2048], F32, "stg", c2) for _ in range(4)]
            load_w(k, ctx, C.ffn_w_up[l], D, 2 * DFF, "w_up", gain=gf, stg=stg, w=w_up)
            load_w(k, ctx, C.ffn_w_down[l], DFF, D, "w_dn", stg=stg, w=w_dn)
        k.S.end_phase()
        cw = k.sb([128, 2 * NJ, 3], F32, "cw", ctx)
        k.dma(cw[:, :, :], C.ffn_conv[l], [], [cw])
        gfin = k.sb([128, 8], F32, "gfin", ctx)
        k.dma(gfin[:, :], C.final_norm, [], [gfin])
        xt = [k.sb([128, 8, TF], F32, "xt", ctx) for _ in range(2)]
        xn = [k.sb([128, 8, 2 + TF], BF16, "xn", ctx) for _ in range(2)]
        for b in xn:
            k.memset("pool", b[:, :, :], 0.0, [b])
        sqb = [k.sb([128, 8, TF], BF16, "sq", ctx)] * 2
        rrb = [k.sb([128, TF], F32, "rr", ctx) for _ in range(2)]
        actb = [[k.sb([128, TF], BF16, "actb", ctx) for _ in range(NJ)] for _ in range(2)]
        yb = [[k.sb([128, TF], F32, "yb", ctx) for _ in range(4)] for _ in range(2)]
        sg = [[k.sb([128, TF], F32, "sg", ctx) for _ in range(2)] for _ in range(2)]
        ps = C.ps
        st = {"pi": 0}
        W2 = 2 + TF

        def nps():
            p = ps[st["pi"] % 6]
            st["pi"] += 1
            return p

        def load(t):
            k.dma(xt[t % 2][:, :, :], hin[:, t * TF:(t + 1) * TF].rearrange("(c p) t -> p c t", p=128),
                  [hin_r[(t * TF) // TT]], [xt[t % 2]])

        def tile(t):
            x, xn_, ab = xt[t % 2], xn[t % 2], actb[t % 2]
            nb = (sqb[t % 2], rrb[t % 2], C.ps_norm)
            if t > 0:
                k.copy("pool", xn_[:, :, 0:2], xn[(t - 1) % 2][:, :, TF:TF + 2], [xn[(t - 1) % 2]], [xn_])
            rmsnorm(k, C, x, xn_, TF, nb=nb, xoff=2)
            yield
            pend = None
            for j in range(NJ):
                ys = []
                for which in range(2):
                    c = which * NJ + j
                    p = nps()
                    y = yb[t % 2][(2 * j + which) % 4]
                    for kc in range(8):
                        k.mm(p[:, 0:W2], w_up[:, kc, c * 128:(c + 1) * 128], xn_[:, kc, 0:W2], kc == 0, kc == 7,
                             [w_up, xn_], [p])
                    k.act(y[:, :], p[:, 2:W2], AF.Copy, [p, cw], [y], scale=cw[:, c, 2:3])
                    k.stt("dve", y[:, :], p[:, 1:1 + TF], cw[:, c, 1:2], y[:, :], ALU.mult, ALU.add, [p, cw, y], [y])
                    k.stt("dve", y[:, :], p[:, 0:TF], cw[:, c, 0:1], y[:, :], ALU.mult, ALU.add, [p, cw, y], [y])
                    ys.append(y)
                if pend is not None:
                    pend()

                def fin(j=j, ys=ys):
                    s_ = sg[t % 2][j % 2]
                    k.act(s_[:, :], ys[0][:, :], AF.Silu, [ys[0]], [s_])
                    k.tt("pool", ab[j][:, :], s_[:, :], ys[1][:, :], ALU.mult, [s_, ys[1]], [ab[j]])
                pend = fin
                if j % 2 == 1:
                    yield
            pend()
            yield
            for m in range(8):
                p = nps()
                for j in range(NJ):
                    k.mm(p[:, 0:TF], w_dn[:, j, m * 128:(m + 1) * 128], ab[j][:, :], j == 0, j == NJ - 1,
                         [w_dn, ab[j]], [p])
                k.tt("dve", x[:, m, :], x[:, m, :], p[:, 0:TF], ALU.add, [x, p], [x])
                if m == 3:
                    yield
            if final:
                fin_x = k.sb
                rmsnorm(k, C, x, xfin, TF, nb=nb)
                for m in range(8):
                    k.stt("dve", x[:, m, :], x[:, m, :], gfin[:, m:m + 1], nb[1][:, 0:TF], ALU.mult, ALU.mult,
                          [x, gfin, nb[1]], [x])
            k.dma(hout[:, t * TF:(t + 1) * TF].rearrange("(c p) t -> p c t", p=128), x[:, :, :],
                  [x], [hout_r[(t * TF) // TT]], sem=x)
            if t + 2 < NTF:
                load(t + 2)

        xfin = k.sb([128, 8, TF], BF16, "xfin", ctx) if final else None
        load(0)
        load(1)
        run_pipelined(tile, NTF, 6)
    k.S.end_phase()
    k.S.reorder_on = True


def _v8(v):
    return np.ascontiguousarray(np.asarray(v, np.float32).reshape(-1, 128).T)


def core_inputs(inp, b, S, consts):
    f = lambda a: np.ascontiguousarray(np.asarray(a, np.float32))
    d = dict(consts)
    d["xT"] = f(np.asarray(inp["x"][b]).T)
    d["memT"] = f(np.asarray(inp["mem"][b]).T)
    d["mem_norm"] = _v8(inp["mem_norm"])
    d["final_norm"] = _v8(inp["final_norm"])
    for l in range(2):
        d[f"norm_mix{l}"] = _v8(inp["norm_mix"][l])
        d[f"norm_xattn{l}"] = _v8(inp["norm_xattn"][l])
        d[f"norm_ffn{l}"] = _v8(inp["norm_ffn"][l])
        d[f"xattn_wq{l}"] = f(inp["xattn_wq"][l])
        d[f"xattn_wkv{l}"] = f(inp["xattn_wkv"][l])
        d[f"xattn_wo{l}"] = f(inp["xattn_wo"][l])
        d[f"ffn_w_up{l}"] = f(inp["ffn_w_up"][l])
        d[f"ffn_w_down{l}"] = f(inp["ffn_w_down"][l])
        d[f"ffn_conv{l}"] = f(np.asarray(inp["ffn_conv"][l]).reshape(3, -1, 128).transpose(2, 1, 0))
    d["ev_w_in"] = f(inp["ev_w_in"][0])
    d["ev_w_out"] = f(inp["ev_w_out"][0])
    d["od_w_out"] = f(inp["od_w_out"][0])
    d["pool_w"] = f(inp["pool_w"][0])
    d["pool_scale"] = _v8(inp["pool_scale"][0])
    d["od_w_in"] = f(inp["od_w_in"][0])
    d["sgu_w"] = f(inp["sgu_w"][0])
    d["sgu_ln_g"] = _v8(inp["sgu_ln_g"][0])
    d["sgu_ln_b"] = _v8(inp["sgu_ln_b"][0])
    d["sgu_b"] = f(np.broadcast_to(np.asarray(inp["sgu_b"][0])[None], (128, 4, 128)))
    d["dn_conv"] = f(np.asarray(inp["dn_conv"][0]).reshape(4, 12, 128).transpose(2, 1, 0))
    d["dn_a_log"] = f(np.broadcast_to(np.asarray(inp["dn_a_log"][0])[None], (128, 4)))
    d["dn_dt_bias"] = f(np.broadcast_to(np.asarray(inp["dn_dt_bias"][0])[None], (128, 4)))
    d["dn_norm_g"] = f(np.broadcast_to(np.asarray(inp["dn_norm_g"][0])[None], (128, 128)))
    return d


def phase_l1_inproj(k, C, S):
    NT = S // TT
    GC1 = 1.5957691216057308
    with ExitStack() as ctx:
        C.sq = k.sb([128, 8, TT], BF16, "sq", ctx)
        C.rr = k.sb([128, TT], F32, "rr", ctx)
        gain = k.sb([128, 8], F32, "gain", ctx)
        k.dma(gain[:, :], C.norm_mix[1], [], [gain])
        w_in = k.sb([128, 8, 3080], BF16, "w_in1", ctx)
        wsT = k.sb([128, 4, 128], BF16, "wsT", ctx)
        with ExitStack() as c2:
            stg = [k.sb([128, 2048], F32, "stg", c2) for _ in range(4)]
            load_w(k, ctx, C.od_w_in, D, 3080, "w_in1", gain=gain, stg=stg, w=w_in)
            wsf = k.sb([128, 4, 128], F32, "wsf", c2)
            k.dma(wsf[:, :, :], C.sgu_w.rearrange("g t s -> t g s"), [], [wsf])
            tril = k.sb([128, 128], F32, "tril", c2)
            k.dma(tril[:, :], C.tril01, [], [tril])
            wsm = k.sb([128, 4, 128], BF16, "wsm", c2)
            k.tt("dve", wsm[:, :, :], wsf[:, :, :],
                 tril[:, :].rearrange("p (o s) -> p o s", o=1).to_broadcast([128, 4, 128]), ALU.mult,
                 [wsf, tril], [wsm])
            for g in range(4):
                k.tr(C.ps_bf[:, g * 128:(g + 1) * 128], wsm[:, g, :], C.ident_bf[:, :], [wsm, C.ident_bf], [C.ps_bf])
            k.copy("act", wsT[:, :, :], C.ps_bf[:, 0:512].rearrange("p (g t) -> p g t", g=4), [C.ps_bf], [wsT])
        k.S.end_phase()
        lng = k.sb([128, 4], F32, "lng", ctx)
        lnb = k.sb([128, 4], F32, "lnb", ctx)
        k.dma(lng[:, :], C.sgu_ln_g, [], [lng])
        k.dma(lnb[:, :], C.sgu_ln_b, [], [lnb])
        bsb = k.sb([128, 4, 128], F32, "bsb", ctx)
        k.dma(bsb[:, :, :], C.sgu_b, [], [bsb])
        dcw = k.sb([128, 12, 4], F32, "dcw", ctx)
        k.dma(dcw[:, :, :], C.dn_conv, [], [dcw])
        qsc = k.sb([128, 1], F32, "qsc", ctx)
        k.memset("pool", qsc[:, :], float(np.log(128.0 ** -0.5)), [qsc])
        zero_c = k.sb([128, 1], F32, "zero_c", ctx)
        k.memset("pool", zero_c[:, :], 0.0, [zero_c])

        xt = [k.sb([128, 8, TT], F32, "xt", ctx) for _ in range(2)]
        xn = k.sb([128, 8, TT], BF16, "xn", ctx)
        x2 = [k.sb([128, TT], F32, "x2", ctx) for _ in range(2)]
        sgm = [k.sb([128, TT], F32, "sgm", ctx) for _ in range(2)]
        ub_ = k.sb([128, 4, TT], F32, "u", ctx)
        v_ = k.sb([128, 4, TT], F32, "v", ctx)
        vb = k.sb([128, 4, TT], BF16, "vb", ctx)
        vsq = k.sb([128, 4, TT], BF16, "vsq", ctx)
        mean = k.sb([128, TT], F32, "mean", ctx)
        m2 = k.sb([128, TT], F32, "m2", ctx)
        rstd = k.sb([128, TT], F32, "rstd", ctx)
        vn = k.sb([128, 4, TT], BF16, "vn", ctx)
        vtok = k.sb([128, 4, 4, 128], BF16, "vtok", ctx)
        t1 = k.sb([128, 4, 128], F32, "t1", ctx)
        cout = [k.sb([128, 4, TT], BF16, "cout", ctx) for _ in range(2)]
        cu = [k.sb([128, 3 + TT], F32, "cu", ctx) for _ in range(2)]
        cy = [k.sb([128, TT], F32, "cy", ctx) for _ in range(2)]
        chist = k.sb([128, 12, 3], F32, "chist", ctx)
        k.memset("pool", chist[:, :, :], 0.0, [chist])
        dq4 = k.sb([128, 4, TT], F32, "dq4", ctx)
        dsq4 = k.sb([128, 4, TT], BF16, "dsq4", ctx)
        rn4 = k.sb([128, 4, TT], F32, "rn4", ctx)
        dno = [k.sb([128, 12, TT], BF16, "dno", ctx)] * 2
        gto = [k.sb([128, 4, 512], BF16, "gto", ctx)] * 2
        bao = [k.sb([128, 4, 8], F32, "bao", ctx) for _ in range(2)]
        ps = C.ps
        pi = 0

        def load(t):
            k.dma(xt[t % 2][:, :, :], C.hB[:, t * TT:(t + 1) * TT].rearrange("(c p) t -> p c t", p=128),
                  [C.hB_r[t]], [xt[t % 2]])

        load(0)
        for t in range(NT):
            if t + 1 < NT:
                load(t + 1)
            x = xt[t % 2]
            t0 = t * TT
            rmsnorm(k, C, x, xn, TT)
            for m in range(8):
                p = ps[pi % 6]; pi += 1
                a2, sg = x2[m % 2], sgm[m % 2]
                proj_fm(k, p, w_in, xn, m * 128, TT, 8, [w_in, xn])
                k.act(a2[:, :], p[:, 0:TT], AF.Square, [p], [a2])
                k.ts("dve", a2[:, :], a2[:, :], 0.044715, 1.0, ALU.mult, ALU.add, [a2], [a2])
                k.tt("dve", a2[:, :], a2[:, :], p[:, 0:TT], ALU.mult, [a2, p], [a2])
                k.act(sg[:, :], a2[:, :], AF.Sigmoid, [a2], [sg], scale=GC1)
                dst = ub_ if m < 4 else v_
                k.tt("dve", dst[:, m % 4, :], sg[:, :], p[:, 0:TT], ALU.mult, [sg, p], [dst])
            k.copy("pool", vb[:, :, :], v_[:, :, :], [v_], [vb])
            k.act(vsq[:, :, :], v_[:, :, :], AF.Square, [v_], [vsq])
            pm = ps[pi % 6]; pi += 1
            pq = ps[pi % 6]; pi += 1
            for c in range(4):
                k.mm(pm[:, 0:TT], C.ones_bf[:, :], vb[:, c, :], c == 0, c == 3, [C.ones_bf, vb], [pm])
            for c in range(4):
                k.mm(pq[:, 0:TT], C.ones_bf[:, :], vsq[:, c, :], c == 0, c == 3, [C.ones_bf, vsq], [pq])
            k.ts("dve", mean[:, :], pm[:, 0:TT], 1.0 / 512, None, ALU.mult, None, [pm], [mean])
            k.tt("pool", m2[:, :], mean[:, :], mean[:, :], ALU.mult, [mean], [m2])
            k.stt("dve", m2[:, :], pq[:, 0:TT], 1.0 / 512, m2[:, :], ALU.mult, ALU.subtract, [pq, m2], [m2])
            k.act(rstd[:, :], m2[:, :], AF.Ln, [m2, C.eps_col], [rstd], bias=C.eps_col[:, 0:1])
            k.act(rstd[:, :], rstd[:, :], AF.Exp, [rstd], [rstd], scale=-0.5)
            bc = lambda a: a[:, :].rearrange("p (o t) -> p o t", o=1).to_broadcast([128, 4, TT])
            k.tt("dve", v_[:, :, :], v_[:, :, :], bc(mean), ALU.subtract, [v_, mean], [v_])
            k.tt("pool", v_[:, :, :], v_[:, :, :], bc(rstd), ALU.mult, [v_, rstd], [v_])
            for c in range(4):
                k.ts("dve", vn[:, c, :], v_[:, c, :], lng[:, c:c + 1], lnb[:, c:c + 1], ALU.mult, ALU.add,
                     [v_, lng, lnb], [vn])
            for half in range(2):
                for nn in range(2):
                    n = half * 2 + nn
                    for c in range(4):
                        k.tr(C.ps_bf[:, (nn * 4 + c) * 128:(nn * 4 + c + 1) * 128], vn[:, c, n * 128:(n + 1) * 128],
                             C.ident_bf[:, :], [vn, C.ident_bf], [C.ps_bf])
                k.copy("act", vtok[:, half * 2:half * 2 + 2, :, :],
                       C.ps_bf[:, 0:1024].rearrange("p (n c s) -> p n c s", n=2, c=4), [C.ps_bf], [vtok])
            co = cout[t % 2]
            for g in range(4):
                p = ps[pi % 6]; pi += 1
                for n in range(4):
                    k.mm(p[:, n * 128:(n + 1) * 128], vtok[:, n, g, :], wsT[:, g, :], True, True, [vtok, wsT], [p])
                for n in range(4):
                    pass
                k.tt("dve", t1[:, :, :], p[:, 0:512].rearrange("p (n t) -> p n t", n=4),
                     bsb[:, g, :].rearrange("p (o t) -> p o t", o=1).to_broadcast([128, 4, 128]), ALU.add,
                     [p, bsb], [t1])
                k.tt("pool", co[:, g, :], t1[:, :, :].rearrange("p n t -> p (n t)"), ub_[:, g, :], ALU.mult,
                     [t1, ub_], [co])
            k.dma(C.mixT[0:512, t0:t0 + TT].rearrange("(g p) t -> p g t", p=128), co[:, :, :],
                  [co], [C.mixT_r[t]], sem=co)
            do = dno[t % 2]
            for grp in ((0, 1, 2, 3), (4, 5, 6, 7), (8, 9, 10, 11)):
                for c in grp:
                    p = ps[pi % 6]; pi += 1
                    u, y = cu[c % 2], cy[c % 2]
                    proj_fm(k, p, w_in, xn, 1024 + c * 128, TT, 8, [w_in, xn])
                    k.copy("pool", u[:, 0:3], chist[:, c, :], [chist], [u])
                    k.copy("act", u[:, 3:3 + TT], p[:, 0:TT], [p], [u])
                    k.act(y[:, :], p[:, 0:TT], AF.Copy, [p, dcw], [y], scale=dcw[:, c, 3:4])
                    k.copy("pool", chist[:, c, :], u[:, TT:TT + 3], [u], [chist])
                    for kk in range(3):
                        k.stt("dve", y[:, :], u[:, kk:kk + TT], dcw[:, c, kk:kk + 1], y[:, :], ALU.mult, ALU.add,
                              [u, dcw, y], [y])
                    if c >= 8:
                        k.act(do[:, c, :], y[:, :], AF.Silu, [y], [do])
                    else:
                        k.act(dq4[:, c % 4, :], y[:, :], AF.Silu, [y], [dq4])
                if grp[0] >= 8:
                    continue
                k.act(dsq4[:, :, :], dq4[:, :, :], AF.Square, [dq4], [dsq4])
                for c in grp:
                    p2 = ps[pi % 6]; pi += 1
                    k.mm(p2[:, 0:TT], C.ones_bf[:, :], dsq4[:, c % 4, :], True, True, [C.ones_bf, dsq4], [p2])
                    k.act(rn4[:, c % 4, :], p2[:, 0:TT], AF.Ln, [p2, C.eps_col], [rn4], bias=C.eps_col[:, 0:1])
                k.act(rn4[:, :, :], rn4[:, :, :], AF.Exp, [rn4, qsc, zero_c], [rn4], scale=-0.5,
                      bias=(qsc if grp[0] < 4 else zero_c)[:, 0:1])
                k.tt("dve", do[:, grp[0]:grp[0] + 4, :], dq4[:, :, :], rn4[:, :, :], ALU.mult, [dq4, rn4], [do])
            k.dma(C.dnT[:, t0:t0 + TT].rearrange("(c p) t -> p c t", p=128), do[:, :, :], [do], [C.dnT_r[t]], sem=do)
            go, bo = gto[t % 2], bao[t % 2]
            for n in range(4):
                p = ps[pi % 6]; pi += 1
                for kc in range(8):
                    k.mm(p[:, 0:512], xn[:, kc, n * 128:(n + 1) * 128], w_in[:, kc, 2560:3072], kc == 0, kc == 7,
                         [xn, w_in], [p])
                k.act(go[:, n, :], p[:, 0:512], AF.Silu, [p], [go])
                p = ps[pi % 6]; pi += 1
                for kc in range(8):
                    k.mm(p[:, 0:8], xn[:, kc, n * 128:(n + 1) * 128], w_in[:, kc, 3072:3080], kc == 0, kc == 7,
                         [xn, w_in], [p])
                k.copy("dve", bo[:, n, :], p[:, 0:8], [p], [bo])
            k.dma(C.gtok[t0:t0 + TT, :].rearrange("(n p) c -> p n c", p=128), go[:, :, :], [go], [C.gtok_r[t]], sem=go)
            k.dma(C.batok[t0:t0 + TT, :].rearrange("(n p) c -> p n c", p=128), bo[:, :, :], [bo], [C.batok_r[t]], sem=bo)
    k.S.end_phase()


def phase_deltanet(k, C, S):
    NCH = S // 128
    with ExitStack() as ctx:
        tri = k.sb([128, 128], F32, "tri", ctx)
        k.dma(tri[:, :], C.tri_le, [], [tri])
        causT = k.sb([128, 128], F32, "causT", ctx)
        k.dma(causT[:, :], C.causT_neg, [], [causT])
        strT = k.sb([128, 128], F32, "strT", ctx)
        k.dma(strT[:, :], C.strictT01, [], [strT])
        onesf = k.sb([128, 128], F32, "onesf", ctx)
        k.memset("pool", onesf[:, :], 1.0, [onesf])
        one_c = k.sb([128, 1], F32, "one_c", ctx)
        k.memset("pool", one_c[:, :], 1.0, [one_c])
        dtb = k.sb([128, 4], F32, "dtb", ctx)
        k.dma(dtb[:, :], C.dn_dt_bias, [], [dtb])
        nexpA = k.sb([128, 4], F32, "nexpA", ctx)
        k.dma(nexpA[:, :], C.dn_a_log, [], [nexpA])
        k.act(nexpA[:, :], nexpA[:, :], AF.Exp, [nexpA], [nexpA])
        k.ts("dve", nexpA[:, :], nexpA[:, :], -1.0, None, ALU.mult, None, [nexpA], [nexpA])
        ngb = k.sb([128, 128], F32, "ngb", ctx)
        k.dma(ngb[:, :], C.dn_norm_g, [], [ngb])

        St = k.sb([128, 4, 128], F32, "St", ctx)
        Sb = k.sb([128, 4, 128], BF16, "Sb", ctx)
        k.memset("pool", St[:, :, :], 0.0, [St])
        k.memset("pool", Sb[:, :, :], 0.0, [Sb])
        lvl = k.sb([128, 8, 128], BF16, "lvl", ctx)
        k.dma(lvl[:, :, :], C.lvlmask, [], [lvl])
        NB_ = 4
        BETA, G, GC, GLAST, E, F_, GL, NGC, BE, TMP = range(10)

        def mk():
            b = Ctx()
            b.dn = k.sb([128, 12, 128], BF16, "dn", ctx)
            b.gt = k.sb([128, 512], BF16, "gt", ctx)
            b.ba = k.sb([128, 8], F32, "ba", ctx)
            b.sc = k.sb([128, 10, 4], F32, "sc", ctx)
            b.dg = [k.sb([128, 4, 128], F32, "dg", ctx) for _ in range(3)]
            b.kbT = k.sb([128, 4, 128], BF16, "kbT", ctx)
            b.qeT = k.sb([128, 4, 128], BF16, "qeT", ctx)
            b.X = k.sb([128, 4, 128], F32, "X", ctx)
            b.DcT = k.sb([128, 4, 128], F32, "DcT", ctx)
            b.DcsT = k.sb([128, 4, 128], F32, "DcsT", ctx)
            b.tok = k.sb([128, 8, 128], BF16, "tok", ctx)
            b.rhs0 = k.sb([128, 4, 256], BF16, "rhs0", ctx)
            b.y = k.sb([128, 4, 256], BF16, "y", ctx)
            b.kd = k.sb([128, 4, 128], BF16, "kd", ctx)
            b.qkT = k.sb([128, 4, 128], BF16, "qkT", ctx)
            b.A = k.sb([128, 4, 128], BF16, "A", ctx)
            b.AT = k.sb([128, 4, 128], BF16, "AT", ctx)
            b.Tm = [k.sb([128, 4, 128], BF16, "Tm", ctx) for _ in range(2)]
            b.TTm = [k.sb([128, 4, 128], BF16, "TTm", ctx) for _ in range(2)]
            b.Am = k.sb([128, 4, 128], BF16, "Am", ctx)
            b.AmT = [k.sb([128, 4, 128], BF16, "AmT", ctx) for _ in range(2)]
            b.Um = k.sb([128, 4, 128], BF16, "Um", ctx)
            b.wT = k.sb([128, 4, 128], BF16, "wT", ctx)
            b.vnew = k.sb([128, 4, 128], BF16, "vnew", ctx)
            b.osb = k.sb([128, 4, 128], F32, "osb", ctx)
            b.junk = k.sb([128, 128], F32, "junk", ctx)
            b.ss = k.sb([128, 4], F32, "ss", ctx)
            b.dtk = k.sb([128, 4, 128], BF16, "dtk", ctx)
            b.doT = k.sb([128, 4, 128], BF16, "doT", ctx)
            return b

        BUFS = [mk() for _ in range(NB_)]
        ps = C.ps + [C.ps_norm]
        st_ = {"pi": 0}

        def nxt():
            p = ps[st_["pi"] % 7]
            st_["pi"] += 1
            return p

        def load(n):
            b = BUFS[n % NB_]
            tl = (n * 128) // TT
            k.dma(b.dn[:, :, :], C.dnT[:, n * 128:(n + 1) * 128].rearrange("(c p) t -> p c t", p=128),
                  [C.dnT_r[tl]], [b.dn])
            k.dma(b.gt[:, :], C.gtok[n * 128:(n + 1) * 128, :], [C.gtok_r[tl]], [b.gt])
            k.dma(b.ba[:, :], C.batok[n * 128:(n + 1) * 128, :], [C.batok_r[tl]], [b.ba])

        ident4 = C.ident_f[:, :].rearrange("p (o t) -> p o t", o=1).to_broadcast([128, 4, 128])
        m4 = lambda a: a[:, :].rearrange("p (o t) -> p o t", o=1).to_broadcast([128, 4, 128])
        v4 = lambda p_, w=128: p_[:, 0:4 * w].rearrange("p (h t) -> p h t", h=4)
        lm4 = lambda l: lvl[:, l, :].rearrange("p (o t) -> p o t", o=1).to_broadcast([128, 4, 128])
        identb4 = C.ident_bf[:, :].rearrange("p (o t) -> p o t", o=1).to_broadcast([128, 4, 128])

        def chunk(n):
            b = BUFS[n % NB_]
            sc, d, g_, b_ = b.sc, b.dn, b.gt, b.ba

            def bcol(j):
                return sc[:, j, :].rearrange("p (h o) -> p h o", o=1).to_broadcast([128, 4, 128])
            k.act(sc[:, BETA, :], b_[:, 0:4], AF.Exp, [b_], [sc], scale=-1.0)
            k.ts("dve", sc[:, BETA, :], sc[:, BETA, :], 1.0, None, ALU.add, None, [sc], [sc])
            k.op("dve", lambda e: e.reciprocal(sc[:, BETA, :], sc[:, BETA, :]), [sc], [sc])
            k.tt("dve", sc[:, TMP, :], b_[:, 4:8], dtb[:, :], ALU.add, [b_, dtb], [sc])
            k.act(sc[:, TMP, :], sc[:, TMP, :], AF.Exp, [sc], [sc])
            k.act(sc[:, TMP, :], sc[:, TMP, :], AF.Ln, [sc, one_c], [sc], bias=one_c[:, 0:1])
            k.tt("dve", sc[:, G, :], sc[:, TMP, :], nexpA[:, :], ALU.mult, [sc, nexpA], [sc])
            p = nxt()
            k.mm(p[:, 0:4], tri[:, :], sc[:, G, :], True, True, [tri, sc], [p])
            k.mm(p[:, 4:8], onesf[:, :], sc[:, G, :], True, True, [onesf, sc], [p])
            k.copy("dve", sc[:, GC:GLAST + 1, :], p[:, 0:8].rearrange("p (a h) -> p a h", a=2), [p], [sc])
            yield
            k.act(sc[:, E, :], sc[:, GC, :], AF.Exp, [sc], [sc])
            k.tt("dve", sc[:, TMP, :], sc[:, GLAST, :], sc[:, GC, :], ALU.subtract, [sc], [sc])
            k.act(sc[:, F_, :], sc[:, TMP, :], AF.Exp, [sc], [sc])
            k.act(sc[:, GL, :], sc[:, GLAST, :], AF.Exp, [sc], [sc])
            k.ts("dve", sc[:, NGC, :], sc[:, GC, :], -1.0, None, ALU.mult, None, [sc], [sc])
            k.tt("dve", sc[:, BE, :], sc[:, BETA, :], sc[:, E, :], ALU.mult, [sc], [sc])
            for j in range(8):
                k.tr(C.ps_bf[:, j * 128:(j + 1) * 128], d[:, 4 + j, :], C.ident_bf[:, :], [d, C.ident_bf], [C.ps_bf])
            k.copy("act", b.tok[:, :, :], C.ps_bf[:, 0:1024].rearrange("p (j t) -> p j t", j=8), [C.ps_bf], [b.tok])
            yield
            pB, pE, pG = nxt(), nxt(), nxt()
            for dgi, (col, pp) in enumerate(((BETA, pB), (E, pE), (GC, pG))):
                k.tt("pool" if dgi == 1 else "dve", b.dg[dgi][:, :, :], ident4, bcol(col), ALU.mult,
                     [C.ident_f, sc], [b.dg[dgi]])
                k.mm(pp[:, 0:512], onesf[:, :], b.dg[dgi][:, :, :].rearrange("p h t -> p (h t)"), True, True,
                     [onesf, b.dg[dgi]], [pp])
            k.tt("dve", b.kbT[:, :, :], d[:, 4:8, :], v4(pB), ALU.mult, [d, pB], [b.kbT])
            k.tt("dve", b.qeT[:, :, :], d[:, 0:4, :], v4(pE), ALU.mult, [d, pE], [b.qeT])
            k.tt("dve", b.X[:, :, :], v4(pG), m4(causT), ALU.add, [pG, causT], [b.X])
            for h in range(4):
                k.ts("dve", b.rhs0[:, h, 0:128], b.tok[:, 4 + h, :], sc[:, BETA, h:h + 1], None, ALU.mult, None,
                     [b.tok, sc], [b.rhs0])
                k.act(b.rhs0[:, h, 128:256], b.tok[:, h, :], AF.Copy, [b.tok, sc], [b.rhs0], scale=sc[:, BE, h:h + 1])
                k.act(b.kd[:, h, :], b.tok[:, h, :], AF.Copy, [b.tok, sc], [b.kd], scale=sc[:, F_, h:h + 1])
            yield
            for h in range(4):
                k.act(b.DcT[:, h, :], b.X[:, h, :], AF.Exp, [b.X, sc], [b.DcT], bias=sc[:, NGC, h:h + 1])
            k.tt("pool", b.DcsT[:, :, :], b.DcT[:, :, :], m4(strT), ALU.mult, [b.DcT, strT], [b.DcsT])
            pQ, pK = nxt(), nxt()
            for h in range(4):
                k.mm(pQ[:, h * 128:(h + 1) * 128], d[:, 4 + h, :], d[:, h, :], True, True, [d], [pQ])
                k.mm(pK[:, h * 128:(h + 1) * 128], d[:, 4 + h, :], b.kbT[:, h, :], True, True, [d, b.kbT], [pK])
            k.tt("dve", b.qkT[:, :, :], v4(pQ), b.DcT[:, :, :], ALU.mult, [pQ, b.DcT], [b.qkT])
            k.tt("dve", b.AT[:, :, :], v4(pK), b.DcsT[:, :, :], ALU.mult, [pK, b.DcsT], [b.AT])
            yield
            for h in range(4):
                k.tr(C.ps_bf[:, h * 128:(h + 1) * 128], b.AT[:, h, :], C.ident_bf[:, :], [b.AT, C.ident_bf], [C.ps_bf])
            k.copy("act", b.A[:, :, :], C.ps_bf[:, 0:512].rearrange("p (h t) -> p h t", h=4), [C.ps_bf], [b.A])
            Tc, TTc = b.Tm[0], b.TTm[0]
            k.tt("pool", b.AmT[0][:, :, :], b.AT[:, :, :], lm4(0), ALU.mult, [b.AT, lvl], [b.AmT[0]])
            k.tt("dve", TTc[:, :, :], identb4, b.AmT[0][:, :, :], ALU.subtract, [C.ident_bf, b.AmT[0]], [TTc])
            k.tt("pool", b.Am[:, :, :], b.A[:, :, :], lm4(7), ALU.mult, [b.A, lvl], [b.Am])
            k.tt("dve", Tc[:, :, :], identb4, b.Am[:, :, :], ALU.subtract, [C.ident_bf, b.Am], [Tc])
            yield
            for l in range(1, 7):
                Tn, TTn = b.Tm[l % 2], b.TTm[l % 2]
                amt = b.AmT[l % 2]
                k.tt("pool", amt[:, :, :], b.AT[:, :, :], lm4(l), ALU.mult, [b.AT, lvl], [amt])
                pU = nxt()
                for h in range(4):
                    k.mm(pU[:, h * 128:(h + 1) * 128], amt[:, h, :], Tc[:, h, :], True, True, [amt, Tc], [pU])
                k.copy("act", b.Um[:, :, :], v4(pU), [pU], [b.Um])
                pVT = nxt()
                for h in range(4):
                    k.mm(pVT[:, h * 128:(h + 1) * 128], b.Um[:, h, :], TTc[:, h, :], True, True, [b.Um, TTc], [pVT])
                if l < 6:
                    pV = nxt()
                    for h in range(4):
                        k.mm(pV[:, h * 128:(h + 1) * 128], TTc[:, h, :], b.Um[:, h, :], True, True, [TTc, b.Um], [pV])
                k.tt("dve", TTn[:, :, :], TTc[:, :, :], v4(pVT), ALU.subtract, [TTc, pVT], [TTn])
                if l < 6:
                    k.tt("dve", Tn[:, :, :], Tc[:, :, :], v4(pV), ALU.subtract, [Tc, pV], [Tn])
                Tc, TTc = Tn, TTn
                yield
            pY0, pY1 = nxt(), nxt()
            for h in range(4):
                py = (pY0, pY1)[h // 2]
                k.mm(py[:, (h % 2) * 256:(h % 2 + 1) * 256], TTc[:, h, :], b.rhs0[:, h, :], True, True,
                     [TTc, b.rhs0], [py])
            ycur = b.y
            k.copy("act", ycur[:, 0:2, :], pY0[:, 0:512].rearrange("p (h t) -> p h t", h=2), [pY0], [ycur])
            k.copy("dve", ycur[:, 2:4, :], pY1[:, 0:512].rearrange("p (h t) -> p h t", h=2), [pY1], [ycur])
            for h in range(4):
                k.tr(C.ps_bf[:, h * 128:(h + 1) * 128], ycur[:, h, 128:256], C.ident_bf[:, :], [ycur, C.ident_bf],
                     [C.ps_bf])
            k.copy("act", b.wT[:, :, :], C.ps_bf[:, 0:512].rearrange("p (h t) -> p h t", h=4), [C.ps_bf], [b.wT])
            yield
            p1 = nxt()
            for h in range(4):
                k.mm(p1[:, h * 128:(h + 1) * 128], b.wT[:, h, :], Sb[:, h, :], True, True, [b.wT, Sb], [p1])
            k.tt("dve", b.vnew[:, :, :], ycur[:, :, 0:128], v4(p1), ALU.subtract, [ycur, p1], [b.vnew])
            p2, p3 = nxt(), nxt()
            for h in range(4):
                k.mm(p3[:, h * 128:(h + 1) * 128], b.kd[:, h, :], b.vnew[:, h, :], True, True, [b.kd, b.vnew], [p3])
            for h in range(4):
                k.mm(p2[:, h * 128:(h + 1) * 128], b.qeT[:, h, :], Sb[:, h, :], True, False, [b.qeT, Sb], [p2])
                k.mm(p2[:, h * 128:(h + 1) * 128], b.qkT[:, h, :], b.vnew[:, h, :], False, True, [b.qkT, b.vnew], [p2])
            for h in range(4):
                k.stt("dve", St[:, h, :], St[:, h, :], sc[:, GL, h:h + 1], p3[:, h * 128:(h + 1) * 128],
                      ALU.mult, ALU.add, [St, sc, p3], [St])
            k.copy("pool", Sb[:, :, :], St[:, :, :], [St], [Sb])
            k.copy("act", b.osb[:, :, :], v4(p2), [p2], [b.osb])
            yield
            k.memset("pool", b.ss[:, :], 0.0, [b.ss])
            for h in range(4):
                k.act(b.junk[:, :], b.osb[:, h, :], AF.Square, [b.osb], [b.junk, b.ss], accum_out=b.ss[:, h:h + 1])
            k.act(b.ss[:, :], b.ss[:, :], AF.Ln, [b.ss, C.eps_col], [b.ss], bias=C.eps_col[:, 0:1], scale=1.0 / 128)
            k.act(b.ss[:, :], b.ss[:, :], AF.Exp, [b.ss], [b.ss], scale=-0.5)
            for h in range(4):
                k.stt("dve", b.osb[:, h, :], b.osb[:, h, :], b.ss[:, h:h + 1], ngb[:, :], ALU.mult, ALU.mult,
                      [b.osb, b.ss, ngb], [b.osb])
            k.tt("pool", b.dtk[:, :, :], b.osb[:, :, :], g_[:, :].rearrange("p (h t) -> p h t", h=4), ALU.mult,
                 [b.osb, g_], [b.dtk])
            yield
            for h in range(4):
                k.tr(C.ps_bf[:, h * 128:(h + 1) * 128], b.dtk[:, h, :], C.ident_bf[:, :], [b.dtk, C.ident_bf], [C.ps_bf])
            k.copy("act", b.doT[:, :, :], C.ps_bf[:, 0:512].rearrange("p (h t) -> p h t", h=4), [C.ps_bf], [b.doT])
            k.dma(C.mixT[512:1024, n * 128:(n + 1) * 128].rearrange("(h p) t -> p h t", p=128), b.doT[:, :, :],
                  [b.doT], [C.mixT_r[(n * 128) // TT]], sem=b.doT)
            if n + NB_ < NCH:
                load(n + NB_)

        for n in range(min(NB_, NCH)):
            load(n)
        run_pipelined(chunk, NCH, 3, maxlive=NB_)
    k.S.end_phase()


SEQ = 8192
N_ACTIVE = 2


def kernel(**inputs):
    S = SEQ
    nc = build(S)
    consts = host_consts(S)
    in_maps = [core_inputs(inputs, b, S, consts) for b in range(N_ACTIVE)]
    res = run_bass_kernel_spmd(nc, in_maps, core_ids=list(range(N_ACTIVE)))
    out = np.stack([np.ascontiguousarray(res.results[b]["outT"].T) for b in range(N_ACTIVE)], axis=0)
    return out.astype(np.float32)
```

```python
import numpy as np
import ml_dtypes
from contextlib import ExitStack
import concourse.bass as bass
import concourse.mybir as mybir
from concourse.bass_utils import run_bass_kernel_spmd

F32 = mybir.dt.float32
BF16 = mybir.dt.bfloat16
AF = mybir.ActivationFunctionType
ALU = mybir.AluOpType
AX = mybir.AxisListType

D = 1024
NMEM = 256
EPS = 1e-6
DFF = 2816
NEG = -30000.0


class Res:
    __slots__ = ("name", "writers", "readers", "dsem")

    def __init__(self, name):
        self.name = name
        self.writers = []
        self.readers = []
        self.dsem = None


class Op:
    __slots__ = ("eng", "fn", "deps", "signal", "count", "dtok", "idx", "after", "dur", "gidx",
                 "nun", "rdy", "fin", "users")

    def __init__(self, eng, fn):
        self.eng = eng
        self.fn = fn
        self.deps = []
        self.signal = False
        self.count = 0
        self.dtok = None
        self.idx = 0
        self.after = []
        self.dur = 300.0


ENGS = ("pe", "act", "dve", "pool", "sp")


class Sched:
    def __init__(self, nc, n_dma_sems=40):
        self.nc = nc
        self.ops = {e: [] for e in ENGS}
        self.n_dma_sems = n_dma_sems
        self.dma_counts = [0] * n_dma_sems
        self.n_sp_sems = n_dma_sems - 8
        self.free_dsems = list(range(self.n_sp_sems))
        self.free_dsems_pool = list(range(self.n_sp_sems, n_dma_sems))
        self.phase_res = []
        self.all_res = []
        self.reorder_on = True
        self.seg_flags = []

    def res(self, name):
        r = Res(name)
        self.all_res.append(r)
        return r

    def _add(self, eng, fn, reads, writes, acc):
        o = Op(eng, fn)
        o.idx = len(self.ops[eng])
        deps = []
        for r in reads:
            deps.extend(r.writers)
        for w in writes:
            deps.extend(w.readers)
            if not acc:
                deps.extend(w.writers)
            elif acc == 'dma':
                deps.extend(t for t in w.writers if t[0] != 'D')
            else:
                for t in w.writers:
                    if t[0] == 'E' and t[1].eng == eng:
                        o.after.append(t[1])
                    else:
                        deps.append(t)
        o.deps = deps
        self.ops[eng].append(o)
        return o

    def _commit(self, tok, reads, writes):
        for r in reads:
            r.readers.append(tok)
        for w in writes:
            if w.readers:
                w.writers = [tok]
                w.readers = []
            else:
                w.writers.append(tok)
                if len(w.writers) > 64:
                    w.writers = w.writers[-64:]

    def op(self, eng, fn, reads=(), writes=(), acc=False):
        o = self._add(eng, fn, reads, writes, acc)
        self._commit(('E', o), reads, writes)
        return o

    def dma(self, eng, out, in_, reads=(), writes=(), sem_res=None):
        sr = sem_res if sem_res is not None else writes[0]
        if sr.dsem is None:
            sr.dsem = (self.free_dsems_pool if eng == "pool" else self.free_dsems).pop(0)
        o = self._add(eng, lambda e, out=out, in_=in_: e.dma_start(out=out, in_=in_), reads, writes, 'dma')
        self.dma_counts[sr.dsem] += 16
        o.dtok = (sr.dsem, self.dma_counts[sr.dsem])
        self._commit(('D', sr.dsem, o.dtok[1]), reads, writes)
        return o

    def end_phase(self):
        toks = []
        for e in ENGS:
            if self.ops[e]:
                last = self.ops[e][-1]
                if last.dtok is None:
                    toks.append(('E', last))
        for i in range(self.n_dma_sems):
            if self.dma_counts[i] > 0:
                toks.append(('D', i, self.dma_counts[i]))
        self.seg_flags.append(self.reorder_on)
        for e in ENGS:
            o = Op(e, None)
            o.idx = len(self.ops[e])
            o.deps = list(toks)
            self.ops[e].append(o)
        for r in self.all_res:
            r.dsem = None
            r.writers = []
            r.readers = []
        self.free_dsems = list(range(self.n_sp_sems))
        self.free_dsems_pool = list(range(self.n_sp_sems, self.n_dma_sems))

    def reorder(self, window=24):
        import heapq
        dma_of = {}
        for e in ENGS:
            for o in self.ops[e]:
                if o.dtok is not None:
                    dma_of[o.dtok] = o
        pos = {e: 0 for e in ENGS}
        new_ops = {e: [] for e in ENGS}
        segi = -1
        while any(pos[e] < len(self.ops[e]) for e in ENGS):
            segi += 1
            seg = {}
            for e in ENGS:
                lst = self.ops[e]
                i = pos[e]
                j = i
                while j < len(lst) and lst[j].fn is not None:
                    j += 1
                seg[e] = lst[i:j]
                pos[e] = j + 1 if j < len(lst) else j
                bar = lst[j] if j < len(lst) else None
                seg[e + "_bar"] = bar
            if segi < len(self.seg_flags) and not self.seg_flags[segi]:
                self._fix_barrier(seg, {e: seg[e] for e in ENGS})
                for e in ENGS:
                    new_ops[e].extend(seg[e])
                    if seg[e + "_bar"] is not None:
                        new_ops[e].append(seg[e + "_bar"])
                continue
            allops = [o for e in ENGS for o in seg[e]]
            inseg = set(id(o) for o in allops)
            for o in allops:
                o.users = []
                o.fin = None
            for o in allops:
                n = 0
                dl = []
                for t in o.deps:
                    d = t[1] if t[0] == 'E' else dma_of.get((t[1], t[2]))
                    if d is not None and id(d) in inseg and d.fn is not None:
                        dl.append(d)
                for d in o.after:
                    if id(d) in inseg:
                        dl.append(d)
                o.nun = len(dl)
                o.rdy = 0.0
                for d in dl:
                    d.users.append(o)
            free = {e: 0.0 for e in ENGS}
            pend = {e: list(seg[e]) for e in ENGS}
            head = {e: 0 for e in ENGS}
            issued = {e: [] for e in ENGS}
            remaining = len(allops)
            while remaining:
                best = None
                for e in ENGS:
                    lst = pend[e]
                    h = head[e]
                    while h < len(lst) and lst[h] is None:
                        h += 1
                    head[e] = h
                    if h >= len(lst):
                        continue
                    w = 1 if e == "sp" else window
                    cnt = 0
                    i = h
                    while i < len(lst) and cnt < w:
                        o = lst[i]
                        if o is not None:
                            cnt += 1
                            if o.nun == 0:
                                stt = o.rdy if o.rdy > free[e] else free[e]
                                if best is None or stt < best[0]:
                                    best = (stt, e, i)
                                if stt <= free[e]:
                                    break
                        i += 1
                if best is None:
                    raise RuntimeError("reorder: no schedulable op (cyclic deps?)")
                stt, e, i = best
                o = pend[e][i]
                pend[e][i] = None
                issued[e].append(o)
                remaining -= 1
                if o.dtok is not None:
                    free[e] = stt + 60.0
                    o.fin = stt + 2500.0 + o.dur
                else:
                    free[e] = stt + o.dur
                    o.fin = stt + o.dur + 150.0
                for u in o.users:
                    u.nun -= 1
                    if o in u.after and o not in [ (t[1] if t[0] == 'E' else None) for t in u.deps]:
                        r = stt
                    else:
                        r = o.fin
                    if r > u.rdy:
                        u.rdy = r
            self._fix_barrier(seg, issued)
            for e in ENGS:
                new_ops[e].extend(issued[e])
                if seg[e + "_bar"] is not None:
                    new_ops[e].append(seg[e + "_bar"])
        self.ops = new_ops

    @staticmethod
    def _fix_barrier(seg, order):
        etoks = []
        for e in ENGS:
            for o in reversed(order[e]):
                if o.fn is not None and o.dtok is None:
                    etoks.append(('E', o))
                    break
        for e in ENGS:
            bar = seg[e + "_bar"]
            if bar is not None:
                bar.deps = [t for t in bar.deps if t[0] == 'D'] + etoks

    def emit(self, esems, dsems):
        nc = self.nc
        if getattr(self, "do_reorder", True):
            self.reorder()
        for e in ENGS:
            for o in self.ops[e]:
                for t in o.deps:
                    if t[0] == 'E':
                        t[1].signal = True
        for e in ENGS:
            c = 0
            for o in self.ops[e]:
                if o.signal and o.fn is not None and o.dtok is None:
                    c += 1
                    o.count = c
                elif o.signal:
                    o.count = c
        sched = self

        def run(e_name, eng):
            seen = {}
            for o in sched.ops[e_name]:
                need = {}
                for t in o.deps:
                    if t[0] == 'E':
                        d = t[1]
                        if d.count == 0:
                            continue
                        key = ('E', d.eng)
                        val = d.count
                    else:
                        key = ('D', t[1])
                        val = t[2]
                    if need.get(key, 0) < val:
                        need[key] = val
                for key, val in need.items():
                    if seen.get(key, 0) >= val:
                        continue
                    seen[key] = val
                    sem = esems[key[1]] if key[0] == 'E' else dsems[key[1]]
                    eng.wait_ge(sem, val)
                if o.fn is None:
                    continue
                ins = o.fn(eng)
                if o.dtok is not None:
                    ins.then_inc(dsems[o.dtok[0]], 16)
                elif o.signal:
                    ins.then_inc(esems[e_name], 1)

        with nc.Block() as block:
            @block.tensor
            def _(eng):
                run("pe", eng)

            @block.scalar
            def _(eng):
                run("act", eng)

            @block.vector
            def _(eng):
                run("dve", eng)

            @block.gpsimd
            def _(eng):
                run("pool", eng)

            @block.sync
            def _(eng):
                run("sp", eng)


class Buf:
    def __init__(self, S, t, name):
        self.t = t
        self.r = S.res(name)

    def __getitem__(self, k):
        return self.t[k]


class K:
    def __init__(self, nc):
        self.nc = nc
        self.S = Sched(nc)
        self.stack = ExitStack()
        self.n = 0

    def sb(self, shape, dt, name=None, ctx=None):
        self.n += 1
        name = f"{name or 'sb'}_{self.n}"
        t = (ctx or self.stack).enter_context(self.nc.sbuf_tensor(name, list(shape), dt))
        return Buf(self.S, t, name)

    def psum(self, shape, dt=F32, name=None, ctx=None):
        self.n += 1
        name = f"{name or 'ps'}_{self.n}"
        t = (ctx or self.stack).enter_context(self.nc.psum_tensor(name, list(shape), dt))
        return Buf(self.S, t, name)

    def dram(self, shape, dt, name):
        t = self.nc.dram_tensor(name, list(shape), dt)
        b = Buf(self.S, t.ap(), name)
        return b

    def _rw(self, reads, writes):
        return [b.r for b in reads], [b.r for b in writes]

    def op(self, eng, fn, reads, writes, acc=False, dur=None):
        r, w = self._rw(reads, writes)
        o = self.S.op(eng, fn, r, w, acc)
        if dur is not None:
            o.dur = dur
        return o

    @staticmethod
    def _fsz(ap):
        n = 1
        for d in list(ap.shape)[1:]:
            n *= int(d)
        return n

    def dma(self, out, in_, reads, writes, eng="sp", sem=None):
        r, w = self._rw(reads, writes)
        return self.S.dma(eng, out, in_, r, w, sem.r if sem is not None else None)

    def mm(self, out, lhsT, rhs, start, stop, reads, writes):
        return self.op("pe", lambda e: e.matmul(out, lhsT, rhs, start=start, stop=stop), reads, writes, acc=True,
                       dur=70.0 + 0.45 * self._fsz(rhs) * (4 if rhs.dtype == F32 else 1))

    def tr(self, out, in_, ident, reads, writes):
        return self.op("pe", lambda e: e.transpose(out, in_, ident), reads, writes, acc=True, dur=130.0)

    def act(self, out, in_, func, reads, writes, bias=None, scale=None, accum_out=None, eng="act"):
        kw = {}
        if bias is not None:
            kw["bias"] = bias
        if scale is not None:
            kw["scale"] = scale
        if accum_out is not None:
            kw["accum_out"] = accum_out
        return self.op("act", lambda e: e.activation(out, in_, func, **kw), reads, writes,
                       dur=220.0 + 0.95 * self._fsz(out))

    def tt(self, eng, out, in0, in1, op, reads, writes):
        return self.op(eng, lambda e: e.tensor_tensor(out, in0, in1, op), reads, writes,
                       dur=(120.0 + 1.0 * self._fsz(out)) * (2.0 if eng == "pool" else 1.0))

    def ts(self, eng, out, in0, s1, s2, op0, op1, reads, writes):
        du = 120.0 + 0.9 * self._fsz(out)
        if op1 is None:
            return self.op(eng, lambda e: e.tensor_scalar(out, in0, s1, None, op0), reads, writes, dur=du)
        return self.op(eng, lambda e: e.tensor_scalar(out, in0, s1, s2, op0, op1), reads, writes, dur=du)

    def stt(self, eng, out, in0, scalar, in1, op0, op1, reads, writes):
        return self.op(eng, lambda e: e.scalar_tensor_tensor(out, in0, scalar, in1, op0, op1), reads, writes,
                       dur=120.0 + 1.4 * self._fsz(out))

    def copy(self, eng, out, in_, reads, writes):
        if eng == "act":
            return self.op("act", lambda e: e.copy(out, in_), reads, writes, dur=220.0 + 0.8 * self._fsz(out))
        return self.op(eng, lambda e: e.tensor_copy(out, in_), reads, writes,
                       dur=(120.0 + 0.7 * self._fsz(out)) * (2.0 if eng == "pool" else 1.0))

    def memset(self, eng, ap, val, writes):
        return self.op(eng, lambda e: e.memset(ap, val), [], writes)


TT = 512


class Ctx:
    pass


def load_w(k, ctx, src, kin, n, name, gain=None, stg=None, col0=0, w=None, q="sp", dve_only=False):
    kc_n = kin // 128
    if w is None:
        w = k.sb([128, kc_n, n], BF16, name, ctx)
    i = 0
    for kc in range(kc_n):
        for n0 in range(0, n, 2048):
            wd = min(2048, n - n0)
            s = stg[i % len(stg)]
            k.dma(s[:, 0:wd], src[kc * 128:(kc + 1) * 128, col0 + n0:col0 + n0 + wd], [], [s],
                  eng=q)
            if gain is None:
                k.copy("dve" if dve_only else ("dve", "pool")[i % 2], w[:, kc, n0:n0 + wd], s[:, 0:wd], [s], [w])
            elif i % 2 == 0 or dve_only:
                k.ts("dve", w[:, kc, n0:n0 + wd], s[:, 0:wd], gain[:, kc:kc + 1], None, ALU.mult, None,
                     [s, gain], [w])
            else:
                k.act(w[:, kc, n0:n0 + wd], s[:, 0:wd], AF.Copy, [s, gain], [w], scale=gain[:, kc:kc + 1])
            i += 1
    return w


def rmsnorm(k, C, x, xn, T, kc_n=8, dim=D, nb=None, xoff=0):
    sq, r, ps = nb if nb is not None else (C.sq, C.rr, C.ps_norm)
    k.act(sq[:, 0:kc_n, 0:T], x[:, 0:kc_n, 0:T], AF.Square, [x], [sq])
    for kc in range(kc_n):
        k.mm(ps[:, 0:T], C.ones_bf[:, :], sq[:, kc, 0:T], kc == 0, kc == kc_n - 1, [sq, C.ones_bf], [ps])
    k.act(r[:, 0:T], ps[:, 0:T], AF.Ln, [ps, C.eps_col], [r], bias=C.eps_col[:, 0:1], scale=1.0 / dim)
    k.act(r[:, 0:T], r[:, 0:T], AF.Exp, [r], [r], scale=-0.5)
    k.tt("dve", xn[:, 0:kc_n, xoff:xoff + T], x[:, 0:kc_n, 0:T],
         r[:, 0:T].rearrange("p (o t) -> p o t", o=1).to_broadcast([128, kc_n, T]), ALU.mult, [x, r], [xn])


def run_pipelined(make_gen, n, lag, maxlive=2):
    gens = [make_gen(t) for t in range(n)]
    prog = [0] * n
    done = [False] * n
    first = 0
    while first < n:
        for t in range(first, n):
            if t > first and not (done[t - 1] or prog[t - 1] >= lag):
                break
            if t - maxlive >= 0 and not done[t - maxlive]:
                break
            if done[t]:
                continue
            try:
                next(gens[t])
                prog[t] += 1
            except StopIteration:
                done[t] = True
        while first < n and done[first]:
            first += 1


def proj_fm(k, ps, w, xn, m0, T, kc_n, reads, col=None):
    for kc in range(kc_n):
        k.mm(ps[:, 0:T], w[:, kc, m0:m0 + 128], xn[:, kc, 0:T], kc == 0, kc == kc_n - 1, reads, [ps])


def phase_l0_inproj(k, C, S):
    nc = k.nc
    NT = S // TT
    with ExitStack() as ctx:
        C.sq = k.sb([128, 8, TT], BF16, "sq", ctx)
        C.rr = k.sb([128, TT], F32, "rr", ctx)
        stg = [k.sb([128, 2048], F32, "stg", ctx) for _ in range(2)]
        gain = k.sb([128, 8], F32, "gain", ctx)
        k.dma(gain[:, :], C.norm_mix[0], [], [gain])
        w_in = load_w(k, ctx, C.ev_w_in, D, 2048, "w_in", gain=gain, stg=stg)
        pw_f = k.sb([128, 4, 128], F32, "pw_f", ctx)
        pw = k.sb([128, 4, 128], BF16, "pw", ctx)
        k.dma(pw_f[:, :, :], C.pool_w.rearrange("g c d -> c g d"), [], [pw_f])
        k.copy("dve", pw[:, :, :], pw_f[:, :, :], [pw_f], [pw])
        pscale = k.sb([128, 4], F32, "pscale", ctx)
        k.dma(pscale[:, :], C.pool_scale, [], [pscale])
        corr = k.sb([128, 4, 16], F32, "corr", ctx)
        k.dma(corr[:, :, :], C.pool_corr, [], [corr])

        xt = [k.sb([128, 8, TT], F32, "xt", ctx) for _ in range(2)]
        xn = k.sb([128, 8, TT], BF16, "xn", ctx)
        qk = [k.sb([128, 8, TT], BF16, "qk", ctx) for _ in range(2)]
        vt = [k.sb([128, 4, 8, 65], BF16, "vt", ctx) for _ in range(2)]
        for b in vt:
            k.memset("pool", b[:, :, :, :], 1.0, [b])
        pb = k.sb([128, 4, 16 + TT], F32, "pb", ctx)
        k.memset("pool", pb[:, :, :], 0.0, [pb])
        wa = k.sb([128, 16 + TT], F32, "wa", ctx)
        wb = k.sb([128, 16 + TT], F32, "wb", ctx)
        k.memset("pool", wa[:, :], 0.0, [wa])
        k.memset("pool", wb[:, :], 0.0, [wb])
        pooled = k.sb([128, 4, TT], BF16, "pooled", ctx)
        bout = [k.sb([128, 4, TT], BF16, "bout", ctx) for _ in range(2)]
        ksum = k.sb([128, 4, 2], F32, "ksum", ctx)
        ps = C.ps
        pi = 0

        def load_x(t):
            b = xt[t % 2]
            k.dma(b[:, :, :], C.xT[:, t * TT:(t + 1) * TT].rearrange("(c p) t -> p c t", p=128), [], [b])

        load_x(0)
        for t in range(NT):
            if t + 1 < NT:
                load_x(t + 1)
            x = xt[t % 2]
            t0 = t * TT
            rmsnorm(k, C, x, xn, TT)
            qkb = qk[t % 2]
            for m in range(8):
                p = ps[pi % 6]; pi += 1
                proj_fm(k, p, w_in, xn, m * 128, TT, 8, [w_in, xn])
                k.copy("act", qkb[:, m, :], p[:, 0:TT], [p], [qkb])
            k.op("dve", lambda e, qkb=qkb: e.tensor_reduce(
                ksum[:, :, :], qkb[:, 4:8, :].rearrange("p c (b t) -> p c b t", b=2), AX.X, ALU.add),
                [qkb], [ksum])
            k.ts("dve", C.kmean[:, :, 2 * t:2 * t + 2], ksum[:, :, :], 1.0 / 256, None, ALU.mult, None,
                 [ksum], [C.kmean])
            for c in range(4):
                for hp in range(2):
                    k.dma(C.qaugT[2 * c + hp, 0:64, t0:t0 + TT], qkb[hp * 64:(hp + 1) * 64, c, :],
                          [qkb], [C.qaugT_r[t]], sem=qkb)
            k.dma(C.kT[:, t0:t0 + TT].rearrange("(c p) t -> p c t", p=128), qkb[:, 4:8, :],
                  [qkb], [C.kT_r[t]], sem=qkb)
            vb = vt[t % 2]
            for sub in range(4):
                p = ps[pi % 6]; pi += 1
                for kc in range(8):
                    k.mm(p[:, 0:512], xn[:, kc, sub * 128:(sub + 1) * 128], w_in[:, kc, 1024:1536],
                         kc == 0, kc == 7, [w_in, xn], [p])
                k.copy("dve", vb[:, sub, :, 1:65], p[:, 0:512].rearrange("p (h d) -> p h d", h=8), [p], [vb])
            k.dma(C.vaug[t0:t0 + TT, :].rearrange("(s p) c -> p s c", p=128),
                  vb[:, :, :, :].rearrange("p s h d -> p s (h d)"), [vb], [C.vaug_r[t]], sem=vb)
            for g in range(4):
                p = ps[pi % 6]; pi += 1
                proj_fm(k, p, w_in, xn, 1536 + g * 128, TT, 8, [w_in, xn])
                k.copy("act", pb[:, g, 16:16 + TT], p[:, 0:TT], [p], [pb])
            W = 16 + TT
            for g in range(4):
                eng = ("dve", "pool")[g % 2]
                src = pb
                cur = pb[:, g, :]
                bufs = [wa, wb]
                for lvl in range(g + 1):
                    sh = 1 << lvl
                    dst = bufs[lvl % 2]
                    k.tt(eng, dst[:, sh:W], cur[:, sh:W], cur[:, 0:W - sh], ALU.add, [src], [dst])
                    src = dst
                    cur = dst[:, :]
                wnd = 2 << g
                if t == 0:
                    k.tt(eng, cur[:, 16:32], cur[:, 16:32], corr[:, g, :], ALU.mult, [src, corr], [src])
                k.stt("dve", pooled[:, g, :], cur[:, 16:W], 1.0 / wnd, pb[:, g, 16:W], ALU.mult, ALU.subtract,
                      [src, pb], [pooled])
            k.copy("pool", pb[:, :, 0:16], pb[:, :, TT:TT + 16], [pb], [pb])
            bo = bout[t % 2]
            for g in range(4):
                p = ps[pi % 6]; pi += 1
                k.mm(p[:, 0:TT], pw[:, g, :], pooled[:, g, :], True, True, [pw, pooled], [p])
                k.ts("dve", bo[:, g, :], p[:, 0:TT], pscale[:, g:g + 1], None, ALU.mult, None, [p, pscale], [bo])
            k.dma(C.mixT[512:1024, t0:t0 + TT].rearrange("(g p) t -> p g t", p=128), bo[:, :, :],
                  [bo], [C.mixT_r[t]], sem=bo)
    k.S.end_phase()


class RB:
    def __init__(self, S, name):
        self.r = S.res(name)


def host_consts(S):
    c = {}
    c["ident_bf"] = np.eye(128, dtype=np.float32).astype(ml_dtypes.bfloat16)
    c["ident_f"] = np.eye(128, dtype=np.float32)
    corr = np.ones((4, 16), np.float32)
    for g, w in enumerate((2, 4, 8, 16)):
        for t in range(16):
            corr[g, t] = w / min(t + 1, w)
    c["pool_corr"] = np.ascontiguousarray(np.broadcast_to(corr[None], (128, 4, 16)))
    vb = np.concatenate([np.zeros(32, np.float32), np.full(32, -1e30, np.float32)])
    c["validbias"] = np.ascontiguousarray(np.broadcast_to(vb[None], (128, 64)))
    khot = np.zeros((32, S), np.float32)
    for j in range(S // 256):
        khot[j, j * 256:(j + 1) * 256] = 30000.0
    c["khot"] = khot.astype(ml_dtypes.bfloat16)
    kk = np.arange(128)[:, None, None]
    jj = np.arange(2)[None, :, None]
    qq = np.arange(256)[None, None, :]
    c["causal01"] = ((jj * 128 + kk) <= qq).astype(np.float32).astype(ml_dtypes.bfloat16)
    a = np.arange(128)
    c["tril01"] = (a[None, :] <= a[:, None]).astype(np.float32)
    c["tri_le"] = (a[:, None] <= a[None, :]).astype(np.float32)
    c["causT_neg"] = np.where(a[None, :] >= a[:, None], 0.0, -1e4).astype(np.float32)
    c["strictT01"] = (a[None, :] > a[:, None]).astype(np.float32)
    lv = np.zeros((128, 8, 128), np.float32)
    ii, jj2 = a[:, None], a[None, :]
    for l in range(7):
        b = 1 << l
        m = ((ii // (2 * b)) == (jj2 // (2 * b))) & ((ii % (2 * b)) >= b) & ((jj2 % (2 * b)) < b)
        lv[:, l, :] = m.T
        if l == 0:
            lv[:, 7, :] = m
    c["lvlmask"] = lv.astype(ml_dtypes.bfloat16)
    return c


def build(S, phases=None, debug_out=()):
    nc = bass.Bass("TRN2", target_bir_lowering=False)
    k = K(nc)
    C = Ctx()
    NT = S // TT

    def ext(name, shape, dt=F32):
        return Buf(k.S, nc.dram_tensor(name, list(shape), dt, kind="ExternalInput").ap(), name)

    def scratch(name, shape, dt):
        kind = "ExternalOutput" if name in debug_out else "Internal"
        return nc.dram_tensor(name, list(shape), dt, kind=kind).ap()

    C.xT = ext("xT", [D, S]).t
    C.norm_mix = [ext(f"norm_mix{l}", [128, 8]).t for l in range(2)]
    C.ev_w_in = ext("ev_w_in", [D, 2048]).t
    C.pool_w = ext("pool_w", [4, 128, 128]).t
    C.pool_scale = ext("pool_scale", [128, 4]).t
    C.pool_corr = ext("pool_corr", [128, 4, 16]).t
    C.validbias = ext("validbias", [128, 64]).t
    C.khot = ext("khot", [32, S], BF16).t
    C.causal01 = ext("causal01", [128, 2, 256], BF16).t
    C.memT = ext("memT", [D, NMEM]).t
    C.mem_norm = ext("mem_norm", [128, 8]).t
    C.norm_xattn = [ext(f"norm_xattn{l}", [128, 8]).t for l in range(2)]
    C.norm_ffn = [ext(f"norm_ffn{l}", [128, 8]).t for l in range(2)]
    C.final_norm = ext("final_norm", [128, 8]).t
    C.ev_w_out = ext("ev_w_out", [D, D]).t
    C.od_w_out = ext("od_w_out", [D, D]).t
    C.xattn_wq = [ext(f"xattn_wq{l}", [D, D]).t for l in range(2)]
    C.xattn_wkv = [ext(f"xattn_wkv{l}", [D, 2 * D]).t for l in range(2)]
    C.xattn_wo = [ext(f"xattn_wo{l}", [D, D]).t for l in range(2)]
    C.ffn_w_up = [ext(f"ffn_w_up{l}", [D, 2 * DFF]).t for l in range(2)]
    C.ffn_conv = [ext(f"ffn_conv{l}", [128, 2 * DFF // 128, 3]).t for l in range(2)]
    C.ffn_w_down = [ext(f"ffn_w_down{l}", [DFF, D]).t for l in range(2)]
    C.od_w_in = ext("od_w_in", [D, 3080]).t
    C.sgu_w = ext("sgu_w", [4, 128, 128]).t
    C.sgu_ln_g = ext("sgu_ln_g", [128, 4]).t
    C.sgu_ln_b = ext("sgu_ln_b", [128, 4]).t
    C.sgu_b = ext("sgu_b", [128, 4, 128]).t
    C.dn_conv = ext("dn_conv", [128, 12, 4]).t
    C.dn_a_log = ext("dn_a_log", [128, 4]).t
    C.dn_dt_bias = ext("dn_dt_bias", [128, 4]).t
    C.dn_norm_g = ext("dn_norm_g", [128, 128]).t
    C.tril01 = ext("tril01", [128, 128]).t
    C.tri_le = ext("tri_le", [128, 128]).t
    C.causT_neg = ext("causT_neg", [128, 128]).t
    C.strictT01 = ext("strictT01", [128, 128]).t
    C.lvlmask = ext("lvlmask", [128, 8, 128], BF16).t
    ident_bf_d = ext("ident_bf", [128, 128], BF16).t
    ident_f_d = ext("ident_f", [128, 128]).t
    C.qaugT = scratch("qaugT", [8, 96, S], BF16)
    C.kT = scratch("kT", [512, S], BF16)
    C.vaug = scratch("vaug", [S, 520], BF16)
    C.mixT = scratch("mixT", [D, S], BF16)
    C.dnT = scratch("dnT", [1536, S], BF16)
    C.gtok = scratch("gtok", [S, 512], BF16)
    C.batok = scratch("batok", [S, 8], F32)
    C.hA = scratch("hA", [D, S], F32)
    C.hB = scratch("hB", [D, S], F32)
    C.outT = nc.dram_tensor("outT", [D, S], F32, kind="ExternalOutput").ap()
    for nm in ("qaugT", "kT", "vaug", "mixT", "hA", "hB", "xT", "outT", "dnT", "gtok", "batok"):
        setattr(C, nm + "_r", [RB(k.S, f"{nm}{t}") for t in range(NT)])

    with k.stack:
        C.ps = [k.psum([128, 512], F32, "ps") for _ in range(6)]
        C.ps_norm = k.psum([128, 512], F32, "psn")
        C.ps_bf = k.psum([128, 1024], BF16, "psbf")
        C.ones_bf = k.sb([128, 128], BF16, "ones")
        k.memset("pool", C.ones_bf[:, :], 1.0, [C.ones_bf])
        C.ident_bf = k.sb([128, 128], BF16, "identb")
        k.dma(C.ident_bf[:, :], ident_bf_d, [], [C.ident_bf])
        C.ident_f = k.sb([128, 128], F32, "identf")
        k.dma(C.ident_f[:, :], ident_f_d, [], [C.ident_f])
        C.eps_col = k.sb([128, 1], F32, "epsc")
        k.memset("pool", C.eps_col[:, :], EPS, [C.eps_col])
        C.kmean = k.sb([128, 4, 32], F32, "kmean")
        k.memset("pool", C.kmean[:, :, :], 0.0, [C.kmean])
        k.S.end_phase()

        if phases is None or 1 in phases:
            phase_l0_inproj(k, C, S)
        if phases is None or 2 in phases:
            phase_moba_gate(k, C, S)
        with ExitStack() as span:
            pre = None
            pf = None
            if phases is None or (22 in phases and 3 in phases):
                pgq = k.sb([128, 8], F32, "pgq", span)
                pw = [k.sb([128, 8, D], BF16, nm, span) for nm in ("w_out", "wq", "wo")]
                pstg = [k.sb([128, 2048], F32, "pstg", span) for _ in range(2)]
                pre = tuple(pw)

                def pf():
                    k.dma(pgq[:, :], C.norm_xattn[0], [], [pgq], eng="pool")
                    load_w(k, span, C.ev_w_out, D, D, "w_out", stg=pstg, w=pw[0], q="pool", dve_only=True)
                    load_w(k, span, C.xattn_wq[0], D, D, "wq", gain=pgq, stg=pstg, w=pw[1], q="pool", dve_only=True)
                    load_w(k, span, C.xattn_wo[0], D, D, "wo", stg=pstg, w=pw[2], q="pool", dve_only=True)
            if phases is None or 22 in phases:
                phase_moba_attn(k, C, S, prefetch=pf)
            if phases is None or 3 in phases:
                phase_outproj_xattn(k, C, S, 0, C.ev_w_out, C.xT, C.xT_r, C.hA, C.hA_r, pre=pre)
        if phases is None or 4 in phases:
            phase_ffn(k, C, S, 0, C.hA, C.hA_r, C.hB, C.hB_r, False)
        if phases is None or 5 in phases:
            phase_l1_inproj(k, C, S)
        if phases is None or 6 in phases:
            phase_deltanet(k, C, S)
        if phases is None or 7 in phases:
            phase_outproj_xattn(k, C, S, 1, C.od_w_out, C.hB, C.hB_r, C.hA, C.hA_r)
        if phases is None or 8 in phases:
            phase_ffn(k, C, S, 1, C.hA, C.hA_r, C.outT, C.outT_r, True)

        esems = {}
        for e in ENGS:
            esems[e] = k.stack.enter_context(nc.semaphore(f"es_{e}"))
        dsems = [k.stack.enter_context(nc.semaphore(f"ds_{i}")) for i in range(k.S.n_dma_sems)]
        k.S.emit(esems, dsems)
    return nc


def phase_moba_gate(k, C, S):
    GS = 9
    NT = S // TT
    with ExitStack() as ctx:
        kmb = k.sb([128, 4, 64], BF16, "kmb", ctx)
        k.memset("pool", kmb[:, :, :], 0.0, [kmb])
        k.copy("dve", kmb[0:64, :, 0:32], C.kmean[0:64, :, :], [C.kmean], [kmb])
        k.copy("dve", kmb[64:128, :, 32:64], C.kmean[64:128, :, :], [C.kmean], [kmb])
        vbias = k.sb([128, 64], F32, "vbias", ctx)
        k.dma(vbias[:, :], C.validbias, [], [vbias])
        qc = [k.sb([128, 4, TT], BF16, "qc", ctx) for _ in range(2)]
        gsb = k.sb([128, 8, 32], F32, "gsb", ctx)
        top8 = k.sb([128, 8, 8], F32, "top8", ctx)
        sel = k.sb([128, 8, 32], F32, "sel", ctx)
        mb = k.sb([128, 8, 96], BF16, "mb", ctx)
        k.memset("pool", mb[:, :, :], 0.0, [mb])
        mrow = [k.sb([128, 8, TT], BF16, "mrow", ctx) for _ in range(2)]
        ps = C.ps

        def load_q(t):
            b = qc[t % 2]
            for h in range(8):
                k.dma(b[(h % 2) * 64:(h % 2) * 64 + 64, h // 2, :], C.qaugT[h, 0:64, t * TT:(t + 1) * TT],
                      [C.qaugT_r[t]], [b])

        load_q(0)
        for t in range(NT):
            if t + 1 < NT:
                load_q(t + 1)
            q = qc[t % 2]
            mr = mrow[t % 2]
            for sub in range(4):
                qb = (t * TT + sub * 128) // 256
                gp = ps[sub % 2]
                for c in range(4):
                    k.mm(gp[:, c * 64:(c + 1) * 64], q[:, c, sub * 128:(sub + 1) * 128],
                         kmb[:, c, :], True, True, [q, kmb], [gp])
                k.tt("dve", gsb[:, :, :], gp[:, 0:256].rearrange("p (h n) -> p h n", h=8),
                     vbias[:, 32 - qb:64 - qb].rearrange("p (o n) -> p o n", o=1).to_broadcast([128, 8, 32]),
                     ALU.add, [gp, vbias], [gsb])
                if GS <= 1:
                    continue
                for h in range(8):
                    k.op("dve", lambda e, h=h: e.max(top8[:, h, :], gsb[:, h, :]), [gsb], [top8], acc=True)
                k.tt("dve", sel[:, :, :], gsb[:, :, :], top8[:, :, 2:3].to_broadcast([128, 8, 32]), ALU.is_ge,
                     [gsb, top8], [sel])
                k.ts("dve", mb[:, :, 64:96], sel[:, :, :], -1.0, None, ALU.add, None, [sel], [mb])
                k.memset("dve", mb[:, :, 64 + qb:65 + qb], 0.0, [mb])
                if GS <= 2:
                    continue
                for half in range(2):
                    mp = ps[2 + (sub * 2 + half) % 4]
                    for hh in range(4):
                        h = half * 4 + hh
                        k.mm(mp[0:96, hh * 128:(hh + 1) * 128], mb[:, h, :], C.ident_bf[:, :], True, True,
                             [mb, C.ident_bf], [mp])
                    k.copy("act", mr[64:96, half * 4:half * 4 + 4, sub * 128:(sub + 1) * 128],
                           mp[64:96, 0:512].rearrange("p (h q) -> p h q", h=4), [mp], [mr])
            if GS <= 3:
                continue
            k.dma(C.qaugT[:, 64:96, t * TT:(t + 1) * TT].rearrange("h r t -> r h t"), mr[64:96, :, :],
                  [mr], [C.qaugT_r[t]], sem=mr)
    k.S.end_phase()


def phase_moba_attn(k, C, S, prefetch=None):
    NB = S // 256
    NKT = S // 128
    with ExitStack() as ctx:
        vsb = k.sb([128, NKT, 520], BF16, "vsb", ctx)
        for i in range(0, NKT, 8):
            n = min(8, NKT - i)
            k.dma(vsb[:, i:i + n, :], C.vaug[i * 128:(i + n) * 128, :].rearrange("(t p) c -> p t c", p=128),
                  [C.vaug_r[(i * 128) // TT], C.vaug_r[min(S // TT - 1, ((i + n) * 128 - 1) // TT)]], [vsb])
        kaug = [k.sb([96, S], BF16, "kaug", ctx) for _ in range(2)]
        for b in kaug:
            k.dma(b[64:96, :], C.khot, [], [b])
        caus = k.sb([128, 2, 256], BF16, "caus", ctx)
        k.dma(caus[:, :, :], C.causal01, [], [caus])
        onesf = k.sb([128, 65], F32, "onesf", ctx)
        k.memset("pool", onesf[:, :], 1.0, [onesf])
        pt = [k.sb([128, 2, 256], BF16, "pt", ctx) for _ in range(4)]
        rden = [k.sb([1, 256], F32, "rden", ctx) for _ in range(2)]
        bcs = k.sb([65, 256], F32, "bcs", ctx)
        ao = [k.sb([65, 512], BF16, "ao", ctx) for _ in range(2)]
        sps = C.ps[0:3]
        ops = [C.ps[3], C.ps[4], C.ps_norm]
        bps = C.ps[5]
        all_k = [C.kT_r[t] for t in range(S // TT)]
        NQ = 4
        qa = [k.sb([96, 256], BF16, "qa", ctx) for _ in range(NQ)]
        groups = [(h, qb) for h in range(8) for qb in range(NB)]
        tasks = []
        for gi, (h, qb) in enumerate(groups):
            for pr in range(qb + 1):
                tasks.append((gi, h, qb, pr))

        def load_q(gi):
            h, qb = groups[gi]
            b = qa[gi % NQ]
            k.dma(b[:, :], C.qaugT[h, :, qb * 256:(qb + 1) * 256], [C.qaugT_r[(qb * 256) // TT]], [b])

        def load_k(h):
            kb = kaug[h % 2]
            k.dma(kb[0:64, :], C.kT[h * 64:(h + 1) * 64, :], all_k, [kb])

        def emit_S(i):
            gi, h, qb, pr = tasks[i]
            if pr == 0:
                if qb == 0 and h + 1 < 8:
                    load_k(h + 1)
                if gi + NQ - 1 < len(groups):
                    load_q(gi + NQ - 1)
            kb = kaug[h % 2]
            q = qa[gi % NQ]
            sp = sps[i % 3]
            p = pt[i % 4]
            for j in range(2):
                kt = 2 * pr + j
                k.mm(sp[:, j * 256:(j + 1) * 256], kb[0:96, kt * 128:(kt + 1) * 128], q[0:96, :],
                     True, True, [kb, q], [sp])
            k.act(p[:, :, :], sp[:, 0:512].rearrange("p (j q) -> p j q", j=2), AF.Exp, [sp], [p], scale=0.125)
            if pr == qb:
                k.tt("pool", p[:, :, :], p[:, :, :], caus[:, :, :], ALU.mult, [p, caus], [p])

        deferred = []

        def emit_PV(i):
            gi, h, qb, pr = tasks[i]
            p = pt[i % 4]
            op_ = ops[gi % 3]
            for j in range(2):
                kt = 2 * pr + j
                k.mm(op_[0:65, 0:256], vsb[:, kt, h * 65:(h + 1) * 65], p[:, j, :],
                     pr == 0 and j == 0, pr == qb and j == 1, [vsb, p], [op_])
            if pr == qb:
                rd = rden[gi % 2]
                k.op("dve", lambda e, op_=op_, rd=rd: e.reciprocal(rd[0:1, :], op_[0:1, 0:256]), [op_], [rd])

                def tail(gi=gi, h=h, qb=qb, op_=op_, rd=rd):
                    k.mm(bps[0:65, 0:256], onesf[0:1, 0:65], rd[0:1, :], True, True, [onesf, rd], [bps])
                    k.copy("act", bcs[:, :], bps[0:65, 0:256], [bps], [bcs])
                    a = ao[(qb // 2) % 2]
                    k.tt("dve", a[:, (qb % 2) * 256:(qb % 2 + 1) * 256], op_[0:65, 0:256], bcs[:, :], ALU.mult,
                         [op_, bcs], [a])
                    if qb % 2 == 1:
                        t = qb // 2
                        k.dma(C.mixT[h * 64:(h + 1) * 64, t * TT:(t + 1) * TT], a[1:65, :], [a], [C.mixT_r[t]], sem=a)
                deferred.append((i + 2, tail))

        load_k(0)
        for gi in range(min(NQ - 1, len(groups))):
            load_q(gi)
        if prefetch is not None:
            prefetch()
        n = len(tasks)
        emit_S(0)
        if n > 1:
            emit_S(1)
        for i in range(n):
            while deferred and deferred[0][0] <= i:
                deferred.pop(0)[1]()
            if i + 2 < n:
                emit_S(i + 2)
            emit_PV(i)
        while deferred:
            deferred.pop(0)[1]()
    k.S.end_phase()


def phase_outproj_xattn(k, C, S, l, w_out_d, hin, hin_r, hout, hout_r, pre=None):
    NT = S // TT
    k.S.reorder_on = False
    with ExitStack() as ctx:
        C.sq = k.sb([128, 8, TT], BF16, "sq", ctx)
        C.rr = k.sb([128, TT], F32, "rr", ctx)
        kmemT = k.sb([128, 8, NMEM], BF16, "kmemT", ctx)
        vmem = k.sb([128, 2, D], BF16, "vmem", ctx)
        with ExitStack() as c2:
            stg = [k.sb([128, 2048], F32, "stg", c2) for _ in range(4)]
            gm = k.sb([128, 8], F32, "gm", c2)
            k.dma(gm[:, :], C.mem_norm, [], [gm])
            wkv = load_w(k, c2, C.xattn_wkv[l], D, 2048, "wkv", gain=gm, stg=stg)
            mt_ = k.sb([128, 8, NMEM], F32, "memt", c2)
            k.dma(mt_[:, :, :], C.memT.rearrange("(c p) m -> p c m", p=128), [], [mt_])
            memn = k.sb([128, 8, NMEM], BF16, "memn", c2)
            rmsnorm(k, C, mt_, memn, NMEM)
            for dc in range(8):
                p = C.ps[dc % 6]
                proj_fm(k, p, wkv, memn, dc * 128, NMEM, 8, [wkv, memn])
                k.copy("act", kmemT[:, dc, :], p[:, 0:NMEM], [p], [kmemT])
            for mt in range(2):
                for half in range(2):
                    p = C.ps[(mt * 2 + half) % 6]
                    for kc in range(8):
                        k.mm(p[:, 0:512], memn[:, kc, mt * 128:(mt + 1) * 128],
                             wkv[:, kc, 1024 + half * 512:1536 + half * 512], kc == 0, kc == 7, [memn, wkv], [p])
                    k.copy("dve", vmem[:, mt, half * 512:(half + 1) * 512], p[:, 0:512], [p], [vmem])
        k.S.end_phase()
        if pre is not None:
            w_out, wq, wo = pre
        else:
            stg = [k.sb([128, 2048], F32, "stg", ctx) for _ in range(2)]
            gq = k.sb([128, 8], F32, "gq", ctx)
            k.dma(gq[:, :], C.norm_xattn[l], [], [gq])
            w_out = load_w(k, ctx, w_out_d, D, D, "w_out", stg=stg)
            wq = load_w(k, ctx, C.xattn_wq[l], D, D, "wq", gain=gq, stg=stg)
            wo = load_w(k, ctx, C.xattn_wo[l], D, D, "wo", stg=stg)
        xt = [k.sb([128, 8, TT], F32, "xt", ctx) for _ in range(2)]
        mx = [k.sb([128, 8, TT], BF16, "mx", ctx) for _ in range(2)]
        xn = [k.sb([128, 8, TT], BF16, "xn", ctx) for _ in range(2)]
        qx = [k.sb([128, 8, TT], BF16, "qx", ctx) for _ in range(2)]
        ox = [k.sb([128, 8, TT], BF16, "ox", ctx) for _ in range(2)]
        pt = [[k.sb([128, 2, TT], BF16, "pt", ctx) for _ in range(2)] for _ in range(2)]
        rec = [[k.sb([128, TT], F32, "rec", ctx) for _ in range(2)] for _ in range(2)]
        rrb = [C.rr, k.sb([128, TT], F32, "rr2", ctx)]
        ps = C.ps
        st = {"pi": 0}

        def nps():
            p = ps[st["pi"] % 6]
            st["pi"] += 1
            return p

        def load(t):
            k.dma(xt[t % 2][:, :, :], hin[:, t * TT:(t + 1) * TT].rearrange("(c p) t -> p c t", p=128),
                  [hin_r[t]], [xt[t % 2]])
            k.dma(mx[t % 2][:, :, :], C.mixT[:, t * TT:(t + 1) * TT].rearrange("(c p) t -> p c t", p=128),
                  [C.mixT_r[t]], [mx[t % 2]])

        def tile(t):
            x, m_, xn_, qx_, ox_ = xt[t % 2], mx[t % 2], xn[t % 2], qx[t % 2], ox[t % 2]
            for m in range(8):
                p = nps()
                proj_fm(k, p, w_out, m_, m * 128, TT, 8, [w_out, m_])
                k.tt("dve", x[:, m, :], x[:, m, :], p[:, 0:TT], ALU.add, [x, p], [x])
            yield
            rmsnorm(k, C, x, xn_, TT, nb=(C.sq, rrb[t % 2], C.ps_norm))
            yield
            for m in range(8):
                p = nps()
                proj_fm(k, p, wq, xn_, m * 128, TT, 8, [wq, xn_])
                k.copy("act", qx_[:, m, :], p[:, 0:TT], [p], [qx_])
                if m == 3:
                    yield
            yield
            for hd in range(4):
                ptb = pt[t % 2][hd % 2]
                rc = rec[t % 2][hd % 2]
                for mt in range(2):
                    p = nps()
                    for dd in range(2):
                        dc = 2 * hd + dd
                        k.mm(p[:, 0:TT], kmemT[:, dc, mt * 128:(mt + 1) * 128], qx_[:, dc, :], dd == 0, dd == 1,
                             [kmemT, qx_], [p])
                    k.act(ptb[:, mt, :], p[:, 0:TT], AF.Exp, [p], [ptb], scale=1.0 / 16)
                p = nps()
                for mt in range(2):
                    k.mm(p[:, 0:TT], C.ones_bf[:, :], ptb[:, mt, :], mt == 0, mt == 1, [C.ones_bf, ptb], [p])
                k.op("dve", lambda e, rc=rc, p=p: e.reciprocal(rc[:, :], p[:, 0:TT]), [p], [rc])
                for dd in range(2):
                    p = nps()
                    c0 = hd * 256 + dd * 128
                    for mt in range(2):
                        k.mm(p[:, 0:TT], vmem[:, mt, c0:c0 + 128], ptb[:, mt, :], mt == 0, mt == 1, [vmem, ptb], [p])
                    k.tt("dve", ox_[:, 2 * hd + dd, :], p[:, 0:TT], rc[:, :], ALU.mult, [p, rc], [ox_])
                yield
            for m in range(8):
                p = nps()
                proj_fm(k, p, wo, ox_, m * 128, TT, 8, [wo, ox_])
                k.tt("dve", x[:, m, :], x[:, m, :], p[:, 0:TT], ALU.add, [x, p], [x])
                if m == 3:
                    yield
            k.dma(hout[:, t * TT:(t + 1) * TT].rearrange("(c p) t -> p c t", p=128), x[:, :, :],
                  [x], [hout_r[t]], sem=x)
            if t + 2 < NT:
                load(t + 2)

        load(0)
        if NT > 1:
            load(1)
        run_pipelined(tile, NT, 6)
    k.S.end_phase()
    k.S.reorder_on = True


TF = 256


def phase_ffn(k, C, S, l, hin, hin_r, hout, hout_r, final):
    k.S.reorder_on = False
    NTF = S // TF
    NJ = DFF // 128
    with ExitStack() as ctx:
        gf = k.sb([128, 8], F32, "gf", ctx)
        k.dma(gf[:, :], C.norm_ffn[l], [], [gf])
        w_up = k.sb([128, 8, 2 * DFF], BF16, "w_up", ctx)
        w_dn = k.sb([128, NJ, D], BF16, "w_dn", ctx)
        with ExitStack() as c2:
            stg = [k.sb([128, 2048], F32, "stg", c2) for _ in range(4)]
            load_w(k, ctx, C.ffn_w_up[l], D, 2 * DFF, "w_up", gain=gf, stg=stg, w=w_up)
            load_w(k, ctx, C.ffn_w_down[l], DFF, D, "w_dn", stg=stg, w=w_dn)
        k.S.end_phase()
        cw = k.sb([128, 2 * NJ, 3], F32, "cw", ctx)
        k.dma(cw[:, :, :], C.ffn_conv[l], [], [cw])
        gfin = k.sb([128, 8], F32, "gfin", ctx)
        k.dma(gfin[:, :], C.final_norm, [], [gfin])
        xt = [k.sb([128, 8, TF], F32, "xt", ctx) for _ in range(2)]
        xn = [k.sb([128, 8, 2 + TF], BF16, "xn", ctx) for _ in range(2)]
        for b in xn:
            k.memset("pool", b[:, :, :], 0.0, [b])
        sqb = [k.sb([128, 8, TF], BF16, "sq", ctx)] * 2
        rrb = [k.sb([128, TF], F32, "rr", ctx) for _ in range(2)]
        actb = [[k.sb([128, TF], BF16, "actb", ctx) for _ in range(NJ)] for _ in range(2)]
        yb = [[k.sb([128, TF], F32, "yb", ctx) for _ in range(4)] for _ in range(2)]
        sg = [[k.sb([128, TF], F32, "sg", ctx) for _ in range(2)] for _ in range(2)]
        ps = C.ps
        st = {"pi": 0}
        W2 = 2 + TF

        def nps():
            p = ps[st["pi"] % 6]
            st["pi"] += 1
            return p

        def load(t):
            k.dma(xt[t % 2][:, :, :], hin[:, t * TF:(t + 1) * TF].rearrange("(c p) t -> p c t", p=128),
                  [hin_r[(t * TF) // TT]], [xt[t % 2]])

        def tile(t):
            x, xn_, ab = xt[t % 2], xn[t % 2], actb[t % 2]
            nb = (sqb[t % 2], rrb[t % 2], C.ps_norm)
            if t > 0:
                k.copy("pool", xn_[:, :, 0:2], xn[(t - 1) % 2][:, :, TF:TF + 2], [xn[(t - 1) % 2]], [xn_])
            rmsnorm(k, C, x, xn_, TF, nb=nb, xoff=2)
            yield
            pend = None
            for j in range(NJ):
                ys = []
                for which in range(2):
                    c = which * NJ + j
                    p = nps()
                    y = yb[t % 2][(2 * j + which) % 4]
                    for kc in range(8):
                        k.mm(p[:, 0:W2], w_up[:, kc, c * 128:(c + 1) * 128], xn_[:, kc, 0:W2], kc == 0, kc == 7,
                             [w_up, xn_], [p])
                    k.act(y[:, :], p[:, 2:W2], AF.Copy, [p, cw], [y], scale=cw[:, c, 2:3])
                    k.stt("dve", y[:, :], p[:, 1:1 + TF], cw[:, c, 1:2], y[:, :], ALU.mult, ALU.add, [p, cw, y], [y])
                    k.stt("dve", y[:, :], p[:, 0:TF], cw[:, c, 0:1], y[:, :], ALU.mult, ALU.add, [p, cw, y], [y])
                    ys.append(y)
                if pend is not None:
                    pend()

                def fin(j=j, ys=ys):
                    s_ = sg[t % 2][j % 2]
                    k.act(s_[:, :], ys[0][:, :], AF.Silu, [ys[0]], [s_])
                    k.tt("pool", ab[j][:, :], s_[:, :], ys[1][:, :], ALU.mult, [s_, ys[1]], [ab[j]])
                pend = fin
                if j % 2 == 1:
                    yield
            pend()
            yield
            for m in range(8):
                p = nps()
                for j in range(NJ):
                    k.mm(p[:, 0:TF], w_dn[:, j, m * 128:(m + 1) * 128], ab[j][:, :], j == 0, j == NJ - 1,
                         [w_dn, ab[j]], [p])
                k.tt("dve", x[:, m, :], x[:, m, :], p[:, 0:TF], ALU.add, [x, p], [x])
                if m == 3:
                    yield
            if final:
                fin_x = k.sb
                rmsnorm(k, C, x, xfin, TF, nb=nb)
                for m in range(8):
                    k.stt("dve", x[:, m, :], x[:, m, :], gfin[:, m:m + 1], nb[1][:, 0:TF], ALU.mult, ALU.mult,
                          [x, gfin, nb[1]], [x])
            k.dma(hout[:, t * TF:(t + 1) * TF].rearrange("(c p) t -> p c t", p=128), x[:, :, :],
                  [x], [hout_r[(t * TF) // TT]], sem=x)
            if t + 2 < NTF:
                load(t + 2)

        xfin = k.sb([128, 8, TF], BF16, "xfin", ctx) if final else None
        load(0)
        load(1)
        run_pipelined(tile, NTF, 6)
    k.S.end_phase()
    k.S.reorder_on = True


def _v8(v):
    return np.ascontiguousarray(np.asarray(v, np.float32).reshape(-1, 128).T)


def core_inputs(inp, b, S, consts):
    f = lambda a: np.ascontiguousarray(np.asarray(a, np.float32))
    d = dict(consts)
    d["xT"] = f(np.asarray(inp["x"][b]).T)
    d["memT"] = f(np.asarray(inp["mem"][b]).T)
    d["mem_norm"] = _v8(inp["mem_norm"])
    d["final_norm"] = _v8(inp["final_norm"])
    for l in range(2):
        d[f"norm_mix{l}"] = _v8(inp["norm_mix"][l])
        d[f"norm_xattn{l}"] = _v8(inp["norm_xattn"][l])
        d[f"norm_ffn{l}"] = _v8(inp["norm_ffn"][l])
        d[f"xattn_wq{l}"] = f(inp["xattn_wq"][l])
        d[f"xattn_wkv{l}"] = f(inp["xattn_wkv"][l])
        d[f"xattn_wo{l}"] = f(inp["xattn_wo"][l])
        d[f"ffn_w_up{l}"] = f(inp["ffn_w_up"][l])
        d[f"ffn_w_down{l}"] = f(inp["ffn_w_down"][l])
        d[f"ffn_conv{l}"] = f(np.asarray(inp["ffn_conv"][l]).reshape(3, -1, 128).transpose(2, 1, 0))
    d["ev_w_in"] = f(inp["ev_w_in"][0])
    d["ev_w_out"] = f(inp["ev_w_out"][0])
    d["od_w_out"] = f(inp["od_w_out"][0])
    d["pool_w"] = f(inp["pool_w"][0])
    d["pool_scale"] = _v8(inp["pool_scale"][0])
    d["od_w_in"] = f(inp["od_w_in"][0])
    d["sgu_w"] = f(inp["sgu_w"][0])
    d["sgu_ln_g"] = _v8(inp["sgu_ln_g"][0])
    d["sgu_ln_b"] = _v8(inp["sgu_ln_b"][0])
    d["sgu_b"] = f(np.broadcast_to(np.asarray(inp["sgu_b"][0])[None], (128, 4, 128)))
    d["dn_conv"] = f(np.asarray(inp["dn_conv"][0]).reshape(4, 12, 128).transpose(2, 1, 0))
    d["dn_a_log"] = f(np.broadcast_to(np.asarray(inp["dn_a_log"][0])[None], (128, 4)))
    d["dn_dt_bias"] = f(np.broadcast_to(np.asarray(inp["dn_dt_bias"][0])[None], (128, 4)))
    d["dn_norm_g"] = f(np.broadcast_to(np.asarray(inp["dn_norm_g"][0])[None], (128, 128)))
    return d


def phase_l1_inproj(k, C, S):
    NT = S // TT
    GC1 = 1.5957691216057308
    with ExitStack() as ctx:
        C.sq = k.sb([128, 8, TT], BF16, "sq", ctx)
        C.rr = k.sb([128, TT], F32, "rr", ctx)
        gain = k.sb([128, 8], F32, "gain", ctx)
        k.dma(gain[:, :], C.norm_mix[1], [], [gain])
        w_in = k.sb([128, 8, 3080], BF16, "w_in1", ctx)
        wsT = k.sb([128, 4, 128], BF16, "wsT", ctx)
        with ExitStack() as c2:
            stg = [k.sb([128, 2048], F32, "stg", c2) for _ in range(4)]
            load_w(k, ctx, C.od_w_in, D, 3080, "w_in1", gain=gain, stg=stg, w=w_in)
            wsf = k.sb([128, 4, 128], F32, "wsf", c2)
            k.dma(wsf[:, :, :], C.sgu_w.rearrange("g t s -> t g s"), [], [wsf])
            tril = k.sb([128, 128], F32, "tril", c2)
            k.dma(tril[:, :], C.tril01, [], [tril])
            wsm = k.sb([128, 4, 128], BF16, "wsm", c2)
            k.tt("dve", wsm[:, :, :], wsf[:, :, :],
                 tril[:, :].rearrange("p (o s) -> p o s", o=1).to_broadcast([128, 4, 128]), ALU.mult,
                 [wsf, tril], [wsm])
            for g in range(4):
                k.tr(C.ps_bf[:, g * 128:(g + 1) * 128], wsm[:, g, :], C.ident_bf[:, :], [wsm, C.ident_bf], [C.ps_bf])
            k.copy("act", wsT[:, :, :], C.ps_bf[:, 0:512].rearrange("p (g t) -> p g t", g=4), [C.ps_bf], [wsT])
        k.S.end_phase()
        lng = k.sb([128, 4], F32, "lng", ctx)
        lnb = k.sb([128, 4], F32, "lnb", ctx)
        k.dma(lng[:, :], C.sgu_ln_g, [], [lng])
        k.dma(lnb[:, :], C.sgu_ln_b, [], [lnb])
        bsb = k.sb([128, 4, 128], F32, "bsb", ctx)
        k.dma(bsb[:, :, :], C.sgu_b, [], [bsb])
        dcw = k.sb([128, 12, 4], F32, "dcw", ctx)
        k.dma(dcw[:, :, :], C.dn_conv, [], [dcw])
        qsc = k.sb([128, 1], F32, "qsc", ctx)
        k.memset("pool", qsc[:, :], float(np.log(128.0 ** -0.5)), [qsc])
        zero_c = k.sb([128, 1], F32, "zero_c", ctx)
        k.memset("pool", zero_c[:, :], 0.0, [zero_c])

        xt = [k.sb([128, 8, TT], F32, "xt", ctx) for _ in range(2)]
        xn = k.sb([128, 8, TT], BF16, "xn", ctx)
        x2 = [k.sb([128, TT], F32, "x2", ctx) for _ in range(2)]
        sgm = [k.sb([128, TT], F32, "sgm", ctx) for _ in range(2)]
        ub_ = k.sb([128, 4, TT], F32, "u", ctx)
        v_ = k.sb([128, 4, TT], F32, "v", ctx)
        vb = k.sb([128, 4, TT], BF16, "vb", ctx)
        vsq = k.sb([128, 4, TT], BF16, "vsq", ctx)
        mean = k.sb([128, TT], F32, "mean", ctx)
        m2 = k.sb([128, TT], F32, "m2", ctx)
        rstd = k.sb([128, TT], F32, "rstd", ctx)
        vn = k.sb([128, 4, TT], BF16, "vn", ctx)
        vtok = k.sb([128, 4, 4, 128], BF16, "vtok", ctx)
        t1 = k.sb([128, 4, 128], F32, "t1", ctx)
        cout = [k.sb([128, 4, TT], BF16, "cout", ctx) for _ in range(2)]
        cu = [k.sb([128, 3 + TT], F32, "cu", ctx) for _ in range(2)]
        cy = [k.sb([128, TT], F32, "cy", ctx) for _ in range(2)]
        chist = k.sb([128, 12, 3], F32, "chist", ctx)
        k.memset("pool", chist[:, :, :], 0.0, [chist])
        dq4 = k.sb([128, 4, TT], F32, "dq4", ctx)
        dsq4 = k.sb([128, 4, TT], BF16, "dsq4", ctx)
        rn4 = k.sb([128, 4, TT], F32, "rn4", ctx)
        dno = [k.sb([128, 12, TT], BF16, "dno", ctx)] * 2
        gto = [k.sb([128, 4, 512], BF16, "gto", ctx)] * 2
        bao = [k.sb([128, 4, 8], F32, "bao", ctx) for _ in range(2)]
        ps = C.ps
        pi = 0

        def load(t):
            k.dma(xt[t % 2][:, :, :], C.hB[:, t * TT:(t + 1) * TT].rearrange("(c p) t -> p c t", p=128),
                  [C.hB_r[t]], [xt[t % 2]])

        load(0)
        for t in range(NT):
            if t + 1 < NT:
                load(t + 1)
            x = xt[t % 2]
            t0 = t * TT
            rmsnorm(k, C, x, xn, TT)
            for m in range(8):
                p = ps[pi % 6]; pi += 1
                a2, sg = x2[m % 2], sgm[m % 2]
                proj_fm(k, p, w_in, xn, m * 128, TT, 8, [w_in, xn])
                k.act(a2[:, :], p[:, 0:TT], AF.Square, [p], [a2])
                k.ts("dve", a2[:, :], a2[:, :], 0.044715, 1.0, ALU.mult, ALU.add, [a2], [a2])
                k.tt("dve", a2[:, :], a2[:, :], p[:, 0:TT], ALU.mult, [a2, p], [a2])
                k.act(sg[:, :], a2[:, :], AF.Sigmoid, [a2], [sg], scale=GC1)
                dst = ub_ if m < 4 else v_
                k.tt("dve", dst[:, m % 4, :], sg[:, :], p[:, 0:TT], ALU.mult, [sg, p], [dst])
            k.copy("pool", vb[:, :, :], v_[:, :, :], [v_], [vb])
            k.act(vsq[:, :, :], v_[:, :, :], AF.Square, [v_], [vsq])
            pm = ps[pi % 6]; pi += 1
            pq = ps[pi % 6]; pi += 1
            for c in range(4):
                k.mm(pm[:, 0:TT], C.ones_bf[:, :], vb[:, c, :], c == 0, c == 3, [C.ones_bf, vb], [pm])
            for c in range(4):
                k.mm(pq[:, 0:TT], C.ones_bf[:, :], vsq[:, c, :], c == 0, c == 3, [C.ones_bf, vsq], [pq])
            k.ts("dve", mean[:, :], pm[:, 0:TT], 1.0 / 512, None, ALU.mult, None, [pm], [mean])
            k.tt("pool", m2[:, :], mean[:, :], mean[:, :], ALU.mult, [mean], [m2])
            k.stt("dve", m2[:, :], pq[:, 0:TT], 1.0 / 512, m2[:, :], ALU.mult, ALU.subtract, [pq, m2], [m2])
            k.act(rstd[:, :], m2[:, :], AF.Ln, [m2, C.eps_col], [rstd], bias=C.eps_col[:, 0:1])
            k.act(rstd[:, :], rstd[:, :], AF.Exp, [rstd], [rstd], scale=-0.5)
            bc = lambda a: a[:, :].rearrange("p (o t) -> p o t", o=1).to_broadcast([128, 4, TT])
            k.tt("dve", v_[:, :, :], v_[:, :, :], bc(mean), ALU.subtract, [v_, mean], [v_])
            k.tt("pool", v_[:, :, :], v_[:, :, :], bc(rstd), ALU.mult, [v_, rstd], [v_])
            for c in range(4):
                k.ts("dve", vn[:, c, :], v_[:, c, :], lng[:, c:c + 1], lnb[:, c:c + 1], ALU.mult, ALU.add,
                     [v_, lng, lnb], [vn])
            for half in range(2):
                for nn in range(2):
                    n = half * 2 + nn
                    for c in range(4):
                        k.tr(C.ps_bf[:, (nn * 4 + c) * 128:(nn * 4 + c + 1) * 128], vn[:, c, n * 128:(n + 1) * 128],
                             C.ident_bf[:, :], [vn, C.ident_bf], [C.ps_bf])
                k.copy("act", vtok[:, half * 2:half * 2 + 2, :, :],
                       C.ps_bf[:, 0:1024].rearrange("p (n c s) -> p n c s", n=2, c=4), [C.ps_bf], [vtok])
            co = cout[t % 2]
            for g in range(4):
                p = ps[pi % 6]; pi += 1
                for n in range(4):
                    k.mm(p[:, n * 128:(n + 1) * 128], vtok[:, n, g, :], wsT[:, g, :], True, True, [vtok, wsT], [p])
                for n in range(4):
                    pass
                k.tt("dve", t1[:, :, :], p[:, 0:512].rearrange("p (n t) -> p n t", n=4),
                     bsb[:, g, :].rearrange("p (o t) -> p o t", o=1).to_broadcast([128, 4, 128]), ALU.add,
                     [p, bsb], [t1])
                k.tt("pool", co[:, g, :], t1[:, :, :].rearrange("p n t -> p (n t)"), ub_[:, g, :], ALU.mult,
                     [t1, ub_], [co])
            k.dma(C.mixT[0:512, t0:t0 + TT].rearrange("(g p) t -> p g t", p=128), co[:, :, :],
                  [co], [C.mixT_r[t]], sem=co)
            do = dno[t % 2]
            for grp in ((0, 1, 2, 3), (4, 5, 6, 7), (8, 9, 10, 11)):
                for c in grp:
                    p = ps[pi % 6]; pi += 1
                    u, y = cu[c % 2], cy[c % 2]
                    proj_fm(k, p, w_in, xn, 1024 + c * 128, TT, 8, [w_in, xn])
                    k.copy("pool", u[:, 0:3], chist[:, c, :], [chist], [u])
                    k.copy("act", u[:, 3:3 + TT], p[:, 0:TT], [p], [u])
                    k.act(y[:, :], p[:, 0:TT], AF.Copy, [p, dcw], [y], scale=dcw[:, c, 3:4])
                    k.copy("pool", chist[:, c, :], u[:, TT:TT + 3], [u], [chist])
                    for kk in range(3):
                        k.stt("dve", y[:, :], u[:, kk:kk + TT], dcw[:, c, kk:kk + 1], y[:, :], ALU.mult, ALU.add,
                              [u, dcw, y], [y])
                    if c >= 8:
                        k.act(do[:, c, :], y[:, :], AF.Silu, [y], [do])
                    else:
                        k.act(dq4[:, c % 4, :], y[:, :], AF.Silu, [y], [dq4])
                if grp[0] >= 8:
                    continue
                k.act(dsq4[:, :, :], dq4[:, :, :], AF.Square, [dq4], [dsq4])
                for c in grp:
                    p2 = ps[pi % 6]; pi += 1
                    k.mm(p2[:, 0:TT], C.ones_bf[:, :], dsq4[:, c % 4, :], True, True, [C.ones_bf, dsq4], [p2])
                    k.act(rn4[:, c % 4, :], p2[:, 0:TT], AF.Ln, [p2, C.eps_col], [rn4], bias=C.eps_col[:, 0:1])
                k.act(rn4[:, :, :], rn4[:, :, :], AF.Exp, [rn4, qsc, zero_c], [rn4], scale=-0.5,
                      bias=(qsc if grp[0] < 4 else zero_c)[:, 0:1])
                k.tt("dve", do[:, grp[0]:grp[0] + 4, :], dq4[:, :, :], rn4[:, :, :], ALU.mult, [dq4, rn4], [do])
            k.dma(C.dnT[:, t0:t0 + TT].rearrange("(c p) t -> p c t", p=128), do[:, :, :], [do], [C.dnT_r[t]], sem=do)
            go, bo = gto[t % 2], bao[t % 2]
            for n in range(4):
                p = ps[pi % 6]; pi += 1
                for kc in range(8):
                    k.mm(p[:, 0:512], xn[:, kc, n * 128:(n + 1) * 128], w_in[:, kc, 2560:3072], kc == 0, kc == 7,
                         [xn, w_in], [p])
                k.act(go[:, n, :], p[:, 0:512], AF.Silu, [p], [go])
                p = ps[pi % 6]; pi += 1
                for kc in range(8):
                    k.mm(p[:, 0:8], xn[:, kc, n * 128:(n + 1) * 128], w_in[:, kc, 3072:3080], kc == 0, kc == 7,
                         [xn, w_in], [p])
                k.copy("dve", bo[:, n, :], p[:, 0:8], [p], [bo])
            k.dma(C.gtok[t0:t0 + TT, :].rearrange("(n p) c -> p n c", p=128), go[:, :, :], [go], [C.gtok_r[t]], sem=go)
            k.dma(C.batok[t0:t0 + TT, :].rearrange("(n p) c -> p n c", p=128), bo[:, :, :], [bo], [C.batok_r[t]], sem=bo)
    k.S.end_phase()


def phase_deltanet(k, C, S):
    NCH = S // 128
    with ExitStack() as ctx:
        tri = k.sb([128, 128], F32, "tri", ctx)
        k.dma(tri[:, :], C.tri_le, [], [tri])
        causT = k.sb([128, 128], F32, "causT", ctx)
        k.dma(causT[:, :], C.causT_neg, [], [causT])
        strT = k.sb([128, 128], F32, "strT", ctx)
        k.dma(strT[:, :], C.strictT01, [], [strT])
        onesf = k.sb([128, 128], F32, "onesf", ctx)
        k.memset("pool", onesf[:, :], 1.0, [onesf])
        one_c = k.sb([128, 1], F32, "one_c", ctx)
        k.memset("pool", one_c[:, :], 1.0, [one_c])
        dtb = k.sb([128, 4], F32, "dtb", ctx)
        k.dma(dtb[:, :], C.dn_dt_bias, [], [dtb])
        nexpA = k.sb([128, 4], F32, "nexpA", ctx)
        k.dma(nexpA[:, :], C.dn_a_log, [], [nexpA])
        k.act(nexpA[:, :], nexpA[:, :], AF.Exp, [nexpA], [nexpA])
        k.ts("dve", nexpA[:, :], nexpA[:, :], -1.0, None, ALU.mult, None, [nexpA], [nexpA])
        ngb = k.sb([128, 128], F32, "ngb", ctx)
        k.dma(ngb[:, :], C.dn_norm_g, [], [ngb])

        St = k.sb([128, 4, 128], F32, "St", ctx)
        Sb = k.sb([128, 4, 128], BF16, "Sb", ctx)
        k.memset("pool", St[:, :, :], 0.0, [St])
        k.memset("pool", Sb[:, :, :], 0.0, [Sb])
        lvl = k.sb([128, 8, 128], BF16, "lvl", ctx)
        k.dma(lvl[:, :, :], C.lvlmask, [], [lvl])
        NB_ = 4
        BETA, G, GC, GLAST, E, F_, GL, NGC, BE, TMP = range(10)

        def mk():
            b = Ctx()
            b.dn = k.sb([128, 12, 128], BF16, "dn", ctx)
            b.gt = k.sb([128, 512], BF16, "gt", ctx)
            b.ba = k.sb([128, 8], F32, "ba", ctx)
            b.sc = k.sb([128, 10, 4], F32, "sc", ctx)
            b.dg = [k.sb([128, 4, 128], F32, "dg", ctx) for _ in range(3)]
            b.kbT = k.sb([128, 4, 128], BF16, "kbT", ctx)
            b.qeT = k.sb([128, 4, 128], BF16, "qeT", ctx)
            b.X = k.sb([128, 4, 128], F32, "X", ctx)
            b.DcT = k.sb([128, 4, 128], F32, "DcT", ctx)
            b.DcsT = k.sb([128, 4, 128], F32, "DcsT", ctx)
            b.tok = k.sb([128, 8, 128], BF16, "tok", ctx)
            b.rhs0 = k.sb([128, 4, 256], BF16, "rhs0", ctx)
            b.y = k.sb([128, 4, 256], BF16, "y", ctx)
            b.kd = k.sb([128, 4, 128], BF16, "kd", ctx)
            b.qkT = k.sb([128, 4, 128], BF16, "qkT", ctx)
            b.A = k.sb([128, 4, 128], BF16, "A", ctx)
            b.AT = k.sb([128, 4, 128], BF16, "AT", ctx)
            b.Tm = [k.sb([128, 4, 128], BF16, "Tm", ctx) for _ in range(2)]
            b.TTm = [k.sb([128, 4, 128], BF16, "TTm", ctx) for _ in range(2)]
            b.Am = k.sb([128, 4, 128], BF16, "Am", ctx)
            b.AmT = [k.sb([128, 4, 128], BF16, "AmT", ctx) for _ in range(2)]
            b.Um = k.sb([128, 4, 128], BF16, "Um", ctx)
            b.wT = k.sb([128, 4, 128], BF16, "wT", ctx)
            b.vnew = k.sb([128, 4, 128], BF16, "vnew", ctx)
            b.osb = k.sb([128, 4, 128], F32, "osb", ctx)
            b.junk = k.sb([128, 128], F32, "junk", ctx)
            b.ss = k.sb([128, 4], F32, "ss", ctx)
            b.dtk = k.sb([128, 4, 128], BF16, "dtk", ctx)
            b.doT = k.sb([128, 4, 128], BF16, "doT", ctx)
            return b

        BUFS = [mk() for _ in range(NB_)]
        ps = C.ps + [C.ps_norm]
        st_ = {"pi": 0}

        def nxt():
            p = ps[st_["pi"] % 7]
            st_["pi"] += 1
            return p

        def load(n):
            b = BUFS[n % NB_]
            tl = (n * 128) // TT
            k.dma(b.dn[:, :, :], C.dnT[:, n * 128:(n + 1) * 128].rearrange("(c p) t -> p c t", p=128),
                  [C.dnT_r[tl]], [b.dn])
            k.dma(b.gt[:, :], C.gtok[n * 128:(n + 1) * 128, :], [C.gtok_r[tl]], [b.gt])
            k.dma(b.ba[:, :], C.batok[n * 128:(n + 1) * 128, :], [C.batok_r[tl]], [b.ba])

        ident4 = C.ident_f[:, :].rearrange("p (o t) -> p o t", o=1).to_broadcast([128, 4, 128])
        m4 = lambda a: a[:, :].rearrange("p (o t) -> p o t", o=1).to_broadcast([128, 4, 128])
        v4 = lambda p_, w=128: p_[:, 0:4 * w].rearrange("p (h t) -> p h t", h=4)
        lm4 = lambda l: lvl[:, l, :].rearrange("p (o t) -> p o t", o=1).to_broadcast([128, 4, 128])
        identb4 = C.ident_bf[:, :].rearrange("p (o t) -> p o t", o=1).to_broadcast([128, 4, 128])

        def chunk(n):
            b = BUFS[n % NB_]
            sc, d, g_, b_ = b.sc, b.dn, b.gt, b.ba

            def bcol(j):
                return sc[:, j, :].rearrange("p (h o) -> p h o", o=1).to_broadcast([128, 4, 128])
            k.act(sc[:, BETA, :], b_[:, 0:4], AF.Exp, [b_], [sc], scale=-1.0)
            k.ts("dve", sc[:, BETA, :], sc[:, BETA, :], 1.0, None, ALU.add, None, [sc], [sc])
            k.op("dve", lambda e: e.reciprocal(sc[:, BETA, :], sc[:, BETA, :]), [sc], [sc])
            k.tt("dve", sc[:, TMP, :], b_[:, 4:8], dtb[:, :], ALU.add, [b_, dtb], [sc])
            k.act(sc[:, TMP, :], sc[:, TMP, :], AF.Exp, [sc], [sc])
            k.act(sc[:, TMP, :], sc[:, TMP, :], AF.Ln, [sc, one_c], [sc], bias=one_c[:, 0:1])
            k.tt("dve", sc[:, G, :], sc[:, TMP, :], nexpA[:, :], ALU.mult, [sc, nexpA], [sc])
            p = nxt()
            k.mm(p[:, 0:4], tri[:, :], sc[:, G, :], True, True, [tri, sc], [p])
            k.mm(p[:, 4:8], onesf[:, :], sc[:, G, :], True, True, [onesf, sc], [p])
            k.copy("dve", sc[:, GC:GLAST + 1, :], p[:, 0:8].rearrange("p (a h) -> p a h", a=2), [p], [sc])
            yield
            k.act(sc[:, E, :], sc[:, GC, :], AF.Exp, [sc], [sc])
            k.tt("dve", sc[:, TMP, :], sc[:, GLAST, :], sc[:, GC, :], ALU.subtract, [sc], [sc])
            k.act(sc[:, F_, :], sc[:, TMP, :], AF.Exp, [sc], [sc])
            k.act(sc[:, GL, :], sc[:, GLAST, :], AF.Exp, [sc], [sc])
            k.ts("dve", sc[:, NGC, :], sc[:, GC, :], -1.0, None, ALU.mult, None, [sc], [sc])
            k.tt("dve", sc[:, BE, :], sc[:, BETA, :], sc[:, E, :], ALU.mult, [sc], [sc])
            for j in range(8):
                k.tr(C.ps_bf[:, j * 128:(j + 1) * 128], d[:, 4 + j, :], C.ident_bf[:, :], [d, C.ident_bf], [C.ps_bf])
            k.copy("act", b.tok[:, :, :], C.ps_bf[:, 0:1024].rearrange("p (j t) -> p j t", j=8), [C.ps_bf], [b.tok])
            yield
            pB, pE, pG = nxt(), nxt(), nxt()
            for dgi, (col, pp) in enumerate(((BETA, pB), (E, pE), (GC, pG))):
                k.tt("pool" if dgi == 1 else "dve", b.dg[dgi][:, :, :], ident4, bcol(col), ALU.mult,
                     [C.ident_f, sc], [b.dg[dgi]])
                k.mm(pp[:, 0:512], onesf[:, :], b.dg[dgi][:, :, :].rearrange("p h t -> p (h t)"), True, True,
                     [onesf, b.dg[dgi]], [pp])
            k.tt("dve", b.kbT[:, :, :], d[:, 4:8, :], v4(pB), ALU.mult, [d, pB], [b.kbT])
            k.tt("dve", b.qeT[:, :, :], d[:, 0:4, :], v4(pE), ALU.mult, [d, pE], [b.qeT])
            k.tt("dve", b.X[:, :, :], v4(pG), m4(causT), ALU.add, [pG, causT], [b.X])
            for h in range(4):
                k.ts("dve", b.rhs0[:, h, 0:128], b.tok[:, 4 + h, :], sc[:, BETA, h:h + 1], None, ALU.mult, None,
                     [b.tok, sc], [b.rhs0])
                k.act(b.rhs0[:, h, 128:256], b.tok[:, h, :], AF.Copy, [b.tok, sc], [b.rhs0], scale=sc[:, BE, h:h + 1])
                k.act(b.kd[:, h, :], b.tok[:, h, :], AF.Copy, [b.tok, sc], [b.kd], scale=sc[:, F_, h:h + 1])
            yield
            for h in range(4):
                k.act(b.DcT[:, h, :], b.X[:, h, :], AF.Exp, [b.X, sc], [b.DcT], bias=sc[:, NGC, h:h + 1])
            k.tt("pool", b.DcsT[:, :, :], b.DcT[:, :, :], m4(strT), ALU.mult, [b.DcT, strT], [b.DcsT])
            pQ, pK = nxt(), nxt()
            for h in range(4):
                k.mm(pQ[:, h * 128:(h + 1) * 128], d[:, 4 + h, :], d[:, h, :], True, True, [d], [pQ])
                k.mm(pK[:, h * 128:(h + 1) * 128], d[:, 4 + h, :], b.kbT[:, h, :], True, True, [d, b.kbT], [pK])
            k.tt("dve", b.qkT[:, :, :], v4(pQ), b.DcT[:, :, :], ALU.mult, [pQ, b.DcT], [b.qkT])
            k.tt("dve", b.AT[:, :, :], v4(pK), b.DcsT[:, :, :], ALU.mult, [pK, b.DcsT], [b.AT])
            yield
            for h in range(4):
                k.tr(C.ps_bf[:, h * 128:(h + 1) * 128], b.AT[:, h, :], C.ident_bf[:, :], [b.AT, C.ident_bf], [C.ps_bf])
            k.copy("act", b.A[:, :, :], C.ps_bf[:, 0:512].rearrange("p (h t) -> p h t", h=4), [C.ps_bf], [b.A])
            Tc, TTc = b.Tm[0], b.TTm[0]
            k.tt("pool", b.AmT[0][:, :, :], b.AT[:, :, :], lm4(0), ALU.mult, [b.AT, lvl], [b.AmT[0]])
            k.tt("dve", TTc[:, :, :], identb4, b.AmT[0][:, :, :], ALU.subtract, [C.ident_bf, b.AmT[0]], [TTc])
            k.tt("pool", b.Am[:, :, :], b.A[:, :, :], lm4(7), ALU.mult, [b.A, lvl], [b.Am])
            k.tt("dve", Tc[:, :, :], identb4, b.Am[:, :, :], ALU.subtract, [C.ident_bf, b.Am], [Tc])
            yield
            for l in range(1, 7):
                Tn, TTn = b.Tm[l % 2], b.TTm[l % 2]
                amt = b.AmT[l % 2]
                k.tt("pool", amt[:, :, :], b.AT[:, :, :], lm4(l), ALU.mult, [b.AT, lvl], [amt])
                pU = nxt()
                for h in range(4):
                    k.mm(pU[:, h * 128:(h + 1) * 128], amt[:, h, :], Tc[:, h, :], True, True, [amt, Tc], [pU])
                k.copy("act", b.Um[:, :, :], v4(pU), [pU], [b.Um])
                pVT = nxt()
                for h in range(4):
                    k.mm(pVT[:, h * 128:(h + 1) * 128], b.Um[:, h, :], TTc[:, h, :], True, True, [b.Um, TTc], [pVT])
                if l < 6:
                    pV = nxt()
                    for h in range(4):
                        k.mm(pV[:, h * 128:(h + 1) * 128], TTc[:, h, :], b.Um[:, h, :], True, True, [TTc, b.Um], [pV])
                k.tt("dve", TTn[:, :, :], TTc[:, :, :], v4(pVT), ALU.subtract, [TTc, pVT], [TTn])
                if l < 6:
                    k.tt("dve", Tn[:, :, :], Tc[:, :, :], v4(pV), ALU.subtract, [Tc, pV], [Tn])
                Tc, TTc = Tn, TTn
                yield
            pY0, pY1 = nxt(), nxt()
            for h in range(4):
                py = (pY0, pY1)[h // 2]
                k.mm(py[:, (h % 2) * 256:(h % 2 + 1) * 256], TTc[:, h, :], b.rhs0[:, h, :], True, True,
                     [TTc, b.rhs0], [py])
            ycur = b.y
            k.copy("act", ycur[:, 0:2, :], pY0[:, 0:512].rearrange("p (h t) -> p h t", h=2), [pY0], [ycur])
            k.copy("dve", ycur[:, 2:4, :], pY1[:, 0:512].rearrange("p (h t) -> p h t", h=2), [pY1], [ycur])
            for h in range(4):
                k.tr(C.ps_bf[:, h * 128:(h + 1) * 128], ycur[:, h, 128:256], C.ident_bf[:, :], [ycur, C.ident_bf],
                     [C.ps_bf])
            k.copy("act", b.wT[:, :, :], C.ps_bf[:, 0:512].rearrange("p (h t) -> p h t", h=4), [C.ps_bf], [b.wT])
            yield
            p1 = nxt()
            for h in range(4):
                k.mm(p1[:, h * 128:(h + 1) * 128], b.wT[:, h, :], Sb[:, h, :], True, True, [b.wT, Sb], [p1])
            k.tt("dve", b.vnew[:, :, :], ycur[:, :, 0:128], v4(p1), ALU.subtract, [ycur, p1], [b.vnew])
            p2, p3 = nxt(), nxt()
            for h in range(4):
                k.mm(p3[:, h * 128:(h + 1) * 128], b.kd[:, h, :], b.vnew[:, h, :], True, True, [b.kd, b.vnew], [p3])
            for h in range(4):
                k.mm(p2[:, h * 128:(h + 1) * 128], b.qeT[:, h, :], Sb[:, h, :], True, False, [b.qeT, Sb], [p2])
                k.mm(p2[:, h * 128:(h + 1) * 128], b.qkT[:, h, :], b.vnew[:, h, :], False, True, [b.qkT, b.vnew], [p2])
            for h in range(4):
                k.stt("dve", St[:, h, :], St[:, h, :], sc[:, GL, h:h + 1], p3[:, h * 128:(h + 1) * 128],
                      ALU.mult, ALU.add, [St, sc, p3], [St])
            k.copy("pool", Sb[:, :, :], St[:, :, :], [St], [Sb])
            k.copy("act", b.osb[:, :, :], v4(p2), [p2], [b.osb])
            yield
            k.memset("pool", b.ss[:, :], 0.0, [b.ss])
            for h in range(4):
                k.act(b.junk[:, :], b.osb[:, h, :], AF.Square, [b.osb], [b.junk, b.ss], accum_out=b.ss[:, h:h + 1])
            k.act(b.ss[:, :], b.ss[:, :], AF.Ln, [b.ss, C.eps_col], [b.ss], bias=C.eps_col[:, 0:1], scale=1.0 / 128)
            k.act(b.ss[:, :], b.ss[:, :], AF.Exp, [b.ss], [b.ss], scale=-0.5)
            for h in range(4):
                k.stt("dve", b.osb[:, h, :], b.osb[:, h, :], b.ss[:, h:h + 1], ngb[:, :], ALU.mult, ALU.mult,
                      [b.osb, b.ss, ngb], [b.osb])
            k.tt("pool", b.dtk[:, :, :], b.osb[:, :, :], g_[:, :].rearrange("p (h t) -> p h t", h=4), ALU.mult,
                 [b.osb, g_], [b.dtk])
            yield
            for h in range(4):
                k.tr(C.ps_bf[:, h * 128:(h + 1) * 128], b.dtk[:, h, :], C.ident_bf[:, :], [b.dtk, C.ident_bf], [C.ps_bf])
            k.copy("act", b.doT[:, :, :], C.ps_bf[:, 0:512].rearrange("p (h t) -> p h t", h=4), [C.ps_bf], [b.doT])
            k.dma(C.mixT[512:1024, n * 128:(n + 1) * 128].rearrange("(h p) t -> p h t", p=128), b.doT[:, :, :],
                  [b.doT], [C.mixT_r[(n * 128) // TT]], sem=b.doT)
            if n + NB_ < NCH:
                load(n + NB_)

        for n in range(min(NB_, NCH)):
            load(n)
        run_pipelined(chunk, NCH, 5, maxlive=NB_)
    k.S.end_phase()


SEQ = 8192
N_ACTIVE = 2


def kernel(**inputs):
    S = SEQ
    nc = build(S)
    consts = host_consts(S)
    in_maps = [core_inputs(inputs, b, S, consts) for b in range(N_ACTIVE)]
    res = run_bass_kernel_spmd(nc, in_maps, core_ids=list(range(N_ACTIVE)))
    out = np.stack([np.ascontiguousarray(res.results[b]["outT"].T) for b in range(N_ACTIVE)], axis=0)
    return out.astype(np.float32)
```

```python
import numpy as np
import ml_dtypes
from contextlib import ExitStack
import concourse.bass as bass
import concourse.mybir as mybir
from concourse.bass_utils import run_bass_kernel_spmd

F32 = mybir.dt.float32
BF16 = mybir.dt.bfloat16
AF = mybir.ActivationFunctionType
ALU = mybir.AluOpType
AX = mybir.AxisListType

D = 1024
NMEM = 256
EPS = 1e-6
DFF = 2816
NEG = -30000.0


class Res:
    __slots__ = ("name", "writers", "readers", "dsem")

    def __init__(self, name):
        self.name = name
        self.writers = []
        self.readers = []
        self.dsem = None


class Op:
    __slots__ = ("eng", "fn", "deps", "signal", "count", "dtok", "idx", "after", "dur", "gidx",
                 "nun", "rdy", "fin", "users")

    def __init__(self, eng, fn):
        self.eng = eng
        self.fn = fn
        self.deps = []
        self.signal = False
        self.count = 0
        self.dtok = None
        self.idx = 0
        self.after = []
        self.dur = 300.0


ENGS = ("pe", "act", "dve", "pool", "sp")


class Sched:
    def __init__(self, nc, n_dma_sems=40):
        self.nc = nc
        self.ops = {e: [] for e in ENGS}
        self.n_dma_sems = n_dma_sems
        self.dma_counts = [0] * n_dma_sems
        self.n_sp_sems = n_dma_sems - 8
        self.free_dsems = list(range(self.n_sp_sems))
        self.free_dsems_pool = list(range(self.n_sp_sems, n_dma_sems))
        self.phase_res = []
        self.all_res = []
        self.reorder_on = True
        self.seg_flags = []

    def res(self, name):
        r = Res(name)
        self.all_res.append(r)
        return r

    def _add(self, eng, fn, reads, writes, acc):
        o = Op(eng, fn)
        o.idx = len(self.ops[eng])
        deps = []
        for r in reads:
            deps.extend(r.writers)
        for w in writes:
            deps.extend(w.readers)
            if not acc:
                deps.extend(w.writers)
            elif acc == 'dma':
                deps.extend(t for t in w.writers if t[0] != 'D')
            else:
                for t in w.writers:
                    if t[0] == 'E' and t[1].eng == eng:
                        o.after.append(t[1])
                    else:
                        deps.append(t)
        o.deps = deps
        self.ops[eng].append(o)
        return o

    def _commit(self, tok, reads, writes):
        for r in reads:
            r.readers.append(tok)
        for w in writes:
            if w.readers:
                w.writers = [tok]
                w.readers = []
            else:
                w.writers.append(tok)
                if len(w.writers) > 64:
                    w.writers = w.writers[-64:]

    def op(self, eng, fn, reads=(), writes=(), acc=False):
        o = self._add(eng, fn, reads, writes, acc)
        self._commit(('E', o), reads, writes)
        return o

    def dma(self, eng, out, in_, reads=(), writes=(), sem_res=None):
        sr = sem_res if sem_res is not None else writes[0]
        if sr.dsem is None:
            sr.dsem = (self.free_dsems_pool if eng == "pool" else self.free_dsems).pop(0)
        o = self._add(eng, lambda e, out=out, in_=in_: e.dma_start(out=out, in_=in_), reads, writes, 'dma')
        self.dma_counts[sr.dsem] += 16
        o.dtok = (sr.dsem, self.dma_counts[sr.dsem])
        self._commit(('D', sr.dsem, o.dtok[1]), reads, writes)
        return o

    def end_phase(self):
        toks = []
        for e in ENGS:
            if self.ops[e]:
                last = self.ops[e][-1]
                if last.dtok is None:
                    toks.append(('E', last))
        for i in range(self.n_dma_sems):
            if self.dma_counts[i] > 0:
                toks.append(('D', i, self.dma_counts[i]))
        self.seg_flags.append(self.reorder_on)
        for e in ENGS:
            o = Op(e, None)
            o.idx = len(self.ops[e])
            o.deps = list(toks)
            self.ops[e].append(o)
        for r in self.all_res:
            r.dsem = None
            r.writers = []
            r.readers = []
        self.free_dsems = list(range(self.n_sp_sems))
        self.free_dsems_pool = list(range(self.n_sp_sems, self.n_dma_sems))

    def reorder(self, window=24):
        import heapq
        dma_of = {}
        for e in ENGS:
            for o in self.ops[e]:
                if o.dtok is not None:
                    dma_of[o.dtok] = o
        pos = {e: 0 for e in ENGS}
        new_ops = {e: [] for e in ENGS}
        segi = -1
        while any(pos[e] < len(self.ops[e]) for e in ENGS):
            segi += 1
            seg = {}
            for e in ENGS:
                lst = self.ops[e]
                i = pos[e]
                j = i
                while j < len(lst) and lst[j].fn is not None:
                    j += 1
                seg[e] = lst[i:j]
                pos[e] = j + 1 if j < len(lst) else j
                bar = lst[j] if j < len(lst) else None
                seg[e + "_bar"] = bar
            if segi < len(self.seg_flags) and not self.seg_flags[segi]:
                self._fix_barrier(seg, {e: seg[e] for e in ENGS})
                for e in ENGS:
                    new_ops[e].extend(seg[e])
                    if seg[e + "_bar"] is not None:
                        new_ops[e].append(seg[e + "_bar"])
                continue
            allops = [o for e in ENGS for o in seg[e]]
            inseg = set(id(o) for o in allops)
            for o in allops:
                o.users = []
                o.fin = None
            for o in allops:
                n = 0
                dl = []
                for t in o.deps:
                    d = t[1] if t[0] == 'E' else dma_of.get((t[1], t[2]))
                    if d is not None and id(d) in inseg and d.fn is not None:
                        dl.append(d)
                for d in o.after:
                    if id(d) in inseg:
                        dl.append(d)
                o.nun = len(dl)
                o.rdy = 0.0
                for d in dl:
                    d.users.append(o)
            free = {e: 0.0 for e in ENGS}
            pend = {e: list(seg[e]) for e in ENGS}
            head = {e: 0 for e in ENGS}
            issued = {e: [] for e in ENGS}
            remaining = len(allops)
            while remaining:
                best = None
                for e in ENGS:
                    lst = pend[e]
                    h = head[e]
                    while h < len(lst) and lst[h] is None:
                        h += 1
                    head[e] = h
                    if h >= len(lst):
                        continue
                    w = 1 if e == "sp" else window
                    cnt = 0
                    i = h
                    while i < len(lst) and cnt < w:
                        o = lst[i]
                        if o is not None:
                            cnt += 1
                            if o.nun == 0:
                                stt = o.rdy if o.rdy > free[e] else free[e]
                                if best is None or stt < best[0]:
                                    best = (stt, e, i)
                                if stt <= free[e]:
                                    break
                        i += 1
                if best is None:
                    raise RuntimeError("reorder: no schedulable op (cyclic deps?)")
                stt, e, i = best
                o = pend[e][i]
                pend[e][i] = None
                issued[e].append(o)
                remaining -= 1
                if o.dtok is not None:
                    free[e] = stt + 60.0
                    o.fin = stt + 2500.0 + o.dur
                else:
                    free[e] = stt + o.dur
                    o.fin = stt + o.dur + 150.0
                for u in o.users:
                    u.nun -= 1
                    if o in u.after and o not in [ (t[1] if t[0] == 'E' else None) for t in u.deps]:
                        r = stt
                    else:
                        r = o.fin
                    if r > u.rdy:
                        u.rdy = r
            self._fix_barrier(seg, issued)
            for e in ENGS:
                new_ops[e].extend(issued[e])
                if seg[e + "_bar"] is not None:
                    new_ops[e].append(seg[e + "_bar"])
        self.ops = new_ops

    @staticmethod
    def _fix_barrier(seg, order):
        etoks = []
        for e in ENGS:
            for o in reversed(order[e]):
                if o.fn is not None and o.dtok is None:
                    etoks.append(('E', o))
                    break
        for e in ENGS:
            bar = seg[e + "_bar"]
            if bar is not None:
                bar.deps = [t for t in bar.deps if t[0] == 'D'] + etoks

    def emit(self, esems, dsems):
        nc = self.nc
        if getattr(self, "do_reorder", True):
            self.reorder()
        for e in ENGS:
            for o in self.ops[e]:
                for t in o.deps:
                    if t[0] == 'E':
                        t[1].signal = True
        for e in ENGS:
            c = 0
            for o in self.ops[e]:
                if o.signal and o.fn is not None and o.dtok is None:
                    c += 1
                    o.count = c
                elif o.signal:
                    o.count = c
        sched = self

        def run(e_name, eng):
            seen = {}
            for o in sched.ops[e_name]:
                need = {}
                for t in o.deps:
                    if t[0] == 'E':
                        d = t[1]
                        if d.count == 0:
                            continue
                        key = ('E', d.eng)
                        val = d.count
                    else:
                        key = ('D', t[1])
                        val = t[2]
                    if need.get(key, 0) < val:
                        need[key] = val
                for key, val in need.items():
                    if seen.get(key, 0) >= val:
                        continue
                    seen[key] = val
                    sem = esems[key[1]] if key[0] == 'E' else dsems[key[1]]
                    eng.wait_ge(sem, val)
                if o.fn is None:
                    continue
                ins = o.fn(eng)
                if o.dtok is not None:
                    ins.then_inc(dsems[o.dtok[0]], 16)
                elif o.signal:
                    ins.then_inc(esems[e_name], 1)

        with nc.Block() as block:
            @block.tensor
            def _(eng):
                run("pe", eng)

            @block.scalar
            def _(eng):
                run("act", eng)

            @block.vector
            def _(eng):
                run("dve", eng)

            @block.gpsimd
            def _(eng):
                run("pool", eng)

            @block.sync
            def _(eng):
                run("sp", eng)


class Buf:
    def __init__(self, S, t, name):
        self.t = t
        self.r = S.res(name)

    def __getitem__(self, k):
        return self.t[k]


class K:
    def __init__(self, nc):
        self.nc = nc
        self.S = Sched(nc)
        self.stack = ExitStack()
        self.n = 0

    def sb(self, shape, dt, name=None, ctx=None):
        self.n += 1
        name = f"{name or 'sb'}_{self.n}"
        t = (ctx or self.stack).enter_context(self.nc.sbuf_tensor(name, list(shape), dt))
        return Buf(self.S, t, name)

    def psum(self, shape, dt=F32, name=None, ctx=None):
        self.n += 1
        name = f"{name or 'ps'}_{self.n}"
        t = (ctx or self.stack).enter_context(self.nc.psum_tensor(name, list(shape), dt))
        return Buf(self.S, t, name)

    def dram(self, shape, dt, name):
        t = self.nc.dram_tensor(name, list(shape), dt)
        b = Buf(self.S, t.ap(), name)
        return b

    def _rw(self, reads, writes):
        return [b.r for b in reads], [b.r for b in writes]

    def op(self, eng, fn, reads, writes, acc=False, dur=None):
        r, w = self._rw(reads, writes)
        o = self.S.op(eng, fn, r, w, acc)
        if dur is not None:
            o.dur = dur
        return o

    @staticmethod
    def _fsz(ap):
        n = 1
        for d in list(ap.shape)[1:]:
            n *= int(d)
        return n

    def dma(self, out, in_, reads, writes, eng="sp", sem=None):
        r, w = self._rw(reads, writes)
        return self.S.dma(eng, out, in_, r, w, sem.r if sem is not None else None)

    def mm(self, out, lhsT, rhs, start, stop, reads, writes):
        return self.op("pe", lambda e: e.matmul(out, lhsT, rhs, start=start, stop=stop), reads, writes, acc=True,
                       dur=70.0 + 0.45 * self._fsz(rhs) * (4 if rhs.dtype == F32 else 1))

    def tr(self, out, in_, ident, reads, writes):
        return self.op("pe", lambda e: e.transpose(out, in_, ident), reads, writes, acc=True, dur=130.0)

    def act(self, out, in_, func, reads, writes, bias=None, scale=None, accum_out=None, eng="act"):
        kw = {}
        if bias is not None:
            kw["bias"] = bias
        if scale is not None:
            kw["scale"] = scale
        if accum_out is not None:
            kw["accum_out"] = accum_out
        return self.op("act", lambda e: e.activation(out, in_, func, **kw), reads, writes,
                       dur=220.0 + 0.95 * self._fsz(out))

    def tt(self, eng, out, in0, in1, op, reads, writes):
        return self.op(eng, lambda e: e.tensor_tensor(out, in0, in1, op), reads, writes,
                       dur=(120.0 + 1.0 * self._fsz(out)) * (2.0 if eng == "pool" else 1.0))

    def ts(self, eng, out, in0, s1, s2, op0, op1, reads, writes):
        du = 120.0 + 0.9 * self._fsz(out)
        if op1 is None:
            return self.op(eng, lambda e: e.tensor_scalar(out, in0, s1, None, op0), reads, writes, dur=du)
        return self.op(eng, lambda e: e.tensor_scalar(out, in0, s1, s2, op0, op1), reads, writes, dur=du)

    def stt(self, eng, out, in0, scalar, in1, op0, op1, reads, writes):
        return self.op(eng, lambda e: e.scalar_tensor_tensor(out, in0, scalar, in1, op0, op1), reads, writes,
                       dur=120.0 + 1.4 * self._fsz(out))

    def copy(self, eng, out, in_, reads, writes):
        if eng == "act":
            return self.op("act", lambda e: e.copy(out, in_), reads, writes, dur=220.0 + 0.8 * self._fsz(out))
        return self.op(eng, lambda e: e.tensor_copy(out, in_), reads, writes,
                       dur=(120.0 + 0.7 * self._fsz(out)) * (2.0 if eng == "pool" else 1.0))

    def memset(self, eng, ap, val, writes):
        return self.op(eng, lambda e: e.memset(ap, val), [], writes)


TT = 512


class Ctx:
    pass


def load_w(k, ctx, src, kin, n, name, gain=None, stg=None, col0=0, w=None):
    kc_n = kin // 128
    if w is None:
        w = k.sb([128, kc_n, n], BF16, name, ctx)
    i = 0
    for kc in range(kc_n):
        for n0 in range(0, n, 2048):
            wd = min(2048, n - n0)
            s = stg[i % len(stg)]
            k.dma(s[:, 0:wd], src[kc * 128:(kc + 1) * 128, col0 + n0:col0 + n0 + wd], [], [s],
                  eng="sp")
            if gain is None:
                k.copy(("dve", "pool")[i % 2], w[:, kc, n0:n0 + wd], s[:, 0:wd], [s], [w])
            elif i % 2 == 0:
                k.ts("dve", w[:, kc, n0:n0 + wd], s[:, 0:wd], gain[:, kc:kc + 1], None, ALU.mult, None,
                     [s, gain], [w])
            else:
                k.act(w[:, kc, n0:n0 + wd], s[:, 0:wd], AF.Copy, [s, gain], [w], scale=gain[:, kc:kc + 1])
            i += 1
    return w


def rmsnorm(k, C, x, xn, T, kc_n=8, dim=D, nb=None, xoff=0):
    sq, r, ps = nb if nb is not None else (C.sq, C.rr, C.ps_norm)
    k.act(sq[:, 0:kc_n, 0:T], x[:, 0:kc_n, 0:T], AF.Square, [x], [sq])
    for kc in range(kc_n):
        k.mm(ps[:, 0:T], C.ones_bf[:, :], sq[:, kc, 0:T], kc == 0, kc == kc_n - 1, [sq, C.ones_bf], [ps])
    k.act(r[:, 0:T], ps[:, 0:T], AF.Ln, [ps, C.eps_col], [r], bias=C.eps_col[:, 0:1], scale=1.0 / dim)
    k.act(r[:, 0:T], r[:, 0:T], AF.Exp, [r], [r], scale=-0.5)
    k.tt("dve", xn[:, 0:kc_n, xoff:xoff + T], x[:, 0:kc_n, 0:T],
         r[:, 0:T].rearrange("p (o t) -> p o t", o=1).to_broadcast([128, kc_n, T]), ALU.mult, [x, r], [xn])


def run_pipelined(make_gen, n, lag, maxlive=2):
    gens = [make_gen(t) for t in range(n)]
    prog = [0] * n
    done = [False] * n
    first = 0
    while first < n:
        for t in range(first, n):
            if t > first and not (done[t - 1] or prog[t - 1] >= lag):
                break
            if t - maxlive >= 0 and not done[t - maxlive]:
                break
            if done[t]:
                continue
            try:
                next(gens[t])
                prog[t] += 1
            except StopIteration:
                done[t] = True
        while first < n and done[first]:
            first += 1


def proj_fm(k, ps, w, xn, m0, T, kc_n, reads, col=None):
    for kc in range(kc_n):
        k.mm(ps[:, 0:T], w[:, kc, m0:m0 + 128], xn[:, kc, 0:T], kc == 0, kc == kc_n - 1, reads, [ps])


def phase_l0_inproj(k, C, S):
    nc = k.nc
    NT = S // TT
    with ExitStack() as ctx:
        C.sq = k.sb([128, 8, TT], BF16, "sq", ctx)
        C.rr = k.sb([128, TT], F32, "rr", ctx)
        stg = [k.sb([128, 2048], F32, "stg", ctx) for _ in range(2)]
        gain = k.sb([128, 8], F32, "gain", ctx)
        k.dma(gain[:, :], C.norm_mix[0], [], [gain])
        w_in = load_w(k, ctx, C.ev_w_in, D, 2048, "w_in", gain=gain, stg=stg)
        pw_f = k.sb([128, 4, 128], F32, "pw_f", ctx)
        pw = k.sb([128, 4, 128], BF16, "pw", ctx)
        k.dma(pw_f[:, :, :], C.pool_w.rearrange("g c d -> c g d"), [], [pw_f])
        k.copy("dve", pw[:, :, :], pw_f[:, :, :], [pw_f], [pw])
        pscale = k.sb([128, 4], F32, "pscale", ctx)
        k.dma(pscale[:, :], C.pool_scale, [], [pscale])
        corr = k.sb([128, 4, 16], F32, "corr", ctx)
        k.dma(corr[:, :, :], C.pool_corr, [], [corr])

        xt = [k.sb([128, 8, TT], F32, "xt", ctx) for _ in range(2)]
        xn = k.sb([128, 8, TT], BF16, "xn", ctx)
        qk = [k.sb([128, 8, TT], BF16, "qk", ctx) for _ in range(2)]
        vt = [k.sb([128, 4, 8, 65], BF16, "vt", ctx) for _ in range(2)]
        for b in vt:
            k.memset("pool", b[:, :, :, :], 1.0, [b])
        pb = k.sb([128, 4, 16 + TT], F32, "pb", ctx)
        k.memset("pool", pb[:, :, :], 0.0, [pb])
        wa = k.sb([128, 16 + TT], F32, "wa", ctx)
        wb = k.sb([128, 16 + TT], F32, "wb", ctx)
        k.memset("pool", wa[:, :], 0.0, [wa])
        k.memset("pool", wb[:, :], 0.0, [wb])
        pooled = k.sb([128, 4, TT], BF16, "pooled", ctx)
        bout = [k.sb([128, 4, TT], BF16, "bout", ctx) for _ in range(2)]
        ksum = k.sb([128, 4, 2], F32, "ksum", ctx)
        ps = C.ps
        pi = 0

        def load_x(t):
            b = xt[t % 2]
            k.dma(b[:, :, :], C.xT[:, t * TT:(t + 1) * TT].rearrange("(c p) t -> p c t", p=128), [], [b])

        load_x(0)
        for t in range(NT):
            if t + 1 < NT:
                load_x(t + 1)
            x = xt[t % 2]
            t0 = t * TT
            rmsnorm(k, C, x, xn, TT)
            qkb = qk[t % 2]
            for m in range(8):
                p = ps[pi % 6]; pi += 1
                proj_fm(k, p, w_in, xn, m * 128, TT, 8, [w_in, xn])
                k.copy("act", qkb[:, m, :], p[:, 0:TT], [p], [qkb])
            k.op("dve", lambda e, qkb=qkb: e.tensor_reduce(
                ksum[:, :, :], qkb[:, 4:8, :].rearrange("p c (b t) -> p c b t", b=2), AX.X, ALU.add),
                [qkb], [ksum])
            k.ts("dve", C.kmean[:, :, 2 * t:2 * t + 2], ksum[:, :, :], 1.0 / 256, None, ALU.mult, None,
                 [ksum], [C.kmean])
            for c in range(4):
                for hp in range(2):
                    k.dma(C.qaugT[2 * c + hp, 0:64, t0:t0 + TT], qkb[hp * 64:(hp + 1) * 64, c, :],
                          [qkb], [C.qaugT_r[t]], sem=qkb)
            k.dma(C.kT[:, t0:t0 + TT].rearrange("(c p) t -> p c t", p=128), qkb[:, 4:8, :],
                  [qkb], [C.kT_r[t]], sem=qkb)
            vb = vt[t % 2]
            for sub in range(4):
                p = ps[pi % 6]; pi += 1
                for kc in range(8):
                    k.mm(p[:, 0:512], xn[:, kc, sub * 128:(sub + 1) * 128], w_in[:, kc, 1024:1536],
                         kc == 0, kc == 7, [w_in, xn], [p])
                k.copy("dve", vb[:, sub, :, 1:65], p[:, 0:512].rearrange("p (h d) -> p h d", h=8), [p], [vb])
            k.dma(C.vaug[t0:t0 + TT, :].rearrange("(s p) c -> p s c", p=128),
                  vb[:, :, :, :].rearrange("p s h d -> p s (h d)"), [vb], [C.vaug_r[t]], sem=vb)
            for g in range(4):
                p = ps[pi % 6]; pi += 1
                proj_fm(k, p, w_in, xn, 1536 + g * 128, TT, 8, [w_in, xn])
                k.copy("act", pb[:, g, 16:16 + TT], p[:, 0:TT], [p], [pb])
            W = 16 + TT
            for g in range(4):
                eng = ("dve", "pool")[g % 2]
                src = pb
                cur = pb[:, g, :]
                bufs = [wa, wb]
                for lvl in range(g + 1):
                    sh = 1 << lvl
                    dst = bufs[lvl % 2]
                    k.tt(eng, dst[:, sh:W], cur[:, sh:W], cur[:, 0:W - sh], ALU.add, [src], [dst])
                    src = dst
                    cur = dst[:, :]
                wnd = 2 << g
                if t == 0:
                    k.tt(eng, cur[:, 16:32], cur[:, 16:32], corr[:, g, :], ALU.mult, [src, corr], [src])
                k.stt("dve", pooled[:, g, :], cur[:, 16:W], 1.0 / wnd, pb[:, g, 16:W], ALU.mult, ALU.subtract,
                      [src, pb], [pooled])
            k.copy("pool", pb[:, :, 0:16], pb[:, :, TT:TT + 16], [pb], [pb])
            bo = bout[t % 2]
            for g in range(4):
                p = ps[pi % 6]; pi += 1
                k.mm(p[:, 0:TT], pw[:, g, :], pooled[:, g, :], True, True, [pw, pooled], [p])
                k.ts("dve", bo[:, g, :], p[:, 0:TT], pscale[:, g:g + 1], None, ALU.mult, None, [p, pscale], [bo])
            k.dma(C.mixT[512:1024, t0:t0 + TT].rearrange("(g p) t -> p g t", p=128), bo[:, :, :],
                  [bo], [C.mixT_r[t]], sem=bo)
    k.S.end_phase()


class RB:
    def __init__(self, S, name):
        self.r = S.res(name)


def host_consts(S):
    c = {}
    c["ident_bf"] = np.eye(128, dtype=np.float32).astype(ml_dtypes.bfloat16)
    c["ident_f"] = np.eye(128, dtype=np.float32)
    corr = np.ones((4, 16), np.float32)
    for g, w in enumerate((2, 4, 8, 16)):
        for t in range(16):
            corr[g, t] = w / min(t + 1, w)
    c["pool_corr"] = np.ascontiguousarray(np.broadcast_to(corr[None], (128, 4, 16)))
    vb = np.concatenate([np.zeros(32, np.float32), np.full(32, -1e30, np.float32)])
    c["validbias"] = np.ascontiguousarray(np.broadcast_to(vb[None], (128, 64)))
    khot = np.zeros((32, S), np.float32)
    for j in range(S // 256):
        khot[j, j * 256:(j + 1) * 256] = 30000.0
    c["khot"] = khot.astype(ml_dtypes.bfloat16)
    kk = np.arange(128)[:, None, None]
    jj = np.arange(2)[None, :, None]
    qq = np.arange(256)[None, None, :]
    c["causal01"] = ((jj * 128 + kk) <= qq).astype(np.float32).astype(ml_dtypes.bfloat16)
    a = np.arange(128)
    c["tril01"] = (a[None, :] <= a[:, None]).astype(np.float32)
    c["tri_le"] = (a[:, None] <= a[None, :]).astype(np.float32)
    c["causT_neg"] = np.where(a[None, :] >= a[:, None], 0.0, -1e4).astype(np.float32)
    c["strictT01"] = (a[None, :] > a[:, None]).astype(np.float32)
    lv = np.zeros((128, 8, 128), np.float32)
    ii, jj2 = a[:, None], a[None, :]
    for l in range(7):
        b = 1 << l
        m = ((ii // (2 * b)) == (jj2 // (2 * b))) & ((ii % (2 * b)) >= b) & ((jj2 % (2 * b)) < b)
        lv[:, l, :] = m.T
        if l == 0:
            lv[:, 7, :] = m
    c["lvlmask"] = lv.astype(ml_dtypes.bfloat16)
    return c


def build(S, phases=None, debug_out=()):
    nc = bass.Bass("TRN2", target_bir_lowering=False)
    k = K(nc)
    C = Ctx()
    NT = S // TT

    def ext(name, shape, dt=F32):
        return Buf(k.S, nc.dram_tensor(name, list(shape), dt, kind="ExternalInput").ap(), name)

    def scratch(name, shape, dt):
        kind = "ExternalOutput" if name in debug_out else "Internal"
        return nc.dram_tensor(name, list(shape), dt, kind=kind).ap()

    C.xT = ext("xT", [D, S]).t
    C.norm_mix = [ext(f"norm_mix{l}", [128, 8]).t for l in range(2)]
    C.ev_w_in = ext("ev_w_in", [D, 2048]).t
    C.pool_w = ext("pool_w", [4, 128, 128]).t
    C.pool_scale = ext("pool_scale", [128, 4]).t
    C.pool_corr = ext("pool_corr", [128, 4, 16]).t
    C.validbias = ext("validbias", [128, 64]).t
    C.khot = ext("khot", [32, S], BF16).t
    C.causal01 = ext("causal01", [128, 2, 256], BF16).t
    C.memT = ext("memT", [D, NMEM]).t
    C.mem_norm = ext("mem_norm", [128, 8]).t
    C.norm_xattn = [ext(f"norm_xattn{l}", [128, 8]).t for l in range(2)]
    C.norm_ffn = [ext(f"norm_ffn{l}", [128, 8]).t for l in range(2)]
    C.final_norm = ext("final_norm", [128, 8]).t
    C.ev_w_out = ext("ev_w_out", [D, D]).t
    C.od_w_out = ext("od_w_out", [D, D]).t
    C.xattn_wq = [ext(f"xattn_wq{l}", [D, D]).t for l in range(2)]
    C.xattn_wkv = [ext(f"xattn_wkv{l}", [D, 2 * D]).t for l in range(2)]
    C.xattn_wo = [ext(f"xattn_wo{l}", [D, D]).t for l in range(2)]
    C.ffn_w_up = [ext(f"ffn_w_up{l}", [D, 2 * DFF]).t for l in range(2)]
    C.ffn_conv = [ext(f"ffn_conv{l}", [128, 2 * DFF // 128, 3]).t for l in range(2)]
    C.ffn_w_down = [ext(f"ffn_w_down{l}", [DFF, D]).t for l in range(2)]
    C.od_w_in = ext("od_w_in", [D, 3080]).t
    C.sgu_w = ext("sgu_w", [4, 128, 128]).t
    C.sgu_ln_g = ext("sgu_ln_g", [128, 4]).t
    C.sgu_ln_b = ext("sgu_ln_b", [128, 4]).t
    C.sgu_b = ext("sgu_b", [128, 4, 128]).t
    C.dn_conv = ext("dn_conv", [128, 12, 4]).t
    C.dn_a_log = ext("dn_a_log", [128, 4]).t
    C.dn_dt_bias = ext("dn_dt_bias", [128, 4]).t
    C.dn_norm_g = ext("dn_norm_g", [128, 128]).t
    C.tril01 = ext("tril01", [128, 128]).t
    C.tri_le = ext("tri_le", [128, 128]).t
    C.causT_neg = ext("causT_neg", [128, 128]).t
    C.strictT01 = ext("strictT01", [128, 128]).t
    C.lvlmask = ext("lvlmask", [128, 8, 128], BF16).t
    ident_bf_d = ext("ident_bf", [128, 128], BF16).t
    ident_f_d = ext("ident_f", [128, 128]).t
    C.qaugT = scratch("qaugT", [8, 96, S], BF16)
    C.kT = scratch("kT", [512, S], BF16)
    C.vaug = scratch("vaug", [S, 520], BF16)
    C.mixT = scratch("mixT", [D, S], BF16)
    C.dnT = scratch("dnT", [1536, S], BF16)
    C.gtok = scratch("gtok", [S, 512], BF16)
    C.batok = scratch("batok", [S, 8], F32)
    C.hA = scratch("hA", [D, S], F32)
    C.hB = scratch("hB", [D, S], F32)
    C.outT = nc.dram_tensor("outT", [D, S], F32, kind="ExternalOutput").ap()
    for nm in ("qaugT", "kT", "vaug", "mixT", "hA", "hB", "xT", "outT", "dnT", "gtok", "batok"):
        setattr(C, nm + "_r", [RB(k.S, f"{nm}{t}") for t in range(NT)])

    with k.stack:
        C.ps = [k.psum([128, 512], F32, "ps") for _ in range(6)]
        C.ps_norm = k.psum([128, 512], F32, "psn")
        C.ps_bf = k.psum([128, 1024], BF16, "psbf")
        C.ones_bf = k.sb([128, 128], BF16, "ones")
        k.memset("pool", C.ones_bf[:, :], 1.0, [C.ones_bf])
        C.ident_bf = k.sb([128, 128], BF16, "identb")
        k.dma(C.ident_bf[:, :], ident_bf_d, [], [C.ident_bf])
        C.ident_f = k.sb([128, 128], F32, "identf")
        k.dma(C.ident_f[:, :], ident_f_d, [], [C.ident_f])
        C.eps_col = k.sb([128, 1], F32, "epsc")
        k.memset("pool", C.eps_col[:, :], EPS, [C.eps_col])
        C.kmean = k.sb([128, 4, 32], F32, "kmean")
        k.memset("pool", C.kmean[:, :, :], 0.0, [C.kmean])
        k.S.end_phase()

        if phases is None or 1 in phases:
            phase_l0_inproj(k, C, S)
        if phases is None or 2 in phases:
            phase_moba_gate(k, C, S)
        if phases is None or 22 in phases:
            phase_moba_attn(k, C, S)
        if phases is None or 3 in phases:
            phase_outproj_xattn(k, C, S, 0, C.ev_w_out, C.xT, C.xT_r, C.hA, C.hA_r)
        if phases is None or 4 in phases:
            phase_ffn(k, C, S, 0, C.hA, C.hA_r, C.hB, C.hB_r, False)
        if phases is None or 5 in phases:
            phase_l1_inproj(k, C, S)
        if phases is None or 6 in phases:
            phase_deltanet(k, C, S)
        if phases is None or 7 in phases:
            phase_outproj_xattn(k, C, S, 1, C.od_w_out, C.hB, C.hB_r, C.hA, C.hA_r)
        if phases is None or 8 in phases:
            phase_ffn(k, C, S, 1, C.hA, C.hA_r, C.outT, C.outT_r, True)

        esems = {}
        for e in ENGS:
            esems[e] = k.stack.enter_context(nc.semaphore(f"es_{e}"))
        dsems = [k.stack.enter_context(nc.semaphore(f"ds_{i}")) for i in range(k.S.n_dma_sems)]
        k.S.emit(esems, dsems)
    return nc


def phase_moba_gate(k, C, S):
    GS = 9
    NT = S // TT
    with ExitStack() as ctx:
        kmb = k.sb([128, 4, 64], BF16, "kmb", ctx)
        k.memset("pool", kmb[:, :, :], 0.0, [kmb])
        k.copy("dve", kmb[0:64, :, 0:32], C.kmean[0:64, :, :], [C.kmean], [kmb])
        k.copy("dve", kmb[64:128, :, 32:64], C.kmean[64:128, :, :], [C.kmean], [kmb])
        vbias = k.sb([128, 64], F32, "vbias", ctx)
        k.dma(vbias[:, :], C.validbias, [], [vbias])
        qc = [k.sb([128, 4, TT], BF16, "qc", ctx) for _ in range(2)]
        gsb = k.sb([128, 8, 32], F32, "gsb", ctx)
        top8 = k.sb([128, 8, 8], F32, "top8", ctx)
        sel = k.sb([128, 8, 32], F32, "sel", ctx)
        mb = k.sb([128, 8, 96], BF16, "mb", ctx)
        k.memset("pool", mb[:, :, :], 0.0, [mb])
        mrow = [k.sb([128, 8, TT], BF16, "mrow", ctx) for _ in range(2)]
        ps = C.ps

        def load_q(t):
            b = qc[t % 2]
            for h in range(8):
                k.dma(b[(h % 2) * 64:(h % 2) * 64 + 64, h // 2, :], C.qaugT[h, 0:64, t * TT:(t + 1) * TT],
                      [C.qaugT_r[t]], [b])

        load_q(0)
        for t in range(NT):
            if t + 1 < NT:
                load_q(t + 1)
            q = qc[t % 2]
            mr = mrow[t % 2]
            for sub in range(4):
                qb = (t * TT + sub * 128) // 256
                gp = ps[sub % 2]
                for c in range(4):
                    k.mm(gp[:, c * 64:(c + 1) * 64], q[:, c, sub * 128:(sub + 1) * 128],
                         kmb[:, c, :], True, True, [q, kmb], [gp])
                k.tt("dve", gsb[:, :, :], gp[:, 0:256].rearrange("p (h n) -> p h n", h=8),
                     vbias[:, 32 - qb:64 - qb].rearrange("p (o n) -> p o n", o=1).to_broadcast([128, 8, 32]),
                     ALU.add, [gp, vbias], [gsb])
                if GS <= 1:
                    continue
                for h in range(8):
                    k.op("dve", lambda e, h=h: e.max(top8[:, h, :], gsb[:, h, :]), [gsb], [top8], acc=True)
                k.tt("dve", sel[:, :, :], gsb[:, :, :], top8[:, :, 2:3].to_broadcast([128, 8, 32]), ALU.is_ge,
                     [gsb, top8], [sel])
                k.ts("dve", mb[:, :, 64:96], sel[:, :, :], -1.0, None, ALU.add, None, [sel], [mb])
                k.memset("dve", mb[:, :, 64 + qb:65 + qb], 0.0, [mb])
                if GS <= 2:
                    continue
                for half in range(2):
                    mp = ps[2 + (sub * 2 + half) % 4]
                    for hh in range(4):
                        h = half * 4 + hh
                        k.mm(mp[0:96, hh * 128:(hh + 1) * 128], mb[:, h, :], C.ident_bf[:, :], True, True,
                             [mb, C.ident_bf], [mp])
                    k.copy("act", mr[64:96, half * 4:half * 4 + 4, sub * 128:(sub + 1) * 128],
                           mp[64:96, 0:512].rearrange("p (h q) -> p h q", h=4), [mp], [mr])
            if GS <= 3:
                continue
            k.dma(C.qaugT[:, 64:96, t * TT:(t + 1) * TT].rearrange("h r t -> r h t"), mr[64:96, :, :],
                  [mr], [C.qaugT_r[t]], sem=mr)
    k.S.end_phase()


def phase_moba_attn(k, C, S):
    NB = S // 256
    NKT = S // 128
    with ExitStack() as ctx:
        vsb = k.sb([128, NKT, 520], BF16, "vsb", ctx)
        for i in range(0, NKT, 8):
            n = min(8, NKT - i)
            k.dma(vsb[:, i:i + n, :], C.vaug[i * 128:(i + n) * 128, :].rearrange("(t p) c -> p t c", p=128),
                  [C.vaug_r[(i * 128) // TT], C.vaug_r[min(S // TT - 1, ((i + n) * 128 - 1) // TT)]], [vsb])
        kaug = [k.sb([96, S], BF16, "kaug", ctx) for _ in range(2)]
        for b in kaug:
            k.dma(b[64:96, :], C.khot, [], [b])
        caus = k.sb([128, 2, 256], BF16, "caus", ctx)
        k.dma(caus[:, :, :], C.causal01, [], [caus])
        onesf = k.sb([128, 65], F32, "onesf", ctx)
        k.memset("pool", onesf[:, :], 1.0, [onesf])
        pt = [k.sb([128, 2, 256], BF16, "pt", ctx) for _ in range(4)]
        rden = [k.sb([1, 256], F32, "rden", ctx) for _ in range(2)]
        bcs = k.sb([65, 256], F32, "bcs", ctx)
        ao = [k.sb([65, 512], BF16, "ao", ctx) for _ in range(2)]
        sps = C.ps[0:3]
        ops = [C.ps[3], C.ps[4], C.ps_norm]
        bps = C.ps[5]
        all_k = [C.kT_r[t] for t in range(S // TT)]
        NQ = 4
        qa = [k.sb([96, 256], BF16, "qa", ctx) for _ in range(NQ)]
        groups = [(h, qb) for h in range(8) for qb in range(NB)]
        tasks = []
        for gi, (h, qb) in enumerate(groups):
            for pr in range(qb + 1):
                tasks.append((gi, h, qb, pr))

        def load_q(gi):
            h, qb = groups[gi]
            b = qa[gi % NQ]
            k.dma(b[:, :], C.qaugT[h, :, qb * 256:(qb + 1) * 256], [C.qaugT_r[(qb * 256) // TT]], [b])

        def load_k(h):
            kb = kaug[h % 2]
            k.dma(kb[0:64, :], C.kT[h * 64:(h + 1) * 64, :], all_k, [kb])

        def emit_S(i):
            gi, h, qb, pr = tasks[i]
            if pr == 0:
                if qb == 0 and h + 1 < 8:
                    load_k(h + 1)
                if gi + NQ - 1 < len(groups):
                    load_q(gi + NQ - 1)
            kb = kaug[h % 2]
            q = qa[gi % NQ]
            sp = sps[i % 3]
            p = pt[i % 4]
            for j in range(2):
                kt = 2 * pr + j
                k.mm(sp[:, j * 256:(j + 1) * 256], kb[0:96, kt * 128:(kt + 1) * 128], q[0:96, :],
                     True, True, [kb, q], [sp])
            k.act(p[:, :, :], sp[:, 0:512].rearrange("p (j q) -> p j q", j=2), AF.Exp, [sp], [p], scale=0.125)
            if pr == qb:
                k.tt("pool", p[:, :, :], p[:, :, :], caus[:, :, :], ALU.mult, [p, caus], [p])

        deferred = []

        def emit_PV(i):
            gi, h, qb, pr = tasks[i]
            p = pt[i % 4]
            op_ = ops[gi % 3]
            for j in range(2):
                kt = 2 * pr + j
                k.mm(op_[0:65, 0:256], vsb[:, kt, h * 65:(h + 1) * 65], p[:, j, :],
                     pr == 0 and j == 0, pr == qb and j == 1, [vsb, p], [op_])
            if pr == qb:
                rd = rden[gi % 2]
                k.op("dve", lambda e, op_=op_, rd=rd: e.reciprocal(rd[0:1, :], op_[0:1, 0:256]), [op_], [rd])

                def tail(gi=gi, h=h, qb=qb, op_=op_, rd=rd):
                    k.mm(bps[0:65, 0:256], onesf[0:1, 0:65], rd[0:1, :], True, True, [onesf, rd], [bps])
                    k.copy("act", bcs[:, :], bps[0:65, 0:256], [bps], [bcs])
                    a = ao[(qb // 2) % 2]
                    k.tt("dve", a[:, (qb % 2) * 256:(qb % 2 + 1) * 256], op_[0:65, 0:256], bcs[:, :], ALU.mult,
                         [op_, bcs], [a])
                    if qb % 2 == 1:
                        t = qb // 2
                        k.dma(C.mixT[h * 64:(h + 1) * 64, t * TT:(t + 1) * TT], a[1:65, :], [a], [C.mixT_r[t]], sem=a)
                deferred.append((i + 2, tail))

        load_k(0)
        for gi in range(min(NQ - 1, len(groups))):
            load_q(gi)
        n = len(tasks)
        emit_S(0)
        if n > 1:
            emit_S(1)
        for i in range(n):
            while deferred and deferred[0][0] <= i:
                deferred.pop(0)[1]()
            if i + 2 < n:
                emit_S(i + 2)
            emit_PV(i)
        while deferred:
            deferred.pop(0)[1]()
    k.S.end_phase()


def phase_outproj_xattn(k, C, S, l, w_out_d, hin, hin_r, hout, hout_r):
    NT = S // TT
    k.S.reorder_on = False
    with ExitStack() as ctx:
        C.sq = k.sb([128, 8, TT], BF16, "sq", ctx)
        C.rr = k.sb([128, TT], F32, "rr", ctx)
        kmemT = k.sb([128, 8, NMEM], BF16, "kmemT", ctx)
        vmem = k.sb([128, 2, D], BF16, "vmem", ctx)
        with ExitStack() as c2:
            stg = [k.sb([128, 2048], F32, "stg", c2) for _ in range(4)]
            gm = k.sb([128, 8], F32, "gm", c2)
            k.dma(gm[:, :], C.mem_norm, [], [gm])
            wkv = load_w(k, c2, C.xattn_wkv[l], D, 2048, "wkv", gain=gm, stg=stg)
            mt_ = k.sb([128, 8, NMEM], F32, "memt", c2)
            k.dma(mt_[:, :, :], C.memT.rearrange("(c p) m -> p c m", p=128), [], [mt_])
            memn = k.sb([128, 8, NMEM], BF16, "memn", c2)
            rmsnorm(k, C, mt_, memn, NMEM)
            for dc in range(8):
                p = C.ps[dc % 6]
                proj_fm(k, p, wkv, memn, dc * 128, NMEM, 8, [wkv, memn])
                k.copy("act", kmemT[:, dc, :], p[:, 0:NMEM], [p], [kmemT])
            for mt in range(2):
                for half in range(2):
                    p = C.ps[(mt * 2 + half) % 6]
                    for kc in range(8):
                        k.mm(p[:, 0:512], memn[:, kc, mt * 128:(mt + 1) * 128],
                             wkv[:, kc, 1024 + half * 512:1536 + half * 512], kc == 0, kc == 7, [memn, wkv], [p])
                    k.copy("dve", vmem[:, mt, half * 512:(half + 1) * 512], p[:, 0:512], [p], [vmem])
        k.S.end_phase()
        stg = [k.sb([128, 2048], F32, "stg", ctx) for _ in range(2)]
        gq = k.sb([128, 8], F32, "gq", ctx)
        k.dma(gq[:, :], C.norm_xattn[l], [], [gq])
        w_out = load_w(k, ctx, w_out_d, D, D, "w_out", stg=stg)
        wq = load_w(k, ctx, C.xattn_wq[l], D, D, "wq", gain=gq, stg=stg)
        wo = load_w(k, ctx, C.xattn_wo[l], D, D, "wo", stg=stg)
        T3 = 256
        NB3 = 3
        NT3 = S // T3
        xt = [k.sb([128, 8, T3], F32, "xt", ctx) for _ in range(NB3)]
        mx = [k.sb([128, 8, T3], BF16, "mx", ctx) for _ in range(NB3)]
        xn = [k.sb([128, 8, T3], BF16, "xn", ctx) for _ in range(NB3)]
        qx = [k.sb([128, 8, T3], BF16, "qx", ctx) for _ in range(NB3)]
        ox = [k.sb([128, 8, T3], BF16, "ox", ctx) for _ in range(NB3)]
        pt = [[k.sb([128, 2, T3], BF16, "pt", ctx) for _ in range(2)] for _ in range(NB3)]
        rec = [[k.sb([128, T3], F32, "rec", ctx) for _ in range(2)] for _ in range(NB3)]
        rrb = [k.sb([128, T3], F32, "rr3", ctx) for _ in range(NB3)]
        ps = C.ps
        st = {"pi": 0}

        def nps():
            p = ps[st["pi"] % 6]
            st["pi"] += 1
            return p

        def load(t):
            b = t % NB3
            tr = (t * T3) // TT
            k.dma(xt[b][:, :, :], hin[:, t * T3:(t + 1) * T3].rearrange("(c p) t -> p c t", p=128),
                  [hin_r[tr]], [xt[b]])
            k.dma(mx[b][:, :, :], C.mixT[:, t * T3:(t + 1) * T3].rearrange("(c p) t -> p c t", p=128),
                  [C.mixT_r[tr]], [mx[b]])

        def tile(t):
            b = t % NB3
            x, m_, xn_, qx_, ox_ = xt[b], mx[b], xn[b], qx[b], ox[b]
            for m in range(8):
                p = nps()
                proj_fm(k, p, w_out, m_, m * 128, T3, 8, [w_out, m_])
                k.tt("dve", x[:, m, :], x[:, m, :], p[:, 0:T3], ALU.add, [x, p], [x])
            yield
            rmsnorm(k, C, x, xn_, T3, nb=(C.sq, rrb[b], C.ps_norm))
            yield
            for m in range(8):
                p = nps()
                proj_fm(k, p, wq, xn_, m * 128, T3, 8, [wq, xn_])
                k.copy("act", qx_[:, m, :], p[:, 0:T3], [p], [qx_])
                if m == 3:
                    yield
            yield
            for hd in range(4):
                ptb = pt[b][hd % 2]
                rc = rec[b][hd % 2]
                for mt in range(2):
                    p = nps()
                    for dd in range(2):
                        dc = 2 * hd + dd
                        k.mm(p[:, 0:T3], kmemT[:, dc, mt * 128:(mt + 1) * 128], qx_[:, dc, :], dd == 0, dd == 1,
                             [kmemT, qx_], [p])
                    k.act(ptb[:, mt, :], p[:, 0:T3], AF.Exp, [p], [ptb], scale=1.0 / 16)
                p = nps()
                for mt in range(2):
                    k.mm(p[:, 0:T3], C.ones_bf[:, :], ptb[:, mt, :], mt == 0, mt == 1, [C.ones_bf, ptb], [p])
                k.op("dve", lambda e, rc=rc, p=p: e.reciprocal(rc[:, :], p[:, 0:T3]), [p], [rc])
                for dd in range(2):
                    p = nps()
                    c0 = hd * 256 + dd * 128
                    for mt in range(2):
                        k.mm(p[:, 0:T3], vmem[:, mt, c0:c0 + 128], ptb[:, mt, :], mt == 0, mt == 1, [vmem, ptb], [p])
                    k.tt("dve", ox_[:, 2 * hd + dd, :], p[:, 0:T3], rc[:, :], ALU.mult, [p, rc], [ox_])
                yield
            for m in range(8):
                p = nps()
                proj_fm(k, p, wo, ox_, m * 128, T3, 8, [wo, ox_])
                k.tt("dve", x[:, m, :], x[:, m, :], p[:, 0:T3], ALU.add, [x, p], [x])
                if m == 3:
                    yield
            k.dma(hout[:, t * T3:(t + 1) * T3].rearrange("(c p) t -> p c t", p=128), x[:, :, :],
                  [x], [hout_r[(t * T3) // TT]], sem=x)
            if t + NB3 < NT3:
                load(t + NB3)

        for t in range(min(NB3, NT3)):
            load(t)
        run_pipelined(tile, NT3, 5, maxlive=NB3)
    k.S.end_phase()
    k.S.reorder_on = True


TF = 256


def phase_ffn(k, C, S, l, hin, hin_r, hout, hout_r, final):
    k.S.reorder_on = False
    NTF = S // TF
    NJ = DFF // 128
    with ExitStack() as ctx:
        gf = k.sb([128, 8], F32, "gf", ctx)
        k.dma(gf[:, :], C.norm_ffn[l], [], [gf])
        w_up = k.sb([128, 8, 2 * DFF], BF16, "w_up", ctx)
        w_dn = k.sb([128, NJ, D], BF16, "w_dn", ctx)
        with ExitStack() as c2:
            stg = [k.sb([128, 2048], F32, "stg", c2) for _ in range(4)]
            load_w(k, ctx, C.ffn_w_up[l], D, 2 * DFF, "w_up", gain=gf, stg=stg, w=w_up)
            load_w(k, ctx, C.ffn_w_down[l], DFF, D, "w_dn", stg=stg, w=w_dn)
        k.S.end_phase()
        cw = k.sb([128, 2 * NJ, 3], F32, "cw", ctx)
        k.dma(cw[:, :, :], C.ffn_conv[l], [], [cw])
        gfin = k.sb([128, 8], F32, "gfin", ctx)
        k.dma(gfin[:, :], C.final_norm, [], [gfin])
        xt = [k.sb([128, 8, TF], F32, "xt", ctx) for _ in range(2)]
        xn = [k.sb([128, 8, 2 + TF], BF16, "xn", ctx) for _ in range(2)]
        for b in xn:
            k.memset("pool", b[:, :, :], 0.0, [b])
        sqb = [k.sb([128, 8, TF], BF16, "sq", ctx)] * 2
        rrb = [k.sb([128, TF], F32, "rr", ctx) for _ in range(2)]
        actb = [[k.sb([128, TF], BF16, "actb", ctx) for _ in range(NJ)] for _ in range(2)]
        yb = [[k.sb([128, TF], F32, "yb", ctx) for _ in range(4)] for _ in range(2)]
        sg = [[k.sb([128, TF], F32, "sg", ctx) for _ in range(2)] for _ in range(2)]
        ps = C.ps
        st = {"pi": 0}
        W2 = 2 + TF

        def nps():
            p = ps[st["pi"] % 6]
            st["pi"] += 1
            return p

        def load(t):
            k.dma(xt[t % 2][:, :, :], hin[:, t * TF:(t + 1) * TF].rearrange("(c p) t -> p c t", p=128),
                  [hin_r[(t * TF) // TT]], [xt[t % 2]])

        def tile(t):
            x, xn_, ab = xt[t % 2], xn[t % 2], actb[t % 2]
            nb = (sqb[t % 2], rrb[t % 2], C.ps_norm)
            if t > 0:
                k.copy("pool", xn_[:, :, 0:2], xn[(t - 1) % 2][:, :, TF:TF + 2], [xn[(t - 1) % 2]], [xn_])
            rmsnorm(k, C, x, xn_, TF, nb=nb, xoff=2)
            yield
            pend = None
            for j in range(NJ):
                ys = []
                for which in range(2):
                    c = which * NJ + j
                    p = nps()
                    y = yb[t % 2][(2 * j + which) % 4]
                    for kc in range(8):
                        k.mm(p[:, 0:W2], w_up[:, kc, c * 128:(c + 1) * 128], xn_[:, kc, 0:W2], kc == 0, kc == 7,
                             [w_up, xn_], [p])
                    k.act(y[:, :], p[:, 2:W2], AF.Copy, [p, cw], [y], scale=cw[:, c, 2:3])
                    k.stt("dve", y[:, :], p[:, 1:1 + TF], cw[:, c, 1:2], y[:, :], ALU.mult, ALU.add, [p, cw, y], [y])
                    k.stt("dve", y[:, :], p[:, 0:TF], cw[:, c, 0:1], y[:, :], ALU.mult, ALU.add, [p, cw, y], [y])
                    ys.append(y)
                if pend is not None:
                    pend()

                def fin(j=j, ys=ys):
                    s_ = sg[t % 2][j % 2]
                    k.act(s_[:, :], ys[0][:, :], AF.Silu, [ys[0]], [s_])
                    k.tt("pool", ab[j][:, :], s_[:, :], ys[1][:, :], ALU.mult, [s_, ys[1]], [ab[j]])
                pend = fin
                if j % 2 == 1:
                    yield
            pend()
            yield
            for m in range(8):
                p = nps()
                for j in range(NJ):
                    k.mm(p[:, 0:TF], w_dn[:, j, m * 128:(m + 1) * 128], ab[j][:, :], j == 0, j == NJ - 1,
                         [w_dn, ab[j]], [p])
                k.tt("dve", x[:, m, :], x[:, m, :], p[:, 0:TF], ALU.add, [x, p], [x])
                if m == 3:
                    yield
            if final:
                fin_x = k.sb
                rmsnorm(k, C, x, xfin, TF, nb=nb)
                for m in range(8):
                    k.stt("dve", x[:, m, :], x[:, m, :], gfin[:, m:m + 1], nb[1][:, 0:TF], ALU.mult, ALU.mult,
                          [x, gfin, nb[1]], [x])
            k.dma(hout[:, t * TF:(t + 1) * TF].rearrange("(c p) t -> p c t", p=128), x[:, :, :],
                  [x], [hout_r[(t * TF) // TT]], sem=x)
            if t + 2 < NTF:
                load(t + 2)

        xfin = k.sb([128, 8, TF], BF16, "xfin", ctx) if final else None
        load(0)
        load(1)
        run_pipelined(tile, NTF, 6)
    k.S.end_phase()
    k.S.reorder_on = True


def _v8(v):
    return np.ascontiguousarray(np.asarray(v, np.float32).reshape(-1, 128).T)


def core_inputs(inp, b, S, consts):
    f = lambda a: np.ascontiguousarray(np.asarray(a, np.float32))
    d = dict(consts)
    d["xT"] = f(np.asarray(inp["x"][b]).T)
    d["memT"] = f(np.asarray(inp["mem"][b]).T)
    d["mem_norm"] = _v8(inp["mem_norm"])
    d["final_norm"] = _v8(inp["final_norm"])
    for l in range(2):
        d[f"norm_mix{l}"] = _v8(inp["norm_mix"][l])
        d[f"norm_xattn{l}"] = _v8(inp["norm_xattn"][l])
        d[f"norm_ffn{l}"] = _v8(inp["norm_ffn"][l])
        d[f"xattn_wq{l}"] = f(inp["xattn_wq"][l])
        d[f"xattn_wkv{l}"] = f(inp["xattn_wkv"][l])
        d[f"xattn_wo{l}"] = f(inp["xattn_wo"][l])
        d[f"ffn_w_up{l}"] = f(inp["ffn_w_up"][l])
        d[f"ffn_w_down{l}"] = f(inp["ffn_w_down"][l])
        d[f"ffn_conv{l}"] = f(np.asarray(inp["ffn_conv"][l]).reshape(3, -1, 128).transpose(2, 1, 0))
    d["ev_w_in"] = f(inp["ev_w_in"][0])
    d["ev_w_out"] = f(inp["ev_w_out"][0])
    d["od_w_out"] = f(inp["od_w_out"][0])
    d["pool_w"] = f(inp["pool_w"][0])
    d["pool_scale"] = _v8(inp["pool_scale"][0])
    d["od_w_in"] = f(inp["od_w_in"][0])
    d["sgu_w"] = f(inp["sgu_w"][0])
    d["sgu_ln_g"] = _v8(inp["sgu_ln_g"][0])
    d["sgu_ln_b"] = _v8(inp["sgu_ln_b"][0])
    d["sgu_b"] = f(np.broadcast_to(np.asarray(inp["sgu_b"][0])[None], (128, 4, 128)))
    d["dn_conv"] = f(np.asarray(inp["dn_conv"][0]).reshape(4, 12, 128).transpose(2, 1, 0))
    d["dn_a_log"] = f(np.broadcast_to(np.asarray(inp["dn_a_log"][0])[None], (128, 4)))
    d["dn_dt_bias"] = f(np.broadcast_to(np.asarray(inp["dn_dt_bias"][0])[None], (128, 4)))
    d["dn_norm_g"] = f(np.broadcast_to(np.asarray(inp["dn_norm_g"][0])[None], (128, 128)))
    return d


def phase_l1_inproj(k, C, S):
    NT = S // TT
    GC1 = 1.5957691216057308
    with ExitStack() as ctx:
        C.sq = k.sb([128, 8, TT], BF16, "sq", ctx)
        C.rr = k.sb([128, TT], F32, "rr", ctx)
        gain = k.sb([128, 8], F32, "gain", ctx)
        k.dma(gain[:, :], C.norm_mix[1], [], [gain])
        w_in = k.sb([128, 8, 3080], BF16, "w_in1", ctx)
        wsT = k.sb([128, 4, 128], BF16, "wsT", ctx)
        with ExitStack() as c2:
            stg = [k.sb([128, 2048], F32, "stg", c2) for _ in range(4)]
            load_w(k, ctx, C.od_w_in, D, 3080, "w_in1", gain=gain, stg=stg, w=w_in)
            wsf = k.sb([128, 4, 128], F32, "wsf", c2)
            k.dma(wsf[:, :, :], C.sgu_w.rearrange("g t s -> t g s"), [], [wsf])
            tril = k.sb([128, 128], F32, "tril", c2)
            k.dma(tril[:, :], C.tril01, [], [tril])
            wsm = k.sb([128, 4, 128], BF16, "wsm", c2)
            k.tt("dve", wsm[:, :, :], wsf[:, :, :],
                 tril[:, :].rearrange("p (o s) -> p o s", o=1).to_broadcast([128, 4, 128]), ALU.mult,
                 [wsf, tril], [wsm])
            for g in range(4):
                k.tr(C.ps_bf[:, g * 128:(g + 1) * 128], wsm[:, g, :], C.ident_bf[:, :], [wsm, C.ident_bf], [C.ps_bf])
            k.copy("act", wsT[:, :, :], C.ps_bf[:, 0:512].rearrange("p (g t) -> p g t", g=4), [C.ps_bf], [wsT])
        k.S.end_phase()
        lng = k.sb([128, 4], F32, "lng", ctx)
        lnb = k.sb([128, 4], F32, "lnb", ctx)
        k.dma(lng[:, :], C.sgu_ln_g, [], [lng])
        k.dma(lnb[:, :], C.sgu_ln_b, [], [lnb])
        bsb = k.sb([128, 4, 128], F32, "bsb", ctx)
        k.dma(bsb[:, :, :], C.sgu_b, [], [bsb])
        dcw = k.sb([128, 12, 4], F32, "dcw", ctx)
        k.dma(dcw[:, :, :], C.dn_conv, [], [dcw])
        qsc = k.sb([128, 1], F32, "qsc", ctx)
        k.memset("pool", qsc[:, :], float(np.log(128.0 ** -0.5)), [qsc])
        zero_c = k.sb([128, 1], F32, "zero_c", ctx)
        k.memset("pool", zero_c[:, :], 0.0, [zero_c])

        xt = [k.sb([128, 8, TT], F32, "xt", ctx) for _ in range(2)]
        xn = k.sb([128, 8, TT], BF16, "xn", ctx)
        x2 = [k.sb([128, TT], F32, "x2", ctx) for _ in range(2)]
        sgm = [k.sb([128, TT], F32, "sgm", ctx) for _ in range(2)]
        ub_ = k.sb([128, 4, TT], F32, "u", ctx)
        v_ = k.sb([128, 4, TT], F32, "v", ctx)
        vb = k.sb([128, 4, TT], BF16, "vb", ctx)
        vsq = k.sb([128, 4, TT], BF16, "vsq", ctx)
        mean = k.sb([128, TT], F32, "mean", ctx)
        m2 = k.sb([128, TT], F32, "m2", ctx)
        rstd = k.sb([128, TT], F32, "rstd", ctx)
        vn = k.sb([128, 4, TT], BF16, "vn", ctx)
        vtok = k.sb([128, 4, 4, 128], BF16, "vtok", ctx)
        t1 = k.sb([128, 4, 128], F32, "t1", ctx)
        cout = [k.sb([128, 4, TT], BF16, "cout", ctx) for _ in range(2)]
        cu = [k.sb([128, 3 + TT], F32, "cu", ctx) for _ in range(2)]
        cy = [k.sb([128, TT], F32, "cy", ctx) for _ in range(2)]
        chist = k.sb([128, 12, 3], F32, "chist", ctx)
        k.memset("pool", chist[:, :, :], 0.0, [chist])
        dq4 = k.sb([128, 4, TT], F32, "dq4", ctx)
        dsq4 = k.sb([128, 4, TT], BF16, "dsq4", ctx)
        rn4 = k.sb([128, 4, TT], F32, "rn4", ctx)
        dno = [k.sb([128, 12, TT], BF16, "dno", ctx)] * 2
        gto = [k.sb([128, 4, 512], BF16, "gto", ctx)] * 2
        bao = [k.sb([128, 4, 8], F32, "bao", ctx) for _ in range(2)]
        ps = C.ps
        pi = 0

        def load(t):
            k.dma(xt[t % 2][:, :, :], C.hB[:, t * TT:(t + 1) * TT].rearrange("(c p) t -> p c t", p=128),
                  [C.hB_r[t]], [xt[t % 2]])

        load(0)
        for t in range(NT):
            if t + 1 < NT:
                load(t + 1)
            x = xt[t % 2]
            t0 = t * TT
            rmsnorm(k, C, x, xn, TT)
            for m in range(8):
                p = ps[pi % 6]; pi += 1
                a2, sg = x2[m % 2], sgm[m % 2]
                proj_fm(k, p, w_in, xn, m * 128, TT, 8, [w_in, xn])
                k.act(a2[:, :], p[:, 0:TT], AF.Square, [p], [a2])
                k.ts("dve", a2[:, :], a2[:, :], 0.044715, 1.0, ALU.mult, ALU.add, [a2], [a2])
                k.tt("dve", a2[:, :], a2[:, :], p[:, 0:TT], ALU.mult, [a2, p], [a2])
                k.act(sg[:, :], a2[:, :], AF.Sigmoid, [a2], [sg], scale=GC1)
                dst = ub_ if m < 4 else v_
                k.tt("dve", dst[:, m % 4, :], sg[:, :], p[:, 0:TT], ALU.mult, [sg, p], [dst])
            k.copy("pool", vb[:, :, :], v_[:, :, :], [v_], [vb])
            k.act(vsq[:, :, :], v_[:, :, :], AF.Square, [v_], [vsq])
            pm = ps[pi % 6]; pi += 1
            pq = ps[pi % 6]; pi += 1
            for c in range(4):
                k.mm(pm[:, 0:TT], C.ones_bf[:, :], vb[:, c, :], c == 0, c == 3, [C.ones_bf, vb], [pm])
            for c in range(4):
                k.mm(pq[:, 0:TT], C.ones_bf[:, :], vsq[:, c, :], c == 0, c == 3, [C.ones_bf, vsq], [pq])
            k.ts("dve", mean[:, :], pm[:, 0:TT], 1.0 / 512, None, ALU.mult, None, [pm], [mean])
            k.tt("pool", m2[:, :], mean[:, :], mean[:, :], ALU.mult, [mean], [m2])
            k.stt("dve", m2[:, :], pq[:, 0:TT], 1.0 / 512, m2[:, :], ALU.mult, ALU.subtract, [pq, m2], [m2])
            k.act(rstd[:, :], m2[:, :], AF.Ln, [m2, C.eps_col], [rstd], bias=C.eps_col[:, 0:1])
            k.act(rstd[:, :], rstd[:, :], AF.Exp, [rstd], [rstd], scale=-0.5)
            bc = lambda a: a[:, :].rearrange("p (o t) -> p o t", o=1).to_broadcast([128, 4, TT])
            k.tt("dve", v_[:, :, :], v_[:, :, :], bc(mean), ALU.subtract, [v_, mean], [v_])
            k.tt("pool", v_[:, :, :], v_[:, :, :], bc(rstd), ALU.mult, [v_, rstd], [v_])
            for c in range(4):
                k.ts("dve", vn[:, c, :], v_[:, c, :], lng[:, c:c + 1], lnb[:, c:c + 1], ALU.mult, ALU.add,
                     [v_, lng, lnb], [vn])
            for half in range(2):
                for nn in range(2):
                    n = half * 2 + nn
                    for c in range(4):
                        k.tr(C.ps_bf[:, (nn * 4 + c) * 128:(nn * 4 + c + 1) * 128], vn[:, c, n * 128:(n + 1) * 128],
                             C.ident_bf[:, :], [vn, C.ident_bf], [C.ps_bf])
                k.copy("act", vtok[:, half * 2:half * 2 + 2, :, :],
                       C.ps_bf[:, 0:1024].rearrange("p (n c s) -> p n c s", n=2, c=4), [C.ps_bf], [vtok])
            co = cout[t % 2]
            for g in range(4):
                p = ps[pi % 6]; pi += 1
                for n in range(4):
                    k.mm(p[:, n * 128:(n + 1) * 128], vtok[:, n, g, :], wsT[:, g, :], True, True, [vtok, wsT], [p])
                for n in range(4):
                    pass
                k.tt("dve", t1[:, :, :], p[:, 0:512].rearrange("p (n t) -> p n t", n=4),
                     bsb[:, g, :].rearrange("p (o t) -> p o t", o=1).to_broadcast([128, 4, 128]), ALU.add,
                     [p, bsb], [t1])
                k.tt("pool", co[:, g, :], t1[:, :, :].rearrange("p n t -> p (n t)"), ub_[:, g, :], ALU.mult,
                     [t1, ub_], [co])
            k.dma(C.mixT[0:512, t0:t0 + TT].rearrange("(g p) t -> p g t", p=128), co[:, :, :],
                  [co], [C.mixT_r[t]], sem=co)
            do = dno[t % 2]
            for grp in ((0, 1, 2, 3), (4, 5, 6, 7), (8, 9, 10, 11)):
                for c in grp:
                    p = ps[pi % 6]; pi += 1
                    u, y = cu[c % 2], cy[c % 2]
                    proj_fm(k, p, w_in, xn, 1024 + c * 128, TT, 8, [w_in, xn])
                    k.copy("pool", u[:, 0:3], chist[:, c, :], [chist], [u])
                    k.copy("act", u[:, 3:3 + TT], p[:, 0:TT], [p], [u])
                    k.act(y[:, :], p[:, 0:TT], AF.Copy, [p, dcw], [y], scale=dcw[:, c, 3:4])
                    k.copy("pool", chist[:, c, :], u[:, TT:TT + 3], [u], [chist])
                    for kk in range(3):
                        k.stt("dve", y[:, :], u[:, kk:kk + TT], dcw[:, c, kk:kk + 1], y[:, :], ALU.mult, ALU.add,
                              [u, dcw, y], [y])
                    if c >= 8:
                        k.act(do[:, c, :], y[:, :], AF.Silu, [y], [do])
                    else:
                        k.act(dq4[:, c % 4, :], y[:, :], AF.Silu, [y], [dq4])
                if grp[0] >= 8:
                    continue
                k.act(dsq4[:, :, :], dq4[:, :, :], AF.Square, [dq4], [dsq4])
                for c in grp:
                    p2 = ps[pi % 6]; pi += 1
                    k.mm(p2[:, 0:TT], C.ones_bf[:, :], dsq4[:, c % 4, :], True, True, [C.ones_bf, dsq4], [p2])
                    k.act(rn4[:, c % 4, :], p2[:, 0:TT], AF.Ln, [p2, C.eps_col], [rn4], bias=C.eps_col[:, 0:1])
                k.act(rn4[:, :, :], rn4[:, :, :], AF.Exp, [rn4, qsc, zero_c], [rn4], scale=-0.5,
                      bias=(qsc if grp[0] < 4 else zero_c)[:, 0:1])
                k.tt("dve", do[:, grp[0]:grp[0] + 4, :], dq4[:, :, :], rn4[:, :, :], ALU.mult, [dq4, rn4], [do])
            k.dma(C.dnT[:, t0:t0 + TT].rearrange("(c p) t -> p c t", p=128), do[:, :, :], [do], [C.dnT_r[t]], sem=do)
            go, bo = gto[t % 2], bao[t % 2]
            for n in range(4):
                p = ps[pi % 6]; pi += 1
                for kc in range(8):
                    k.mm(p[:, 0:512], xn[:, kc, n * 128:(n + 1) * 128], w_in[:, kc, 2560:3072], kc == 0, kc == 7,
                         [xn, w_in], [p])
                k.act(go[:, n, :], p[:, 0:512], AF.Silu, [p], [go])
                p = ps[pi % 6]; pi += 1
                for kc in range(8):
                    k.mm(p[:, 0:8], xn[:, kc, n * 128:(n + 1) * 128], w_in[:, kc, 3072:3080], kc == 0, kc == 7,
                         [xn, w_in], [p])
                k.copy("dve", bo[:, n, :], p[:, 0:8], [p], [bo])
            k.dma(C.gtok[t0:t0 + TT, :].rearrange("(n p) c -> p n c", p=128), go[:, :, :], [go], [C.gtok_r[t]], sem=go)
            k.dma(C.batok[t0:t0 + TT, :].rearrange("(n p) c -> p n c", p=128), bo[:, :, :], [bo], [C.batok_r[t]], sem=bo)
    k.S.end_phase()


def phase_deltanet(k, C, S):
    NCH = S // 128
    with ExitStack() as ctx:
        tri = k.sb([128, 128], F32, "tri", ctx)
        k.dma(tri[:, :], C.tri_le, [], [tri])
        causT = k.sb([128, 128], F32, "causT", ctx)
        k.dma(causT[:, :], C.causT_neg, [], [causT])
        strT = k.sb([128, 128], F32, "strT", ctx)
        k.dma(strT[:, :], C.strictT01, [], [strT])
        onesf = k.sb([128, 128], F32, "onesf", ctx)
        k.memset("pool", onesf[:, :], 1.0, [onesf])
        one_c = k.sb([128, 1], F32, "one_c", ctx)
        k.memset("pool", one_c[:, :], 1.0, [one_c])
        dtb = k.sb([128, 4], F32, "dtb", ctx)
        k.dma(dtb[:, :], C.dn_dt_bias, [], [dtb])
        nexpA = k.sb([128, 4], F32, "nexpA", ctx)
        k.dma(nexpA[:, :], C.dn_a_log, [], [nexpA])
        k.act(nexpA[:, :], nexpA[:, :], AF.Exp, [nexpA], [nexpA])
        k.ts("dve", nexpA[:, :], nexpA[:, :], -1.0, None, ALU.mult, None, [nexpA], [nexpA])
        ngb = k.sb([128, 128], F32, "ngb", ctx)
        k.dma(ngb[:, :], C.dn_norm_g, [], [ngb])

        St = k.sb([128, 4, 128], F32, "St", ctx)
        Sb = k.sb([128, 4, 128], BF16, "Sb", ctx)
        k.memset("pool", St[:, :, :], 0.0, [St])
        k.memset("pool", Sb[:, :, :], 0.0, [Sb])
        lvl = k.sb([128, 8, 128], BF16, "lvl", ctx)
        k.dma(lvl[:, :, :], C.lvlmask, [], [lvl])
        NB_ = 4
        BETA, G, GC, GLAST, E, F_, GL, NGC, BE, TMP = range(10)

        def mk():
            b = Ctx()
            b.dn = k.sb([128, 12, 128], BF16, "dn", ctx)
            b.gt = k.sb([128, 512], BF16, "gt", ctx)
            b.ba = k.sb([128, 8], F32, "ba", ctx)
            b.sc = k.sb([128, 10, 4], F32, "sc", ctx)
            b.dg = [k.sb([128, 4, 128], F32, "dg", ctx) for _ in range(3)]
            b.kbT = k.sb([128, 4, 128], BF16, "kbT", ctx)
            b.qeT = k.sb([128, 4, 128], BF16, "qeT", ctx)
            b.X = k.sb([128, 4, 128], F32, "X", ctx)
            b.DcT = k.sb([128, 4, 128], F32, "DcT", ctx)
            b.DcsT = k.sb([128, 4, 128], F32, "DcsT", ctx)
            b.tok = k.sb([128, 8, 128], BF16, "tok", ctx)
            b.rhs0 = k.sb([128, 4, 256], BF16, "rhs0", ctx)
            b.y = k.sb([128, 4, 256], BF16, "y", ctx)
            b.kd = k.sb([128, 4, 128], BF16, "kd", ctx)
            b.qkT = k.sb([128, 4, 128], BF16, "qkT", ctx)
            b.A = k.sb([128, 4, 128], BF16, "A", ctx)
            b.AT = k.sb([128, 4, 128], BF16, "AT", ctx)
            b.Tm = [k.sb([128, 4, 128], BF16, "Tm", ctx) for _ in range(2)]
            b.TTm = [k.sb([128, 4, 128], BF16, "TTm", ctx) for _ in range(2)]
            b.Am = k.sb([128, 4, 128], BF16, "Am", ctx)
            b.AmT = [k.sb([128, 4, 128], BF16, "AmT", ctx) for _ in range(2)]
            b.Um = k.sb([128, 4, 128], BF16, "Um", ctx)
            b.wT = k.sb([128, 4, 128], BF16, "wT", ctx)
            b.vnew = k.sb([128, 4, 128], BF16, "vnew", ctx)
            b.osb = k.sb([128, 4, 128], F32, "osb", ctx)
            b.junk = k.sb([128, 128], F32, "junk", ctx)
            b.ss = k.sb([128, 4], F32, "ss", ctx)
            b.dtk = k.sb([128, 4, 128], BF16, "dtk", ctx)
            b.doT = k.sb([128, 4, 128], BF16, "doT", ctx)
            return b

        BUFS = [mk() for _ in range(NB_)]
        ps = C.ps + [C.ps_norm]
        st_ = {"pi": 0}

        def nxt():
            p = ps[st_["pi"] % 7]
            st_["pi"] += 1
            return p

        def load(n):
            b = BUFS[n % NB_]
            tl = (n * 128) // TT
            k.dma(b.dn[:, :, :], C.dnT[:, n * 128:(n + 1) * 128].rearrange("(c p) t -> p c t", p=128),
                  [C.dnT_r[tl]], [b.dn])
            k.dma(b.gt[:, :], C.gtok[n * 128:(n + 1) * 128, :], [C.gtok_r[tl]], [b.gt])
            k.dma(b.ba[:, :], C.batok[n * 128:(n + 1) * 128, :], [C.batok_r[tl]], [b.ba])

        ident4 = C.ident_f[:, :].rearrange("p (o t) -> p o t", o=1).to_broadcast([128, 4, 128])
        m4 = lambda a: a[:, :].rearrange("p (o t) -> p o t", o=1).to_broadcast([128, 4, 128])
        v4 = lambda p_, w=128: p_[:, 0:4 * w].rearrange("p (h t) -> p h t", h=4)
        lm4 = lambda l: lvl[:, l, :].rearrange("p (o t) -> p o t", o=1).to_broadcast([128, 4, 128])
        identb4 = C.ident_bf[:, :].rearrange("p (o t) -> p o t", o=1).to_broadcast([128, 4, 128])

        def chunk(n):
            b = BUFS[n % NB_]
            sc, d, g_, b_ = b.sc, b.dn, b.gt, b.ba

            def bcol(j):
                return sc[:, j, :].rearrange("p (h o) -> p h o", o=1).to_broadcast([128, 4, 128])
            k.act(sc[:, BETA, :], b_[:, 0:4], AF.Exp, [b_], [sc], scale=-1.0)
            k.ts("dve", sc[:, BETA, :], sc[:, BETA, :], 1.0, None, ALU.add, None, [sc], [sc])
            k.op("dve", lambda e: e.reciprocal(sc[:, BETA, :], sc[:, BETA, :]), [sc], [sc])
            k.tt("dve", sc[:, TMP, :], b_[:, 4:8], dtb[:, :], ALU.add, [b_, dtb], [sc])
            k.act(sc[:, TMP, :], sc[:, TMP, :], AF.Exp, [sc], [sc])
            k.act(sc[:, TMP, :], sc[:, TMP, :], AF.Ln, [sc, one_c], [sc], bias=one_c[:, 0:1])
            k.tt("dve", sc[:, G, :], sc[:, TMP, :], nexpA[:, :], ALU.mult, [sc, nexpA], [sc])
            p = nxt()
            k.mm(p[:, 0:4], tri[:, :], sc[:, G, :], True, True, [tri, sc], [p])
            k.mm(p[:, 4:8], onesf[:, :], sc[:, G, :], True, True, [onesf, sc], [p])
            k.copy("dve", sc[:, GC:GLAST + 1, :], p[:, 0:8].rearrange("p (a h) -> p a h", a=2), [p], [sc])
            yield
            k.act(sc[:, E, :], sc[:, GC, :], AF.Exp, [sc], [sc])
            k.tt("dve", sc[:, TMP, :], sc[:, GLAST, :], sc[:, GC, :], ALU.subtract, [sc], [sc])
            k.act(sc[:, F_, :], sc[:, TMP, :], AF.Exp, [sc], [sc])
            k.act(sc[:, GL, :], sc[:, GLAST, :], AF.Exp, [sc], [sc])
            k.ts("dve", sc[:, NGC, :], sc[:, GC, :], -1.0, None, ALU.mult, None, [sc], [sc])
            k.tt("dve", sc[:, BE, :], sc[:, BETA, :], sc[:, E, :], ALU.mult, [sc], [sc])
            for j in range(8):
                k.tr(C.ps_bf[:, j * 128:(j + 1) * 128], d[:, 4 + j, :], C.ident_bf[:, :], [d, C.ident_bf], [C.ps_bf])
            k.copy("act", b.tok[:, :, :], C.ps_bf[:, 0:1024].rearrange("p (j t) -> p j t", j=8), [C.ps_bf], [b.tok])
            yield
            pB, pE, pG = nxt(), nxt(), nxt()
            for dgi, (col, pp) in enumerate(((BETA, pB), (E, pE), (GC, pG))):
                k.tt("pool" if dgi == 1 else "dve", b.dg[dgi][:, :, :], ident4, bcol(col), ALU.mult,
                     [C.ident_f, sc], [b.dg[dgi]])
                k.mm(pp[:, 0:512], onesf[:, :], b.dg[dgi][:, :, :].rearrange("p h t -> p (h t)"), True, True,
                     [onesf, b.dg[dgi]], [pp])
            k.tt("dve", b.kbT[:, :, :], d[:, 4:8, :], v4(pB), ALU.mult, [d, pB], [b.kbT])
            k.tt("dve", b.qeT[:, :, :], d[:, 0:4, :], v4(pE), ALU.mult, [d, pE], [b.qeT])
            k.tt("dve", b.X[:, :, :], v4(pG), m4(causT), ALU.add, [pG, causT], [b.X])
            for h in range(4):
                k.ts("dve", b.rhs0[:, h, 0:128], b.tok[:, 4 + h, :], sc[:, BETA, h:h + 1], None, ALU.mult, None,
                     [b.tok, sc], [b.rhs0])
                k.act(b.rhs0[:, h, 128:256], b.tok[:, h, :], AF.Copy, [b.tok, sc], [b.rhs0], scale=sc[:, BE, h:h + 1])
                k.act(b.kd[:, h, :], b.tok[:, h, :], AF.Copy, [b.tok, sc], [b.kd], scale=sc[:, F_, h:h + 1])
            yield
            for h in range(4):
                k.act(b.DcT[:, h, :], b.X[:, h, :], AF.Exp, [b.X, sc], [b.DcT], bias=sc[:, NGC, h:h + 1])
            k.tt("pool", b.DcsT[:, :, :], b.DcT[:, :, :], m4(strT), ALU.mult, [b.DcT, strT], [b.DcsT])
            pQ, pK = nxt(), nxt()
            for h in range(4):
                k.mm(pQ[:, h * 128:(h + 1) * 128], d[:, 4 + h, :], d[:, h, :], True, True, [d], [pQ])
                k.mm(pK[:, h * 128:(h + 1) * 128], d[:, 4 + h, :], b.kbT[:, h, :], True, True, [d, b.kbT], [pK])
            k.tt("dve", b.qkT[:, :, :], v4(pQ), b.DcT[:, :, :], ALU.mult, [pQ, b.DcT], [b.qkT])
            k.tt("dve", b.AT[:, :, :], v4(pK), b.DcsT[:, :, :], ALU.mult, [pK, b.DcsT], [b.AT])
            yield
            for h in range(4):
                k.tr(C.ps_bf[:, h * 128:(h + 1) * 128], b.AT[:, h, :], C.ident_bf[:, :], [b.AT, C.ident_bf], [C.ps_bf])
            k.copy("act", b.A[:, :, :], C.ps_bf[:, 0:512].rearrange("p (h t) -> p h t", h=4), [C.ps_bf], [b.A])
            Tc, TTc = b.Tm[0], b.TTm[0]
            k.tt("pool", b.AmT[0][:, :, :], b.AT[:, :, :], lm4(0), ALU.mult, [b.AT, lvl], [b.AmT[0]])
            k.tt("dve", TTc[:, :, :], identb4, b.AmT[0][:, :, :], ALU.subtract, [C.ident_bf, b.AmT[0]], [TTc])
            k.tt("pool", b.Am[:, :, :], b.A[:, :, :], lm4(7), ALU.mult, [b.A, lvl], [b.Am])
            k.tt("dve", Tc[:, :, :], identb4, b.Am[:, :, :], ALU.subtract, [C.ident_bf, b.Am], [Tc])
            yield
            for l in range(1, 7):
                Tn, TTn = b.Tm[l % 2], b.TTm[l % 2]
                amt = b.AmT[l % 2]
                k.tt("pool", amt[:, :, :], b.AT[:, :, :], lm4(l), ALU.mult, [b.AT, lvl], [amt])
                pU = nxt()
                for h in range(4):
                    k.mm(pU[:, h * 128:(h + 1) * 128], amt[:, h, :], Tc[:, h, :], True, True, [amt, Tc], [pU])
                k.copy("act", b.Um[:, :, :], v4(pU), [pU], [b.Um])
                pVT = nxt()
                for h in range(4):
                    k.mm(pVT[:, h * 128:(h + 1) * 128], b.Um[:, h, :], TTc[:, h, :], True, True, [b.Um, TTc], [pVT])
                if l < 6:
                    pV = nxt()
                    for h in range(4):
                        k.mm(pV[:, h * 128:(h + 1) * 128], TTc[:, h, :], b.Um[:, h, :], True, True, [TTc, b.Um], [pV])
                k.tt("dve", TTn[:, :, :], TTc[:, :, :], v4(pVT), ALU.subtract, [TTc, pVT], [TTn])
                if l < 6:
                    k.tt("dve", Tn[:, :, :], Tc[:, :, :], v4(pV), ALU.subtract, [Tc, pV], [Tn])
                Tc, TTc = Tn, TTn
                yield
            pY0, pY1 = nxt(), nxt()
            for h in range(4):
                py = (pY0, pY1)[h // 2]
                k.mm(py[:, (h % 2) * 256:(h % 2 + 1) * 256], TTc[:, h, :], b.rhs0[:, h, :], True, True,
                     [TTc, b.rhs0], [py])
            ycur = b.y
            k.copy("act", ycur[:, 0:2, :], pY0[:, 0:512].rearrange("p (h t) -> p h t", h=2), [pY0], [ycur])
            k.copy("dve", ycur[:, 2:4, :], pY1[:, 0:512].rearrange("p (h t) -> p h t", h=2), [pY1], [ycur])
            for h in range(4):
                k.tr(C.ps_bf[:, h * 128:(h + 1) * 128], ycur[:, h, 128:256], C.ident_bf[:, :], [ycur, C.ident_bf],
                     [C.ps_bf])
            k.copy("act", b.wT[:, :, :], C.ps_bf[:, 0:512].rearrange("p (h t) -> p h t", h=4), [C.ps_bf], [b.wT])
            yield
            p1 = nxt()
            for h in range(4):
                k.mm(p1[:, h * 128:(h + 1) * 128], b.wT[:, h, :], Sb[:, h, :], True, True, [b.wT, Sb], [p1])
            k.tt("dve", b.vnew[:, :, :], ycur[:, :, 0:128], v4(p1), ALU.subtract, [ycur, p1], [b.vnew])
            p2, p3 = nxt(), nxt()
            for h in range(4):
                k.mm(p3[:, h * 128:(h + 1) * 128], b.kd[:, h, :], b.vnew[:, h, :], True, True, [b.kd, b.vnew], [p3])
            for h in range(4):
                k.mm(p2[:, h * 128:(h + 1) * 128], b.qeT[:, h, :], Sb[:, h, :], True, False, [b.qeT, Sb], [p2])
                k.mm(p2[:, h * 128:(h + 1) * 128], b.qkT[:, h, :], b.vnew[:, h, :], False, True, [b.qkT, b.vnew], [p2])
            for h in range(4):
                k.stt("dve", St[:, h, :], St[:, h, :], sc[:, GL, h:h + 1], p3[:, h * 128:(h + 1) * 128],
                      ALU.mult, ALU.add, [St, sc, p3], [St])
            k.copy("pool", Sb[:, :, :], St[:, :, :], [St], [Sb])
            k.copy("act", b.osb[:, :, :], v4(p2), [p2], [b.osb])
            yield
            k.memset("pool", b.ss[:, :], 0.0, [b.ss])
            for h in range(4):
                k.act(b.junk[:, :], b.osb[:, h, :], AF.Square, [b.osb], [b.junk, b.ss], accum_out=b.ss[:, h:h + 1])
            k.act(b.ss[:, :], b.ss[:, :], AF.Ln, [b.ss, C.eps_col], [b.ss], bias=C.eps_col[:, 0:1], scale=1.0 / 128)
            k.act(b.ss[:, :], b.ss[:, :], AF.Exp, [b.ss], [b.ss], scale=-0.5)
            for h in range(4):
                k.stt("dve", b.osb[:, h, :], b.osb[:, h, :], b.ss[:, h:h + 1], ngb[:, :], ALU.mult, ALU.mult,
                      [b.osb, b.ss, ngb], [b.osb])
            k.tt("pool", b.dtk[:, :, :], b.osb[:, :, :], g_[:, :].rearrange("p (h t) -> p h t", h=4), ALU.mult,
                 [b.osb, g_], [b.dtk])
            yield
            for h in range(4):
                k.tr(C.ps_bf[:, h * 128:(h + 1) * 128], b.dtk[:, h, :], C.ident_bf[:, :], [b.dtk, C.ident_bf], [C.ps_bf])
            k.copy("act", b.doT[:, :, :], C.ps_bf[:, 0:512].rearrange("p (h t) -> p h t", h=4), [C.ps_bf], [b.doT])
            k.dma(C.mixT[512:1024, n * 128:(n + 1) * 128].rearrange("(h p) t -> p h t", p=128), b.doT[:, :, :],
                  [b.doT], [C.mixT_r[(n * 128) // TT]], sem=b.doT)
            if n + NB_ < NCH:
                load(n + NB_)

        for n in range(min(NB_, NCH)):
            load(n)
        run_pipelined(chunk, NCH, 5, maxlive=NB_)
    k.S.end_phase()


SEQ = 8192
N_ACTIVE = 2


def kernel(**inputs):
    S = SEQ
    nc = build(S)
    consts = host_consts(S)
    in_maps = [core_inputs(inputs, b, S, consts) for b in range(N_ACTIVE)]
    res = run_bass_kernel_spmd(nc, in_maps, core_ids=list(range(N_ACTIVE)))
    out = np.stack([np.ascontiguousarray(res.results[b]["outT"].T) for b in range(N_ACTIVE)], axis=0)
    return out.astype(np.float32)
```

```python
import numpy as np
import ml_dtypes
from contextlib import ExitStack
import concourse.bass as bass
import concourse.mybir as mybir
from concourse.bass_utils import run_bass_kernel_spmd

F32 = mybir.dt.float32
BF16 = mybir.dt.bfloat16
AF = mybir.ActivationFunctionType
ALU = mybir.AluOpType
AX = mybir.AxisListType

D = 1024
NMEM = 256
EPS = 1e-6
DFF = 2816
NEG = -30000.0


class Res:
    __slots__ = ("name", "writers", "readers", "dsem")

    def __init__(self, name):
        self.name = name
        self.writers = []
        self.readers = []
        self.dsem = None


class Op:
    __slots__ = ("eng", "fn", "deps", "signal", "count", "dtok", "idx", "after", "dur", "gidx",
                 "nun", "rdy", "fin", "users")

    def __init__(self, eng, fn):
        self.eng = eng
        self.fn = fn
        self.deps = []
        self.signal = False
        self.count = 0
        self.dtok = None
        self.idx = 0
        self.after = []
        self.dur = 300.0


ENGS = ("pe", "act", "dve", "pool", "sp")


class Sched:
    def __init__(self, nc, n_dma_sems=40):
        self.nc = nc
        self.ops = {e: [] for e in ENGS}
        self.n_dma_sems = n_dma_sems
        self.dma_counts = [0] * n_dma_sems
        self.n_sp_sems = n_dma_sems - 8
        self.free_dsems = list(range(self.n_sp_sems))
        self.free_dsems_pool = list(range(self.n_sp_sems, n_dma_sems))
        self.phase_res = []
        self.all_res = []
        self.reorder_on = True
        self.seg_flags = []

    def res(self, name):
        r = Res(name)
        self.all_res.append(r)
        return r

    def _add(self, eng, fn, reads, writes, acc):
        o = Op(eng, fn)
        o.idx = len(self.ops[eng])
        deps = []
        for r in reads:
            deps.extend(r.writers)
        for w in writes:
            deps.extend(w.readers)
            if not acc:
                deps.extend(w.writers)
            elif acc == 'dma':
                deps.extend(t for t in w.writers if t[0] != 'D')
            else:
                for t in w.writers:
                    if t[0] == 'E' and t[1].eng == eng:
                        o.after.append(t[1])
                    else:
                        deps.append(t)
        o.deps = deps
        self.ops[eng].append(o)
        return o

    def _commit(self, tok, reads, writes):
        for r in reads:
            r.readers.append(tok)
        for w in writes:
            if w.readers:
                w.writers = [tok]
                w.readers = []
            else:
                w.writers.append(tok)
                if len(w.writers) > 64:
                    w.writers = w.writers[-64:]

    def op(self, eng, fn, reads=(), writes=(), acc=False):
        o = self._add(eng, fn, reads, writes, acc)
        self._commit(('E', o), reads, writes)
        return o

    def dma(self, eng, out, in_, reads=(), writes=(), sem_res=None):
        sr = sem_res if sem_res is not None else writes[0]
        if sr.dsem is None:
            sr.dsem = (self.free_dsems_pool if eng == "pool" else self.free_dsems).pop(0)
        o = self._add(eng, lambda e, out=out, in_=in_: e.dma_start(out=out, in_=in_), reads, writes, 'dma')
        self.dma_counts[sr.dsem] += 16
        o.dtok = (sr.dsem, self.dma_counts[sr.dsem])
        self._commit(('D', sr.dsem, o.dtok[1]), reads, writes)
        return o

    def end_phase(self):
        toks = []
        for e in ENGS:
            if self.ops[e]:
                last = self.ops[e][-1]
                if last.dtok is None:
                    toks.append(('E', last))
        for i in range(self.n_dma_sems):
            if self.dma_counts[i] > 0:
                toks.append(('D', i, self.dma_counts[i]))
        self.seg_flags.append(self.reorder_on)
        for e in ENGS:
            o = Op(e, None)
            o.idx = len(self.ops[e])
            o.deps = list(toks)
            self.ops[e].append(o)
        for r in self.all_res:
            r.dsem = None
            r.writers = []
            r.readers = []
        self.free_dsems = list(range(self.n_sp_sems))
        self.free_dsems_pool = list(range(self.n_sp_sems, self.n_dma_sems))

    def reorder(self, window=24):
        import heapq
        dma_of = {}
        for e in ENGS:
            for o in self.ops[e]:
                if o.dtok is not None:
                    dma_of[o.dtok] = o
        pos = {e: 0 for e in ENGS}
        new_ops = {e: [] for e in ENGS}
        segi = -1
        while any(pos[e] < len(self.ops[e]) for e in ENGS):
            segi += 1
            seg = {}
            for e in ENGS:
                lst = self.ops[e]
                i = pos[e]
                j = i
                while j < len(lst) and lst[j].fn is not None:
                    j += 1
                seg[e] = lst[i:j]
                pos[e] = j + 1 if j < len(lst) else j
                bar = lst[j] if j < len(lst) else None
                seg[e + "_bar"] = bar
            if segi < len(self.seg_flags) and not self.seg_flags[segi]:
                self._fix_barrier(seg, {e: seg[e] for e in ENGS})
                for e in ENGS:
                    new_ops[e].extend(seg[e])
                    if seg[e + "_bar"] is not None:
                        new_ops[e].append(seg[e + "_bar"])
                continue
            allops = [o for e in ENGS for o in seg[e]]
            inseg = set(id(o) for o in allops)
            for o in allops:
                o.users = []
                o.fin = None
            for o in allops:
                n = 0
                dl = []
                for t in o.deps:
                    d = t[1] if t[0] == 'E' else dma_of.get((t[1], t[2]))
                    if d is not None and id(d) in inseg and d.fn is not None:
                        dl.append(d)
                for d in o.after:
                    if id(d) in inseg:
                        dl.append(d)
                o.nun = len(dl)
                o.rdy = 0.0
                for d in dl:
                    d.users.append(o)
            free = {e: 0.0 for e in ENGS}
            pend = {e: list(seg[e]) for e in ENGS}
            head = {e: 0 for e in ENGS}
            issued = {e: [] for e in ENGS}
            remaining = len(allops)
            while remaining:
                best = None
                for e in ENGS:
                    lst = pend[e]
                    h = head[e]
                    while h < len(lst) and lst[h] is None:
                        h += 1
                    head[e] = h
                    if h >= len(lst):
                        continue
                    w = 1 if e == "sp" else window
                    cnt = 0
                    i = h
                    while i < len(lst) and cnt < w:
                        o = lst[i]
                        if o is not None:
                            cnt += 1
                            if o.nun == 0:
                                stt = o.rdy if o.rdy > free[e] else free[e]
                                if best is None or stt < best[0]:
                                    best = (stt, e, i)
                                if stt <= free[e]:
                                    break
                        i += 1
                if best is None:
                    raise RuntimeError("reorder: no schedulable op (cyclic deps?)")
                stt, e, i = best
                o = pend[e][i]
                pend[e][i] = None
                issued[e].append(o)
                remaining -= 1
                if o.dtok is not None:
                    free[e] = stt + 60.0
                    o.fin = stt + 2500.0 + o.dur
                else:
                    free[e] = stt + o.dur
                    o.fin = stt + o.dur + 150.0
                for u in o.users:
                    u.nun -= 1
                    if o in u.after and o not in [ (t[1] if t[0] == 'E' else None) for t in u.deps]:
                        r = stt
                    else:
                        r = o.fin
                    if r > u.rdy:
                        u.rdy = r
            self._fix_barrier(seg, issued)
            for e in ENGS:
                new_ops[e].extend(issued[e])
                if seg[e + "_bar"] is not None:
                    new_ops[e].append(seg[e + "_bar"])
        self.ops = new_ops

    @staticmethod
    def _fix_barrier(seg, order):
        etoks = []
        for e in ENGS:
            for o in reversed(order[e]):
                if o.fn is not None and o.dtok is None:
                    etoks.append(('E', o))
                    break
        for e in ENGS:
            bar = seg[e + "_bar"]
            if bar is not None:
                bar.deps = [t for t in bar.deps if t[0] == 'D'] + etoks

    def emit(self, esems, dsems):
        nc = self.nc
        if getattr(self, "do_reorder", True):
            self.reorder()
        for e in ENGS:
            for o in self.ops[e]:
                for t in o.deps:
                    if t[0] == 'E':
                        t[1].signal = True
        for e in ENGS:
            c = 0
            for o in self.ops[e]:
                if o.signal and o.fn is not None and o.dtok is None:
                    c += 1
                    o.count = c
                elif o.signal:
                    o.count = c
        sched = self

        def run(e_name, eng):
            seen = {}
            for o in sched.ops[e_name]:
                need = {}
                for t in o.deps:
                    if t[0] == 'E':
                        d = t[1]
                        if d.count == 0:
                            continue
                        key = ('E', d.eng)
                        val = d.count
                    else:
                        key = ('D', t[1])
                        val = t[2]
                    if need.get(key, 0) < val:
                        need[key] = val
                for key, val in need.items():
                    if seen.get(key, 0) >= val:
                        continue
                    seen[key] = val
                    sem = esems[key[1]] if key[0] == 'E' else dsems[key[1]]
                    eng.wait_ge(sem, val)
                if o.fn is None:
                    continue
                ins = o.fn(eng)
                if o.dtok is not None:
                    ins.then_inc(dsems[o.dtok[0]], 16)
                elif o.signal:
                    ins.then_inc(esems[e_name], 1)

        with nc.Block() as block:
            @block.tensor
            def _(eng):
                run("pe", eng)

            @block.scalar
            def _(eng):
                run("act", eng)

            @block.vector
            def _(eng):
                run("dve", eng)

            @block.gpsimd
            def _(eng):
                run("pool", eng)

            @block.sync
            def _(eng):
                run("sp", eng)


class Buf:
    def __init__(self, S, t, name):
        self.t = t
        self.r = S.res(name)

    def __getitem__(self, k):
        return self.t[k]


class K:
    def __init__(self, nc):
        self.nc = nc
        self.S = Sched(nc)
        self.stack = ExitStack()
        self.n = 0

    def sb(self, shape, dt, name=None, ctx=None):
        self.n += 1
        name = f"{name or 'sb'}_{self.n}"
        t = (ctx or self.stack).enter_context(self.nc.sbuf_tensor(name, list(shape), dt))
        return Buf(self.S, t, name)

    def psum(self, shape, dt=F32, name=None, ctx=None):
        self.n += 1
        name = f"{name or 'ps'}_{self.n}"
        t = (ctx or self.stack).enter_context(self.nc.psum_tensor(name, list(shape), dt))
        return Buf(self.S, t, name)

    def dram(self, shape, dt, name):
        t = self.nc.dram_tensor(name, list(shape), dt)
        b = Buf(self.S, t.ap(), name)
        return b

    def _rw(self, reads, writes):
        return [b.r for b in reads], [b.r for b in writes]

    def op(self, eng, fn, reads, writes, acc=False, dur=None):
        r, w = self._rw(reads, writes)
        o = self.S.op(eng, fn, r, w, acc)
        if dur is not None:
            o.dur = dur
        return o

    @staticmethod
    def _fsz(ap):
        n = 1
        for d in list(ap.shape)[1:]:
            n *= int(d)
        return n

    def dma(self, out, in_, reads, writes, eng="sp", sem=None):
        r, w = self._rw(reads, writes)
        return self.S.dma(eng, out, in_, r, w, sem.r if sem is not None else None)

    def mm(self, out, lhsT, rhs, start, stop, reads, writes):
        return self.op("pe", lambda e: e.matmul(out, lhsT, rhs, start=start, stop=stop), reads, writes, acc=True,
                       dur=70.0 + 0.45 * self._fsz(rhs) * (4 if rhs.dtype == F32 else 1))

    def tr(self, out, in_, ident, reads, writes):
        return self.op("pe", lambda e: e.transpose(out, in_, ident), reads, writes, acc=True, dur=130.0)

    def act(self, out, in_, func, reads, writes, bias=None, scale=None, accum_out=None, eng="act"):
        kw = {}
        if bias is not None:
            kw["bias"] = bias
        if scale is not None:
            kw["scale"] = scale
        if accum_out is not None:
            kw["accum_out"] = accum_out
        return self.op("act", lambda e: e.activation(out, in_, func, **kw), reads, writes,
                       dur=220.0 + 0.95 * self._fsz(out))

    def tt(self, eng, out, in0, in1, op, reads, writes):
        return self.op(eng, lambda e: e.tensor_tensor(out, in0, in1, op), reads, writes,
                       dur=(120.0 + 1.0 * self._fsz(out)) * (2.0 if eng == "pool" else 1.0))

    def ts(self, eng, out, in0, s1, s2, op0, op1, reads, writes):
        du = 120.0 + 0.9 * self._fsz(out)
        if op1 is None:
            return self.op(eng, lambda e: e.tensor_scalar(out, in0, s1, None, op0), reads, writes, dur=du)
        return self.op(eng, lambda e: e.tensor_scalar(out, in0, s1, s2, op0, op1), reads, writes, dur=du)

    def stt(self, eng, out, in0, scalar, in1, op0, op1, reads, writes):
        return self.op(eng, lambda e: e.scalar_tensor_tensor(out, in0, scalar, in1, op0, op1), reads, writes,
                       dur=120.0 + 1.4 * self._fsz(out))

    def copy(self, eng, out, in_, reads, writes):
        if eng == "act":
            return self.op("act", lambda e: e.copy(out, in_), reads, writes, dur=220.0 + 0.8 * self._fsz(out))
        return self.op(eng, lambda e: e.tensor_copy(out, in_), reads, writes,
                       dur=(120.0 + 0.7 * self._fsz(out)) * (2.0 if eng == "pool" else 1.0))

    def memset(self, eng, ap, val, writes):
        return self.op(eng, lambda e: e.memset(ap, val), [], writes)


TT = 512


class Ctx:
    pass


def load_w(k, ctx, src, kin, n, name, gain=None, stg=None, col0=0, w=None):
    kc_n = kin // 128
    if w is None:
        w = k.sb([128, kc_n, n], BF16, name, ctx)
    i = 0
    for kc in range(kc_n):
        for n0 in range(0, n, 2048):
            wd = min(2048, n - n0)
            s = stg[i % len(stg)]
            k.dma(s[:, 0:wd], src[kc * 128:(kc + 1) * 128, col0 + n0:col0 + n0 + wd], [], [s],
                  eng="sp")
            if gain is None:
                k.copy(("dve", "pool")[i % 2], w[:, kc, n0:n0 + wd], s[:, 0:wd], [s], [w])
            elif i % 2 == 0:
                k.ts("dve", w[:, kc, n0:n0 + wd], s[:, 0:wd], gain[:, kc:kc + 1], None, ALU.mult, None,
                     [s, gain], [w])
            else:
                k.act(w[:, kc, n0:n0 + wd], s[:, 0:wd], AF.Copy, [s, gain], [w], scale=gain[:, kc:kc + 1])
            i += 1
    return w


def rmsnorm(k, C, x, xn, T, kc_n=8, dim=D, nb=None, xoff=0):
    sq, r, ps = nb if nb is not None else (C.sq, C.rr, C.ps_norm)
    k.act(sq[:, 0:kc_n, 0:T], x[:, 0:kc_n, 0:T], AF.Square, [x], [sq])
    for kc in range(kc_n):
        k.mm(ps[:, 0:T], C.ones_bf[:, :], sq[:, kc, 0:T], kc == 0, kc == kc_n - 1, [sq, C.ones_bf], [ps])
    k.act(r[:, 0:T], ps[:, 0:T], AF.Ln, [ps, C.eps_col], [r], bias=C.eps_col[:, 0:1], scale=1.0 / dim)
    k.act(r[:, 0:T], r[:, 0:T], AF.Exp, [r], [r], scale=-0.5)
    k.tt("dve", xn[:, 0:kc_n, xoff:xoff + T], x[:, 0:kc_n, 0:T],
         r[:, 0:T].rearrange("p (o t) -> p o t", o=1).to_broadcast([128, kc_n, T]), ALU.mult, [x, r], [xn])


def run_pipelined(make_gen, n, lag, maxlive=2):
    gens = [make_gen(t) for t in range(n)]
    prog = [0] * n
    done = [False] * n
    first = 0
    while first < n:
        for t in range(first, n):
            if t > first and not (done[t - 1] or prog[t - 1] >= lag):
                break
            if t - maxlive >= 0 and not done[t - maxlive]:
                break
            if done[t]:
                continue
            try:
                next(gens[t])
                prog[t] += 1
            except StopIteration:
                done[t] = True
        while first < n and done[first]:
            first += 1


def proj_fm(k, ps, w, xn, m0, T, kc_n, reads, col=None):
    for kc in range(kc_n):
        k.mm(ps[:, 0:T], w[:, kc, m0:m0 + 128], xn[:, kc, 0:T], kc == 0, kc == kc_n - 1, reads, [ps])


def phase_l0_inproj(k, C, S):
    nc = k.nc
    NT = S // TT
    with ExitStack() as ctx:
        C.sq = k.sb([128, 8, TT], BF16, "sq", ctx)
        C.rr = k.sb([128, TT], F32, "rr", ctx)
        stg = [k.sb([128, 2048], F32, "stg", ctx) for _ in range(2)]
        gain = k.sb([128, 8], F32, "gain", ctx)
        k.dma(gain[:, :], C.norm_mix[0], [], [gain])
        w_in = load_w(k, ctx, C.ev_w_in, D, 2048, "w_in", gain=gain, stg=stg)
        pw_f = k.sb([128, 4, 128], F32, "pw_f", ctx)
        pw = k.sb([128, 4, 128], BF16, "pw", ctx)
        k.dma(pw_f[:, :, :], C.pool_w.rearrange("g c d -> c g d"), [], [pw_f])
        k.copy("dve", pw[:, :, :], pw_f[:, :, :], [pw_f], [pw])
        pscale = k.sb([128, 4], F32, "pscale", ctx)
        k.dma(pscale[:, :], C.pool_scale, [], [pscale])
        corr = k.sb([128, 4, 16], F32, "corr", ctx)
        k.dma(corr[:, :, :], C.pool_corr, [], [corr])

        xt = [k.sb([128, 8, TT], F32, "xt", ctx) for _ in range(2)]
        xn = k.sb([128, 8, TT], BF16, "xn", ctx)
        qk = [k.sb([128, 8, TT], BF16, "qk", ctx) for _ in range(2)]
        vt = [k.sb([128, 4, 8, 65], BF16, "vt", ctx) for _ in range(2)]
        for b in vt:
            k.memset("pool", b[:, :, :, :], 1.0, [b])
        pb = k.sb([128, 4, 16 + TT], F32, "pb", ctx)
        k.memset("pool", pb[:, :, :], 0.0, [pb])
        wa = k.sb([128, 16 + TT], F32, "wa", ctx)
        wb = k.sb([128, 16 + TT], F32, "wb", ctx)
        k.memset("pool", wa[:, :], 0.0, [wa])
        k.memset("pool", wb[:, :], 0.0, [wb])
        pooled = k.sb([128, 4, TT], BF16, "pooled", ctx)
        bout = [k.sb([128, 4, TT], BF16, "bout", ctx) for _ in range(2)]
        ksum = k.sb([128, 4, 2], F32, "ksum", ctx)
        ps = C.ps
        pi = 0

        def load_x(t):
            b = xt[t % 2]
            k.dma(b[:, :, :], C.xT[:, t * TT:(t + 1) * TT].rearrange("(c p) t -> p c t", p=128), [], [b])

        load_x(0)
        for t in range(NT):
            if t + 1 < NT:
                load_x(t + 1)
            x = xt[t % 2]
            t0 = t * TT
            rmsnorm(k, C, x, xn, TT)
            qkb = qk[t % 2]
            for m in range(8):
                p = ps[pi % 6]; pi += 1
                proj_fm(k, p, w_in, xn, m * 128, TT, 8, [w_in, xn])
                k.copy("act", qkb[:, m, :], p[:, 0:TT], [p], [qkb])
            k.op("dve", lambda e, qkb=qkb: e.tensor_reduce(
                ksum[:, :, :], qkb[:, 4:8, :].rearrange("p c (b t) -> p c b t", b=2), AX.X, ALU.add),
                [qkb], [ksum])
            k.ts("dve", C.kmean[:, :, 2 * t:2 * t + 2], ksum[:, :, :], 1.0 / 256, None, ALU.mult, None,
                 [ksum], [C.kmean])
            for c in range(4):
                for hp in range(2):
                    k.dma(C.qaugT[2 * c + hp, 0:64, t0:t0 + TT], qkb[hp * 64:(hp + 1) * 64, c, :],
                          [qkb], [C.qaugT_r[t]], sem=qkb)
            k.dma(C.kT[:, t0:t0 + TT].rearrange("(c p) t -> p c t", p=128), qkb[:, 4:8, :],
                  [qkb], [C.kT_r[t]], sem=qkb)
            vb = vt[t % 2]
            for sub in range(4):
                p = ps[pi % 6]; pi += 1
                for kc in range(8):
                    k.mm(p[:, 0:512], xn[:, kc, sub * 128:(sub + 1) * 128], w_in[:, kc, 1024:1536],
                         kc == 0, kc == 7, [w_in, xn], [p])
                k.copy("dve", vb[:, sub, :, 1:65], p[:, 0:512].rearrange("p (h d) -> p h d", h=8), [p], [vb])
            k.dma(C.vaug[t0:t0 + TT, :].rearrange("(s p) c -> p s c", p=128),
                  vb[:, :, :, :].rearrange("p s h d -> p s (h d)"), [vb], [C.vaug_r[t]], sem=vb)
            for g in range(4):
                p = ps[pi % 6]; pi += 1
                proj_fm(k, p, w_in, xn, 1536 + g * 128, TT, 8, [w_in, xn])
                k.copy("act", pb[:, g, 16:16 + TT], p[:, 0:TT], [p], [pb])
            W = 16 + TT
            for g in range(4):
                eng = ("dve", "pool")[g % 2]
                src = pb
                cur = pb[:, g, :]
                bufs = [wa, wb]
                for lvl in range(g + 1):
                    sh = 1 << lvl
                    dst = bufs[lvl % 2]
                    k.tt(eng, dst[:, sh:W], cur[:, sh:W], cur[:, 0:W - sh], ALU.add, [src], [dst])
                    src = dst
                    cur = dst[:, :]
                wnd = 2 << g
                if t == 0:
                    k.tt(eng, cur[:, 16:32], cur[:, 16:32], corr[:, g, :], ALU.mult, [src, corr], [src])
                k.stt("dve", pooled[:, g, :], cur[:, 16:W], 1.0 / wnd, pb[:, g, 16:W], ALU.mult, ALU.subtract,
                      [src, pb], [pooled])
            k.copy("pool", pb[:, :, 0:16], pb[:, :, TT:TT + 16], [pb], [pb])
            bo = bout[t % 2]
            for g in range(4):
                p = ps[pi % 6]; pi += 1
                k.mm(p[:, 0:TT], pw[:, g, :], pooled[:, g, :], True, True, [pw, pooled], [p])
                k.ts("dve", bo[:, g, :], p[:, 0:TT], pscale[:, g:g + 1], None, ALU.mult, None, [p, pscale], [bo])
            k.dma(C.mixT[512:1024, t0:t0 + TT].rearrange("(g p) t -> p g t", p=128), bo[:, :, :],
                  [bo], [C.mixT_r[t]], sem=bo)
    k.S.end_phase()


class RB:
    def __init__(self, S, name):
        self.r = S.res(name)


def host_consts(S):
    c = {}
    c["ident_bf"] = np.eye(128, dtype=np.float32).astype(ml_dtypes.bfloat16)
    c["ident_f"] = np.eye(128, dtype=np.float32)
    corr = np.ones((4, 16), np.float32)
    for g, w in enumerate((2, 4, 8, 16)):
        for t in range(16):
            corr[g, t] = w / min(t + 1, w)
    c["pool_corr"] = np.ascontiguousarray(np.broadcast_to(corr[None], (128, 4, 16)))
    vb = np.concatenate([np.zeros(32, np.float32), np.full(32, -1e30, np.float32)])
    c["validbias"] = np.ascontiguousarray(np.broadcast_to(vb[None], (128, 64)))
    khot = np.zeros((32, S), np.float32)
    for j in range(S // 256):
        khot[j, j * 256:(j + 1) * 256] = 30000.0
    c["khot"] = khot.astype(ml_dtypes.bfloat16)
    kk = np.arange(128)[:, None, None]
    jj = np.arange(2)[None, :, None]
    qq = np.arange(256)[None, None, :]
    c["causal01"] = ((jj * 128 + kk) <= qq).astype(np.float32).astype(ml_dtypes.bfloat16)
    a = np.arange(128)
    c["tril01"] = (a[None, :] <= a[:, None]).astype(np.float32)
    c["tri_le"] = (a[:, None] <= a[None, :]).astype(np.float32)
    c["causT_neg"] = np.where(a[None, :] >= a[:, None], 0.0, -1e4).astype(np.float32)
    c["strictT01"] = (a[None, :] > a[:, None]).astype(np.float32)
    lv = np.zeros((128, 8, 128), np.float32)
    ii, jj2 = a[:, None], a[None, :]
    for l in range(7):
        b = 1 << l
        m = ((ii // (2 * b)) == (jj2 // (2 * b))) & ((ii % (2 * b)) >= b) & ((jj2 % (2 * b)) < b)
        lv[:, l, :] = m.T
        if l == 0:
            lv[:, 7, :] = m
    c["lvlmask"] = lv.astype(ml_dtypes.bfloat16)
    return c


def build(S, phases=None, debug_out=()):
    nc = bass.Bass("TRN2", target_bir_lowering=False)
    k = K(nc)
    C = Ctx()
    NT = S // TT

    def ext(name, shape, dt=F32):
        return Buf(k.S, nc.dram_tensor(name, list(shape), dt, kind="ExternalInput").ap(), name)

    def scratch(name, shape, dt):
        kind = "ExternalOutput" if name in debug_out else "Internal"
        return nc.dram_tensor(name, list(shape), dt, kind=kind).ap()

    C.xT = ext("xT", [D, S]).t
    C.norm_mix = [ext(f"norm_mix{l}", [128, 8]).t for l in range(2)]
    C.ev_w_in = ext("ev_w_in", [D, 2048]).t
    C.pool_w = ext("pool_w", [4, 128, 128]).t
    C.pool_scale = ext("pool_scale", [128, 4]).t
    C.pool_corr = ext("pool_corr", [128, 4, 16]).t
    C.validbias = ext("validbias", [128, 64]).t
    C.khot = ext("khot", [32, S], BF16).t
    C.causal01 = ext("causal01", [128, 2, 256], BF16).t
    C.memT = ext("memT", [D, NMEM]).t
    C.mem_norm = ext("mem_norm", [128, 8]).t
    C.norm_xattn = [ext(f"norm_xattn{l}", [128, 8]).t for l in range(2)]
    C.norm_ffn = [ext(f"norm_ffn{l}", [128, 8]).t for l in range(2)]
    C.final_norm = ext("final_norm", [128, 8]).t
    C.ev_w_out = ext("ev_w_out", [D, D]).t
    C.od_w_out = ext("od_w_out", [D, D]).t
    C.xattn_wq = [ext(f"xattn_wq{l}", [D, D]).t for l in range(2)]
    C.xattn_wkv = [ext(f"xattn_wkv{l}", [D, 2 * D]).t for l in range(2)]
    C.xattn_wo = [ext(f"xattn_wo{l}", [D, D]).t for l in range(2)]
    C.ffn_w_up = [ext(f"ffn_w_up{l}", [D, 2 * DFF]).t for l in range(2)]
    C.ffn_conv = [ext(f"ffn_conv{l}", [128, 2 * DFF // 128, 3]).t for l in range(2)]
    C.ffn_w_down = [ext(f"ffn_w_down{l}", [DFF, D]).t for l in range(2)]
    C.od_w_in = ext("od_w_in", [D, 3080]).t
    C.sgu_w = ext("sgu_w", [4, 128, 128]).t
    C.sgu_ln_g = ext("sgu_ln_g", [128, 4]).t
    C.sgu_ln_b = ext("sgu_ln_b", [128, 4]).t
    C.sgu_b = ext("sgu_b", [128, 4, 128]).t
    C.dn_conv = ext("dn_conv", [128, 12, 4]).t
    C.dn_a_log = ext("dn_a_log", [128, 4]).t
    C.dn_dt_bias = ext("dn_dt_bias", [128, 4]).t
    C.dn_norm_g = ext("dn_norm_g", [128, 128]).t
    C.tril01 = ext("tril01", [128, 128]).t
    C.tri_le = ext("tri_le", [128, 128]).t
    C.causT_neg = ext("causT_neg", [128, 128]).t
    C.strictT01 = ext("strictT01", [128, 128]).t
    C.lvlmask = ext("lvlmask", [128, 8, 128], BF16).t
    ident_bf_d = ext("ident_bf", [128, 128], BF16).t
    ident_f_d = ext("ident_f", [128, 128]).t
    C.qaugT = scratch("qaugT", [8, 96, S], BF16)
    C.kT = scratch("kT", [512, S], BF16)
    C.vaug = scratch("vaug", [S, 520], BF16)
    C.mixT = scratch("mixT", [D, S], BF16)
    C.dnT = scratch("dnT", [1536, S], BF16)
    C.gtok = scratch("gtok", [S, 512], BF16)
    C.batok = scratch("batok", [S, 8], F32)
    C.hA = scratch("hA", [D, S], F32)
    C.hB = scratch("hB", [D, S], F32)
    C.outT = nc.dram_tensor("outT", [D, S], F32, kind="ExternalOutput").ap()
    for nm in ("qaugT", "kT", "vaug", "mixT", "hA", "hB", "xT", "outT", "dnT", "gtok", "batok"):
        setattr(C, nm + "_r", [RB(k.S, f"{nm}{t}") for t in range(NT)])

    with k.stack:
        C.ps = [k.psum([128, 512], F32, "ps") for _ in range(6)]
        C.ps_norm = k.psum([128, 512], F32, "psn")
        C.ps_bf = k.psum([128, 1024], BF16, "psbf")
        C.ones_bf = k.sb([128, 128], BF16, "ones")
        k.memset("pool", C.ones_bf[:, :], 1.0, [C.ones_bf])
        C.ident_bf = k.sb([128, 128], BF16, "identb")
        k.dma(C.ident_bf[:, :], ident_bf_d, [], [C.ident_bf])
        C.ident_f = k.sb([128, 128], F32, "identf")
        k.dma(C.ident_f[:, :], ident_f_d, [], [C.ident_f])
        C.eps_col = k.sb([128, 1], F32, "epsc")
        k.memset("pool", C.eps_col[:, :], EPS, [C.eps_col])
        C.kmean = k.sb([128, 4, 32], F32, "kmean")
        k.memset("pool", C.kmean[:, :, :], 0.0, [C.kmean])
        k.S.end_phase()

        if phases is None or 1 in phases:
            phase_l0_inproj(k, C, S)
        if phases is None or 2 in phases:
            phase_moba_gate(k, C, S)
        if phases is None or 22 in phases:
            phase_moba_attn(k, C, S)
        if phases is None or 3 in phases:
            phase_outproj_xattn(k, C, S, 0, C.ev_w_out, C.xT, C.xT_r, C.hA, C.hA_r)
        if phases is None or 4 in phases:
            phase_ffn(k, C, S, 0, C.hA, C.hA_r, C.hB, C.hB_r, False)
        if phases is None or 5 in phases:
            phase_l1_inproj(k, C, S)
        if phases is None or 6 in phases:
            phase_deltanet(k, C, S)
        if phases is None or 7 in phases:
            phase_outproj_xattn(k, C, S, 1, C.od_w_out, C.hB, C.hB_r, C.hA, C.hA_r)
        if phases is None or 8 in phases:
            phase_ffn(k, C, S, 1, C.hA, C.hA_r, C.outT, C.outT_r, True)

        esems = {}
        for e in ENGS:
            esems[e] = k.stack.enter_context(nc.semaphore(f"es_{e}"))
        dsems = [k.stack.enter_context(nc.semaphore(f"ds_{i}")) for i in range(k.S.n_dma_sems)]
        k.S.emit(esems, dsems)
    return nc


def phase_moba_gate(k, C, S):
    GS = 9
    NT = S // TT
    with ExitStack() as ctx:
        kmb = k.sb([128, 4, 64], BF16, "kmb", ctx)
        k.memset("pool", kmb[:, :, :], 0.0, [kmb])
        k.copy("dve", kmb[0:64, :, 0:32], C.kmean[0:64, :, :], [C.kmean], [kmb])
        k.copy("dve", kmb[64:128, :, 32:64], C.kmean[64:128, :, :], [C.kmean], [kmb])
        vbias = k.sb([128, 64], F32, "vbias", ctx)
        k.dma(vbias[:, :], C.validbias, [], [vbias])
        qc = [k.sb([128, 4, TT], BF16, "qc", ctx) for _ in range(2)]
        gsb = k.sb([128, 8, 32], F32, "gsb", ctx)
        top8 = k.sb([128, 8, 8], F32, "top8", ctx)
        sel = k.sb([128, 8, 32], F32, "sel", ctx)
        mb = k.sb([128, 8, 96], BF16, "mb", ctx)
        k.memset("pool", mb[:, :, :], 0.0, [mb])
        mrow = [k.sb([128, 8, TT], BF16, "mrow", ctx) for _ in range(2)]
        ps = C.ps

        def load_q(t):
            b = qc[t % 2]
            for h in range(8):
                k.dma(b[(h % 2) * 64:(h % 2) * 64 + 64, h // 2, :], C.qaugT[h, 0:64, t * TT:(t + 1) * TT],
                      [C.qaugT_r[t]], [b])

        load_q(0)
        for t in range(NT):
            if t + 1 < NT:
                load_q(t + 1)
            q = qc[t % 2]
            mr = mrow[t % 2]
            for sub in range(4):
                qb = (t * TT + sub * 128) // 256
                gp = ps[sub % 2]
                for c in range(4):
                    k.mm(gp[:, c * 64:(c + 1) * 64], q[:, c, sub * 128:(sub + 1) * 128],
                         kmb[:, c, :], True, True, [q, kmb], [gp])
                k.tt("dve", gsb[:, :, :], gp[:, 0:256].rearrange("p (h n) -> p h n", h=8),
                     vbias[:, 32 - qb:64 - qb].rearrange("p (o n) -> p o n", o=1).to_broadcast([128, 8, 32]),
                     ALU.add, [gp, vbias], [gsb])
                if GS <= 1:
                    continue
                for h in range(8):
                    k.op("dve", lambda e, h=h: e.max(top8[:, h, :], gsb[:, h, :]), [gsb], [top8], acc=True)
                k.tt("dve", sel[:, :, :], gsb[:, :, :], top8[:, :, 2:3].to_broadcast([128, 8, 32]), ALU.is_ge,
                     [gsb, top8], [sel])
                k.ts("dve", mb[:, :, 64:96], sel[:, :, :], -1.0, None, ALU.add, None, [sel], [mb])
                k.memset("dve", mb[:, :, 64 + qb:65 + qb], 0.0, [mb])
                if GS <= 2:
                    continue
                for half in range(2):
                    mp = ps[2 + (sub * 2 + half) % 4]
                    for hh in range(4):
                        h = half * 4 + hh
                        k.mm(mp[0:96, hh * 128:(hh + 1) * 128], mb[:, h, :], C.ident_bf[:, :], True, True,
                             [mb, C.ident_bf], [mp])
                    k.copy("act", mr[64:96, half * 4:half * 4 + 4, sub * 128:(sub + 1) * 128],
                           mp[64:96, 0:512].rearrange("p (h q) -> p h q", h=4), [mp], [mr])
            if GS <= 3:
                continue
            k.dma(C.qaugT[:, 64:96, t * TT:(t + 1) * TT].rearrange("h r t -> r h t"), mr[64:96, :, :],
                  [mr], [C.qaugT_r[t]], sem=mr)
    k.S.end_phase()


def phase_moba_attn(k, C, S):
    NB = S // 256
    NKT = S // 128
    with ExitStack() as ctx:
        vsb = k.sb([128, NKT, 520], BF16, "vsb", ctx)
        for i in range(0, NKT, 8):
            n = min(8, NKT - i)
            k.dma(vsb[:, i:i + n, :], C.vaug[i * 128:(i + n) * 128, :].rearrange("(t p) c -> p t c", p=128),
                  [C.vaug_r[(i * 128) // TT], C.vaug_r[min(S // TT - 1, ((i + n) * 128 - 1) // TT)]], [vsb])
        kaug = [k.sb([96, S], BF16, "kaug", ctx) for _ in range(2)]
        for b in kaug:
            k.dma(b[64:96, :], C.khot, [], [b])
        caus = k.sb([128, 2, 256], BF16, "caus", ctx)
        k.dma(caus[:, :, :], C.causal01, [], [caus])
        onesf = k.sb([128, 65], F32, "onesf", ctx)
        k.memset("pool", onesf[:, :], 1.0, [onesf])
        pt = [k.sb([128, 2, 256], BF16, "pt", ctx) for _ in range(4)]
        rden = [k.sb([1, 256], F32, "rden", ctx) for _ in range(2)]
        bcs = k.sb([65, 256], F32, "bcs", ctx)
        ao = [k.sb([65, 512], BF16, "ao", ctx) for _ in range(2)]
        sps = C.ps[0:3]
        ops = [C.ps[3], C.ps[4], C.ps_norm]
        bps = C.ps[5]
        all_k = [C.kT_r[t] for t in range(S // TT)]
        NQ = 4
        qa = [k.sb([96, 256], BF16, "qa", ctx) for _ in range(NQ)]
        groups = [(h, qb) for h in range(8) for qb in range(NB)]
        tasks = []
        for gi, (h, qb) in enumerate(groups):
            for pr in range(qb + 1):
                tasks.append((gi, h, qb, pr))

        def load_q(gi):
            h, qb = groups[gi]
            b = qa[gi % NQ]
            k.dma(b[:, :], C.qaugT[h, :, qb * 256:(qb + 1) * 256], [C.qaugT_r[(qb * 256) // TT]], [b])

        def load_k(h):
            kb = kaug[h % 2]
            k.dma(kb[0:64, :], C.kT[h * 64:(h + 1) * 64, :], all_k, [kb])

        def emit_S(i):
            gi, h, qb, pr = tasks[i]
            if pr == 0:
                if qb == 0 and h + 1 < 8:
                    load_k(h + 1)
                if gi + NQ - 1 < len(groups):
                    load_q(gi + NQ - 1)
            kb = kaug[h % 2]
            q = qa[gi % NQ]
            sp = sps[i % 3]
            p = pt[i % 4]
            for j in range(2):
                kt = 2 * pr + j
                k.mm(sp[:, j * 256:(j + 1) * 256], kb[0:96, kt * 128:(kt + 1) * 128], q[0:96, :],
                     True, True, [kb, q], [sp])
            k.act(p[:, :, :], sp[:, 0:512].rearrange("p (j q) -> p j q", j=2), AF.Exp, [sp], [p], scale=0.125)
            if pr == qb:
                k.tt("pool", p[:, :, :], p[:, :, :], caus[:, :, :], ALU.mult, [p, caus], [p])

        deferred = []

        def emit_PV(i):
            gi, h, qb, pr = tasks[i]
            p = pt[i % 4]
            op_ = ops[gi % 3]
            for j in range(2):
                kt = 2 * pr + j
                k.mm(op_[0:65, 0:256], vsb[:, kt, h * 65:(h + 1) * 65], p[:, j, :],
                     pr == 0 and j == 0, pr == qb and j == 1, [vsb, p], [op_])
            if pr == qb:
                rd = rden[gi % 2]
                k.op("dve", lambda e, op_=op_, rd=rd: e.reciprocal(rd[0:1, :], op_[0:1, 0:256]), [op_], [rd])

                def tail(gi=gi, h=h, qb=qb, op_=op_, rd=rd):
                    k.mm(bps[0:65, 0:256], onesf[0:1, 0:65], rd[0:1, :], True, True, [onesf, rd], [bps])
                    k.copy("act", bcs[:, :], bps[0:65, 0:256], [bps], [bcs])
                    a = ao[(qb // 2) % 2]
                    k.tt("dve", a[:, (qb % 2) * 256:(qb % 2 + 1) * 256], op_[0:65, 0:256], bcs[:, :], ALU.mult,
                         [op_, bcs], [a])
                    if qb % 2 == 1:
                        t = qb // 2
                        k.dma(C.mixT[h * 64:(h + 1) * 64, t * TT:(t + 1) * TT], a[1:65, :], [a], [C.mixT_r[t]], sem=a)
                deferred.append((i + 2, tail))

        load_k(0)
        for gi in range(min(NQ - 1, len(groups))):
            load_q(gi)
        n = len(tasks)
        emit_S(0)
        if n > 1:
            emit_S(1)
        for i in range(n):
            while deferred and deferred[0][0] <= i:
                deferred.pop(0)[1]()
            if i + 2 < n:
                emit_S(i + 2)
            emit_PV(i)
        while deferred:
            deferred.pop(0)[1]()
    k.S.end_phase()


def phase_outproj_xattn(k, C, S, l, w_out_d, hin, hin_r, hout, hout_r):
    NT = S // TT
    k.S.reorder_on = False
    with ExitStack() as ctx:
        C.sq = k.sb([128, 8, TT], BF16, "sq", ctx)
        C.rr = k.sb([128, TT], F32, "rr", ctx)
        kmemT = k.sb([128, 8, NMEM], BF16, "kmemT", ctx)
        vmem = k.sb([128, 2, D], BF16, "vmem", ctx)
        with ExitStack() as c2:
            stg = [k.sb([128, 2048], F32, "stg", c2) for _ in range(4)]
            gm = k.sb([128, 8], F32, "gm", c2)
            k.dma(gm[:, :], C.mem_norm, [], [gm])
            wkv = load_w(k, c2, C.xattn_wkv[l], D, 2048, "wkv", gain=gm, stg=stg)
            mt_ = k.sb([128, 8, NMEM], F32, "memt", c2)
            k.dma(mt_[:, :, :], C.memT.rearrange("(c p) m -> p c m", p=128), [], [mt_])
            memn = k.sb([128, 8, NMEM], BF16, "memn", c2)
            rmsnorm(k, C, mt_, memn, NMEM)
            for dc in range(8):
                p = C.ps[dc % 6]
                proj_fm(k, p, wkv, memn, dc * 128, NMEM, 8, [wkv, memn])
                k.copy("act", kmemT[:, dc, :], p[:, 0:NMEM], [p], [kmemT])
            for mt in range(2):
                for half in range(2):
                    p = C.ps[(mt * 2 + half) % 6]
                    for kc in range(8):
                        k.mm(p[:, 0:512], memn[:, kc, mt * 128:(mt + 1) * 128],
                             wkv[:, kc, 1024 + half * 512:1536 + half * 512], kc == 0, kc == 7, [memn, wkv], [p])
                    k.copy("dve", vmem[:, mt, half * 512:(half + 1) * 512], p[:, 0:512], [p], [vmem])
        k.S.end_phase()
        stg = [k.sb([128, 2048], F32, "stg", ctx) for _ in range(2)]
        gq = k.sb([128, 8], F32, "gq", ctx)
        k.dma(gq[:, :], C.norm_xattn[l], [], [gq])
        w_out = load_w(k, ctx, w_out_d, D, D, "w_out", stg=stg)
        wq = load_w(k, ctx, C.xattn_wq[l], D, D, "wq", gain=gq, stg=stg)
        wo = load_w(k, ctx, C.xattn_wo[l], D, D, "wo", stg=stg)
        T3 = 256
        NB3 = 4
        NT3 = S // T3
        xt = [k.sb([128, 8, T3], F32, "xt", ctx) for _ in range(NB3)]
        mx = [k.sb([128, 8, T3], BF16, "mx", ctx) for _ in range(NB3)]
        xn = [k.sb([128, 8, T3], BF16, "xn", ctx) for _ in range(NB3)]
        qx = [k.sb([128, 8, T3], BF16, "qx", ctx) for _ in range(NB3)]
        ox = [k.sb([128, 8, T3], BF16, "ox", ctx) for _ in range(NB3)]
        pt = [[k.sb([128, 2, T3], BF16, "pt", ctx) for _ in range(2)] for _ in range(NB3)]
        rec = [[k.sb([128, T3], F32, "rec", ctx) for _ in range(2)] for _ in range(NB3)]
        rrb = [k.sb([128, T3], F32, "rr3", ctx) for _ in range(NB3)]
        ps = C.ps
        st = {"pi": 0}

        def nps():
            p = ps[st["pi"] % 6]
            st["pi"] += 1
            return p

        def load(t):
            b = t % NB3
            tr = (t * T3) // TT
            k.dma(xt[b][:, :, :], hin[:, t * T3:(t + 1) * T3].rearrange("(c p) t -> p c t", p=128),
                  [hin_r[tr]], [xt[b]])
            k.dma(mx[b][:, :, :], C.mixT[:, t * T3:(t + 1) * T3].rearrange("(c p) t -> p c t", p=128),
                  [C.mixT_r[tr]], [mx[b]])

        def tile(t):
            b = t % NB3
            x, m_, xn_, qx_, ox_ = xt[b], mx[b], xn[b], qx[b], ox[b]
            for m in range(8):
                p = nps()
                proj_fm(k, p, w_out, m_, m * 128, T3, 8, [w_out, m_])
                k.tt("dve", x[:, m, :], x[:, m, :], p[:, 0:T3], ALU.add, [x, p], [x])
            yield
            rmsnorm(k, C, x, xn_, T3, nb=(C.sq, rrb[b], C.ps_norm))
            yield
            for m in range(8):
                p = nps()
                proj_fm(k, p, wq, xn_, m * 128, T3, 8, [wq, xn_])
                k.copy("act", qx_[:, m, :], p[:, 0:T3], [p], [qx_])
                if m == 3:
                    yield
            yield
            for hd in range(4):
                ptb = pt[b][hd % 2]
                rc = rec[b][hd % 2]
                for mt in range(2):
                    p = nps()
                    for dd in range(2):
                        dc = 2 * hd + dd
                        k.mm(p[:, 0:T3], kmemT[:, dc, mt * 128:(mt + 1) * 128], qx_[:, dc, :], dd == 0, dd == 1,
                             [kmemT, qx_], [p])
                    k.act(ptb[:, mt, :], p[:, 0:T3], AF.Exp, [p], [ptb], scale=1.0 / 16)
                p = nps()
                for mt in range(2):
                    k.mm(p[:, 0:T3], C.ones_bf[:, :], ptb[:, mt, :], mt == 0, mt == 1, [C.ones_bf, ptb], [p])
                k.op("dve", lambda e, rc=rc, p=p: e.reciprocal(rc[:, :], p[:, 0:T3]), [p], [rc])
                for dd in range(2):
                    p = nps()
                    c0 = hd * 256 + dd * 128
                    for mt in range(2):
                        k.mm(p[:, 0:T3], vmem[:, mt, c0:c0 + 128], ptb[:, mt, :], mt == 0, mt == 1, [vmem, ptb], [p])
                    k.tt("dve", ox_[:, 2 * hd + dd, :], p[:, 0:T3], rc[:, :], ALU.mult, [p, rc], [ox_])
                yield
            for m in range(8):
                p = nps()
                proj_fm(k, p, wo, ox_, m * 128, T3, 8, [wo, ox_])
                k.tt("dve", x[:, m, :], x[:, m, :], p[:, 0:T3], ALU.add, [x, p], [x])
                if m == 3:
                    yield
            k.dma(hout[:, t * T3:(t + 1) * T3].rearrange("(c p) t -> p c t", p=128), x[:, :, :],
                  [x], [hout_r[(t * T3) // TT]], sem=x)
            if t + NB3 < NT3:
                load(t + NB3)

        for t in range(min(NB3, NT3)):
            load(t)
        run_pipelined(tile, NT3, 4, maxlive=NB3)
    k.S.end_phase()
    k.S.reorder_on = True


TF = 256


def phase_ffn(k, C, S, l, hin, hin_r, hout, hout_r, final):
    k.S.reorder_on = False
    NTF = S // TF
    NJ = DFF // 128
    with ExitStack() as ctx:
        gf = k.sb([128, 8], F32, "gf", ctx)
        k.dma(gf[:, :], C.norm_ffn[l], [], [gf])
        w_up = k.sb([128, 8, 2 * DFF], BF16, "w_up", ctx)
        w_dn = k.sb([128, NJ, D], BF16, "w_dn", ctx)
        with ExitStack() as c2:
            stg = [k.sb([128, 2048], F32, "stg", c2) for _ in range(4)]
            load_w(k, ctx, C.ffn_w_up[l], D, 2 * DFF, "w_up", gain=gf, stg=stg, w=w_up)
            load_w(k, ctx, C.ffn_w_down[l], DFF, D, "w_dn", stg=stg, w=w_dn)
        k.S.end_phase()
        cw = k.sb([128, 2 * NJ, 3], F32, "cw", ctx)
        k.dma(cw[:, :, :], C.ffn_conv[l], [], [cw])
        gfin = k.sb([128, 8], F32, "gfin", ctx)
        k.dma(gfin[:, :], C.final_norm, [], [gfin])
        xt = [k.sb([128, 8, TF], F32, "xt", ctx) for _ in range(2)]
        xn = [k.sb([128, 8, 2 + TF], BF16, "xn", ctx) for _ in range(2)]
        for b in xn:
            k.memset("pool", b[:, :, :], 0.0, [b])
        sqb = [k.sb([128, 8, TF], BF16, "sq", ctx)] * 2
        rrb = [k.sb([128, TF], F32, "rr", ctx) for _ in range(2)]
        actb = [[k.sb([128, TF], BF16, "actb", ctx) for _ in range(NJ)] for _ in range(2)]
        yb = [[k.sb([128, TF], F32, "yb", ctx) for _ in range(4)] for _ in range(2)]
        sg = [[k.sb([128, TF], F32, "sg", ctx) for _ in range(2)] for _ in range(2)]
        ps = C.ps
        st = {"pi": 0}
        W2 = 2 + TF

        def nps():
            p = ps[st["pi"] % 6]
            st["pi"] += 1
            return p

        def load(t):
            k.dma(xt[t % 2][:, :, :], hin[:, t * TF:(t + 1) * TF].rearrange("(c p) t -> p c t", p=128),
                  [hin_r[(t * TF) // TT]], [xt[t % 2]])

        def tile(t):
            x, xn_, ab = xt[t % 2], xn[t % 2], actb[t % 2]
            nb = (sqb[t % 2], rrb[t % 2], C.ps_norm)
            if t > 0:
                k.copy("pool", xn_[:, :, 0:2], xn[(t - 1) % 2][:, :, TF:TF + 2], [xn[(t - 1) % 2]], [xn_])
            rmsnorm(k, C, x, xn_, TF, nb=nb, xoff=2)
            yield
            pend = None
            for j in range(NJ):
                ys = []
                for which in range(2):
                    c = which * NJ + j
                    p = nps()
                    y = yb[t % 2][(2 * j + which) % 4]
                    for kc in range(8):
                        k.mm(p[:, 0:W2], w_up[:, kc, c * 128:(c + 1) * 128], xn_[:, kc, 0:W2], kc == 0, kc == 7,
                             [w_up, xn_], [p])
                    k.act(y[:, :], p[:, 2:W2], AF.Copy, [p, cw], [y], scale=cw[:, c, 2:3])
                    k.stt("dve", y[:, :], p[:, 1:1 + TF], cw[:, c, 1:2], y[:, :], ALU.mult, ALU.add, [p, cw, y], [y])
                    k.stt("dve", y[:, :], p[:, 0:TF], cw[:, c, 0:1], y[:, :], ALU.mult, ALU.add, [p, cw, y], [y])
                    ys.append(y)
                if pend is not None:
                    pend()

                def fin(j=j, ys=ys):
                    s_ = sg[t % 2][j % 2]
                    k.act(s_[:, :], ys[0][:, :], AF.Silu, [ys[0]], [s_])
                    k.tt("pool", ab[j][:, :], s_[:, :], ys[1][:, :], ALU.mult, [s_, ys[1]], [ab[j]])
                pend = fin
                if j % 2 == 1:
                    yield
            pend()
            yield
            for m in range(8):
                p = nps()
                for j in range(NJ):
                    k.mm(p[:, 0:TF], w_dn[:, j, m * 128:(m + 1) * 128], ab[j][:, :], j == 0, j == NJ - 1,
                         [w_dn, ab[j]], [p])
                k.tt("dve", x[:, m, :], x[:, m, :], p[:, 0:TF], ALU.add, [x, p], [x])
                if m == 3:
                    yield
            if final:
                fin_x = k.sb
                rmsnorm(k, C, x, xfin, TF, nb=nb)
                for m in range(8):
                    k.stt("dve", x[:, m, :], x[:, m, :], gfin[:, m:m + 1], nb[1][:, 0:TF], ALU.mult, ALU.mult,
                          [x, gfin, nb[1]], [x])
            k.dma(hout[:, t * TF:(t + 1) * TF].rearrange("(c p) t -> p c t", p=128), x[:, :, :],
                  [x], [hout_r[(t * TF) // TT]], sem=x)
            if t + 2 < NTF:
                load(t + 2)

        xfin = k.sb([128, 8, TF], BF16, "xfin", ctx) if final else None
        load(0)
        load(1)
        run_pipelined(tile, NTF, 6)
    k.S.end_phase()
    k.S.reorder_on = True


def _v8(v):
    return np.ascontiguousarray(np.asarray(v, np.float32).reshape(-1, 128).T)


def core_inputs(inp, b, S, consts):
    f = lambda a: np.ascontiguousarray(np.asarray(a, np.float32))
    d = dict(consts)
    d["xT"] = f(np.asarray(inp["x"][b]).T)
    d["memT"] = f(np.asarray(inp["mem"][b]).T)
    d["mem_norm"] = _v8(inp["mem_norm"])
    d["final_norm"] = _v8(inp["final_norm"])
    for l in range(2):
        d[f"norm_mix{l}"] = _v8(inp["norm_mix"][l])
        d[f"norm_xattn{l}"] = _v8(inp["norm_xattn"][l])
        d[f"norm_ffn{l}"] = _v8(inp["norm_ffn"][l])
        d[f"xattn_wq{l}"] = f(inp["xattn_wq"][l])
        d[f"xattn_wkv{l}"] = f(inp["xattn_wkv"][l])
        d[f"xattn_wo{l}"] = f(inp["xattn_wo"][l])
        d[f"ffn_w_up{l}"] = f(inp["ffn_w_up"][l])
        d[f"ffn_w_down{l}"] = f(inp["ffn_w_down"][l])
        d[f"ffn_conv{l}"] = f(np.asarray(inp["ffn_conv"][l]).reshape(3, -1, 128).transpose(2, 1, 0))
    d["ev_w_in"] = f(inp["ev_w_in"][0])
    d["ev_w_out"] = f(inp["ev_w_out"][0])
    d["od_w_out"] = f(inp["od_w_out"][0])
    d["pool_w"] = f(inp["pool_w"][0])
    d["pool_scale"] = _v8(inp["pool_scale"][0])
    d["od_w_in"] = f(inp["od_w_in"][0])
    d["sgu_w"] = f(inp["sgu_w"][0])
    d["sgu_ln_g"] = _v8(inp["sgu_ln_g"][0])
    d["sgu_ln_b"] = _v8(inp["sgu_ln_b"][0])
    d["sgu_b"] = f(np.broadcast_to(np.asarray(inp["sgu_b"][0])[None], (128, 4, 128)))
    d["dn_conv"] = f(np.asarray(inp["dn_conv"][0]).reshape(4, 12, 128).transpose(2, 1, 0))
    d["dn_a_log"] = f(np.broadcast_to(np.asarray(inp["dn_a_log"][0])[None], (128, 4)))
    d["dn_dt_bias"] = f(np.broadcast_to(np.asarray(inp["dn_dt_bias"][0])[None], (128, 4)))
    d["dn_norm_g"] = f(np.broadcast_to(np.asarray(inp["dn_norm_g"][0])[None], (128, 128)))
    return d


def phase_l1_inproj(k, C, S):
    NT = S // TT
    GC1 = 1.5957691216057308
    with ExitStack() as ctx:
        C.sq = k.sb([128, 8, TT], BF16, "sq", ctx)
        C.rr = k.sb([128, TT], F32, "rr", ctx)
        gain = k.sb([128, 8], F32, "gain", ctx)
        k.dma(gain[:, :], C.norm_mix[1], [], [gain])
        w_in = k.sb([128, 8, 3080], BF16, "w_in1", ctx)
        wsT = k.sb([128, 4, 128], BF16, "wsT", ctx)
        with ExitStack() as c2:
            stg = [k.sb([128, 2048], F32, "stg", c2) for _ in range(4)]
            load_w(k, ctx, C.od_w_in, D, 3080, "w_in1", gain=gain, stg=stg, w=w_in)
            wsf = k.sb([128, 4, 128], F32, "wsf", c2)
            k.dma(wsf[:, :, :], C.sgu_w.rearrange("g t s -> t g s"), [], [wsf])
            tril = k.sb([128, 128], F32, "tril", c2)
            k.dma(tril[:, :], C.tril01, [], [tril])
            wsm = k.sb([128, 4, 128], BF16, "wsm", c2)
            k.tt("dve", wsm[:, :, :], wsf[:, :, :],
                 tril[:, :].rearrange("p (o s) -> p o s", o=1).to_broadcast([128, 4, 128]), ALU.mult,
                 [wsf, tril], [wsm])
            for g in range(4):
                k.tr(C.ps_bf[:, g * 128:(g + 1) * 128], wsm[:, g, :], C.ident_bf[:, :], [wsm, C.ident_bf], [C.ps_bf])
            k.copy("act", wsT[:, :, :], C.ps_bf[:, 0:512].rearrange("p (g t) -> p g t", g=4), [C.ps_bf], [wsT])
        k.S.end_phase()
        lng = k.sb([128, 4], F32, "lng", ctx)
        lnb = k.sb([128, 4], F32, "lnb", ctx)
        k.dma(lng[:, :], C.sgu_ln_g, [], [lng])
        k.dma(lnb[:, :], C.sgu_ln_b, [], [lnb])
        bsb = k.sb([128, 4, 128], F32, "bsb", ctx)
        k.dma(bsb[:, :, :], C.sgu_b, [], [bsb])
        dcw = k.sb([128, 12, 4], F32, "dcw", ctx)
        k.dma(dcw[:, :, :], C.dn_conv, [], [dcw])
        qsc = k.sb([128, 1], F32, "qsc", ctx)
        k.memset("pool", qsc[:, :], float(np.log(128.0 ** -0.5)), [qsc])
        zero_c = k.sb([128, 1], F32, "zero_c", ctx)
        k.memset("pool", zero_c[:, :], 0.0, [zero_c])

        xt = [k.sb([128, 8, TT], F32, "xt", ctx) for _ in range(2)]
        xn = k.sb([128, 8, TT], BF16, "xn", ctx)
        x2 = [k.sb([128, TT], F32, "x2", ctx) for _ in range(2)]
        sgm = [k.sb([128, TT], F32, "sgm", ctx) for _ in range(2)]
        ub_ = k.sb([128, 4, TT], F32, "u", ctx)
        v_ = k.sb([128, 4, TT], F32, "v", ctx)
        vb = k.sb([128, 4, TT], BF16, "vb", ctx)
        vsq = k.sb([128, 4, TT], BF16, "vsq", ctx)
        mean = k.sb([128, TT], F32, "mean", ctx)
        m2 = k.sb([128, TT], F32, "m2", ctx)
        rstd = k.sb([128, TT], F32, "rstd", ctx)
        vn = k.sb([128, 4, TT], BF16, "vn", ctx)
        vtok = k.sb([128, 4, 4, 128], BF16, "vtok", ctx)
        t1 = k.sb([128, 4, 128], F32, "t1", ctx)
        cout = [k.sb([128, 4, TT], BF16, "cout", ctx) for _ in range(2)]
        cu = [k.sb([128, 3 + TT], F32, "cu", ctx) for _ in range(2)]
        cy = [k.sb([128, TT], F32, "cy", ctx) for _ in range(2)]
        chist = k.sb([128, 12, 3], F32, "chist", ctx)
        k.memset("pool", chist[:, :, :], 0.0, [chist])
        dq4 = k.sb([128, 4, TT], F32, "dq4", ctx)
        dsq4 = k.sb([128, 4, TT], BF16, "dsq4", ctx)
        rn4 = k.sb([128, 4, TT], F32, "rn4", ctx)
        dno = [k.sb([128, 12, TT], BF16, "dno", ctx)] * 2
        gto = [k.sb([128, 4, 512], BF16, "gto", ctx)] * 2
        bao = [k.sb([128, 4, 8], F32, "bao", ctx) for _ in range(2)]
        ps = C.ps
        pi = 0

        def load(t):
            k.dma(xt[t % 2][:, :, :], C.hB[:, t * TT:(t + 1) * TT].rearrange("(c p) t -> p c t", p=128),
                  [C.hB_r[t]], [xt[t % 2]])

        load(0)
        for t in range(NT):
            if t + 1 < NT:
                load(t + 1)
            x = xt[t % 2]
            t0 = t * TT
            rmsnorm(k, C, x, xn, TT)
            for m in range(8):
                p = ps[pi % 6]; pi += 1
                a2, sg = x2[m % 2], sgm[m % 2]
                proj_fm(k, p, w_in, xn, m * 128, TT, 8, [w_in, xn])
                k.act(a2[:, :], p[:, 0:TT], AF.Square, [p], [a2])
                k.ts("dve", a2[:, :], a2[:, :], 0.044715, 1.0, ALU.mult, ALU.add, [a2], [a2])
                k.tt("dve", a2[:, :], a2[:, :], p[:, 0:TT], ALU.mult, [a2, p], [a2])
                k.act(sg[:, :], a2[:, :], AF.Sigmoid, [a2], [sg], scale=GC1)
                dst = ub_ if m < 4 else v_
                k.tt("dve", dst[:, m % 4, :], sg[:, :], p[:, 0:TT], ALU.mult, [sg, p], [dst])
            k.copy("pool", vb[:, :, :], v_[:, :, :], [v_], [vb])
            k.act(vsq[:, :, :], v_[:, :, :], AF.Square, [v_], [vsq])
            pm = ps[pi % 6]; pi += 1
            pq = ps[pi % 6]; pi += 1
            for c in range(4):
                k.mm(pm[:, 0:TT], C.ones_bf[:, :], vb[:, c, :], c == 0, c == 3, [C.ones_bf, vb], [pm])
            for c in range(4):
                k.mm(pq[:, 0:TT], C.ones_bf[:, :], vsq[:, c, :], c == 0, c == 3, [C.ones_bf, vsq], [pq])
            k.ts("dve", mean[:, :], pm[:, 0:TT], 1.0 / 512, None, ALU.mult, None, [pm], [mean])
            k.tt("pool", m2[:, :], mean[:, :], mean[:, :], ALU.mult, [mean], [m2])
            k.stt("dve", m2[:, :], pq[:, 0:TT], 1.0 / 512, m2[:, :], ALU.mult, ALU.subtract, [pq, m2], [m2])
            k.act(rstd[:, :], m2[:, :], AF.Ln, [m2, C.eps_col], [rstd], bias=C.eps_col[:, 0:1])
            k.act(rstd[:, :], rstd[:, :], AF.Exp, [rstd], [rstd], scale=-0.5)
            bc = lambda a: a[:, :].rearrange("p (o t) -> p o t", o=1).to_broadcast([128, 4, TT])
            k.tt("dve", v_[:, :, :], v_[:, :, :], bc(mean), ALU.subtract, [v_, mean], [v_])
            k.tt("pool", v_[:, :, :], v_[:, :, :], bc(rstd), ALU.mult, [v_, rstd], [v_])
            for c in range(4):
                k.ts("dve", vn[:, c, :], v_[:, c, :], lng[:, c:c + 1], lnb[:, c:c + 1], ALU.mult, ALU.add,
                     [v_, lng, lnb], [vn])
            for half in range(2):
                for nn in range(2):
                    n = half * 2 + nn
                    for c in range(4):
                        k.tr(C.ps_bf[:, (nn * 4 + c) * 128:(nn * 4 + c + 1) * 128], vn[:, c, n * 128:(n + 1) * 128],
                             C.ident_bf[:, :], [vn, C.ident_bf], [C.ps_bf])
                k.copy("act", vtok[:, half * 2:half * 2 + 2, :, :],
                       C.ps_bf[:, 0:1024].rearrange("p (n c s) -> p n c s", n=2, c=4), [C.ps_bf], [vtok])
            co = cout[t % 2]
            for g in range(4):
                p = ps[pi % 6]; pi += 1
                for n in range(4):
                    k.mm(p[:, n * 128:(n + 1) * 128], vtok[:, n, g, :], wsT[:, g, :], True, True, [vtok, wsT], [p])
                for n in range(4):
                    pass
                k.tt("dve", t1[:, :, :], p[:, 0:512].rearrange("p (n t) -> p n t", n=4),
                     bsb[:, g, :].rearrange("p (o t) -> p o t", o=1).to_broadcast([128, 4, 128]), ALU.add,
                     [p, bsb], [t1])
                k.tt("pool", co[:, g, :], t1[:, :, :].rearrange("p n t -> p (n t)"), ub_[:, g, :], ALU.mult,
                     [t1, ub_], [co])
            k.dma(C.mixT[0:512, t0:t0 + TT].rearrange("(g p) t -> p g t", p=128), co[:, :, :],
                  [co], [C.mixT_r[t]], sem=co)
            do = dno[t % 2]
            for grp in ((0, 1, 2, 3), (4, 5, 6, 7), (8, 9, 10, 11)):
                for c in grp:
                    p = ps[pi % 6]; pi += 1
                    u, y = cu[c % 2], cy[c % 2]
                    proj_fm(k, p, w_in, xn, 1024 + c * 128, TT, 8, [w_in, xn])
                    k.copy("pool", u[:, 0:3], chist[:, c, :], [chist], [u])
                    k.copy("act", u[:, 3:3 + TT], p[:, 0:TT], [p], [u])
                    k.act(y[:, :], p[:, 0:TT], AF.Copy, [p, dcw], [y], scale=dcw[:, c, 3:4])
                    k.copy("pool", chist[:, c, :], u[:, TT:TT + 3], [u], [chist])
                    for kk in range(3):
                        k.stt("dve", y[:, :], u[:, kk:kk + TT], dcw[:, c, kk:kk + 1], y[:, :], ALU.mult, ALU.add,
                              [u, dcw, y], [y])
                    if c >= 8:
                        k.act(do[:, c, :], y[:, :], AF.Silu, [y], [do])
                    else:
                        k.act(dq4[:, c % 4, :], y[:, :], AF.Silu, [y], [dq4])
                if grp[0] >= 8:
                    continue
                k.act(dsq4[:, :, :], dq4[:, :, :], AF.Square, [dq4], [dsq4])
                for c in grp:
                    p2 = ps[pi % 6]; pi += 1
                    k.mm(p2[:, 0:TT], C.ones_bf[:, :], dsq4[:, c % 4, :], True, True, [C.ones_bf, dsq4], [p2])
                    k.act(rn4[:, c % 4, :], p2[:, 0:TT], AF.Ln, [p2, C.eps_col], [rn4], bias=C.eps_col[:, 0:1])
                k.act(rn4[:, :, :], rn4[:, :, :], AF.Exp, [rn4, qsc, zero_c], [rn4], scale=-0.5,
                      bias=(qsc if grp[0] < 4 else zero_c)[:, 0:1])
                k.tt("dve", do[:, grp[0]:grp[0] + 4, :], dq4[:, :, :], rn4[:, :, :], ALU.mult, [dq4, rn4], [do])
            k.dma(C.dnT[:, t0:t0 + TT].rearrange("(c p) t -> p c t", p=128), do[:, :, :], [do], [C.dnT_r[t]], sem=do)
            go, bo = gto[t % 2], bao[t % 2]
            for n in range(4):
                p = ps[pi % 6]; pi += 1
                for kc in range(8):
                    k.mm(p[:, 0:512], xn[:, kc, n * 128:(n + 1) * 128], w_in[:, kc, 2560:3072], kc == 0, kc == 7,
                         [xn, w_in], [p])
                k.act(go[:, n, :], p[:, 0:512], AF.Silu, [p], [go])
                p = ps[pi % 6]; pi += 1
                for kc in range(8):
                    k.mm(p[:, 0:8], xn[:, kc, n * 128:(n + 1) * 128], w_in[:, kc, 3072:3080], kc == 0, kc == 7,
                         [xn, w_in], [p])
                k.copy("dve", bo[:, n, :], p[:, 0:8], [p], [bo])
            k.dma(C.gtok[t0:t0 + TT, :].rearrange("(n p) c -> p n c", p=128), go[:, :, :], [go], [C.gtok_r[t]], sem=go)
            k.dma(C.batok[t0:t0 + TT, :].rearrange("(n p) c -> p n c", p=128), bo[:, :, :], [bo], [C.batok_r[t]], sem=bo)
    k.S.end_phase()


def phase_deltanet(k, C, S):
    NCH = S // 128
    with ExitStack() as ctx:
        tri = k.sb([128, 128], F32, "tri", ctx)
        k.dma(tri[:, :], C.tri_le, [], [tri])
        causT = k.sb([128, 128], F32, "causT", ctx)
        k.dma(causT[:, :], C.causT_neg, [], [causT])
        strT = k.sb([128, 128], F32, "strT", ctx)
        k.dma(strT[:, :], C.strictT01, [], [strT])
        onesf = k.sb([128, 128], F32, "onesf", ctx)
        k.memset("pool", onesf[:, :], 1.0, [onesf])
        one_c = k.sb([128, 1], F32, "one_c", ctx)
        k.memset("pool", one_c[:, :], 1.0, [one_c])
        dtb = k.sb([128, 4], F32, "dtb", ctx)
        k.dma(dtb[:, :], C.dn_dt_bias, [], [dtb])
        nexpA = k.sb([128, 4], F32, "nexpA", ctx)
        k.dma(nexpA[:, :], C.dn_a_log, [], [nexpA])
        k.act(nexpA[:, :], nexpA[:, :], AF.Exp, [nexpA], [nexpA])
        k.ts("dve", nexpA[:, :], nexpA[:, :], -1.0, None, ALU.mult, None, [nexpA], [nexpA])
        ngb = k.sb([128, 128], F32, "ngb", ctx)
        k.dma(ngb[:, :], C.dn_norm_g, [], [ngb])

        St = k.sb([128, 4, 128], F32, "St", ctx)
        Sb = k.sb([128, 4, 128], BF16, "Sb", ctx)
        k.memset("pool", St[:, :, :], 0.0, [St])
        k.memset("pool", Sb[:, :, :], 0.0, [Sb])
        lvl = k.sb([128, 8, 128], BF16, "lvl", ctx)
        k.dma(lvl[:, :, :], C.lvlmask, [], [lvl])
        NB_ = 4
        BETA, G, GC, GLAST, E, F_, GL, NGC, BE, TMP = range(10)

        def mk():
            b = Ctx()
            b.dn = k.sb([128, 12, 128], BF16, "dn", ctx)
            b.gt = k.sb([128, 512], BF16, "gt", ctx)
            b.ba = k.sb([128, 8], F32, "ba", ctx)
            b.sc = k.sb([128, 10, 4], F32, "sc", ctx)
            b.dg = [k.sb([128, 4, 128], F32, "dg", ctx) for _ in range(3)]
            b.kbT = k.sb([128, 4, 128], BF16, "kbT", ctx)
            b.qeT = k.sb([128, 4, 128], BF16, "qeT", ctx)
            b.X = k.sb([128, 4, 128], F32, "X", ctx)
            b.DcT = k.sb([128, 4, 128], F32, "DcT", ctx)
            b.DcsT = k.sb([128, 4, 128], F32, "DcsT", ctx)
            b.tok = k.sb([128, 8, 128], BF16, "tok", ctx)
            b.rhs0 = k.sb([128, 4, 256], BF16, "rhs0", ctx)
            b.y = k.sb([128, 4, 256], BF16, "y", ctx)
            b.kd = k.sb([128, 4, 128], BF16, "kd", ctx)
            b.qkT = k.sb([128, 4, 128], BF16, "qkT", ctx)
            b.A = k.sb([128, 4, 128], BF16, "A", ctx)
            b.AT = k.sb([128, 4, 128], BF16, "AT", ctx)
            b.Tm = [k.sb([128, 4, 128], BF16, "Tm", ctx) for _ in range(2)]
            b.TTm = [k.sb([128, 4, 128], BF16, "TTm", ctx) for _ in range(2)]
            b.Am = k.sb([128, 4, 128], BF16, "Am", ctx)
            b.AmT = [k.sb([128, 4, 128], BF16, "AmT", ctx) for _ in range(2)]
            b.Um = k.sb([128, 4, 128], BF16, "Um", ctx)
            b.wT = k.sb([128, 4, 128], BF16, "wT", ctx)
            b.vnew = k.sb([128, 4, 128], BF16, "vnew", ctx)
            b.osb = k.sb([128, 4, 128], F32, "osb", ctx)
            b.junk = k.sb([128, 128], F32, "junk", ctx)
            b.ss = k.sb([128, 4], F32, "ss", ctx)
            b.dtk = k.sb([128, 4, 128], BF16, "dtk", ctx)
            b.doT = k.sb([128, 4, 128], BF16, "doT", ctx)
            return b

        BUFS = [mk() for _ in range(NB_)]
        ps = C.ps + [C.ps_norm]
        st_ = {"pi": 0}

        def nxt():
            p = ps[st_["pi"] % 7]
            st_["pi"] += 1
            return p

        def load(n):
            b = BUFS[n % NB_]
            tl = (n * 128) // TT
            k.dma(b.dn[:, :, :], C.dnT[:, n * 128:(n + 1) * 128].rearrange("(c p) t -> p c t", p=128),
                  [C.dnT_r[tl]], [b.dn])
            k.dma(b.gt[:, :], C.gtok[n * 128:(n + 1) * 128, :], [C.gtok_r[tl]], [b.gt])
            k.dma(b.ba[:, :], C.batok[n * 128:(n + 1) * 128, :], [C.batok_r[tl]], [b.ba])

        ident4 = C.ident_f[:, :].rearrange("p (o t) -> p o t", o=1).to_broadcast([128, 4, 128])
        m4 = lambda a: a[:, :].rearrange("p (o t) -> p o t", o=1).to_broadcast([128, 4, 128])
        v4 = lambda p_, w=128: p_[:, 0:4 * w].rearrange("p (h t) -> p h t", h=4)
        lm4 = lambda l: lvl[:, l, :].rearrange("p (o t) -> p o t", o=1).to_broadcast([128, 4, 128])
        identb4 = C.ident_bf[:, :].rearrange("p (o t) -> p o t", o=1).to_broadcast([128, 4, 128])

        def chunk(n):
            b = BUFS[n % NB_]
            sc, d, g_, b_ = b.sc, b.dn, b.gt, b.ba

            def bcol(j):
                return sc[:, j, :].rearrange("p (h o) -> p h o", o=1).to_broadcast([128, 4, 128])
            k.act(sc[:, BETA, :], b_[:, 0:4], AF.Exp, [b_], [sc], scale=-1.0)
            k.ts("dve", sc[:, BETA, :], sc[:, BETA, :], 1.0, None, ALU.add, None, [sc], [sc])
            k.op("dve", lambda e: e.reciprocal(sc[:, BETA, :], sc[:, BETA, :]), [sc], [sc])
            k.tt("dve", sc[:, TMP, :], b_[:, 4:8], dtb[:, :], ALU.add, [b_, dtb], [sc])
            k.act(sc[:, TMP, :], sc[:, TMP, :], AF.Exp, [sc], [sc])
            k.act(sc[:, TMP, :], sc[:, TMP, :], AF.Ln, [sc, one_c], [sc], bias=one_c[:, 0:1])
            k.tt("dve", sc[:, G, :], sc[:, TMP, :], nexpA[:, :], ALU.mult, [sc, nexpA], [sc])
            p = nxt()
            k.mm(p[:, 0:4], tri[:, :], sc[:, G, :], True, True, [tri, sc], [p])
            k.mm(p[:, 4:8], onesf[:, :], sc[:, G, :], True, True, [onesf, sc], [p])
            k.copy("dve", sc[:, GC:GLAST + 1, :], p[:, 0:8].rearrange("p (a h) -> p a h", a=2), [p], [sc])
            yield
            k.act(sc[:, E, :], sc[:, GC, :], AF.Exp, [sc], [sc])
            k.tt("dve", sc[:, TMP, :], sc[:, GLAST, :], sc[:, GC, :], ALU.subtract, [sc], [sc])
            k.act(sc[:, F_, :], sc[:, TMP, :], AF.Exp, [sc], [sc])
            k.act(sc[:, GL, :], sc[:, GLAST, :], AF.Exp, [sc], [sc])
            k.ts("dve", sc[:, NGC, :], sc[:, GC, :], -1.0, None, ALU.mult, None, [sc], [sc])
            k.tt("dve", sc[:, BE, :], sc[:, BETA, :], sc[:, E, :], ALU.mult, [sc], [sc])
            for j in range(8):
                k.tr(C.ps_bf[:, j * 128:(j + 1) * 128], d[:, 4 + j, :], C.ident_bf[:, :], [d, C.ident_bf], [C.ps_bf])
            k.copy("act", b.tok[:, :, :], C.ps_bf[:, 0:1024].rearrange("p (j t) -> p j t", j=8), [C.ps_bf], [b.tok])
            yield
            pB, pE, pG = nxt(), nxt(), nxt()
            for dgi, (col, pp) in enumerate(((BETA, pB), (E, pE), (GC, pG))):
                k.tt("pool" if dgi == 1 else "dve", b.dg[dgi][:, :, :], ident4, bcol(col), ALU.mult,
                     [C.ident_f, sc], [b.dg[dgi]])
                k.mm(pp[:, 0:512], onesf[:, :], b.dg[dgi][:, :, :].rearrange("p h t -> p (h t)"), True, True,
                     [onesf, b.dg[dgi]], [pp])
            k.tt("dve", b.kbT[:, :, :], d[:, 4:8, :], v4(pB), ALU.mult, [d, pB], [b.kbT])
            k.tt("dve", b.qeT[:, :, :], d[:, 0:4, :], v4(pE), ALU.mult, [d, pE], [b.qeT])
            k.tt("dve", b.X[:, :, :], v4(pG), m4(causT), ALU.add, [pG, causT], [b.X])
            for h in range(4):
                k.ts("dve", b.rhs0[:, h, 0:128], b.tok[:, 4 + h, :], sc[:, BETA, h:h + 1], None, ALU.mult, None,
                     [b.tok, sc], [b.rhs0])
                k.act(b.rhs0[:, h, 128:256], b.tok[:, h, :], AF.Copy, [b.tok, sc], [b.rhs0], scale=sc[:, BE, h:h + 1])
                k.act(b.kd[:, h, :], b.tok[:, h, :], AF.Copy, [b.tok, sc], [b.kd], scale=sc[:, F_, h:h + 1])
            yield
            for h in range(4):
                k.act(b.DcT[:, h, :], b.X[:, h, :], AF.Exp, [b.X, sc], [b.DcT], bias=sc[:, NGC, h:h + 1])
            k.tt("pool", b.DcsT[:, :, :], b.DcT[:, :, :], m4(strT), ALU.mult, [b.DcT, strT], [b.DcsT])
            pQ, pK = nxt(), nxt()
            for h in range(4):
                k.mm(pQ[:, h * 128:(h + 1) * 128], d[:, 4 + h, :], d[:, h, :], True, True, [d], [pQ])
                k.mm(pK[:, h * 128:(h + 1) * 128], d[:, 4 + h, :], b.kbT[:, h, :], True, True, [d, b.kbT], [pK])
            k.tt("dve", b.qkT[:, :, :], v4(pQ), b.DcT[:, :, :], ALU.mult, [pQ, b.DcT], [b.qkT])
            k.tt("dve", b.AT[:, :, :], v4(pK), b.DcsT[:, :, :], ALU.mult, [pK, b.DcsT], [b.AT])
            yield
            for h in range(4):
                k.tr(C.ps_bf[:, h * 128:(h + 1) * 128], b.AT[:, h, :], C.ident_bf[:, :], [b.AT, C.ident_bf], [C.ps_bf])
            k.copy("act", b.A[:, :, :], C.ps_bf[:, 0:512].rearrange("p (h t) -> p h t", h=4), [C.ps_bf], [b.A])
            Tc, TTc = b.Tm[0], b.TTm[0]
            k.tt("pool", b.AmT[0][:, :, :], b.AT[:, :, :], lm4(0), ALU.mult, [b.AT, lvl], [b.AmT[0]])
            k.tt("dve", TTc[:, :, :], identb4, b.AmT[0][:, :, :], ALU.subtract, [C.ident_bf, b.AmT[0]], [TTc])
            k.tt("pool", b.Am[:, :, :], b.A[:, :, :], lm4(7), ALU.mult, [b.A, lvl], [b.Am])
            k.tt("dve", Tc[:, :, :], identb4, b.Am[:, :, :], ALU.subtract, [C.ident_bf, b.Am], [Tc])
            yield
            for l in range(1, 7):
                Tn, TTn = b.Tm[l % 2], b.TTm[l % 2]
                amt = b.AmT[l % 2]
                k.tt("pool", amt[:, :, :], b.AT[:, :, :], lm4(l), ALU.mult, [b.AT, lvl], [amt])
                pU = nxt()
                for h in range(4):
                    k.mm(pU[:, h * 128:(h + 1) * 128], amt[:, h, :], Tc[:, h, :], True, True, [amt, Tc], [pU])
                k.copy("act", b.Um[:, :, :], v4(pU), [pU], [b.Um])
                pVT = nxt()
                for h in range(4):
                    k.mm(pVT[:, h * 128:(h + 1) * 128], b.Um[:, h, :], TTc[:, h, :], True, True, [b.Um, TTc], [pVT])
                if l < 6:
                    pV = nxt()
                    for h in range(4):
                        k.mm(pV[:, h * 128:(h + 1) * 128], TTc[:, h, :], b.Um[:, h, :], True, True, [TTc, b.Um], [pV])
                k.tt("dve", TTn[:, :, :], TTc[:, :, :], v4(pVT), ALU.subtract, [TTc, pVT], [TTn])
                if l < 6:
                    k.tt("dve", Tn[:, :, :], Tc[:, :, :], v4(pV), ALU.subtract, [Tc, pV], [Tn])
                Tc, TTc = Tn, TTn
                yield
            pY0, pY1 = nxt(), nxt()
            for h in range(4):
                py = (pY0, pY1)[h // 2]
                k.mm(py[:, (h % 2) * 256:(h % 2 + 1) * 256], TTc[:, h, :], b.rhs0[:, h, :], True, True,
                     [TTc, b.rhs0], [py])
            ycur = b.y
            k.copy("act", ycur[:, 0:2, :], pY0[:, 0:512].rearrange("p (h t) -> p h t", h=2), [pY0], [ycur])
            k.copy("dve", ycur[:, 2:4, :], pY1[:, 0:512].rearrange("p (h t) -> p h t", h=2), [pY1], [ycur])
            for h in range(4):
                k.tr(C.ps_bf[:, h * 128:(h + 1) * 128], ycur[:, h, 128:256], C.ident_bf[:, :], [ycur, C.ident_bf],
                     [C.ps_bf])
            k.copy("act", b.wT[:, :, :], C.ps_bf[:, 0:512].rearrange("p (h t) -> p h t", h=4), [C.ps_bf], [b.wT])
            yield
            p1 = nxt()
            for h in range(4):
                k.mm(p1[:, h * 128:(h + 1) * 128], b.wT[:, h, :], Sb[:, h, :], True, True, [b.wT, Sb], [p1])
            k.tt("dve", b.vnew[:, :, :], ycur[:, :, 0:128], v4(p1), ALU.subtract, [ycur, p1], [b.vnew])
            p2, p3 = nxt(), nxt()
            for h in range(4):
                k.mm(p3[:, h * 128:(h + 1) * 128], b.kd[:, h, :], b.vnew[:, h, :], True, True, [b.kd, b.vnew], [p3])
            for h in range(4):
                k.mm(p2[:, h * 128:(h + 1) * 128], b.qeT[:, h, :], Sb[:, h, :], True, False, [b.qeT, Sb], [p2])
                k.mm(p2[:, h * 128:(h + 1) * 128], b.qkT[:, h, :], b.vnew[:, h, :], False, True, [b.qkT, b.vnew], [p2])
            for h in range(4):
                k.stt("dve", St[:, h, :], St[:, h, :], sc[:, GL, h:h + 1], p3[:, h * 128:(h + 1) * 128],
                      ALU.mult, ALU.add, [St, sc, p3], [St])
            k.copy("pool", Sb[:, :, :], St[:, :, :], [St], [Sb])
            k.copy("act", b.osb[:, :, :], v4(p2), [p2], [b.osb])
            yield
            k.memset("pool", b.ss[:, :], 0.0, [b.ss])
            for h in range(4):
                k.act(b.junk[:, :], b.osb[:, h, :], AF.Square, [b.osb], [b.junk, b.ss], accum_out=b.ss[:, h:h + 1])
            k.act(b.ss[:, :], b.ss[:, :], AF.Ln, [b.ss, C.eps_col], [b.ss], bias=C.eps_col[:, 0:1], scale=1.0 / 128)
            k.act(b.ss[:, :], b.ss[:, :], AF.Exp, [b.ss], [b.ss], scale=-0.5)
            for h in range(4):
                k.stt("dve", b.osb[:, h, :], b.osb[:, h, :], b.ss[:, h:h + 1], ngb[:, :], ALU.mult, ALU.mult,
                      [b.osb, b.ss, ngb], [b.osb])
            k.tt("pool", b.dtk[:, :, :], b.osb[:, :, :], g_[:, :].rearrange("p (h t) -> p h t", h=4), ALU.mult,
                 [b.osb, g_], [b.dtk])
            yield
            for h in range(4):
                k.tr(C.ps_bf[:, h * 128:(h + 1) * 128], b.dtk[:, h, :], C.ident_bf[:, :], [b.dtk, C.ident_bf], [C.ps_bf])
            k.copy("act", b.doT[:, :, :], C.ps_bf[:, 0:512].rearrange("p (h t) -> p h t", h=4), [C.ps_bf], [b.doT])
            k.dma(C.mixT[512:1024, n * 128:(n + 1) * 128].rearrange("(h p) t -> p h t", p=128), b.doT[:, :, :],
                  [b.doT], [C.mixT_r[(n * 128) // TT]], sem=b.doT)
            if n + NB_ < NCH:
                load(n + NB_)

        for n in range(min(NB_, NCH)):
            load(n)
        run_pipelined(chunk, NCH, 5, maxlive=NB_)
    k.S.end_phase()


SEQ = 8192
N_ACTIVE = 2


def kernel(**inputs):
    S = SEQ
    nc = build(S)
    consts = host_consts(S)
    in_maps = [core_inputs(inputs, b, S, consts) for b in range(N_ACTIVE)]
    res = run_bass_kernel_spmd(nc, in_maps, core_ids=list(range(N_ACTIVE)))
    out = np.stack([np.ascontiguousarray(res.results[b]["outT"].T) for b in range(N_ACTIVE)], axis=0)
    return out.astype(np.float32)
```

```python
import numpy as np
import ml_dtypes
from contextlib import ExitStack
import concourse.bass as bass
import concourse.mybir as mybir
from concourse.bass_utils import run_bass_kernel_spmd

F32 = mybir.dt.float32
BF16 = mybir.dt.bfloat16
AF = mybir.ActivationFunctionType
ALU = mybir.AluOpType
AX = mybir.AxisListType

D = 1024
NMEM = 256
EPS = 1e-6
DFF = 2816
NEG = -30000.0


class Res:
    __slots__ = ("name", "writers", "readers", "dsem")

    def __init__(self, name):
        self.name = name
        self.writers = []
        self.readers = []
        self.dsem = None


class Op:
    __slots__ = ("eng", "fn", "deps", "signal", "count", "dtok", "idx", "after", "dur", "gidx",
                 "nun", "rdy", "fin", "users")

    def __init__(self, eng, fn):
        self.eng = eng
        self.fn = fn
        self.deps = []
        self.signal = False
        self.count = 0
        self.dtok = None
        self.idx = 0
        self.after = []
        self.dur = 300.0


ENGS = ("pe", "act", "dve", "pool", "sp")


class Sched:
    def __init__(self, nc, n_dma_sems=40):
        self.nc = nc
        self.ops = {e: [] for e in ENGS}
        self.n_dma_sems = n_dma_sems
        self.dma_counts = [0] * n_dma_sems
        self.n_sp_sems = n_dma_sems - 8
        self.free_dsems = list(range(self.n_sp_sems))
        self.free_dsems_pool = list(range(self.n_sp_sems, n_dma_sems))
        self.phase_res = []
        self.all_res = []
        self.reorder_on = True
        self.seg_flags = []

    def res(self, name):
        r = Res(name)
        self.all_res.append(r)
        return r

    def _add(self, eng, fn, reads, writes, acc):
        o = Op(eng, fn)
        o.idx = len(self.ops[eng])
        deps = []
        for r in reads:
            deps.extend(r.writers)
        for w in writes:
            deps.extend(w.readers)
            if not acc:
                deps.extend(w.writers)
            elif acc == 'dma':
                deps.extend(t for t in w.writers if t[0] != 'D')
            else:
                for t in w.writers:
                    if t[0] == 'E' and t[1].eng == eng:
                        o.after.append(t[1])
                    else:
                        deps.append(t)
        o.deps = deps
        self.ops[eng].append(o)
        return o

    def _commit(self, tok, reads, writes):
        for r in reads:
            r.readers.append(tok)
        for w in writes:
            if w.readers:
                w.writers = [tok]
                w.readers = []
            else:
                w.writers.append(tok)
                if len(w.writers) > 64:
                    w.writers = w.writers[-64:]

    def op(self, eng, fn, reads=(), writes=(), acc=False):
        o = self._add(eng, fn, reads, writes, acc)
        self._commit(('E', o), reads, writes)
        return o

    def dma(self, eng, out, in_, reads=(), writes=(), sem_res=None):
        sr = sem_res if sem_res is not None else writes[0]
        if sr.dsem is None:
            sr.dsem = (self.free_dsems_pool if eng == "pool" else self.free_dsems).pop(0)
        o = self._add(eng, lambda e, out=out, in_=in_: e.dma_start(out=out, in_=in_), reads, writes, 'dma')
        self.dma_counts[sr.dsem] += 16
        o.dtok = (sr.dsem, self.dma_counts[sr.dsem])
        self._commit(('D', sr.dsem, o.dtok[1]), reads, writes)
        return o

    def end_phase(self):
        toks = []
        for e in ENGS:
            if self.ops[e]:
                last = self.ops[e][-1]
                if last.dtok is None:
                    toks.append(('E', last))
        for i in range(self.n_dma_sems):
            if self.dma_counts[i] > 0:
                toks.append(('D', i, self.dma_counts[i]))
        self.seg_flags.append(self.reorder_on)
        for e in ENGS:
            o = Op(e, None)
            o.idx = len(self.ops[e])
            o.deps = list(toks)
            self.ops[e].append(o)
        for r in self.all_res:
            r.dsem = None
            r.writers = []
            r.readers = []
        self.free_dsems = list(range(self.n_sp_sems))
        self.free_dsems_pool = list(range(self.n_sp_sems, self.n_dma_sems))

    def reorder(self, window=24):
        import heapq
        dma_of = {}
        for e in ENGS:
            for o in self.ops[e]:
                if o.dtok is not None:
                    dma_of[o.dtok] = o
        pos = {e: 0 for e in ENGS}
        new_ops = {e: [] for e in ENGS}
        segi = -1
        while any(pos[e] < len(self.ops[e]) for e in ENGS):
            segi += 1
            seg = {}
            for e in ENGS:
                lst = self.ops[e]
                i = pos[e]
                j = i
                while j < len(lst) and lst[j].fn is not None:
                    j += 1
                seg[e] = lst[i:j]
                pos[e] = j + 1 if j < len(lst) else j
                bar = lst[j] if j < len(lst) else None
                seg[e + "_bar"] = bar
            if segi < len(self.seg_flags) and not self.seg_flags[segi]:
                self._fix_barrier(seg, {e: seg[e] for e in ENGS})
                for e in ENGS:
                    new_ops[e].extend(seg[e])
                    if seg[e + "_bar"] is not None:
                        new_ops[e].append(seg[e + "_bar"])
                continue
            allops = [o for e in ENGS for o in seg[e]]
            inseg = set(id(o) for o in allops)
            for o in allops:
                o.users = []
                o.fin = None
            for o in allops:
                n = 0
                dl = []
                for t in o.deps:
                    d = t[1] if t[0] == 'E' else dma_of.get((t[1], t[2]))
                    if d is not None and id(d) in inseg and d.fn is not None:
                        dl.append(d)
                for d in o.after:
                    if id(d) in inseg:
                        dl.append(d)
                o.nun = len(dl)
                o.rdy = 0.0
                for d in dl:
                    d.users.append(o)
            free = {e: 0.0 for e in ENGS}
            pend = {e: list(seg[e]) for e in ENGS}
            head = {e: 0 for e in ENGS}
            issued = {e: [] for e in ENGS}
            remaining = len(allops)
            while remaining:
                best = None
                for e in ENGS:
                    lst = pend[e]
                    h = head[e]
                    while h < len(lst) and lst[h] is None:
                        h += 1
                    head[e] = h
                    if h >= len(lst):
                        continue
                    w = 1 if e == "sp" else window
                    cnt = 0
                    i = h
                    while i < len(lst) and cnt < w:
                        o = lst[i]
                        if o is not None:
                            cnt += 1
                            if o.nun == 0:
                                stt = o.rdy if o.rdy > free[e] else free[e]
                                if best is None or stt < best[0]:
                                    best = (stt, e, i)
                                if stt <= free[e]:
                                    break
                        i += 1
                if best is None:
                    raise RuntimeError("reorder: no schedulable op (cyclic deps?)")
                stt, e, i = best
                o = pend[e][i]
                pend[e][i] = None
                issued[e].append(o)
                remaining -= 1
                if o.dtok is not None:
                    free[e] = stt + 60.0
                    o.fin = stt + 2500.0 + o.dur
                else:
                    free[e] = stt + o.dur
                    o.fin = stt + o.dur + 150.0
                for u in o.users:
                    u.nun -= 1
                    if o in u.after and o not in [ (t[1] if t[0] == 'E' else None) for t in u.deps]:
                        r = stt
                    else:
                        r = o.fin
                    if r > u.rdy:
                        u.rdy = r
            self._fix_barrier(seg, issued)
            for e in ENGS:
                new_ops[e].extend(issued[e])
                if seg[e + "_bar"] is not None:
                    new_ops[e].append(seg[e + "_bar"])
        self.ops = new_ops

    @staticmethod
    def _fix_barrier(seg, order):
        etoks = []
        for e in ENGS:
            for o in reversed(order[e]):
                if o.fn is not None and o.dtok is None:
                    etoks.append(('E', o))
                    break
        for e in ENGS:
            bar = seg[e + "_bar"]
            if bar is not None:
                bar.deps = [t for t in bar.deps if t[0] == 'D'] + etoks

    def emit(self, esems, dsems):
        nc = self.nc
        if getattr(self, "do_reorder", True):
            self.reorder()
        for e in ENGS:
            for o in self.ops[e]:
                for t in o.deps:
                    if t[0] == 'E':
                        t[1].signal = True
        for e in ENGS:
            c = 0
            for o in self.ops[e]:
                if o.signal and o.fn is not None and o.dtok is None:
                    c += 1
                    o.count = c
                elif o.signal:
                    o.count = c
        sched = self

        def run(e_name, eng):
            seen = {}
            for o in sched.ops[e_name]:
                need = {}
                for t in o.deps:
                    if t[0] == 'E':
                        d = t[1]
                        if d.count == 0:
                            continue
                        key = ('E', d.eng)
                        val = d.count
                    else:
                        key = ('D', t[1])
                        val = t[2]
                    if need.get(key, 0) < val:
                        need[key] = val
                for key, val in need.items():
                    if seen.get(key, 0) >= val:
                        continue
                    seen[key] = val
                    sem = esems[key[1]] if key[0] == 'E' else dsems[key[1]]
                    eng.wait_ge(sem, val)
                if o.fn is None:
                    continue
                ins = o.fn(eng)
                if o.dtok is not None:
                    ins.then_inc(dsems[o.dtok[0]], 16)
                elif o.signal:
                    ins.then_inc(esems[e_name], 1)

        with nc.Block() as block:
            @block.tensor
            def _(eng):
                run("pe", eng)

            @block.scalar
            def _(eng):
                run("act", eng)

            @block.vector
            def _(eng):
                run("dve", eng)

            @block.gpsimd
            def _(eng):
                run("pool", eng)

            @block.sync
            def _(eng):
                run("sp", eng)


class Buf:
    def __init__(self, S, t, name):
        self.t = t
        self.r = S.res(name)

    def __getitem__(self, k):
        return self.t[k]


class K:
    def __init__(self, nc):
        self.nc = nc
        self.S = Sched(nc)
        self.stack = ExitStack()
        self.n = 0

    def sb(self, shape, dt, name=None, ctx=None):
        self.n += 1
        name = f"{name or 'sb'}_{self.n}"
        t = (ctx or self.stack).enter_context(self.nc.sbuf_tensor(name, list(shape), dt))
        return Buf(self.S, t, name)

    def psum(self, shape, dt=F32, name=None, ctx=None):
        self.n += 1
        name = f"{name or 'ps'}_{self.n}"
        t = (ctx or self.stack).enter_context(self.nc.psum_tensor(name, list(shape), dt))
        return Buf(self.S, t, name)

    def dram(self, shape, dt, name):
        t = self.nc.dram_tensor(name, list(shape), dt)
        b = Buf(self.S, t.ap(), name)
        return b

    def _rw(self, reads, writes):
        return [b.r for b in reads], [b.r for b in writes]

    def op(self, eng, fn, reads, writes, acc=False, dur=None):
        r, w = self._rw(reads, writes)
        o = self.S.op(eng, fn, r, w, acc)
        if dur is not None:
            o.dur = dur
        return o

    @staticmethod
    def _fsz(ap):
        n = 1
        for d in list(ap.shape)[1:]:
            n *= int(d)
        return n

    def dma(self, out, in_, reads, writes, eng="sp", sem=None):
        r, w = self._rw(reads, writes)
        return self.S.dma(eng, out, in_, r, w, sem.r if sem is not None else None)

    def mm(self, out, lhsT, rhs, start, stop, reads, writes):
        return self.op("pe", lambda e: e.matmul(out, lhsT, rhs, start=start, stop=stop), reads, writes, acc=True,
                       dur=70.0 + 0.45 * self._fsz(rhs) * (4 if rhs.dtype == F32 else 1))

    def tr(self, out, in_, ident, reads, writes):
        return self.op("pe", lambda e: e.transpose(out, in_, ident), reads, writes, acc=True, dur=130.0)

    def act(self, out, in_, func, reads, writes, bias=None, scale=None, accum_out=None, eng="act"):
        kw = {}
        if bias is not None:
            kw["bias"] = bias
        if scale is not None:
            kw["scale"] = scale
        if accum_out is not None:
            kw["accum_out"] = accum_out
        return self.op("act", lambda e: e.activation(out, in_, func, **kw), reads, writes,
                       dur=220.0 + 0.95 * self._fsz(out))

    def tt(self, eng, out, in0, in1, op, reads, writes):
        return self.op(eng, lambda e: e.tensor_tensor(out, in0, in1, op), reads, writes,
                       dur=(120.0 + 1.0 * self._fsz(out)) * (2.0 if eng == "pool" else 1.0))

    def ts(self, eng, out, in0, s1, s2, op0, op1, reads, writes):
        du = 120.0 + 0.9 * self._fsz(out)
        if op1 is None:
            return self.op(eng, lambda e: e.tensor_scalar(out, in0, s1, None, op0), reads, writes, dur=du)
        return self.op(eng, lambda e: e.tensor_scalar(out, in0, s1, s2, op0, op1), reads, writes, dur=du)

    def stt(self, eng, out, in0, scalar, in1, op0, op1, reads, writes):
        return self.op(eng, lambda e: e.scalar_tensor_tensor(out, in0, scalar, in1, op0, op1), reads, writes,
                       dur=120.0 + 1.4 * self._fsz(out))

    def copy(self, eng, out, in_, reads, writes):
        if eng == "act":
            return self.op("act", lambda e: e.copy(out, in_), reads, writes, dur=220.0 + 0.8 * self._fsz(out))
        return self.op(eng, lambda e: e.tensor_copy(out, in_), reads, writes,
                       dur=(120.0 + 0.7 * self._fsz(out)) * (2.0 if eng == "pool" else 1.0))

    def memset(self, eng, ap, val, writes):
        return self.op(eng, lambda e: e.memset(ap, val), [], writes)


TT = 512


class Ctx:
    pass


def load_w(k, ctx, src, kin, n, name, gain=None, stg=None, col0=0, w=None):
    kc_n = kin // 128
    if w is None:
        w = k.sb([128, kc_n, n], BF16, name, ctx)
    i = 0
    for kc in range(kc_n):
        for n0 in range(0, n, 2048):
            wd = min(2048, n - n0)
            s = stg[i % len(stg)]
            k.dma(s[:, 0:wd], src[kc * 128:(kc + 1) * 128, col0 + n0:col0 + n0 + wd], [], [s],
                  eng="sp")
            if gain is None:
                k.copy(("dve", "pool")[i % 2], w[:, kc, n0:n0 + wd], s[:, 0:wd], [s], [w])
            elif i % 2 == 0:
                k.ts("dve", w[:, kc, n0:n0 + wd], s[:, 0:wd], gain[:, kc:kc + 1], None, ALU.mult, None,
                     [s, gain], [w])
            else:
                k.act(w[:, kc, n0:n0 + wd], s[:, 0:wd], AF.Copy, [s, gain], [w], scale=gain[:, kc:kc + 1])
            i += 1
    return w


def rmsnorm(k, C, x, xn, T, kc_n=8, dim=D, nb=None, xoff=0):
    sq, r, ps = nb if nb is not None else (C.sq, C.rr, C.ps_norm)
    k.act(sq[:, 0:kc_n, 0:T], x[:, 0:kc_n, 0:T], AF.Square, [x], [sq])
    for kc in range(kc_n):
        k.mm(ps[:, 0:T], C.ones_bf[:, :], sq[:, kc, 0:T], kc == 0, kc == kc_n - 1, [sq, C.ones_bf], [ps])
    k.act(r[:, 0:T], ps[:, 0:T], AF.Ln, [ps, C.eps_col], [r], bias=C.eps_col[:, 0:1], scale=1.0 / dim)
    k.act(r[:, 0:T], r[:, 0:T], AF.Exp, [r], [r], scale=-0.5)
    k.tt("dve", xn[:, 0:kc_n, xoff:xoff + T], x[:, 0:kc_n, 0:T],
         r[:, 0:T].rearrange("p (o t) -> p o t", o=1).to_broadcast([128, kc_n, T]), ALU.mult, [x, r], [xn])


def run_pipelined(make_gen, n, lag, maxlive=2):
    gens = [make_gen(t) for t in range(n)]
    prog = [0] * n
    done = [False] * n
    first = 0
    while first < n:
        for t in range(first, n):
            if t > first and not (done[t - 1] or prog[t - 1] >= lag):
                break
            if t - maxlive >= 0 and not done[t - maxlive]:
                break
            if done[t]:
                continue
            try:
                next(gens[t])
                prog[t] += 1
            except StopIteration:
                done[t] = True
        while first < n and done[first]:
            first += 1


def proj_fm(k, ps, w, xn, m0, T, kc_n, reads, col=None):
    for kc in range(kc_n):
        k.mm(ps[:, 0:T], w[:, kc, m0:m0 + 128], xn[:, kc, 0:T], kc == 0, kc == kc_n - 1, reads, [ps])


def phase_l0_inproj(k, C, S):
    nc = k.nc
    NT = S // TT
    with ExitStack() as ctx:
        C.sq = k.sb([128, 8, TT], BF16, "sq", ctx)
        C.rr = k.sb([128, TT], F32, "rr", ctx)
        stg = [k.sb([128, 2048], F32, "stg", ctx) for _ in range(2)]
        gain = k.sb([128, 8], F32, "gain", ctx)
        k.dma(gain[:, :], C.norm_mix[0], [], [gain])
        w_in = load_w(k, ctx, C.ev_w_in, D, 2048, "w_in", gain=gain, stg=stg)
        pw_f = k.sb([128, 4, 128], F32, "pw_f", ctx)
        pw = k.sb([128, 4, 128], BF16, "pw", ctx)
        k.dma(pw_f[:, :, :], C.pool_w.rearrange("g c d -> c g d"), [], [pw_f])
        k.copy("dve", pw[:, :, :], pw_f[:, :, :], [pw_f], [pw])
        pscale = k.sb([128, 4], F32, "pscale", ctx)
        k.dma(pscale[:, :], C.pool_scale, [], [pscale])
        corr = k.sb([128, 4, 16], F32, "corr", ctx)
        k.dma(corr[:, :, :], C.pool_corr, [], [corr])

        xt = [k.sb([128, 8, TT], F32, "xt", ctx) for _ in range(2)]
        xn = k.sb([128, 8, TT], BF16, "xn", ctx)
        qk = [k.sb([128, 8, TT], BF16, "qk", ctx) for _ in range(2)]
        vt = [k.sb([128, 4, 8, 65], BF16, "vt", ctx) for _ in range(2)]
        for b in vt:
            k.memset("pool", b[:, :, :, :], 1.0, [b])
        pb = k.sb([128, 4, 16 + TT], F32, "pb", ctx)
        k.memset("pool", pb[:, :, :], 0.0, [pb])
        wa = k.sb([128, 16 + TT], F32, "wa", ctx)
        wb = k.sb([128, 16 + TT], F32, "wb", ctx)
        k.memset("pool", wa[:, :], 0.0, [wa])
        k.memset("pool", wb[:, :], 0.0, [wb])
        pooled = k.sb([128, 4, TT], BF16, "pooled", ctx)
        bout = [k.sb([128, 4, TT], BF16, "bout", ctx) for _ in range(2)]
        ksum = k.sb([128, 4, 2], F32, "ksum", ctx)
        ps = C.ps
        pi = 0

        def load_x(t):
            b = xt[t % 2]
            k.dma(b[:, :, :], C.xT[:, t * TT:(t + 1) * TT].rearrange("(c p) t -> p c t", p=128), [], [b])

        load_x(0)
        for t in range(NT):
            if t + 1 < NT:
                load_x(t + 1)
            x = xt[t % 2]
            t0 = t * TT
            rmsnorm(k, C, x, xn, TT)
            qkb = qk[t % 2]
            for m in range(8):
                p = ps[pi % 6]; pi += 1
                proj_fm(k, p, w_in, xn, m * 128, TT, 8, [w_in, xn])
                k.copy("act", qkb[:, m, :], p[:, 0:TT], [p], [qkb])
            k.op("dve", lambda e, qkb=qkb: e.tensor_reduce(
                ksum[:, :, :], qkb[:, 4:8, :].rearrange("p c (b t) -> p c b t", b=2), AX.X, ALU.add),
                [qkb], [ksum])
            k.ts("dve", C.kmean[:, :, 2 * t:2 * t + 2], ksum[:, :, :], 1.0 / 256, None, ALU.mult, None,
                 [ksum], [C.kmean])
            for c in range(4):
                for hp in range(2):
                    k.dma(C.qaugT[2 * c + hp, 0:64, t0:t0 + TT], qkb[hp * 64:(hp + 1) * 64, c, :],
                          [qkb], [C.qaugT_r[t]], sem=qkb)
            k.dma(C.kT[:, t0:t0 + TT].rearrange("(c p) t -> p c t", p=128), qkb[:, 4:8, :],
                  [qkb], [C.kT_r[t]], sem=qkb)
            vb = vt[t % 2]
            for sub in range(4):
                p = ps[pi % 6]; pi += 1
                for kc in range(8):
                    k.mm(p[:, 0:512], xn[:, kc, sub * 128:(sub + 1) * 128], w_in[:, kc, 1024:1536],
                         kc == 0, kc == 7, [w_in, xn], [p])
                k.copy("dve", vb[:, sub, :, 1:65], p[:, 0:512].rearrange("p (h d) -> p h d", h=8), [p], [vb])
            k.dma(C.vaug[t0:t0 + TT, :].rearrange("(s p) c -> p s c", p=128),
                  vb[:, :, :, :].rearrange("p s h d -> p s (h d)"), [vb], [C.vaug_r[t]], sem=vb)
            for g in range(4):
                p = ps[pi % 6]; pi += 1
                proj_fm(k, p, w_in, xn, 1536 + g * 128, TT, 8, [w_in, xn])
                k.copy("act", pb[:, g, 16:16 + TT], p[:, 0:TT], [p], [pb])
            W = 16 + TT
            for g in range(4):
                eng = ("dve", "pool")[g % 2]
                src = pb
                cur = pb[:, g, :]
                bufs = [wa, wb]
                for lvl in range(g + 1):
                    sh = 1 << lvl
                    dst = bufs[lvl % 2]
                    k.tt(eng, dst[:, sh:W], cur[:, sh:W], cur[:, 0:W - sh], ALU.add, [src], [dst])
                    src = dst
                    cur = dst[:, :]
                wnd = 2 << g
                if t == 0:
                    k.tt(eng, cur[:, 16:32], cur[:, 16:32], corr[:, g, :], ALU.mult, [src, corr], [src])
                k.stt("dve", pooled[:, g, :], cur[:, 16:W], 1.0 / wnd, pb[:, g, 16:W], ALU.mult, ALU.subtract,
                      [src, pb], [pooled])
            k.copy("pool", pb[:, :, 0:16], pb[:, :, TT:TT + 16], [pb], [pb])
            bo = bout[t % 2]
            for g in range(4):
                p = ps[pi % 6]; pi += 1
                k.mm(p[:, 0:TT], pw[:, g, :], pooled[:, g, :], True, True, [pw, pooled], [p])
                k.ts("dve", bo[:, g, :], p[:, 0:TT], pscale[:, g:g + 1], None, ALU.mult, None, [p, pscale], [bo])
            k.dma(C.mixT[512:1024, t0:t0 + TT].rearrange("(g p) t -> p g t", p=128), bo[:, :, :],
                  [bo], [C.mixT_r[t]], sem=bo)
    k.S.end_phase()


class RB:
    def __init__(self, S, name):
        self.r = S.res(name)


def host_consts(S):
    c = {}
    c["ident_bf"] = np.eye(128, dtype=np.float32).astype(ml_dtypes.bfloat16)
    c["ident_f"] = np.eye(128, dtype=np.float32)
    corr = np.ones((4, 16), np.float32)
    for g, w in enumerate((2, 4, 8, 16)):
        for t in range(16):
            corr[g, t] = w / min(t + 1, w)
    c["pool_corr"] = np.ascontiguousarray(np.broadcast_to(corr[None], (128, 4, 16)))
    vb = np.concatenate([np.zeros(32, np.float32), np.full(32, -1e30, np.float32)])
    c["validbias"] = np.ascontiguousarray(np.broadcast_to(vb[None], (128, 64)))
    khot = np.zeros((32, S), np.float32)
    for j in range(S // 256):
        khot[j, j * 256:(j + 1) * 256] = 30000.0
    c["khot"] = khot.astype(ml_dtypes.bfloat16)
    kk = np.arange(128)[:, None, None]
    jj = np.arange(2)[None, :, None]
    qq = np.arange(256)[None, None, :]
    c["causal01"] = ((jj * 128 + kk) <= qq).astype(np.float32).astype(ml_dtypes.bfloat16)
    a = np.arange(128)
    c["tril01"] = (a[None, :] <= a[:, None]).astype(np.float32)
    c["tri_le"] = (a[:, None] <= a[None, :]).astype(np.float32)
    c["causT_neg"] = np.where(a[None, :] >= a[:, None], 0.0, -1e4).astype(np.float32)
    c["strictT01"] = (a[None, :] > a[:, None]).astype(np.float32)
    lv = np.zeros((128, 8, 128), np.float32)
    ii, jj2 = a[:, None], a[None, :]
    for l in range(7):
        b = 1 << l
        m = ((ii // (2 * b)) == (jj2 // (2 * b))) & ((ii % (2 * b)) >= b) & ((jj2 % (2 * b)) < b)
        lv[:, l, :] = m.T
        if l == 0:
            lv[:, 7, :] = m
    c["lvlmask"] = lv.astype(ml_dtypes.bfloat16)
    return c


def build(S, phases=None, debug_out=()):
    nc = bass.Bass("TRN2", target_bir_lowering=False)
    k = K(nc)
    C = Ctx()
    NT = S // TT

    def ext(name, shape, dt=F32):
        return Buf(k.S, nc.dram_tensor(name, list(shape), dt, kind="ExternalInput").ap(), name)

    def scratch(name, shape, dt):
        kind = "ExternalOutput" if name in debug_out else "Internal"
        return nc.dram_tensor(name, list(shape), dt, kind=kind).ap()

    C.xT = ext("xT", [D, S]).t
    C.norm_mix = [ext(f"norm_mix{l}", [128, 8]).t for l in range(2)]
    C.ev_w_in = ext("ev_w_in", [D, 2048]).t
    C.pool_w = ext("pool_w", [4, 128, 128]).t
    C.pool_scale = ext("pool_scale", [128, 4]).t
    C.pool_corr = ext("pool_corr", [128, 4, 16]).t
    C.validbias = ext("validbias", [128, 64]).t
    C.khot = ext("khot", [32, S], BF16).t
    C.causal01 = ext("causal01", [128, 2, 256], BF16).t
    C.memT = ext("memT", [D, NMEM]).t
    C.mem_norm = ext("mem_norm", [128, 8]).t
    C.norm_xattn = [ext(f"norm_xattn{l}", [128, 8]).t for l in range(2)]
    C.norm_ffn = [ext(f"norm_ffn{l}", [128, 8]).t for l in range(2)]
    C.final_norm = ext("final_norm", [128, 8]).t
    C.ev_w_out = ext("ev_w_out", [D, D]).t
    C.od_w_out = ext("od_w_out", [D, D]).t
    C.xattn_wq = [ext(f"xattn_wq{l}", [D, D]).t for l in range(2)]
    C.xattn_wkv = [ext(f"xattn_wkv{l}", [D, 2 * D]).t for l in range(2)]
    C.xattn_wo = [ext(f"xattn_wo{l}", [D, D]).t for l in range(2)]
    C.ffn_w_up = [ext(f"ffn_w_up{l}", [D, 2 * DFF]).t for l in range(2)]
    C.ffn_conv = [ext(f"ffn_conv{l}", [128, 2 * DFF // 128, 3]).t for l in range(2)]
    C.ffn_w_down = [ext(f"ffn_w_down{l}", [DFF, D]).t for l in range(2)]
    C.od_w_in = ext("od_w_in", [D, 3080]).t
    C.sgu_w = ext("sgu_w", [4, 128, 128]).t
    C.sgu_ln_g = ext("sgu_ln_g", [128, 4]).t
    C.sgu_ln_b = ext("sgu_ln_b", [128, 4]).t
    C.sgu_b = ext("sgu_b", [128, 4, 128]).t
    C.dn_conv = ext("dn_conv", [128, 12, 4]).t
    C.dn_a_log = ext("dn_a_log", [128, 4]).t
    C.dn_dt_bias = ext("dn_dt_bias", [128, 4]).t
    C.dn_norm_g = ext("dn_norm_g", [128, 128]).t
    C.tril01 = ext("tril01", [128, 128]).t
    C.tri_le = ext("tri_le", [128, 128]).t
    C.causT_neg = ext("causT_neg", [128, 128]).t
    C.strictT01 = ext("strictT01", [128, 128]).t
    C.lvlmask = ext("lvlmask", [128, 8, 128], BF16).t
    ident_bf_d = ext("ident_bf", [128, 128], BF16).t
    ident_f_d = ext("ident_f", [128, 128]).t
    C.qaugT = scratch("qaugT", [8, 96, S], BF16)
    C.kT = scratch("kT", [512, S], BF16)
    C.vaug = scratch("vaug", [S, 520], BF16)
    C.mixT = scratch("mixT", [D, S], BF16)
    C.dnT = scratch("dnT", [1536, S], BF16)
    C.gtok = scratch("gtok", [S, 512], BF16)
    C.batok = scratch("batok", [S, 8], F32)
    C.hA = scratch("hA", [D, S], F32)
    C.hB = scratch("hB", [D, S], F32)
    C.outT = nc.dram_tensor("outT", [D, S], F32, kind="ExternalOutput").ap()
    for nm in ("qaugT", "kT", "vaug", "mixT", "hA", "hB", "xT", "outT", "dnT", "gtok", "batok"):
        setattr(C, nm + "_r", [RB(k.S, f"{nm}{t}") for t in range(NT)])

    with k.stack:
        C.ps = [k.psum([128, 512], F32, "ps") for _ in range(6)]
        C.ps_norm = k.psum([128, 512], F32, "psn")
        C.ps_bf = k.psum([128, 1024], BF16, "psbf")
        C.ones_bf = k.sb([128, 128], BF16, "ones")
        k.memset("pool", C.ones_bf[:, :], 1.0, [C.ones_bf])
        C.ident_bf = k.sb([128, 128], BF16, "identb")
        k.dma(C.ident_bf[:, :], ident_bf_d, [], [C.ident_bf])
        C.ident_f = k.sb([128, 128], F32, "identf")
        k.dma(C.ident_f[:, :], ident_f_d, [], [C.ident_f])
        C.eps_col = k.sb([128, 1], F32, "epsc")
        k.memset("pool", C.eps_col[:, :], EPS, [C.eps_col])
        C.kmean = k.sb([128, 4, 32], F32, "kmean")
        k.memset("pool", C.kmean[:, :, :], 0.0, [C.kmean])
        k.S.end_phase()

        if phases is None or 1 in phases:
            phase_l0_inproj(k, C, S)
        if phases is None or 2 in phases:
            phase_moba_gate(k, C, S)
        if phases is None or 22 in phases:
            phase_moba_attn(k, C, S)
        if phases is None or 3 in phases:
            phase_outproj_xattn(k, C, S, 0, C.ev_w_out, C.xT, C.xT_r, C.hA, C.hA_r)
        if phases is None or 4 in phases:
            phase_ffn(k, C, S, 0, C.hA, C.hA_r, C.hB, C.hB_r, False)
        if phases is None or 5 in phases:
            phase_l1_inproj(k, C, S)
        if phases is None or 6 in phases:
            phase_deltanet(k, C, S)
        if phases is None or 7 in phases:
            phase_outproj_xattn(k, C, S, 1, C.od_w_out, C.hB, C.hB_r, C.hA, C.hA_r)
        if phases is None or 8 in phases:
            phase_ffn(k, C, S, 1, C.hA, C.hA_r, C.outT, C.outT_r, True)

        esems = {}
        for e in ENGS:
            esems[e] = k.stack.enter_context(nc.semaphore(f"es_{e}"))
        dsems = [k.stack.enter_context(nc.semaphore(f"ds_{i}")) for i in range(k.S.n_dma_sems)]
        k.S.emit(esems, dsems)
    return nc


def phase_moba_gate(k, C, S):
    GS = 9
    NT = S // TT
    with ExitStack() as ctx:
        kmb = k.sb([128, 4, 64], BF16, "kmb", ctx)
        k.memset("pool", kmb[:, :, :], 0.0, [kmb])
        k.copy("dve", kmb[0:64, :, 0:32], C.kmean[0:64, :, :], [C.kmean], [kmb])
        k.copy("dve", kmb[64:128, :, 32:64], C.kmean[64:128, :, :], [C.kmean], [kmb])
        vbias = k.sb([128, 64], F32, "vbias", ctx)
        k.dma(vbias[:, :], C.validbias, [], [vbias])
        qc = [k.sb([128, 4, TT], BF16, "qc", ctx) for _ in range(2)]
        gsb = k.sb([128, 8, 32], F32, "gsb", ctx)
        top8 = k.sb([128, 8, 8], F32, "top8", ctx)
        sel = k.sb([128, 8, 32], F32, "sel", ctx)
        mb = k.sb([128, 8, 96], BF16, "mb", ctx)
        k.memset("pool", mb[:, :, :], 0.0, [mb])
        mrow = [k.sb([128, 8, TT], BF16, "mrow", ctx) for _ in range(2)]
        ps = C.ps

        def load_q(t):
            b = qc[t % 2]
            for h in range(8):
                k.dma(b[(h % 2) * 64:(h % 2) * 64 + 64, h // 2, :], C.qaugT[h, 0:64, t * TT:(t + 1) * TT],
                      [C.qaugT_r[t]], [b])

        load_q(0)
        for t in range(NT):
            if t + 1 < NT:
                load_q(t + 1)
            q = qc[t % 2]
            mr = mrow[t % 2]
            for sub in range(4):
                qb = (t * TT + sub * 128) // 256
                gp = ps[sub % 2]
                for c in range(4):
                    k.mm(gp[:, c * 64:(c + 1) * 64], q[:, c, sub * 128:(sub + 1) * 128],
                         kmb[:, c, :], True, True, [q, kmb], [gp])
                k.tt("dve", gsb[:, :, :], gp[:, 0:256].rearrange("p (h n) -> p h n", h=8),
                     vbias[:, 32 - qb:64 - qb].rearrange("p (o n) -> p o n", o=1).to_broadcast([128, 8, 32]),
                     ALU.add, [gp, vbias], [gsb])
                if GS <= 1:
                    continue
                for h in range(8):
                    k.op("dve", lambda e, h=h: e.max(top8[:, h, :], gsb[:, h, :]), [gsb], [top8], acc=True)
                k.tt("dve", sel[:, :, :], gsb[:, :, :], top8[:, :, 2:3].to_broadcast([128, 8, 32]), ALU.is_ge,
                     [gsb, top8], [sel])
                k.ts("dve", mb[:, :, 64:96], sel[:, :, :], -1.0, None, ALU.add, None, [sel], [mb])
                k.memset("dve", mb[:, :, 64 + qb:65 + qb], 0.0, [mb])
                if GS <= 2:
                    continue
                for half in range(2):
                    mp = ps[2 + (sub * 2 + half) % 4]
                    for hh in range(4):
                        h = half * 4 + hh
                        k.mm(mp[0:96, hh * 128:(hh + 1) * 128], mb[:, h, :], C.ident_bf[:, :], True, True,
                             [mb, C.ident_bf], [mp])
                    k.copy("act", mr[64:96, half * 4:half * 4 + 4, sub * 128:(sub + 1) * 128],
                           mp[64:96, 0:512].rearrange("p (h q) -> p h q", h=4), [mp], [mr])
            if GS <= 3:
                continue
            k.dma(C.qaugT[:, 64:96, t * TT:(t + 1) * TT].rearrange("h r t -> r h t"), mr[64:96, :, :],
                  [mr], [C.qaugT_r[t]], sem=mr)
    k.S.end_phase()


def phase_moba_attn(k, C, S):
    NB = S // 256
    NKT = S // 128
    with ExitStack() as ctx:
        vsb = k.sb([128, NKT, 520], BF16, "vsb", ctx)
        for i in range(0, NKT, 8):
            n = min(8, NKT - i)
            k.dma(vsb[:, i:i + n, :], C.vaug[i * 128:(i + n) * 128, :].rearrange("(t p) c -> p t c", p=128),
                  [C.vaug_r[(i * 128) // TT], C.vaug_r[min(S // TT - 1, ((i + n) * 128 - 1) // TT)]], [vsb])
        kaug = [k.sb([96, S], BF16, "kaug", ctx) for _ in range(2)]
        for b in kaug:
            k.dma(b[64:96, :], C.khot, [], [b])
        caus = k.sb([128, 2, 256], BF16, "caus", ctx)
        k.dma(caus[:, :, :], C.causal01, [], [caus])
        onesf = k.sb([128, 65], F32, "onesf", ctx)
        k.memset("pool", onesf[:, :], 1.0, [onesf])
        pt = [k.sb([128, 2, 256], BF16, "pt", ctx) for _ in range(4)]
        rden = [k.sb([1, 256], F32, "rden", ctx) for _ in range(2)]
        bcs = k.sb([65, 256], F32, "bcs", ctx)
        ao = [k.sb([65, 512], BF16, "ao", ctx) for _ in range(2)]
        sps = C.ps[0:3]
        ops = [C.ps[3], C.ps[4], C.ps_norm]
        bps = C.ps[5]
        all_k = [C.kT_r[t] for t in range(S // TT)]
        NQ = 4
        qa = [k.sb([96, 256], BF16, "qa", ctx) for _ in range(NQ)]
        groups = [(h, qb) for h in range(8) for qb in range(NB)]
        tasks = []
        for gi, (h, qb) in enumerate(groups):
            for pr in range(qb + 1):
                tasks.append((gi, h, qb, pr))

        def load_q(gi):
            h, qb = groups[gi]
            b = qa[gi % NQ]
            k.dma(b[:, :], C.qaugT[h, :, qb * 256:(qb + 1) * 256], [C.qaugT_r[(qb * 256) // TT]], [b])

        def load_k(h):
            kb = kaug[h % 2]
            k.dma(kb[0:64, :], C.kT[h * 64:(h + 1) * 64, :], all_k, [kb])

        def emit_S(i):
            gi, h, qb, pr = tasks[i]
            if pr == 0:
                if qb == 0 and h + 1 < 8:
                    load_k(h + 1)
                if gi + NQ - 1 < len(groups):
                    load_q(gi + NQ - 1)
            kb = kaug[h % 2]
            q = qa[gi % NQ]
            sp = sps[i % 3]
            p = pt[i % 4]
            for j in range(2):
                kt = 2 * pr + j
                k.mm(sp[:, j * 256:(j + 1) * 256], kb[0:96, kt * 128:(kt + 1) * 128], q[0:96, :],
                     True, True, [kb, q], [sp])
            k.act(p[:, :, :], sp[:, 0:512].rearrange("p (j q) -> p j q", j=2), AF.Exp, [sp], [p], scale=0.125)
            if pr == qb:
                k.tt("pool", p[:, :, :], p[:, :, :], caus[:, :, :], ALU.mult, [p, caus], [p])

        deferred = []

        def emit_PV(i):
            gi, h, qb, pr = tasks[i]
            p = pt[i % 4]
            op_ = ops[gi % 3]
            for j in range(2):
                kt = 2 * pr + j
                k.mm(op_[0:65, 0:256], vsb[:, kt, h * 65:(h + 1) * 65], p[:, j, :],
                     pr == 0 and j == 0, pr == qb and j == 1, [vsb, p], [op_])
            if pr == qb:
                rd = rden[gi % 2]
                k.op("dve", lambda e, op_=op_, rd=rd: e.reciprocal(rd[0:1, :], op_[0:1, 0:256]), [op_], [rd])

                def tail(gi=gi, h=h, qb=qb, op_=op_, rd=rd):
                    k.mm(bps[0:65, 0:256], onesf[0:1, 0:65], rd[0:1, :], True, True, [onesf, rd], [bps])
                    k.copy("act", bcs[:, :], bps[0:65, 0:256], [bps], [bcs])
                    a = ao[(qb // 2) % 2]
                    k.tt("dve", a[:, (qb % 2) * 256:(qb % 2 + 1) * 256], op_[0:65, 0:256], bcs[:, :], ALU.mult,
                         [op_, bcs], [a])
                    if qb % 2 == 1:
                        t = qb // 2
                        k.dma(C.mixT[h * 64:(h + 1) * 64, t * TT:(t + 1) * TT], a[1:65, :], [a], [C.mixT_r[t]], sem=a)
                deferred.append((i + 2, tail))

        load_k(0)
        for gi in range(min(NQ - 1, len(groups))):
            load_q(gi)
        n = len(tasks)
        emit_S(0)
        if n > 1:
            emit_S(1)
        for i in range(n):
            while deferred and deferred[0][0] <= i:
                deferred.pop(0)[1]()
            if i + 2 < n:
                emit_S(i + 2)
            emit_PV(i)
        while deferred:
            deferred.pop(0)[1]()
    k.S.end_phase()


def phase_outproj_xattn(k, C, S, l, w_out_d, hin, hin_r, hout, hout_r):
    NT = S // TT
    k.S.reorder_on = False
    with ExitStack() as ctx:
        C.sq = k.sb([128, 8, TT], BF16, "sq", ctx)
        C.rr = k.sb([128, TT], F32, "rr", ctx)
        kmemT = k.sb([128, 8, NMEM], BF16, "kmemT", ctx)
        vmem = k.sb([128, 2, D], BF16, "vmem", ctx)
        with ExitStack() as c2:
            stg = [k.sb([128, 2048], F32, "stg", c2) for _ in range(4)]
            gm = k.sb([128, 8], F32, "gm", c2)
            k.dma(gm[:, :], C.mem_norm, [], [gm])
            wkv = load_w(k, c2, C.xattn_wkv[l], D, 2048, "wkv", gain=gm, stg=stg)
            mt_ = k.sb([128, 8, NMEM], F32, "memt", c2)
            k.dma(mt_[:, :, :], C.memT.rearrange("(c p) m -> p c m", p=128), [], [mt_])
            memn = k.sb([128, 8, NMEM], BF16, "memn", c2)
            rmsnorm(k, C, mt_, memn, NMEM)
            for dc in range(8):
                p = C.ps[dc % 6]
                proj_fm(k, p, wkv, memn, dc * 128, NMEM, 8, [wkv, memn])
                k.copy("act", kmemT[:, dc, :], p[:, 0:NMEM], [p], [kmemT])
            for mt in range(2):
                for half in range(2):
                    p = C.ps[(mt * 2 + half) % 6]
                    for kc in range(8):
                        k.mm(p[:, 0:512], memn[:, kc, mt * 128:(mt + 1) * 128],
                             wkv[:, kc, 1024 + half * 512:1536 + half * 512], kc == 0, kc == 7, [memn, wkv], [p])
                    k.copy("dve", vmem[:, mt, half * 512:(half + 1) * 512], p[:, 0:512], [p], [vmem])
        k.S.end_phase()
        stg = [k.sb([128, 2048], F32, "stg", ctx) for _ in range(2)]
        gq = k.sb([128, 8], F32, "gq", ctx)
        k.dma(gq[:, :], C.norm_xattn[l], [], [gq])
        w_out = load_w(k, ctx, w_out_d, D, D, "w_out", stg=stg)
        wq = load_w(k, ctx, C.xattn_wq[l], D, D, "wq", gain=gq, stg=stg)
        wo = load_w(k, ctx, C.xattn_wo[l], D, D, "wo", stg=stg)
        T3 = 256
        NB3 = 4
        NT3 = S // T3
        xt = [k.sb([128, 8, T3], F32, "xt", ctx) for _ in range(NB3)]
        mx = [k.sb([128, 8, T3], BF16, "mx", ctx) for _ in range(NB3)]
        xn = [k.sb([128, 8, T3], BF16, "xn", ctx) for _ in range(NB3)]
        qx = [k.sb([128, 8, T3], BF16, "qx", ctx) for _ in range(NB3)]
        ox = [k.sb([128, 8, T3], BF16, "ox", ctx) for _ in range(NB3)]
        pt = [[k.sb([128, 2, T3], BF16, "pt", ctx) for _ in range(2)] for _ in range(NB3)]
        rec = [[k.sb([128, T3], F32, "rec", ctx) for _ in range(2)] for _ in range(NB3)]
        rrb = [k.sb([128, T3], F32, "rr3", ctx) for _ in range(NB3)]
        ps = C.ps
        st = {"pi": 0}

        def nps():
            p = ps[st["pi"] % 6]
            st["pi"] += 1
            return p

        def load(t):
            b = t % NB3
            tr = (t * T3) // TT
            k.dma(xt[b][:, :, :], hin[:, t * T3:(t + 1) * T3].rearrange("(c p) t -> p c t", p=128),
                  [hin_r[tr]], [xt[b]])
            k.dma(mx[b][:, :, :], C.mixT[:, t * T3:(t + 1) * T3].rearrange("(c p) t -> p c t", p=128),
                  [C.mixT_r[tr]], [mx[b]])

        def tile(t):
            b = t % NB3
            x, m_, xn_, qx_, ox_ = xt[b], mx[b], xn[b], qx[b], ox[b]
            for m in range(8):
                p = nps()
                proj_fm(k, p, w_out, m_, m * 128, T3, 8, [w_out, m_])
                k.tt("dve", x[:, m, :], x[:, m, :], p[:, 0:T3], ALU.add, [x, p], [x])
            yield
            rmsnorm(k, C, x, xn_, T3, nb=(C.sq, rrb[b], C.ps_norm))
            yield
            for m in range(8):
                p = nps()
                proj_fm(k, p, wq, xn_, m * 128, T3, 8, [wq, xn_])
                k.copy("act", qx_[:, m, :], p[:, 0:T3], [p], [qx_])
                if m == 3:
                    yield
            yield
            for hd in range(4):
                ptb = pt[b][hd % 2]
                rc = rec[b][hd % 2]
                for mt in range(2):
                    p = nps()
                    for dd in range(2):
                        dc = 2 * hd + dd
                        k.mm(p[:, 0:T3], kmemT[:, dc, mt * 128:(mt + 1) * 128], qx_[:, dc, :], dd == 0, dd == 1,
                             [kmemT, qx_], [p])
                    k.act(ptb[:, mt, :], p[:, 0:T3], AF.Exp, [p], [ptb], scale=1.0 / 16)
                p = nps()
                for mt in range(2):
                    k.mm(p[:, 0:T3], C.ones_bf[:, :], ptb[:, mt, :], mt == 0, mt == 1, [C.ones_bf, ptb], [p])
                k.op("dve", lambda e, rc=rc, p=p: e.reciprocal(rc[:, :], p[:, 0:T3]), [p], [rc])
                for dd in range(2):
                    p = nps()
                    c0 = hd * 256 + dd * 128
                    for mt in range(2):
                        k.mm(p[:, 0:T3], vmem[:, mt, c0:c0 + 128], ptb[:, mt, :], mt == 0, mt == 1, [vmem, ptb], [p])
                    k.tt("dve", ox_[:, 2 * hd + dd, :], p[:, 0:T3], rc[:, :], ALU.mult, [p, rc], [ox_])
                yield
            for m in range(8):
                p = nps()
                proj_fm(k, p, wo, ox_, m * 128, T3, 8, [wo, ox_])
                k.tt("dve", x[:, m, :], x[:, m, :], p[:, 0:T3], ALU.add, [x, p], [x])
                if m == 3:
                    yield
            k.dma(hout[:, t * T3:(t + 1) * T3].rearrange("(c p) t -> p c t", p=128), x[:, :, :],
                  [x], [hout_r[(t * T3) // TT]], sem=x)
            if t + NB3 < NT3:
                load(t + NB3)

        for t in range(min(NB3, NT3)):
            load(t)
        run_pipelined(tile, NT3, 4, maxlive=NB3)
    k.S.end_phase()
    k.S.reorder_on = True


TF = 256


def phase_ffn(k, C, S, l, hin, hin_r, hout, hout_r, final):
    k.S.reorder_on = False
    NTF = S // TF
    NJ = DFF // 128
    with ExitStack() as ctx:
        gf = k.sb([128, 8], F32, "gf", ctx)
        k.dma(gf[:, :], C.norm_ffn[l], [], [gf])
        w_up = k.sb([128, 8, 2 * DFF], BF16, "w_up", ctx)
        w_dn = k.sb([128, NJ, D], BF16, "w_dn", ctx)
        with ExitStack() as c2:
            stg = [k.sb([128, 2048], F32, "stg", c2) for _ in range(4)]
            load_w(k, ctx, C.ffn_w_up[l], D, 2 * DFF, "w_up", gain=gf, stg=stg, w=w_up)
            load_w(k, ctx, C.ffn_w_down[l], DFF, D, "w_dn", stg=stg, w=w_dn)
        k.S.end_phase()
        cw = k.sb([128, 2 * NJ, 3], F32, "cw", ctx)
        k.dma(cw[:, :, :], C.ffn_conv[l], [], [cw])
        gfin = k.sb([128, 8], F32, "gfin", ctx)
        k.dma(gfin[:, :], C.final_norm, [], [gfin])
        xt = [k.sb([128, 8, TF], F32, "xt", ctx) for _ in range(2)]
        xn = [k.sb([128, 8, 2 + TF], BF16, "xn", ctx) for _ in range(2)]
        for b in xn:
            k.memset("pool", b[:, :, :], 0.0, [b])
        sqb = [k.sb([128, 8, TF], BF16, "sq", ctx)] * 2
        rrb = [k.sb([128, TF], F32, "rr", ctx) for _ in range(2)]
        actb = [[k.sb([128, TF], BF16, "actb", ctx) for _ in range(NJ)] for _ in range(2)]
        yb = [[k.sb([128, TF], F32, "yb", ctx) for _ in range(4)] for _ in range(2)]
        sg = [[k.sb([128, TF], F32, "sg", ctx) for _ in range(2)] for _ in range(2)]
        ps = C.ps
        st = {"pi": 0}
        W2 = 2 + TF

        def nps():
            p = ps[st["pi"] % 6]
            st["pi"] += 1
            return p

        def load(t):
            k.dma(xt[t % 2][:, :, :], hin[:, t * TF:(t + 1) * TF].rearrange("(c p) t -> p c t", p=128),
                  [hin_r[(t * TF) // TT]], [xt[t % 2]])

        def tile(t):
            x, xn_, ab = xt[t % 2], xn[t % 2], actb[t % 2]
            nb = (sqb[t % 2], rrb[t % 2], C.ps_norm)
            if t > 0:
                k.copy("pool", xn_[:, :, 0:2], xn[(t - 1) % 2][:, :, TF:TF + 2], [xn[(t - 1) % 2]], [xn_])
            rmsnorm(k, C, x, xn_, TF, nb=nb, xoff=2)
            yield
            pend = None
            for j in range(NJ):
                ys = []
                for which in range(2):
                    c = which * NJ + j
                    p = nps()
                    y = yb[t % 2][(2 * j + which) % 4]
                    for kc in range(8):
                        k.mm(p[:, 0:W2], w_up[:, kc, c * 128:(c + 1) * 128], xn_[:, kc, 0:W2], kc == 0, kc == 7,
                             [w_up, xn_], [p])
                    k.act(y[:, :], p[:, 2:W2], AF.Copy, [p, cw], [y], scale=cw[:, c, 2:3])
                    k.stt("dve", y[:, :], p[:, 1:1 + TF], cw[:, c, 1:2], y[:, :], ALU.mult, ALU.add, [p, cw, y], [y])
                    k.stt("dve", y[:, :], p[:, 0:TF], cw[:, c, 0:1], y[:, :], ALU.mult, ALU.add, [p, cw, y], [y])
                    ys.append(y)
                if pend is not None:
                    pend()

                def fin(j=j, ys=ys):
                    s_ = sg[t % 2][j % 2]
                    k.act(s_[:, :], ys[0][:, :], AF.Silu, [ys[0]], [s_])
                    k.tt("pool", ab[j][:, :], s_[:, :], ys[1][:, :], ALU.mult, [s_, ys[1]], [ab[j]])
                pend = fin
                if j % 2 == 1:
                    yield
            pend()
            yield
            for m in range(8):
                p = nps()
                for j in range(NJ):
                    k.mm(p[:, 0:TF], w_dn[:, j, m * 128:(m + 1) * 128], ab[j][:, :], j == 0, j == NJ - 1,
                         [w_dn, ab[j]], [p])
                k.tt("dve", x[:, m, :], x[:, m, :], p[:, 0:TF], ALU.add, [x, p], [x])
                if m == 3:
                    yield
            if final:
                fin_x = k.sb
                rmsnorm(k, C, x, xfin, TF, nb=nb)
                for m in range(8):
                    k.stt("dve", x[:, m, :], x[:, m, :], gfin[:, m:m + 1], nb[1][:, 0:TF], ALU.mult, ALU.mult,
                          [x, gfin, nb[1]], [x])
            k.dma(hout[:, t * TF:(t + 1) * TF].rearrange("(c p) t -> p c t", p=128), x[:, :, :],
                  [x], [hout_r[(t * TF) // TT]], sem=x)
            if t + 2 < NTF:
                load(t + 2)

        xfin = k.sb([128, 8, TF], BF16, "xfin", ctx) if final else None
        load(0)
        load(1)
        run_pipelined(tile, NTF, 6)
    k.S.end_phase()
    k.S.reorder_on = True


def _v8(v):
    return np.ascontiguousarray(np.asarray(v, np.float32).reshape(-1, 128).T)


def core_inputs(inp, b, S, consts):
    f = lambda a: np.ascontiguousarray(np.asarray(a, np.float32))
    d = dict(consts)
    d["xT"] = f(np.asarray(inp["x"][b]).T)
    d["memT"] = f(np.asarray(inp["mem"][b]).T)
    d["mem_norm"] = _v8(inp["mem_norm"])
    d["final_norm"] = _v8(inp["final_norm"])
    for l in range(2):
        d[f"norm_mix{l}"] = _v8(inp["norm_mix"][l])
        d[f"norm_xattn{l}"] = _v8(inp["norm_xattn"][l])
        d[f"norm_ffn{l}"] = _v8(inp["norm_ffn"][l])
        d[f"xattn_wq{l}"] = f(inp["xattn_wq"][l])
        d[f"xattn_wkv{l}"] = f(inp["xattn_wkv"][l])
        d[f"xattn_wo{l}"] = f(inp["xattn_wo"][l])
        d[f"ffn_w_up{l}"] = f(inp["ffn_w_up"][l])
        d[f"ffn_w_down{l}"] = f(inp["ffn_w_down"][l])
        d[f"ffn_conv{l}"] = f(np.asarray(inp["ffn_conv"][l]).reshape(3, -1, 128).transpose(2, 1, 0))
    d["ev_w_in"] = f(inp["ev_w_in"][0])
    d["ev_w_out"] = f(inp["ev_w_out"][0])
    d["od_w_out"] = f(inp["od_w_out"][0])
    d["pool_w"] = f(inp["pool_w"][0])
    d["pool_scale"] = _v8(inp["pool_scale"][0])
    d["od_w_in"] = f(inp["od_w_in"][0])
    d["sgu_w"] = f(inp["sgu_w"][0])
    d["sgu_ln_g"] = _v8(inp["sgu_ln_g"][0])
    d["sgu_ln_b"] = _v8(inp["sgu_ln_b"][0])
    d["sgu_b"] = f(np.broadcast_to(np.asarray(inp["sgu_b"][0])[None], (128, 4, 128)))
    d["dn_conv"] = f(np.asarray(inp["dn_conv"][0]).reshape(4, 12, 128).transpose(2, 1, 0))
    d["dn_a_log"] = f(np.broadcast_to(np.asarray(inp["dn_a_log"][0])[None], (128, 4)))
    d["dn_dt_bias"] = f(np.broadcast_to(np.asarray(inp["dn_dt_bias"][0])[None], (128, 4)))
    d["dn_norm_g"] = f(np.broadcast_to(np.asarray(inp["dn_norm_g"][0])[None], (128, 128)))
    return d


def phase_l1_inproj(k, C, S):
    NT = S // TT
    GC1 = 1.5957691216057308
    with ExitStack() as ctx:
        C.sq = k.sb([128, 8, TT], BF16, "sq", ctx)
        C.rr = k.sb([128, TT], F32, "rr", ctx)
        gain = k.sb([128, 8], F32, "gain", ctx)
        k.dma(gain[:, :], C.norm_mix[1], [], [gain])
        w_in = k.sb([128, 8, 3080], BF16, "w_in1", ctx)
        wsT = k.sb([128, 4, 128], BF16, "wsT", ctx)
        with ExitStack() as c2:
            stg = [k.sb([128, 2048], F32, "stg", c2) for _ in range(4)]
            load_w(k, ctx, C.od_w_in, D, 3080, "w_in1", gain=gain, stg=stg, w=w_in)
            wsf = k.sb([128, 4, 128], F32, "wsf", c2)
            k.dma(wsf[:, :, :], C.sgu_w.rearrange("g t s -> t g s"), [], [wsf])
            tril = k.sb([128, 128], F32, "tril", c2)
            k.dma(tril[:, :], C.tril01, [], [tril])
            wsm = k.sb([128, 4, 128], BF16, "wsm", c2)
            k.tt("dve", wsm[:, :, :], wsf[:, :, :],
                 tril[:, :].rearrange("p (o s) -> p o s", o=1).to_broadcast([128, 4, 128]), ALU.mult,
                 [wsf, tril], [wsm])
            for g in range(4):
                k.tr(C.ps_bf[:, g * 128:(g + 1) * 128], wsm[:, g, :], C.ident_bf[:, :], [wsm, C.ident_bf], [C.ps_bf])
            k.copy("act", wsT[:, :, :], C.ps_bf[:, 0:512].rearrange("p (g t) -> p g t", g=4), [C.ps_bf], [wsT])
        k.S.end_phase()
        lng = k.sb([128, 4], F32, "lng", ctx)
        lnb = k.sb([128, 4], F32, "lnb", ctx)
        k.dma(lng[:, :], C.sgu_ln_g, [], [lng])
        k.dma(lnb[:, :], C.sgu_ln_b, [], [lnb])
        bsb = k.sb([128, 4, 128], F32, "bsb", ctx)
        k.dma(bsb[:, :, :], C.sgu_b, [], [bsb])
        dcw = k.sb([128, 12, 4], F32, "dcw", ctx)
        k.dma(dcw[:, :, :], C.dn_conv, [], [dcw])
        qsc = k.sb([128, 1], F32, "qsc", ctx)
        k.memset("pool", qsc[:, :], float(np.log(128.0 ** -0.5)), [qsc])
        zero_c = k.sb([128, 1], F32, "zero_c", ctx)
        k.memset("pool", zero_c[:, :], 0.0, [zero_c])

        xt = [k.sb([128, 8, TT], F32, "xt", ctx) for _ in range(2)]
        xn = k.sb([128, 8, TT], BF16, "xn", ctx)
        x2 = [k.sb([128, TT], F32, "x2", ctx) for _ in range(2)]
        sgm = [k.sb([128, TT], F32, "sgm", ctx) for _ in range(2)]
        ub_ = k.sb([128, 4, TT], F32, "u", ctx)
        v_ = k.sb([128, 4, TT], F32, "v", ctx)
        vb = k.sb([128, 4, TT], BF16, "vb", ctx)
        vsq = k.sb([128, 4, TT], BF16, "vsq", ctx)
        mean = k.sb([128, TT], F32, "mean", ctx)
        m2 = k.sb([128, TT], F32, "m2", ctx)
        rstd = k.sb([128, TT], F32, "rstd", ctx)
        vn = k.sb([128, 4, TT], BF16, "vn", ctx)
        vtok = k.sb([128, 4, 4, 128], BF16, "vtok", ctx)
        t1 = k.sb([128, 4, 128], F32, "t1", ctx)
        cout = [k.sb([128, 4, TT], BF16, "cout", ctx) for _ in range(2)]
        cu = [k.sb([128, 3 + TT], F32, "cu", ctx) for _ in range(2)]
        cy = [k.sb([128, TT], F32, "cy", ctx) for _ in range(2)]
        chist = k.sb([128, 12, 3], F32, "chist", ctx)
        k.memset("pool", chist[:, :, :], 0.0, [chist])
        dq4 = k.sb([128, 4, TT], F32, "dq4", ctx)
        dsq4 = k.sb([128, 4, TT], BF16, "dsq4", ctx)
        rn4 = k.sb([128, 4, TT], F32, "rn4", ctx)
        dno = [k.sb([128, 12, TT], BF16, "dno", ctx)] * 2
        gto = [k.sb([128, 4, 512], BF16, "gto", ctx)] * 2
        bao = [k.sb([128, 4, 8], F32, "bao", ctx) for _ in range(2)]
        ps = C.ps
        pi = 0

        def load(t):
            k.dma(xt[t % 2][:, :, :], C.hB[:, t * TT:(t + 1) * TT].rearrange("(c p) t -> p c t", p=128),
                  [C.hB_r[t]], [xt[t % 2]])

        load(0)
        for t in range(NT):
            if t + 1 < NT:
                load(t + 1)
            x = xt[t % 2]
            t0 = t * TT
            rmsnorm(k, C, x, xn, TT)
            for m in range(8):
                p = ps[pi % 6]; pi += 1
                a2, sg = x2[m % 2], sgm[m % 2]
                proj_fm(k, p, w_in, xn, m * 128, TT, 8, [w_in, xn])
                k.act(a2[:, :], p[:, 0:TT], AF.Square, [p], [a2])
                k.ts("dve", a2[:, :], a2[:, :], 0.044715, 1.0, ALU.mult, ALU.add, [a2], [a2])
                k.tt("dve", a2[:, :], a2[:, :], p[:, 0:TT], ALU.mult, [a2, p], [a2])
                k.act(sg[:, :], a2[:, :], AF.Sigmoid, [a2], [sg], scale=GC1)
                dst = ub_ if m < 4 else v_
                k.tt("dve", dst[:, m % 4, :], sg[:, :], p[:, 0:TT], ALU.mult, [sg, p], [dst])
            k.copy("pool", vb[:, :, :], v_[:, :, :], [v_], [vb])
            k.act(vsq[:, :, :], v_[:, :, :], AF.Square, [v_], [vsq])
            pm = ps[pi % 6]; pi += 1
            pq = ps[pi % 6]; pi += 1
            for c in range(4):
                k.mm(pm[:, 0:TT], C.ones_bf[:, :], vb[:, c, :], c == 0, c == 3, [C.ones_bf, vb], [pm])
            for c in range(4):
                k.mm(pq[:, 0:TT], C.ones_bf[:, :], vsq[:, c, :], c == 0, c == 3, [C.ones_bf, vsq], [pq])
            k.ts("dve", mean[:, :], pm[:, 0:TT], 1.0 / 512, None, ALU.mult, None, [pm], [mean])
            k.tt("pool", m2[:, :], mean[:, :], mean[:, :], ALU.mult, [mean], [m2])
            k.stt("dve", m2[:, :], pq[:, 0:TT], 1.0 / 512, m2[:, :], ALU.mult, ALU.subtract, [pq, m2], [m2])
            k.act(rstd[:, :], m2[:, :], AF.Ln, [m2, C.eps_col], [rstd], bias=C.eps_col[:, 0:1])
            k.act(rstd[:, :], rstd[:, :], AF.Exp, [rstd], [rstd], scale=-0.5)
            bc = lambda a: a[:, :].rearrange("p (o t) -> p o t", o=1).to_broadcast([128, 4, TT])
            k.tt("dve", v_[:, :, :], v_[:, :, :], bc(mean), ALU.subtract, [v_, mean], [v_])
            k.tt("pool", v_[:, :, :], v_[:, :, :], bc(rstd), ALU.mult, [v_, rstd], [v_])
            for c in range(4):
                k.ts("dve", vn[:, c, :], v_[:, c, :], lng[:, c:c + 1], lnb[:, c:c + 1], ALU.mult, ALU.add,
                     [v_, lng, lnb], [vn])
            for half in range(2):
                for nn in range(2):
                    n = half * 2 + nn
                    for c in range(4):
                        k.tr(C.ps_bf[:, (nn * 4 + c) * 128:(nn * 4 + c + 1) * 128], vn[:, c, n * 128:(n + 1) * 128],
                             C.ident_bf[:, :], [vn, C.ident_bf], [C.ps_bf])
                k.copy("act", vtok[:, half * 2:half * 2 + 2, :, :],
                       C.ps_bf[:, 0:1024].rearrange("p (n c s) -> p n c s", n=2, c=4), [C.ps_bf], [vtok])
            co = cout[t % 2]
            for g in range(4):
                p = ps[pi % 6]; pi += 1
                for n in range(4):
                    k.mm(p[:, n * 128:(n + 1) * 128], vtok[:, n, g, :], wsT[:, g, :], True, True, [vtok, wsT], [p])
                for n in range(4):
                    pass
                k.tt("dve", t1[:, :, :], p[:, 0:512].rearrange("p (n t) -> p n t", n=4),
                     bsb[:, g, :].rearrange("p (o t) -> p o t", o=1).to_broadcast([128, 4, 128]), ALU.add,
                     [p, bsb], [t1])
                k.tt("pool", co[:, g, :], t1[:, :, :].rearrange("p n t -> p (n t)"), ub_[:, g, :], ALU.mult,
                     [t1, ub_], [co])
            k.dma(C.mixT[0:512, t0:t0 + TT].rearrange("(g p) t -> p g t", p=128), co[:, :, :],
                  [co], [C.mixT_r[t]], sem=co)
            do = dno[t % 2]
            for grp in ((0, 1, 2, 3), (4, 5, 6, 7), (8, 9, 10, 11)):
                for c in grp:
                    p = ps[pi % 6]; pi += 1
                    u, y = cu[c % 2], cy[c % 2]
                    proj_fm(k, p, w_in, xn, 1024 + c * 128, TT, 8, [w_in, xn])
                    k.copy("pool", u[:, 0:3], chist[:, c, :], [chist], [u])
                    k.copy("act", u[:, 3:3 + TT], p[:, 0:TT], [p], [u])
                    k.act(y[:, :], p[:, 0:TT], AF.Copy, [p, dcw], [y], scale=dcw[:, c, 3:4])
                    k.copy("pool", chist[:, c, :], u[:, TT:TT + 3], [u], [chist])
                    for kk in range(3):
                        k.stt("dve", y[:, :], u[:, kk:kk + TT], dcw[:, c, kk:kk + 1], y[:, :], ALU.mult, ALU.add,
                              [u, dcw, y], [y])
                    if c >= 8:
                        k.act(do[:, c, :], y[:, :], AF.Silu, [y], [do])
                    else:
                        k.act(dq4[:, c % 4, :], y[:, :], AF.Silu, [y], [dq4])
                if grp[0] >= 8:
                    continue
                k.act(dsq4[:, :, :], dq4[:, :, :], AF.Square, [dq4], [dsq4])
                for c in grp:
                    p2 = ps[pi % 6]; pi += 1
                    k.mm(p2[:, 0:TT], C.ones_bf[:, :], dsq4[:, c % 4, :], True, True, [C.ones_bf, dsq4], [p2])
                    k.act(rn4[:, c % 4, :], p2[:, 0:TT], AF.Ln, [p2, C.eps_col], [rn4], bias=C.eps_col[:, 0:1])
                k.act(rn4[:, :, :], rn4[:, :, :], AF.Exp, [rn4, qsc, zero_c], [rn4], scale=-0.5,
                      bias=(qsc if grp[0] < 4 else zero_c)[:, 0:1])
                k.tt("dve", do[:, grp[0]:grp[0] + 4, :], dq4[:, :, :], rn4[:, :, :], ALU.mult, [dq4, rn4], [do])
            k.dma(C.dnT[:, t0:t0 + TT].rearrange("(c p) t -> p c t", p=128), do[:, :, :], [do], [C.dnT_r[t]], sem=do)
            go, bo = gto[t % 2], bao[t % 2]
            for n in range(4):
                p = ps[pi % 6]; pi += 1
                for kc in range(8):
                    k.mm(p[:, 0:512], xn[:, kc, n * 128:(n + 1) * 128], w_in[:, kc, 2560:3072], kc == 0, kc == 7,
                         [xn, w_in], [p])
                k.act(go[:, n, :], p[:, 0:512], AF.Silu, [p], [go])
                p = ps[pi % 6]; pi += 1
                for kc in range(8):
                    k.mm(p[:, 0:8], xn[:, kc, n * 128:(n + 1) * 128], w_in[:, kc, 3072:3080], kc == 0, kc == 7,
                         [xn, w_in], [p])
                k.copy("dve", bo[:, n, :], p[:, 0:8], [p], [bo])
            k.dma(C.gtok[t0:t0 + TT, :].rearrange("(n p) c -> p n c", p=128), go[:, :, :], [go], [C.gtok_r[t]], sem=go)
            k.dma(C.batok[t0:t0 + TT, :].rearrange("(n p) c -> p n c", p=128), bo[:, :, :], [bo], [C.batok_r[t]], sem=bo)
    k.S.end_phase()


def phase_deltanet(k, C, S):
    NCH = S // 128
    with ExitStack() as ctx:
        tri = k.sb([128, 128], F32, "tri", ctx)
        k.dma(tri[:, :], C.tri_le, [], [tri])
        causT = k.sb([128, 128], F32, "causT", ctx)
        k.dma(causT[:, :], C.causT_neg, [], [causT])
        strT = k.sb([128, 128], F32, "strT", ctx)
        k.dma(strT[:, :], C.strictT01, [], [strT])
        onesf = k.sb([128, 128], F32, "onesf", ctx)
        k.memset("pool", onesf[:, :], 1.0, [onesf])
        one_c = k.sb([128, 1], F32, "one_c", ctx)
        k.memset("pool", one_c[:, :], 1.0, [one_c])
        dtb = k.sb([128, 4], F32, "dtb", ctx)
        k.dma(dtb[:, :], C.dn_dt_bias, [], [dtb])
        nexpA = k.sb([128, 4], F32, "nexpA", ctx)
        k.dma(nexpA[:, :], C.dn_a_log, [], [nexpA])
        k.act(nexpA[:, :], nexpA[:, :], AF.Exp, [nexpA], [nexpA])
        k.ts("dve", nexpA[:, :], nexpA[:, :], -1.0, None, ALU.mult, None, [nexpA], [nexpA])
        ngb = k.sb([128, 128], F32, "ngb", ctx)
        k.dma(ngb[:, :], C.dn_norm_g, [], [ngb])

        St = k.sb([128, 4, 128], F32, "St", ctx)
        Sb = k.sb([128, 4, 128], BF16, "Sb", ctx)
        k.memset("pool", St[:, :, :], 0.0, [St])
        k.memset("pool", Sb[:, :, :], 0.0, [Sb])
        lvl = k.sb([128, 8, 128], BF16, "lvl", ctx)
        k.dma(lvl[:, :, :], C.lvlmask, [], [lvl])
        NB_ = 4
        BETA, G, GC, GLAST, E, F_, GL, NGC, BE, TMP = range(10)

        def mk():
            b = Ctx()
            b.dn = k.sb([128, 12, 128], BF16, "dn", ctx)
            b.gt = k.sb([128, 512], BF16, "gt", ctx)
            b.ba = k.sb([128, 8], F32, "ba", ctx)
            b.sc = k.sb([128, 10, 4], F32, "sc", ctx)
            b.dg = [k.sb([128, 4, 128], F32, "dg", ctx) for _ in range(3)]
            b.kbT = k.sb([128, 4, 128], BF16, "kbT", ctx)
            b.qeT = k.sb([128, 4, 128], BF16, "qeT", ctx)
            b.X = k.sb([128, 4, 128], F32, "X", ctx)
            b.DcT = k.sb([128, 4, 128], F32, "DcT", ctx)
            b.DcsT = k.sb([128, 4, 128], F32, "DcsT", ctx)
            b.tok = k.sb([128, 8, 128], BF16, "tok", ctx)
            b.rhs0 = k.sb([128, 4, 256], BF16, "rhs0", ctx)
            b.y = k.sb([128, 4, 256], BF16, "y", ctx)
            b.kd = k.sb([128, 4, 128], BF16, "kd", ctx)
            b.qkT = k.sb([128, 4, 128], BF16, "qkT", ctx)
            b.A = k.sb([128, 4, 128], BF16, "A", ctx)
            b.AT = k.sb([128, 4, 128], BF16, "AT", ctx)
            b.Tm = [k.sb([128, 4, 128], BF16, "Tm", ctx) for _ in range(2)]
            b.TTm = [k.sb([128, 4, 128], BF16, "TTm", ctx) for _ in range(2)]
            b.Am = k.sb([128, 4, 128], BF16, "Am", ctx)
            b.AmT = [k.sb([128, 4, 128], BF16, "AmT", ctx) for _ in range(2)]
            b.Um = k.sb([128, 4, 128], BF16, "Um", ctx)
            b.wT = k.sb([128, 4, 128], BF16, "wT", ctx)
            b.vnew = k.sb([128, 4, 128], BF16, "vnew", ctx)
            b.osb = k.sb([128, 4, 128], F32, "osb", ctx)
            b.junk = k.sb([128, 128], F32, "junk", ctx)
            b.ss = k.sb([128, 4], F32, "ss", ctx)
            b.dtk = k.sb([128, 4, 128], BF16, "dtk", ctx)
            b.doT = k.sb([128, 4, 128], BF16, "doT", ctx)
            return b

        BUFS = [mk() for _ in range(NB_)]
        ps = C.ps + [C.ps_norm]
        st_ = {"pi": 0}

        def nxt():
            p = ps[st_["pi"] % 7]
            st_["pi"] += 1
            return p

        def load(n):
            b = BUFS[n % NB_]
            tl = (n * 128) // TT
            k.dma(b.dn[:, :, :], C.dnT[:, n * 128:(n + 1) * 128].rearrange("(c p) t -> p c t", p=128),
                  [C.dnT_r[tl]], [b.dn])
            k.dma(b.gt[:, :], C.gtok[n * 128:(n + 1) * 128, :], [C.gtok_r[tl]], [b.gt])
            k.dma(b.ba[:, :], C.batok[n * 128:(n + 1) * 128, :], [C.batok_r[tl]], [b.ba])

        ident4 = C.ident_f[:, :].rearrange("p (o t) -> p o t", o=1).to_broadcast([128, 4, 128])
        m4 = lambda a: a[:, :].rearrange("p (o t) -> p o t", o=1).to_broadcast([128, 4, 128])
        v4 = lambda p_, w=128: p_[:, 0:4 * w].rearrange("p (h t) -> p h t", h=4)
        lm4 = lambda l: lvl[:, l, :].rearrange("p (o t) -> p o t", o=1).to_broadcast([128, 4, 128])
        identb4 = C.ident_bf[:, :].rearrange("p (o t) -> p o t", o=1).to_broadcast([128, 4, 128])

        def chunk(n):
            b = BUFS[n % NB_]
            sc, d, g_, b_ = b.sc, b.dn, b.gt, b.ba

            def bcol(j):
                return sc[:, j, :].rearrange("p (h o) -> p h o", o=1).to_broadcast([128, 4, 128])
            k.act(sc[:, BETA, :], b_[:, 0:4], AF.Exp, [b_], [sc], scale=-1.0)
            k.ts("dve", sc[:, BETA, :], sc[:, BETA, :], 1.0, None, ALU.add, None, [sc], [sc])
            k.op("dve", lambda e: e.reciprocal(sc[:, BETA, :], sc[:, BETA, :]), [sc], [sc])
            k.tt("dve", sc[:, TMP, :], b_[:, 4:8], dtb[:, :], ALU.add, [b_, dtb], [sc])
            k.act(sc[:, TMP, :], sc[:, TMP, :], AF.Exp, [sc], [sc])
            k.act(sc[:, TMP, :], sc[:, TMP, :], AF.Ln, [sc, one_c], [sc], bias=one_c[:, 0:1])
            k.tt("dve", sc[:, G, :], sc[:, TMP, :], nexpA[:, :], ALU.mult, [sc, nexpA], [sc])
            p = nxt()
            k.mm(p[:, 0:4], tri[:, :], sc[:, G, :], True, True, [tri, sc], [p])
            k.mm(p[:, 4:8], onesf[:, :], sc[:, G, :], True, True, [onesf, sc], [p])
            k.copy("dve", sc[:, GC:GLAST + 1, :], p[:, 0:8].rearrange("p (a h) -> p a h", a=2), [p], [sc])
            yield
            k.act(sc[:, E, :], sc[:, GC, :], AF.Exp, [sc], [sc])
            k.tt("dve", sc[:, TMP, :], sc[:, GLAST, :], sc[:, GC, :], ALU.subtract, [sc], [sc])
            k.act(sc[:, F_, :], sc[:, TMP, :], AF.Exp, [sc], [sc])
            k.act(sc[:, GL, :], sc[:, GLAST, :], AF.Exp, [sc], [sc])
            k.ts("dve", sc[:, NGC, :], sc[:, GC, :], -1.0, None, ALU.mult, None, [sc], [sc])
            k.tt("dve", sc[:, BE, :], sc[:, BETA, :], sc[:, E, :], ALU.mult, [sc], [sc])
            for j in range(8):
                k.tr(C.ps_bf[:, j * 128:(j + 1) * 128], d[:, 4 + j, :], C.ident_bf[:, :], [d, C.ident_bf], [C.ps_bf])
            k.copy("act", b.tok[:, :, :], C.ps_bf[:, 0:1024].rearrange("p (j t) -> p j t", j=8), [C.ps_bf], [b.tok])
            yield
            pB, pE, pG = nxt(), nxt(), nxt()
            for dgi, (col, pp) in enumerate(((BETA, pB), (E, pE), (GC, pG))):
                k.tt("pool" if dgi == 1 else "dve", b.dg[dgi][:, :, :], ident4, bcol(col), ALU.mult,
                     [C.ident_f, sc], [b.dg[dgi]])
                k.mm(pp[:, 0:512], onesf[:, :], b.dg[dgi][:, :, :].rearrange("p h t -> p (h t)"), True, True,
                     [onesf, b.dg[dgi]], [pp])
            k.tt("dve", b.kbT[:, :, :], d[:, 4:8, :], v4(pB), ALU.mult, [d, pB], [b.kbT])
            k.tt("dve", b.qeT[:, :, :], d[:, 0:4, :], v4(pE), ALU.mult, [d, pE], [b.qeT])
            k.tt("dve", b.X[:, :, :], v4(pG), m4(causT), ALU.add, [pG, causT], [b.X])
            for h in range(4):
                k.ts("dve", b.rhs0[:, h, 0:128], b.tok[:, 4 + h, :], sc[:, BETA, h:h + 1], None, ALU.mult, None,
                     [b.tok, sc], [b.rhs0])
                k.act(b.rhs0[:, h, 128:256], b.tok[:, h, :], AF.Copy, [b.tok, sc], [b.rhs0], scale=sc[:, BE, h:h + 1])
                k.act(b.kd[:, h, :], b.tok[:, h, :], AF.Copy, [b.tok, sc], [b.kd], scale=sc[:, F_, h:h + 1])
            yield
            for h in range(4):
                k.act(b.DcT[:, h, :], b.X[:, h, :], AF.Exp, [b.X, sc], [b.DcT], bias=sc[:, NGC, h:h + 1])
            k.tt("pool", b.DcsT[:, :, :], b.DcT[:, :, :], m4(strT), ALU.mult, [b.DcT, strT], [b.DcsT])
            pQ, pK = nxt(), nxt()
            for h in range(4):
                k.mm(pQ[:, h * 128:(h + 1) * 128], d[:, 4 + h, :], d[:, h, :], True, True, [d], [pQ])
                k.mm(pK[:, h * 128:(h + 1) * 128], d[:, 4 + h, :], b.kbT[:, h, :], True, True, [d, b.kbT], [pK])
            k.tt("dve", b.qkT[:, :, :], v4(pQ), b.DcT[:, :, :], ALU.mult, [pQ, b.DcT], [b.qkT])
            k.tt("dve", b.AT[:, :, :], v4(pK), b.DcsT[:, :, :], ALU.mult, [pK, b.DcsT], [b.AT])
            yield
            for h in range(4):
                k.tr(C.ps_bf[:, h * 128:(h + 1) * 128], b.AT[:, h, :], C.ident_bf[:, :], [b.AT, C.ident_bf], [C.ps_bf])
            k.copy("act", b.A[:, :, :], C.ps_bf[:, 0:512].rearrange("p (h t) -> p h t", h=4), [C.ps_bf], [b.A])
            Tc, TTc = b.Tm[0], b.TTm[0]
            k.tt("pool", b.AmT[0][:, :, :], b.AT[:, :, :], lm4(0), ALU.mult, [b.AT, lvl], [b.AmT[0]])
            k.tt("dve", TTc[:, :, :], identb4, b.AmT[0][:, :, :], ALU.subtract, [C.ident_bf, b.AmT[0]], [TTc])
            k.tt("pool", b.Am[:, :, :], b.A[:, :, :], lm4(7), ALU.mult, [b.A, lvl], [b.Am])
            k.tt("dve", Tc[:, :, :], identb4, b.Am[:, :, :], ALU.subtract, [C.ident_bf, b.Am], [Tc])
            yield
            for l in range(1, 7):
                Tn, TTn = b.Tm[l % 2], b.TTm[l % 2]
                amt = b.AmT[l % 2]
                k.tt("pool", amt[:, :, :], b.AT[:, :, :], lm4(l), ALU.mult, [b.AT, lvl], [amt])
                pU = nxt()
                for h in range(4):
                    k.mm(pU[:, h * 128:(h + 1) * 128], amt[:, h, :], Tc[:, h, :], True, True, [amt, Tc], [pU])
                k.copy("act", b.Um[:, :, :], v4(pU), [pU], [b.Um])
                pVT = nxt()
                for h in range(4):
                    k.mm(pVT[:, h * 128:(h + 1) * 128], b.Um[:, h, :], TTc[:, h, :], True, True, [b.Um, TTc], [pVT])
                if l < 6:
                    pV = nxt()
                    for h in range(4):
                        k.mm(pV[:, h * 128:(h + 1) * 128], TTc[:, h, :], b.Um[:, h, :], True, True, [TTc, b.Um], [pV])
                k.tt("dve", TTn[:, :, :], TTc[:, :, :], v4(pVT), ALU.subtract, [TTc, pVT], [TTn])
                if l < 6:
                    k.tt("dve", Tn[:, :, :], Tc[:, :, :], v4(pV), ALU.subtract, [Tc, pV], [Tn])
                Tc, TTc = Tn, TTn
                yield
            pY0, pY1 = nxt(), nxt()
            for h in range(4):
                py = (pY0, pY1)[h // 2]
                k.mm(py[:, (h % 2) * 256:(h % 2 + 1) * 256], TTc[:, h, :], b.rhs0[:, h, :], True, True,
                     [TTc, b.rhs0], [py])
            ycur = b.y
            k.copy("act", ycur[:, 0:2, :], pY0[:, 0:512].rearrange("p (h t) -> p h t", h=2), [pY0], [ycur])
            k.copy("dve", ycur[:, 2:4, :], pY1[:, 0:512].rearrange("p (h t) -> p h t", h=2), [pY1], [ycur])
            for h in range(4):
                k.tr(C.ps_bf[:, h * 128:(h + 1) * 128], ycur[:, h, 128:256], C.ident_bf[:, :], [ycur, C.ident_bf],
                     [C.ps_bf])
            k.copy("act", b.wT[:, :, :], C.ps_bf[:, 0:512].rearrange("p (h t) -> p h t", h=4), [C.ps_bf], [b.wT])
            yield
            p1 = nxt()
            for h in range(4):
                k.mm(p1[:, h * 128:(h + 1) * 128], b.wT[:, h, :], Sb[:, h, :], True, True, [b.wT, Sb], [p1])
            k.tt("dve", b.vnew[:, :, :], ycur[:, :, 0:128], v4(p1), ALU.subtract, [ycur, p1], [b.vnew])
            p2, p3 = nxt(), nxt()
            for h in range(4):
                k.mm(p3[:, h * 128:(h + 1) * 128], b.kd[:, h, :], b.vnew[:, h, :], True, True, [b.kd, b.vnew], [p3])
            for h in range(4):
                k.mm(p2[:, h * 128:(h + 1) * 128], b.qeT[:, h, :], Sb[:, h, :], True, False, [b.qeT, Sb], [p2])
                k.mm(p2[:, h * 128:(h + 1) * 128], b.qkT[:, h, :], b.vnew[:, h, :], False, True, [b.qkT, b.vnew], [p2])
            for h in range(4):
                k.stt("dve", St[:, h, :], St[:, h, :], sc[:, GL, h:h + 1], p3[:, h * 128:(h + 1) * 128],
                      ALU.mult, ALU.add, [St, sc, p3], [St])
            k.copy("pool", Sb[:, :, :], St[:, :, :], [St], [Sb])
            k.copy("act", b.osb[:, :, :], v4(p2), [p2], [b.osb])
            yield
            k.memset("pool", b.ss[:, :], 0.0, [b.ss])
            for h in range(4):
                k.act(b.junk[:, :], b.osb[:, h, :], AF.Square, [b.osb], [b.junk, b.ss], accum_out=b.ss[:, h:h + 1])
            k.act(b.ss[:, :], b.ss[:, :], AF.Ln, [b.ss, C.eps_col], [b.ss], bias=C.eps_col[:, 0:1], scale=1.0 / 128)
            k.act(b.ss[:, :], b.ss[:, :], AF.Exp, [b.ss], [b.ss], scale=-0.5)
            for h in range(4):
                k.stt("dve", b.osb[:, h, :], b.osb[:, h, :], b.ss[:, h:h + 1], ngb[:, :], ALU.mult, ALU.mult,
                      [b.osb, b.ss, ngb], [b.osb])
            k.tt("pool", b.dtk[:, :, :], b.osb[:, :, :], g_[:, :].rearrange("p (h t) -> p h t", h=4), ALU.mult,
                 [b.osb, g_], [b.dtk])
            yield
            for h in range(4):
                k.tr(C.ps_bf[:, h * 128:(h + 1) * 128], b.dtk[:, h, :], C.ident_bf[:, :], [b.dtk, C.ident_bf], [C.ps_bf])
            k.copy("act", b.doT[:, :, :], C.ps_bf[:, 0:512].rearrange("p (h t) -> p h t", h=4), [C.ps_bf], [b.doT])
            k.dma(C.mixT[512:1024, n * 128:(n + 1) * 128].rearrange("(h p) t -> p h t", p=128), b.doT[:, :, :],
                  [b.doT], [C.mixT_r[(n * 128) // TT]], sem=b.doT)
            if n + NB_ < NCH:
                load(n + NB_)

        for n in range(min(NB_, NCH)):
            load(n)
        run_pipelined(chunk, NCH, 3, maxlive=NB_)
    k.S.end_phase()


SEQ = 8192
N_ACTIVE = 2


def kernel(**inputs):
    S = SEQ
    nc = build(S)
    consts = host_consts(S)
    in_maps = [core_inputs(inputs, b, S, consts) for b in range(N_ACTIVE)]
    res = run_bass_kernel_spmd(nc, in_maps, core_ids=list(range(N_ACTIVE)))
    out = np.stack([np.ascontiguousarray(res.results[b]["outT"].T) for b in range(N_ACTIVE)], axis=0)
    return out.astype(np.float32)
```
